# Optimizing a Trainium2 kernel written in Bass

```python
import math
import jax, jax.numpy as jnp
from jax import lax
import numpy as np

D_MODEL = 1024
BATCH = 4
SEQ = 8192
DEPTH = 1

N_META = 16
HEAD_DIM = 64
N_Q_HEADS = 16
N_KV_HEADS = 4
GQA_GROUP = N_Q_HEADS // N_KV_HEADS
WINDOW = 128
SSM_WIDTH = D_MODEL
SSM_GROUP_CH = 16
SSM_GROUPS = SSM_WIDTH // SSM_GROUP_CH
SSM_STATE = 64
D_FF = -(-8 * D_MODEL // (3 * 256)) * 256
Q_W = N_Q_HEADS * HEAD_DIM
KV_W = N_KV_HEADS * HEAD_DIM
IN_COLS = Q_W + 2 * KV_W + SSM_WIDTH + 2 * D_MODEL
EPS = 1e-6
DT_MIN = 1e-3
DT_MAX = 1e-1

kernel_name = "hybrid_swa_sink_s5_gated_block"


def rmsnorm(x, g):
    xf = x.astype(jnp.float32)
    xf = xf * lax.rsqrt(jnp.mean(xf * xf, axis=-1, keepdims=True) + EPS)
    return (xf * g.astype(jnp.float32)).astype(x.dtype)


def softmax_with_sink(s, sink):
    sink = sink.astype(jnp.float32)[:, :, None, None]
    m = jnp.maximum(jnp.max(s, axis=-1, keepdims=True), sink)
    e = jnp.exp(s - m)
    return e / (jnp.sum(e, axis=-1, keepdims=True) + jnp.exp(sink - m))


def sliding_window_attention(q, k, v, sinks):
    b, L = q.shape[0], q.shape[1]
    n_blk = (L - N_META) // WINDOW
    scale = HEAD_DIM ** -0.5
    sink = sinks.reshape(N_KV_HEADS, GQA_GROUP)
    qm = q[:, :N_META].reshape(b, N_META, N_KV_HEADS, GQA_GROUP, HEAD_DIM)
    km, vm = k[:, :N_META], v[:, :N_META]
    s_m = jnp.einsum('bqkrd,bskd->bkrqs', qm, km, preferred_element_type=jnp.float32) * scale
    causal = jnp.tril(jnp.ones((N_META, N_META), dtype=bool))
    s_m = jnp.where(causal, s_m, -jnp.inf)
    p_m = softmax_with_sink(s_m, sink).astype(v.dtype)
    o_m = jnp.einsum('bkrqs,bskd->bqkrd', p_m, vm).reshape(b, N_META, Q_W)
    qb = q[:, N_META:].reshape(b, n_blk, WINDOW, N_KV_HEADS, GQA_GROUP, HEAD_DIM)
    kb = k[:, N_META:].reshape(b, n_blk, WINDOW, N_KV_HEADS, HEAD_DIM)
    vb = v[:, N_META:].reshape(b, n_blk, WINDOW, N_KV_HEADS, HEAD_DIM)
    pad = ((0, 0), (1, 0), (0, 0), (0, 0), (0, 0))
    k_prev = jnp.pad(kb[:, :-1], pad)
    v_prev = jnp.pad(vb[:, :-1], pad)
    meta_shape = (b, n_blk, N_META, N_KV_HEADS, HEAD_DIM)
    k_win = jnp.concatenate([jnp.broadcast_to(km[:, None], meta_shape), k_prev, kb], axis=2)
    v_win = jnp.concatenate([jnp.broadcast_to(vm[:, None], meta_shape), v_prev, vb], axis=2)
    s = jnp.einsum('bnqkrd,bnskd->bnkrqs', qb, k_win, preferred_element_type=jnp.float32) * scale
    qi = jnp.arange(WINDOW)[:, None]
    kj = jnp.arange(WINDOW)[None, :]
    blk = jnp.arange(n_blk)[:, None, None]
    meta_vis = jnp.ones((n_blk, WINDOW, N_META), dtype=bool)
    prev_vis = (kj > qi)[None] & (blk > 0)
    cur_vis = jnp.broadcast_to((kj <= qi)[None], (n_blk, WINDOW, WINDOW))
    mask = jnp.concatenate([meta_vis, prev_vis, cur_vis], axis=-1)
    s = jnp.where(mask[None, :, None, None], s, -jnp.inf)
    p = softmax_with_sink(s, sink).astype(v.dtype)
    o_r = jnp.einsum('bnkrqs,bnskd->bnqkrd', p, v_win).reshape(b, n_blk * WINDOW, Q_W)
    return jnp.concatenate([o_m, o_r], axis=1)


def s5_ssm(u, lam_re, lam_im, log_dt, b_re, b_im, c_re, c_im, d_skip):
    bsz, L = u.shape[0], u.shape[1]
    uf = u.astype(jnp.float32)
    ug = uf.reshape(bsz, L, SSM_GROUPS, SSM_GROUP_CH)
    dt = jnp.exp(log_dt.astype(jnp.float32))[:, None]
    lr, li = lam_re.astype(jnp.float32), lam_im.astype(jnp.float32)
    mag = jnp.exp(lr * dt)
    ar, ai = mag * jnp.cos(li * dt), mag * jnp.sin(li * dt)
    den = lr * lr + li * li
    nr, ni = ar - 1.0, ai
    fr, fi = (nr * lr + ni * li) / den, (ni * lr - nr * li) / den
    br, bi = b_re.astype(jnp.float32), b_im.astype(jnp.float32)
    bbar_re = fr[..., None] * br - fi[..., None] * bi
    bbar_im = fr[..., None] * bi + fi[..., None] * br
    xr = jnp.einsum('gpc,blgc->blgp', bbar_re, ug)
    xi = jnp.einsum('gpc,blgc->blgp', bbar_im, ug)
    a_re = jnp.broadcast_to(ar[None, None], (1, L, SSM_GROUPS, SSM_STATE))
    a_im = jnp.broadcast_to(ai[None, None], (1, L, SSM_GROUPS, SSM_STATE))

    def combine(e1, e2):
        a1r, a1i, b1r, b1i = e1
        a2r, a2i, b2r, b2i = e2
        return (a1r * a2r - a1i * a2i,
                a1r * a2i + a1i * a2r,
                a2r * b1r - a2i * b1i + b2r,
                a2r * b1i + a2i * b1r + b2i)

    _, _, sr, si = lax.associative_scan(combine, (a_re, a_im, xr, xi), axis=1)
    y = (jnp.einsum('gcp,blgp->blgc', c_re.astype(jnp.float32), sr)
         - jnp.einsum('gcp,blgp->blgc', c_im.astype(jnp.float32), si))
    y = y.reshape(bsz, L, SSM_WIDTH) + d_skip.astype(jnp.float32) * uf
    return y


def setup_inputs(seed: int = 0) -> dict:
    key = jax.random.key(seed)
    ks = jax.random.split(key, 24)
    f32 = jnp.float32
    nrm = lambda k, shape, s: jax.random.normal(k, shape, f32) * s
    gain = lambda k, shape: 1.0 + 0.02 * jax.random.normal(k, shape, f32)
    n_idx = jnp.arange(SSM_STATE, dtype=f32)
    return {
        "x": nrm(ks[0], (BATCH, SEQ, D_MODEL), 1.0),
        "meta_tokens": nrm(ks[1], (N_META, D_MODEL), 1.0),
        "norm_mix": gain(ks[2], (DEPTH, D_MODEL)),
        "w_in": nrm(ks[3], (DEPTH, D_MODEL, IN_COLS), D_MODEL ** -0.5),
        "q_norm": gain(ks[4], (DEPTH, HEAD_DIM)),
        "k_norm": gain(ks[5], (DEPTH, HEAD_DIM)),
        "attn_sinks": nrm(ks[6], (DEPTH, N_Q_HEADS), 0.5),
        "lam_re": -0.5 + nrm(ks[7], (DEPTH, SSM_GROUPS, SSM_STATE), 0.01),
        "lam_im": math.pi * n_idx + nrm(ks[8], (DEPTH, SSM_GROUPS, SSM_STATE), 0.01),
        "log_dt": jax.random.uniform(ks[9], (DEPTH, SSM_GROUPS), f32, math.log(DT_MIN), math.log(DT_MAX)),
        "ssm_b_re": nrm(ks[10], (DEPTH, SSM_GROUPS, SSM_STATE, SSM_GROUP_CH), (2 * SSM_GROUP_CH) ** -0.5),
        "ssm_b_im": nrm(ks[11], (DEPTH, SSM_GROUPS, SSM_STATE, SSM_GROUP_CH), (2 * SSM_GROUP_CH) ** -0.5),
        "ssm_c_re": nrm(ks[12], (DEPTH, SSM_GROUPS, SSM_GROUP_CH, SSM_STATE), (2 * SSM_STATE) ** -0.5),
        "ssm_c_im": nrm(ks[13], (DEPTH, SSM_GROUPS, SSM_GROUP_CH, SSM_STATE), (2 * SSM_STATE) ** -0.5),
        "ssm_d": nrm(ks[14], (DEPTH, SSM_WIDTH), 1.0),
        "w_glu": nrm(ks[15], (DEPTH, SSM_WIDTH, 2 * D_MODEL), SSM_WIDTH ** -0.5),
        "attn_branch_norm": gain(ks[16], (DEPTH, D_MODEL)),
        "ssm_branch_norm": gain(ks[17], (DEPTH, D_MODEL)),
        "w_out": nrm(ks[18], (DEPTH, D_MODEL, D_MODEL), D_MODEL ** -0.5),
        "norm_ffn": gain(ks[19], (DEPTH, D_MODEL)),
        "w_ffn_in": nrm(ks[20], (DEPTH, D_MODEL, 2 * D_FF), D_MODEL ** -0.5),
        "w_ffn_out": nrm(ks[21], (DEPTH, D_FF, D_MODEL), D_FF ** -0.5),
    }


def reference(x, meta_tokens, norm_mix, w_in, q_norm, k_norm, attn_sinks, lam_re, lam_im, log_dt,
              ssm_b_re, ssm_b_im, ssm_c_re, ssm_c_im, ssm_d, w_glu, attn_branch_norm, ssm_branch_norm,
              w_out, norm_ffn, w_ffn_in, w_ffn_out):
    b = x.shape[0]
    meta = jnp.broadcast_to(meta_tokens.astype(x.dtype)[None], (b, N_META, D_MODEL))
    h = jnp.concatenate([meta, x], axis=1)
    L = h.shape[1]
    offs = np.cumsum([Q_W, KV_W, KV_W, SSM_WIDTH, D_MODEL]).tolist()
    for i in range(DEPTH):
        xn = rmsnorm(h, norm_mix[i])
        proj = xn @ w_in[i]
        q, k, v, u, g_att, g_ssm = jnp.split(proj, offs, axis=-1)
        q = rmsnorm(q.reshape(b, L, N_Q_HEADS, HEAD_DIM), q_norm[i])
        k = rmsnorm(k.reshape(b, L, N_KV_HEADS, HEAD_DIM), k_norm[i])
        v = v.reshape(b, L, N_KV_HEADS, HEAD_DIM)
        attn = sliding_window_attention(q, k, v, attn_sinks[i])
        y = s5_ssm(u, lam_re[i], lam_im[i], log_dt[i], ssm_b_re[i], ssm_b_im[i],
                   ssm_c_re[i], ssm_c_im[i], ssm_d[i])
        z = jax.nn.gelu(y).astype(h.dtype)
        za, zb = jnp.split(z @ w_glu[i], 2, axis=-1)
        ssm = za * jax.nn.sigmoid(zb)
        merged = (jax.nn.sigmoid(g_att) * rmsnorm(attn, attn_branch_norm[i])
                  + jax.nn.sigmoid(g_ssm) * rmsnorm(ssm, ssm_branch_norm[i]))
        h = h + merged @ w_out[i]
        hn = rmsnorm(h, norm_ffn[i])
        gate, up = jnp.split(hn @ w_ffn_in[i], 2, axis=-1)
        h = h + (jax.nn.silu(gate) * up) @ w_ffn_out[i]
    return h[:, N_META:]
```

```python
import math
import contextlib
import numpy as np
import concourse.bass as bass
import concourse.mybir as mybir
from concourse.bass_utils import run_bass_kernel_spmd

F32 = mybir.dt.float32
BF16 = mybir.dt.bfloat16
AF = mybir.ActivationFunctionType
ALU = mybir.AluOpType
AX = mybir.AxisListType
ENGS = ("tensor", "vector", "scalar", "gpsimd", "sync")
D = 1024
DFF = 2816
NEG = -30000.0


class FW:
    def __init__(self, nc, n_dma_sems=40):
        self.nc = nc
        self.ops = {e: [] for e in ENGS}
        self.cnt = {e: 0 for e in ENGS}
        self.known = {e: {} for e in ENGS}
        self.last_w = {}
        self.readers = {}
        self.n_dma_sems = n_dma_sems
        self.dma_gen = [0] * n_dma_sems
        self.dma_rr = 0
        self.sem_names = [f"s_{e}" for e in ENGS] + [f"d_{i}" for i in range(n_dma_sems)]

    def _deps(self, reads, writes):
        evs = []
        for k in reads:
            if k in self.last_w:
                evs.append(self.last_w[k])
        for k in writes:
            if k in self.last_w:
                evs.append(self.last_w[k])
            evs.extend(self.readers.get(k, ()))
        return evs

    def _commit(self, ev, reads, writes):
        for k in reads:
            self.readers.setdefault(k, []).append(ev)
        for k in writes:
            self.last_w[k] = ev
            self.readers[k] = []

    def _waits(self, eng, evs):
        best = {}
        for (s, v) in evs:
            if v > best.get(s, 0):
                best[s] = v
        out = []
        kn = self.known[eng]
        for s, v in best.items():
            if eng == "tensor" and s == "s_tensor":
                continue
            if kn.get(s, 0) >= v:
                continue
            kn[s] = v
            out.append((s, v))
        return out

    def op(self, eng, fn, reads=(), writes=(), inc=True):
        evs = self._deps(reads, writes)
        waits = self._waits(eng, evs)
        sname = f"s_{eng}"
        ev = (sname, self.cnt[eng] + 1)
        if inc:
            self.cnt[eng] += 1
        self.ops[eng].append((waits, fn, (sname, 1) if inc else None))
        self._commit(ev, reads, writes)
        return ev

    def dma(self, queue, out, in_, reads=(), writes=(), **kw):
        i = self.dma_rr
        self.dma_rr = (self.dma_rr + 1) % self.n_dma_sems
        sname = f"d_{i}"
        evs = self._deps(reads, writes)
        if self.dma_gen[i] > 0:
            evs.append((sname, 16 * self.dma_gen[i]))
        waits = self._waits(queue, evs)
        self.dma_gen[i] += 1
        ev = (sname, 16 * self.dma_gen[i])
        self.ops[queue].append((waits, lambda e: e.dma_start(out=out, in_=in_, **kw), (sname, 16)))
        self._commit(ev, reads, writes)
        return ev

    def barrier(self):
        fin = []
        for e in ENGS:
            if self.cnt[e] > 0:
                fin.append((f"s_{e}", self.cnt[e]))
        for i in range(self.n_dma_sems):
            if self.dma_gen[i] > 0:
                fin.append((f"d_{i}", 16 * self.dma_gen[i]))
        for e in ENGS:
            w = self._waits(e, fin)
            if w:
                self.ops[e].append((w, None, None))
        self.last_w = {}
        self.readers = {}

    def emit(self):
        nc = self.nc
        self.barrier()
        with contextlib.ExitStack() as st:
            sems = {n: st.enter_context(nc.semaphore(n)) for n in self.sem_names}
            block = st.enter_context(nc.Block())

            def mk(engname):
                lst = self.ops[engname]

                def body(eng):
                    for (waits, fn, inc) in lst:
                        for (s, v) in waits:
                            eng.wait_ge(sems[s], v)
                        if fn is None:
                            continue
                        ins = fn(eng)
                        if inc is not None:
                            ins.then_inc(sems[inc[0]], inc[1])
                return body

            block.tensor(mk("tensor"))
            block.vector(mk("vector"))
            block.scalar(mk("scalar"))
            block.gpsimd(mk("gpsimd"))
            block.sync(mk("sync"))


class Arena:
    def __init__(self, nc, base=16640, limit=224 * 1024):
        self.nc, self.off, self.limit, self.n = nc, base, limit, 0

    def alloc(self, name, shape, dt):
        per = int(np.prod(shape[1:])) * (4 if dt == F32 else 2)
        per = (per + 63) // 64 * 64
        assert self.off + per <= self.limit, (name, self.off, per)
        self.n += 1
        t = self.nc.alloc_sbuf_tensor_at(f"{name}_{self.n}_{self.off}", list(shape), dt, offset=self.off)
        self.off += per
        return t.ap()

    def mark(self):
        return self.off

    def reset(self, off):
        self.off = off


def build(NM, NP, upto=9):
    nc = bass.Bass("TRN2", target_bir_lowering=False)
    fw = FW(nc)
    TM_, TP_ = NM // 128, NP // 128
    NS = TP_ + TM_
    NK = 2 + TM_

    def din(name, shape, dt=F32):
        return nc.dram_tensor(name, list(shape), dt, kind="ExternalInput").ap()

    xmain = din("xmain", [NM, D]); xpre = din("xpre", [NP, D]); xctx = din("xctx", [256, D])
    w_in = din("w_in", [9, 128, 8, 512]); w_glu = din("w_glu", [4, 128, 8, 512]); w_out = din("w_out", [2, 128, 8, 512])
    w_f1 = din("w_f1", [11, 128, 8, 512]); w_f2 = din("w_f2", [2, 128, 22, 512])
    gains = din("gains", [4, 128, D])
    gq = din("gq", [128, 256]); gk = din("gk", [128, 256]); sinks = din("sinks", [128, 16])
    masks = din("masks", [3, 128, 512])
    ident = din("ident", [128, 128])
    lam = din("lam", [3, 128, 32])
    bpad = din("bpad", [128, 32, 2, 128]); cpad = din("cpad", [2, 128, 32, 128]); dcol = din("dcol", [128, 8])
    out = nc.dram_tensor("out", [NM, D], F32, kind="ExternalOutput").ap()

    def dscr(name, shape, dt):
        return nc.dram_tensor(name, list(shape), dt, kind="Internal").ap()

    uT_s = dscr("uT_s", [NS, 128, 8, 128], BF16); qT_s = dscr("qT_s", [TM_, 128, 8, 128], BF16)
    kT_s = dscr("kT_s", [NK, 128, 2, 128], BF16); v_s = dscr("v_s", [NK, 128, 4, 65], BF16)
    g_s = dscr("g_s", [TM_, 128, 2048], BF16); zT_s = dscr("zT_s", [TM_, 128, 8, 128], BF16)
    h1_s = dscr("h1_s", [NM, D], F32)

    ar = Arena(nc)
    identf = ar.alloc("identf", [128, 128], F32); identb = ar.alloc("identb", [128, 128], BF16)
    gsb = ar.alloc("gsb", [128, 4, D], F32)
    fw.dma("sync", identf, ident, writes=["identf"])
    fw.op("vector", lambda e: e.tensor_copy(out=identb, in_=identf), reads=["identf"], writes=["identb"])
    fw.dma("sync", gsb, gains.rearrange("a p d -> p a d"), writes=["gsb"])
    pbank = [nc.alloc_psum_tensor(f"pb{i}", [128, 512], F32).ap() for i in range(8)]
    pcnt = [0]

    def psum():
        i = pcnt[0] % 8
        pcnt[0] += 1
        return pbank[i], f"pb{i}"

    rr = [0]

    def alt():
        rr[0] += 1
        return "vector" if rr[0] % 2 else "gpsimd"

    base0 = ar.mark()

    def load_weights(dst, src, npan, kk, key):
        m = ar.mark()
        st = [ar.alloc("wst", [128, 8, 512], F32) for _ in range(2)]
        n = 0
        for pi in range(npan):
            for k0 in range(0, kk, 8):
                kc = min(8, kk - k0)
                s = st[n % 2]
                fw.dma("sync", s[:, :kc, :], src[pi][:, k0:k0 + kc, :], writes=[f"wst{n % 2}"])
                eng = alt()
                fw.op(eng, lambda e, s=s, pi=pi, k0=k0, kc=kc: e.tensor_copy(out=dst[:, k0:k0 + kc, pi * 512:(pi + 1) * 512], in_=s[:, :kc, :]),
                      reads=[f"wst{n % 2}"], writes=[key])
                n += 1
        fw.barrier()
        ar.reset(m)

    def rms_scale(xin, gidx, xn_out, rkeys, wkeys, ncol=D):
        fw.op("scalar", lambda e: e.activation(out=junk[:, :ncol], in_=xin, func=AF.Square, accum_out=ssq),
              reads=rkeys, writes=["junk", "ssq"])
        fw.op("vector", lambda e: e.tensor_scalar(out=ssq, in0=ssq, scalar1=1.0 / ncol, scalar2=1e-6, op0=ALU.mult, op1=ALU.add),
              reads=["ssq"], writes=["ssq"])
        fw.op("scalar", lambda e: e.activation(out=ssq, in_=ssq, func=AF.Sqrt), reads=["ssq"], writes=["ssq"])
        fw.op("vector", lambda e: e.reciprocal(out=ssq, in_=ssq), reads=["ssq"], writes=["ssq"])
        fw.op("vector", lambda e: e.scalar_tensor_tensor(out=xn_out, in0=xin, scalar=ssq, in1=gsb[:, gidx, :ncol],
                                                         op0=ALU.mult, op1=ALU.mult),
              reads=list(rkeys) + ["ssq", "gsb"], writes=wkeys)

    def transpose8(src_bf, dstT, rkey, wkey, n=8):
        for half in range((n + 3) // 4):
            p, pk = psum()
            m = min(4, n - half * 4)
            for j in range(m):
                c = half * 4 + j
                fw.op("tensor", lambda e, c=c, j=j, p=p: e.matmul(p[:, j * 128:(j + 1) * 128], lhsT=src_bf[:, c * 128:(c + 1) * 128],
                                                                 rhs=identb, start=True, stop=True),
                      reads=[rkey, "identb"], writes=[pk], inc=(j == m - 1))
            fw.op("vector", lambda e, p=p, half=half, m=m: e.tensor_copy(
                out=dstT[:, half * 4:half * 4 + m, :], in_=p[:, :m * 128].rearrange("p (a b) -> p a b", b=128)),
                reads=[pk], writes=[wkey])

    junk = ar.alloc("junk", [128, D], F32); ssq = ar.alloc("ssq", [128, 1], F32)
    base1 = ar.mark()

    dbg = {}
    V = lambda fn, r, w: fw.op("vector", fn, reads=r, writes=w)
    A_ = lambda fn, r, w: fw.op("scalar", fn, reads=r, writes=w)
    G_ = lambda fn, r, w: fw.op("gpsimd", fn, reads=r, writes=w)

    def mm(out_ap, lhsT, rhs, start, stop, reads, pk, inc):
        fw.op("tensor", lambda e: e.matmul(out_ap, lhsT=lhsT, rhs=rhs, start=start, stop=stop), reads=reads, writes=[pk], inc=inc)

    dcs = ar.alloc("dcs", [128, 8], F32)
    esink = ar.alloc("esink", [128, 16], F32); gqk = ar.alloc("gqk", [128, 256], F32)
    maskb = ar.alloc("maskb", [128, 3, 512], BF16)
    base_persist = ar.mark()
    st32 = ar.alloc("st32", [128, 8192], F32)
    fw.dma("sync", dcs, dcol, writes=["dcs"])
    fw.dma("sync", st32[:, 0:16], sinks, writes=["a"])
    A_(lambda e: e.activation(out=esink, in_=st32[:, 0:16], func=AF.Exp), ["a"], ["esink"])
    fw.dma("sync", st32[:, 1024:1280], gq, writes=["b"])
    fw.dma("sync", st32[:, 2048:2304], gk, writes=["c"])
    V(lambda e: e.tensor_tensor(out=gqk, in0=st32[:, 1024:1280], in1=st32[:, 2048:2304], op=ALU.mult), ["b", "c"], ["gqk"])
    fw.dma("sync", st32[:, 4096:5632].rearrange("p (a c) -> p a c", a=3), masks.rearrange("a p c -> p a c"), writes=["d"])
    V(lambda e: e.tensor_copy(out=maskb, in_=st32[:, 4096:5632].rearrange("p (a c) -> p a c", a=3)), ["d"], ["maskb"])
    fw.barrier()
    ar.reset(base_persist)

    m1 = ar.mark()
    Win = ar.alloc("Win", [128, 8, 4608], BF16)
    load_weights(Win, w_in, 9, 8, "Win")
    QO, KVO, UO, GO = 0, 1024, 1536, 2560
    xt = [ar.alloc("xt", [128, D], F32) for _ in range(2)]
    xnb = [ar.alloc("xnb", [128, D], BF16) for _ in range(2)]
    xnT = [ar.alloc("xnT", [128, 8, 128], BF16) for _ in range(2)]
    uTb = [ar.alloc("uTb", [128, 8, 128], BF16) for _ in range(2)]
    qsq = ar.alloc("qsq", [128, 512], F32)
    qss = ar.alloc("qss", [128, 8], F32)
    qn = [ar.alloc("qn", [128, D], BF16) for _ in range(2)]
    qTb = [ar.alloc("qTb", [128, 8, 128], BF16) for _ in range(2)]
    kf = ar.alloc("kf", [128, 256], F32)
    kn = [ar.alloc("kn", [128, 256], BF16) for _ in range(2)]
    kTb = [ar.alloc("kTb", [128, 2, 128], BF16) for _ in range(2)]
    vab = [ar.alloc("vab", [128, 4, 65], BF16) for _ in range(2)]
    gb = [ar.alloc("gb", [128, 2048], BF16) for _ in range(2)]
    for b in range(2):
        V(lambda e, b=b: e.memset(vab[b][:, :, 64:65], 1.0), [], [f"vab{b}"])

    tiles = [("ctx", xctx[0:128, :], 0, None), ("ctx", xctx[128:256, :], 1, None)]
    for t in range(TP_):
        tiles.append(("pre", xpre[t * 128:(t + 1) * 128, :], t, None))
    for t in range(TM_):
        tiles.append(("main", xmain[t * 128:(t + 1) * 128, :], TP_ + t, t))

    def headnorm(p, pk, ncol, nh, dst, dkey, gain=None):
        A_(lambda e: e.activation(out=qsq[:, :ncol], in_=p[:, :ncol], func=AF.Square), [pk], ["qsq"])
        V(lambda e: e.tensor_reduce(out=qss[:, :nh], in_=qsq[:, :ncol].rearrange("p (h d) -> p h d", d=64), axis=AX.X, op=ALU.add), ["qsq"], ["qss"])
        V(lambda e: e.tensor_scalar(out=qss[:, :nh], in0=qss[:, :nh], scalar1=1.0 / 64, scalar2=1e-6, op0=ALU.mult, op1=ALU.add), ["qss"], ["qss"])
        A_(lambda e: e.activation(out=qss[:, :nh], in_=qss[:, :nh], func=AF.Sqrt), ["qss"], ["qss"])
        V(lambda e: e.reciprocal(out=qss[:, :nh], in_=qss[:, :nh]), ["qss"], ["qss"])
        rb = qss[:, :nh].unsqueeze(2).broadcast_to([128, nh, 64])
        if gain is None:
            V(lambda e: e.tensor_tensor(out=dst.rearrange("p (h d) -> p h d", d=64), in0=p[:, :ncol].rearrange("p (h d) -> p h d", d=64), in1=rb, op=ALU.mult),
              [pk, "qss"], [dkey])
        else:
            V(lambda e: e.tensor_tensor(out=kf.rearrange("p (h d) -> p h d", d=64), in0=p[:, :ncol].rearrange("p (h d) -> p h d", d=64), in1=rb, op=ALU.mult),
              [pk, "qss"], ["kf"])
            V(lambda e: e.tensor_tensor(out=dst, in0=kf, in1=gain, op=ALU.mult), ["kf", "gqk"], [dkey])

    for ti, (kind, src, sidx, midx) in enumerate(tiles):
        b = ti % 2
        fw.dma("sync", xt[b], src, writes=[f"xt{b}"])
        rms_scale(xt[b], 0, xnb[b], [f"xt{b}"], [f"xnb{b}"])
        transpose8(xnb[b], xnT[b], f"xnb{b}", f"xnT{b}")
        if kind in ("pre", "main"):
            for half in range(2):
                p, pk = psum()
                for cl in range(4):
                    ct = half * 4 + cl
                    for k in range(8):
                        mm(p[:, cl * 128:(cl + 1) * 128], Win[:, k, UO + ct * 128:UO + (ct + 1) * 128], xnT[b][:, k, :], k == 0, k == 7,
                           [f"xnT{b}", "Win"], pk, (cl == 3 and k == 7))
                V(lambda e, p=p, half=half, b=b: e.tensor_copy(out=uTb[b][:, half * 4:(half + 1) * 4, :], in_=p.rearrange("p (a c) -> p a c", c=128)),
                  [pk], [f"uTb{b}"])
            fw.dma("sync", uT_s[sidx], uTb[b], reads=[f"uTb{b}"], writes=[("uT_s", sidx)])
        if kind == "main":
            for half in range(2):
                p, pk = psum()
                for k in range(8):
                    mm(p, xnT[b][:, k, :], Win[:, k, QO + half * 512:QO + (half + 1) * 512], k == 0, k == 7, [f"xnT{b}", "Win"], pk, k == 7)
                headnorm(p, pk, 512, 8, qn[b][:, half * 512:(half + 1) * 512], f"qn{b}")
            transpose8(qn[b], qTb[b], f"qn{b}", f"qTb{b}")
            fw.dma("sync", qT_s[midx], qTb[b], reads=[f"qTb{b}"], writes=[("qT_s", midx)])
            for j in range(4):
                p, pk = psum()
                for k in range(8):
                    mm(p, xnT[b][:, k, :], Win[:, k, GO + j * 512:GO + (j + 1) * 512], k == 0, k == 7, [f"xnT{b}", "Win"], pk, k == 7)
                A_(lambda e, p=p, j=j, b=b: e.activation(out=gb[b][:, j * 512:(j + 1) * 512], in_=p, func=AF.Sigmoid), [pk], [f"gb{b}"])
            fw.dma("sync", g_s[midx], gb[b], reads=[f"gb{b}"], writes=[("g_s", midx)])
        if kind in ("ctx", "main"):
            kidx = sidx if kind == "ctx" else 2 + midx
            p, pk = psum()
            for k in range(8):
                mm(p, xnT[b][:, k, :], Win[:, k, KVO:KVO + 512], k == 0, k == 7, [f"xnT{b}", "Win"], pk, k == 7)
            headnorm(p, pk, 256, 4, kn[b], f"kn{b}", gain=gqk)
            V(lambda e, p=p, b=b: e.tensor_copy(out=vab[b][:, :, 0:64], in_=p[:, 256:512].rearrange("p (h d) -> p h d", d=64)), [pk], [f"vab{b}"])
            transpose8(kn[b], kTb[b], f"kn{b}", f"kTb{b}", n=2)
            fw.dma("sync", kT_s[kidx], kTb[b], reads=[f"kTb{b}"], writes=[("kT_s", kidx)])
            fw.dma("sync", v_s[kidx], vab[b], reads=[f"vab{b}"], writes=[("v_s", kidx)])
    fw.barrier()
    ar.reset(m1)

    if upto <= 1:
        fw.emit()
        return nc
    lamsb = ar.alloc("lamsb", [128, 3, 32], F32)
    fw.dma("sync", lamsb, lam.rearrange("a p g -> p a g"), writes=["lam"])
    smn = ["dt", "th", "rho", "sn", "cs", "ar", "ai", "fr", "fi", "t1", "t2", "t3", "den", "wr", "wi", "w128r", "w128i", "mk", "x2"]
    sm = {n: ar.alloc(n, [128, 32], F32) for n in smn}
    Er = ar.alloc("Er", [128, 32, 128], F32); Ei = ar.alloc("Ei", [128, 32, 128], F32)
    Bp = ar.alloc("Bp", [128, 32, 2, 128], BF16)
    Cfr = ar.alloc("Cfr", [128, 32, 128], BF16); Cfi = ar.alloc("Cfi", [128, 32, 128], BF16)
    cR = ar.alloc("cR", [128, 2, 32], F32)
    base_ssm = ar.mark()
    lr, li, ld = lamsb[:, 0, :], lamsb[:, 1, :], lamsb[:, 2, :]
    K = ["ssm0"]
    A_(lambda e: e.activation(out=sm["dt"], in_=ld, func=AF.Exp), ["lam"], K)
    V(lambda e: e.tensor_tensor(out=sm["th"], in0=li, in1=sm["dt"], op=ALU.mult), K, K)
    V(lambda e: e.tensor_tensor(out=sm["t1"], in0=lr, in1=sm["dt"], op=ALU.mult), K, K)
    A_(lambda e: e.activation(out=sm["rho"], in_=sm["t1"], func=AF.Exp), K, K)
    for _ in range(5):
        V(lambda e: e.tensor_single_scalar(out=sm["mk"], in_=sm["th"], scalar=math.pi, op=ALU.is_gt), K, K)
        V(lambda e: e.scalar_tensor_tensor(out=sm["th"], in0=sm["mk"], scalar=-2.0 * math.pi, in1=sm["th"], op0=ALU.mult, op1=ALU.add), K, K)
    V(lambda e: e.tensor_scalar(out=sm["t3"], in0=sm["th"], scalar1=0.125, scalar2=None, op0=ALU.mult), K, K)
    V(lambda e: e.tensor_tensor(out=sm["x2"], in0=sm["t3"], in1=sm["t3"], op=ALU.mult), K, K)

    def horner(o, coefs):
        V(lambda e: e.memset(o, coefs[0]), K, K)
        for c in coefs[1:]:
            V(lambda e: e.tensor_tensor(out=o, in0=o, in1=sm["x2"], op=ALU.mult), K, K)
            V(lambda e, c=c: e.tensor_scalar(out=o, in0=o, scalar1=float(c), scalar2=None, op0=ALU.add), K, K)

    horner(sm["sn"], [1 / 39916800.0 * -1, 1 / 362880.0, -1 / 5040.0, 1 / 120.0, -1 / 6.0, 1.0])
    V(lambda e: e.tensor_tensor(out=sm["sn"], in0=sm["sn"], in1=sm["t3"], op=ALU.mult), K, K)
    horner(sm["cs"], [-1 / 3628800.0, 1 / 40320.0, -1 / 720.0, 1 / 24.0, -0.5, 1.0])
    for _ in range(3):
        V(lambda e: e.tensor_tensor(out=sm["t1"], in0=sm["sn"], in1=sm["cs"], op=ALU.mult), K, K)
        V(lambda e: e.tensor_tensor(out=sm["t2"], in0=sm["cs"], in1=sm["cs"], op=ALU.mult), K, K)
        V(lambda e: e.tensor_tensor(out=sm["t3"], in0=sm["sn"], in1=sm["sn"], op=ALU.mult), K, K)
        V(lambda e: e.tensor_scalar(out=sm["sn"], in0=sm["t1"], scalar1=2.0, scalar2=None, op0=ALU.mult), K, K)
        V(lambda e: e.tensor_tensor(out=sm["cs"], in0=sm["t2"], in1=sm["t3"], op=ALU.subtract), K, K)
    V(lambda e: e.tensor_tensor(out=sm["ar"], in0=sm["rho"], in1=sm["cs"], op=ALU.mult), K, K)
    V(lambda e: e.tensor_tensor(out=sm["ai"], in0=sm["rho"], in1=sm["sn"], op=ALU.mult), K, K)
    V(lambda e: e.tensor_scalar(out=sm["t1"], in0=sm["ar"], scalar1=-1.0, scalar2=None, op0=ALU.add), K, K)
    V(lambda e: e.tensor_tensor(out=sm["den"], in0=lr, in1=lr, op=ALU.mult), K, K)
    V(lambda e: e.tensor_tensor(out=sm["t2"], in0=li, in1=li, op=ALU.mult), K, K)
    V(lambda e: e.tensor_tensor(out=sm["den"], in0=sm["den"], in1=sm["t2"], op=ALU.add), K, K)
    V(lambda e: e.reciprocal(out=sm["den"], in_=sm["den"]), K, K)
    V(lambda e: e.tensor_tensor(out=sm["t2"], in0=sm["t1"], in1=lr, op=ALU.mult), K, K)
    V(lambda e: e.tensor_tensor(out=sm["t3"], in0=sm["ai"], in1=li, op=ALU.mult), K, K)
    V(lambda e: e.tensor_tensor(out=sm["t2"], in0=sm["t2"], in1=sm["t3"], op=ALU.add), K, K)
    V(lambda e: e.tensor_tensor(out=sm["fr"], in0=sm["t2"], in1=sm["den"], op=ALU.mult), K, K)
    V(lambda e: e.tensor_tensor(out=sm["t2"], in0=sm["ai"], in1=lr, op=ALU.mult), K, K)
    V(lambda e: e.tensor_tensor(out=sm["t3"], in0=sm["t1"], in1=li, op=ALU.mult), K, K)
    V(lambda e: e.tensor_tensor(out=sm["t2"], in0=sm["t2"], in1=sm["t3"], op=ALU.subtract), K, K)
    V(lambda e: e.tensor_tensor(out=sm["fi"], in0=sm["t2"], in1=sm["den"], op=ALU.mult), K, K)
    V(lambda e: e.memset(Er[:, :, 0:1], 1.0), K, K)
    V(lambda e: e.memset(Ei[:, :, 0:1], 0.0), K, K)
    V(lambda e: e.tensor_copy(out=sm["wr"], in_=sm["cs"]), K, K)
    V(lambda e: e.tensor_copy(out=sm["wi"], in_=sm["sn"]), K, K)
    m0 = ar.mark()
    tA = ar.alloc("tA", [128, 32, 64], F32); tB = ar.alloc("tB", [128, 32, 64], F32)
    for k in range(7):
        n = 1 << k
        wrb = sm["wr"].unsqueeze(2).broadcast_to([128, 32, n]); wib = sm["wi"].unsqueeze(2).broadcast_to([128, 32, n])
        V(lambda e, n=n, wrb=wrb: e.tensor_tensor(out=tA[:, :, :n], in0=Er[:, :, :n], in1=wrb, op=ALU.mult), K, K)
        V(lambda e, n=n, wib=wib: e.tensor_tensor(out=tB[:, :, :n], in0=Ei[:, :, :n], in1=wib, op=ALU.mult), K, K)
        V(lambda e, n=n: e.tensor_tensor(out=Er[:, :, n:2 * n], in0=tA[:, :, :n], in1=tB[:, :, :n], op=ALU.subtract), K, K)
        V(lambda e, n=n, wib=wib: e.tensor_tensor(out=tA[:, :, :n], in0=Er[:, :, :n], in1=wib, op=ALU.mult), K, K)
        V(lambda e, n=n, wrb=wrb: e.tensor_tensor(out=tB[:, :, :n], in0=Ei[:, :, :n], in1=wrb, op=ALU.mult), K, K)
        V(lambda e, n=n: e.tensor_tensor(out=Ei[:, :, n:2 * n], in0=tA[:, :, :n], in1=tB[:, :, :n], op=ALU.add), K, K)
        V(lambda e: e.tensor_tensor(out=sm["t1"], in0=sm["wr"], in1=sm["wr"], op=ALU.mult), K, K)
        V(lambda e: e.tensor_tensor(out=sm["t2"], in0=sm["wi"], in1=sm["wi"], op=ALU.mult), K, K)
        V(lambda e: e.tensor_tensor(out=sm["t3"], in0=sm["wr"], in1=sm["wi"], op=ALU.mult), K, K)
        V(lambda e: e.tensor_tensor(out=sm["wr"], in0=sm["t1"], in1=sm["t2"], op=ALU.subtract), K, K)
        V(lambda e: e.tensor_scalar(out=sm["wi"], in0=sm["t3"], scalar1=2.0, scalar2=None, op0=ALU.mult), K, K)
    V(lambda e: e.tensor_copy(out=sm["w128r"], in_=sm["wr"]), K, K)
    V(lambda e: e.tensor_copy(out=sm["w128i"], in_=sm["wi"]), K, K)
    fw.barrier()
    ar.reset(m0)
    st32b = ar.alloc("st32b", [128, 8192], F32)
    fw.dma("sync", st32b, bpad.rearrange("p g r s -> p (g r s)"), writes=["st32b"])
    V(lambda e: e.tensor_copy(out=Bp.rearrange("p g r s -> p (g r s)"), in_=st32b), ["st32b"], ["Bp"])
    fw.barrier()
    cre = st32b[:, 0:4096].rearrange("p (g c) -> p g c", c=128); cim = st32b[:, 4096:8192].rearrange("p (g c) -> p g c", c=128)
    tC = ar.alloc("tC", [128, 32, 128], F32); tD = ar.alloc("tD", [128, 32, 128], F32)
    fw.dma("sync", st32b[:, 0:8192].rearrange("p (a g c) -> p a g c", a=2, c=128), cpad.rearrange("a p g c -> p a g c"), writes=["st32b"])
    frb = sm["fr"].unsqueeze(2).broadcast_to([128, 32, 128]); fib = sm["fi"].unsqueeze(2).broadcast_to([128, 32, 128])
    V(lambda e: e.tensor_tensor(out=tC, in0=cre, in1=frb, op=ALU.mult), ["st32b"], ["tC"])
    V(lambda e: e.tensor_tensor(out=tD, in0=cim, in1=fib, op=ALU.mult), ["st32b"], ["tD"])
    V(lambda e: e.tensor_tensor(out=Cfr, in0=tC, in1=tD, op=ALU.subtract), ["tC", "tD"], ["Cfr"])
    V(lambda e: e.tensor_tensor(out=tC, in0=cre, in1=fib, op=ALU.mult), ["st32b", "Cfr"], ["tC"])
    V(lambda e: e.tensor_tensor(out=tD, in0=cim, in1=frb, op=ALU.mult), ["st32b", "Cfr"], ["tD"])
    V(lambda e: e.tensor_tensor(out=Cfi, in0=tC, in1=tD, op=ALU.add), ["tC", "tD"], ["Cfi"])
    fw.barrier()
    V(lambda e: e.memset(cR, 0.0), [], ["cR"])
    fw.barrier()
    ar.reset(base_ssm)

    if upto <= 2:
        fw.emit()
        return nc
    uT = [ar.alloc("uT", [128, 8, 128], BF16) for _ in range(2)]
    zTb = [ar.alloc("zTb", [128, 8, 128], BF16) for _ in range(2)]
    RB = []
    for b in range(2):
        d = {n: ar.alloc(n, [128, 4, 128], F32) for n in ["t1", "t2", "t3", "t4", "Xr", "Xi", "Rr", "Ri"]}
        d["Sr"] = ar.alloc("Sr", [128, 4, 128], BF16); d["Si"] = ar.alloc("Si", [128, 4, 128], BF16)
        d["c1"] = ar.alloc("c1", [128, 4], F32); d["c2"] = ar.alloc("c2", [128, 4], F32)
        for n in ["ys", "g1", "g2"]:
            d[n] = ar.alloc(n, [128, 128], F32)
        RB.append(d)
    rcount = 0
    for si in range(NS):
        ub = si % 2
        is_main = si >= TP_
        fw.dma("sync", uT[ub], uT_s[si], reads=[("uT_s", si)], writes=[f"uT{ub}"])
        for r in range(8):
            b = rcount % 2
            rcount += 1
            B = RB[b]
            kb = lambda n, b=b: f"{n}{b}"
            gsl = slice(4 * r, 4 * r + 4)
            pXr, pkr = psum()
            pXi, pki = psum()
            for ri, (pX, pk) in enumerate(((pXr, pkr), (pXi, pki))):
                for gl in range(4):
                    mm(pX[:, gl * 128:(gl + 1) * 128], Bp[:, 4 * r + gl, ri, :], uT[ub][:, r, :], True, True,
                       [f"uT{ub}", "Bp"], pk, gl == 3)
            pXr3 = pXr.rearrange("p (a c) -> p a c", c=128); pXi3 = pXi.rearrange("p (a c) -> p a c", c=128)
            Erg, Eig = Er[:, gsl, :], Ei[:, gsl, :]
            V(lambda e, B=B, a=pXr3, t=Erg: e.tensor_tensor(out=B["t1"], in0=a, in1=t, op=ALU.mult), [pkr], [kb("t1")])
            V(lambda e, B=B, a=pXi3, t=Eig: e.tensor_tensor(out=B["t2"], in0=a, in1=t, op=ALU.mult), [pki], [kb("t2")])
            V(lambda e, B=B, a=pXi3, t=Erg: e.tensor_tensor(out=B["t3"], in0=a, in1=t, op=ALU.mult), [pki], [kb("t3")])
            V(lambda e, B=B, a=pXr3, t=Eig: e.tensor_tensor(out=B["t4"], in0=a, in1=t, op=ALU.mult), [pkr], [kb("t4")])
            G_(lambda e, B=B: e.tensor_tensor(out=B["Xr"], in0=B["t1"], in1=B["t2"], op=ALU.add), [kb("t1"), kb("t2")], [kb("Xr")])
            G_(lambda e, B=B: e.tensor_tensor(out=B["Xi"], in0=B["t3"], in1=B["t4"], op=ALU.subtract), [kb("t3"), kb("t4")], [kb("Xi")])
            for gl in range(4):
                gp = 4 * r + gl
                for nm, xs, ci in (("Rr", "Xr", 0), ("Ri", "Xi", 1)):
                    V(lambda e, B=B, gl=gl, gp=gp, nm=nm, xs=xs, ci=ci: e.tensor_tensor_scan(
                        out=B[nm][:, gl, :], data0=sm["rho"][:, gp:gp + 1].broadcast_to([128, 128]), data1=B[xs][:, gl, :],
                        initial=cR[:, ci, gp:gp + 1], op0=ALU.mult, op1=ALU.add), [kb(xs), ("cR", r)], [kb(nm)])
            wr4, wi4 = sm["w128r"][:, gsl], sm["w128i"][:, gsl]
            Rr7, Ri7 = B["Rr"][:, :, 127], B["Ri"][:, :, 127]
            G_(lambda e, B=B, a=Rr7, w=wr4: e.tensor_tensor(out=B["c1"], in0=a, in1=w, op=ALU.mult), [kb("Rr")], [kb("c1")])
            G_(lambda e, B=B, a=Ri7, w=wi4: e.tensor_tensor(out=B["c2"], in0=a, in1=w, op=ALU.mult), [kb("Ri")], [kb("c2")])
            G_(lambda e, B=B, gsl=gsl: e.tensor_tensor(out=cR[:, 0, gsl], in0=B["c1"], in1=B["c2"], op=ALU.subtract), [kb("c1"), kb("c2")], [("cR", r)])
            G_(lambda e, B=B, a=Ri7, w=wr4: e.tensor_tensor(out=B["c1"], in0=a, in1=w, op=ALU.mult), [kb("Ri")], [kb("c1")])
            G_(lambda e, B=B, a=Rr7, w=wi4: e.tensor_tensor(out=B["c2"], in0=a, in1=w, op=ALU.mult), [kb("Rr")], [kb("c2")])
            G_(lambda e, B=B, gsl=gsl: e.tensor_tensor(out=cR[:, 1, gsl], in0=B["c1"], in1=B["c2"], op=ALU.add), [kb("c1"), kb("c2")], [("cR", r)])
            if not is_main:
                continue
            G_(lambda e, B=B, t=Erg: e.tensor_tensor(out=B["t1"], in0=B["Rr"], in1=t, op=ALU.mult), [kb("Rr")], [kb("t1")])
            G_(lambda e, B=B, t=Eig: e.tensor_tensor(out=B["t2"], in0=B["Ri"], in1=t, op=ALU.mult), [kb("Ri")], [kb("t2")])
            G_(lambda e, B=B, t=Erg: e.tensor_tensor(out=B["t3"], in0=B["Ri"], in1=t, op=ALU.mult), [kb("Ri")], [kb("t3")])
            G_(lambda e, B=B, t=Eig: e.tensor_tensor(out=B["t4"], in0=B["Rr"], in1=t, op=ALU.mult), [kb("Rr")], [kb("t4")])
            V(lambda e, B=B: e.tensor_tensor(out=B["Sr"], in0=B["t1"], in1=B["t2"], op=ALU.subtract), [kb("t1"), kb("t2")], [kb("Sr")])
            V(lambda e, B=B: e.scalar_tensor_tensor(out=B["Si"], in0=B["t3"], scalar=-1.0, in1=B["t4"], op0=ALU.mult, op1=ALU.subtract),
              [kb("t3"), kb("t4")], [kb("Si")])
            py, pky = psum()
            n_ = 0
            for (Cm, Sn) in ((Cfr, "Sr"), (Cfi, "Si")):
                for gl in range(4):
                    mm(py[:, 0:128], Cm[:, 4 * r + gl, :], B[Sn][:, gl, :], n_ == 0, n_ == 7, [kb(Sn), "Cf"], pky, n_ == 7)
                    n_ += 1
            zb = (si - TP_) % 2
            V(lambda e, B=B, ub=ub, r=r, py=py: e.scalar_tensor_tensor(out=B["ys"], in0=uT[ub][:, r, :], scalar=dcs[:, r:r + 1], in1=py[:, 0:128],
                                                                    op0=ALU.mult, op1=ALU.add), [pky, f"uT{ub}"], [kb("ys")])
            G_(lambda e, B=B: e.tensor_tensor(out=B["g1"], in0=B["ys"], in1=B["ys"], op=ALU.mult), [kb("ys")], [kb("g1")])
            G_(lambda e, B=B: e.tensor_scalar(out=B["g1"], in0=B["g1"], scalar1=0.044715, scalar2=1.0, op0=ALU.mult, op1=ALU.add), [kb("g1")], [kb("g1")])
            G_(lambda e, B=B: e.tensor_tensor(out=B["g1"], in0=B["g1"], in1=B["ys"], op=ALU.mult), [kb("g1"), kb("ys")], [kb("g1")])
            A_(lambda e, B=B: e.activation(out=B["g2"], in_=B["g1"], func=AF.Sigmoid, scale=1.5957691216057308), [kb("g1")], [kb("g2")])
            V(lambda e, B=B, zb=zb, r=r: e.tensor_tensor(out=zTb[zb][:, r, :], in0=B["ys"], in1=B["g2"], op=ALU.mult), [kb("ys"), kb("g2")], [f"zTb{zb}"])
        if is_main:
            zb = (si - TP_) % 2
            fw.dma("sync", zT_s[si - TP_], zTb[zb], reads=[f"zTb{zb}"], writes=[("zT_s", si - TP_)])
    fw.barrier()
    ar.reset(base_persist)

    if upto <= 3:
        fw.emit()
        return nc
    Wg = ar.alloc("Wg", [128, 8, 2048], BF16); Wo = ar.alloc("Wo", [128, 8, 1024], BF16)
    load_weights(Wg, w_glu, 4, 8, "Wg")
    load_weights(Wo, w_out, 2, 8, "Wo")
    kme = ar.alloc("kme", [128, 2, 128], BF16); vme = ar.alloc("vme", [128, 4, 65], BF16)
    fw.dma("sync", kme, kT_s[0], writes=["kme"]); fw.dma("sync", vme, v_s[0], writes=["vme"])
    qTl = [ar.alloc("qTl", [128, 8, 128], BF16) for _ in range(2)]
    kTl = [ar.alloc("kTl", [128, 2, 128], BF16) for _ in range(3)]
    vl = [ar.alloc("vl", [128, 4, 65], BF16) for _ in range(3)]
    gl_ = [ar.alloc("gl", [128, 2048], BF16) for _ in range(2)]
    zTl = [ar.alloc("zTl", [128, 8, 128], BF16) for _ in range(2)]
    xr = [ar.alloc("xr", [128, D], F32) for _ in range(2)]
    Pc = [ar.alloc("Pc", [128, 512], BF16) for _ in range(2)]
    Pp = [ar.alloc("Pp", [128, 512], BF16) for _ in range(2)]
    Pm = [ar.alloc("Pm", [128, 512], BF16) for _ in range(2)]
    den = ar.alloc("den", [128, 4], F32)
    for b_ in range(2):
        V(lambda e, b_=b_: e.memset(Pm[b_], 0.0), [], [f"Pm{b_}"])
    attn = ar.alloc("attn", [128, D], F32); An = ar.alloc("An", [128, D], F32)
    sig = ar.alloc("sig", [128, 512], F32); ssm = ar.alloc("ssm", [128, D], F32); Bn = ar.alloc("Bn", [128, D], F32)
    mg = ar.alloc("mg", [128, D], BF16); mgT = ar.alloc("mgT", [128, 8, 128], BF16)
    h1 = [ar.alloc("h1", [128, D], F32) for _ in range(2)]
    fw.dma("sync", kTl[1], kT_s[1], writes=["kTl1"]); fw.dma("sync", vl[1], v_s[1], writes=["vl1"])
    pcount = 0
    for i in range(TM_):
        b = i % 2
        jc, jp = 2 + i, 1 + i
        sc, sp = jc % 3, jp % 3
        fw.dma("sync", kTl[sc], kT_s[jc], reads=[("kT_s", jc)], writes=[f"kTl{sc}"])
        fw.dma("sync", vl[sc], v_s[jc], reads=[("v_s", jc)], writes=[f"vl{sc}"])
        fw.dma("sync", qTl[b], qT_s[i], writes=[f"qTl{b}"])
        fw.dma("sync", gl_[b], g_s[i], writes=[f"gl{b}"])
        fw.dma("sync", zTl[b], zT_s[i], writes=[f"zTl{b}"])
        fw.dma("sync", xr[b], xmain[i * 128:(i + 1) * 128, :], writes=[f"xr{b}"])
        for grp in range(4):
            pb = pcount % 2
            pcount += 1
            bs = (grp % 2) * 64
            kc = grp // 2
            qsel = qTl[b][bs:bs + 64, kc * 4:(kc + 1) * 4, :]
            pS, pkS = psum()
            mm(pS, kTl[sc][bs:bs + 64, kc, :], qsel, True, True, [f"kTl{sc}", f"qTl{b}"], pkS, True)
            A_(lambda e, pS=pS, pb=pb: e.activation(out=Pc[pb], in_=pS, func=AF.Exp, scale=0.125), [pkS], [f"Pc{pb}"])
            G_(lambda e, pb=pb: e.tensor_tensor(out=Pc[pb], in0=Pc[pb], in1=maskb[:, 0, :], op=ALU.mult), [f"Pc{pb}", "maskb"], [f"Pc{pb}"])
            pS2, pkS2 = psum()
            mm(pS2, kTl[sp][bs:bs + 64, kc, :], qsel, True, True, [f"kTl{sp}", f"qTl{b}"], pkS2, True)
            A_(lambda e, pS2=pS2, pb=pb: e.activation(out=Pp[pb], in_=pS2, func=AF.Exp, scale=0.125), [pkS2], [f"Pp{pb}"])
            mi = 2 if i == 0 else 1
            G_(lambda e, pb=pb, mi=mi: e.tensor_tensor(out=Pp[pb], in0=Pp[pb], in1=maskb[:, mi, :], op=ALU.mult), [f"Pp{pb}", "maskb"], [f"Pp{pb}"])
            pS3, pkS3 = psum()
            mm(pS3[0:16, :], kme[bs:bs + 64, kc, 0:16], qsel, True, True, ["kme", f"qTl{b}"], pkS3, True)
            A_(lambda e, pS3=pS3, pb=pb: e.activation(out=Pm[pb][0:16, :], in_=pS3[0:16, :], func=AF.Exp, scale=0.125), [pkS3], [f"Pm{pb}"])
            pO, pkO = psum()
            for r in range(4):
                o = pO[:, r * 65:(r + 1) * 65]
                mm(o, Pm[pb][:, r * 128:(r + 1) * 128], vme[:, grp, :], True, False, [f"Pm{pb}", "vme"], pkO, False)
                mm(o, Pp[pb][:, r * 128:(r + 1) * 128], vl[sp][:, grp, :], False, False, [f"Pp{pb}", f"vl{sp}"], pkO, False)
                mm(o, Pc[pb][:, r * 128:(r + 1) * 128], vl[sc][:, grp, :], False, True, [f"Pc{pb}", f"vl{sc}"], pkO, r == 3)
            pO3 = pO[:, 0:260].rearrange("p (r c) -> p r c", c=65)
            V(lambda e, pO3=pO3, grp=grp: e.tensor_tensor(out=den, in0=pO3[:, :, 64], in1=esink[:, grp * 4:(grp + 1) * 4], op=ALU.add), [pkO, "esink"], ["den"])
            V(lambda e: e.reciprocal(out=den, in_=den), ["den"], ["den"])
            V(lambda e, pO3=pO3, grp=grp: e.tensor_tensor(out=attn[:, grp * 256:(grp + 1) * 256].rearrange("p (r d) -> p r d", d=64), in0=pO3[:, :, 0:64],
                                                         in1=den.unsqueeze(2).broadcast_to([128, 4, 64]), op=ALU.mult), [pkO, "den"], ["attn"])
        rms_scale(attn, 1, An, ["attn"], ["An"])
        G_(lambda e, b=b: e.tensor_tensor(out=An, in0=An, in1=gl_[b][:, 0:1024], op=ALU.mult), ["An", f"gl{b}"], ["An"])
        for half in range(2):
            pa, pka = psum()
            for k in range(8):
                mm(pa, zTl[b][:, k, :], Wg[:, k, half * 512:(half + 1) * 512], k == 0, k == 7, [f"zTl{b}", "Wg"], pka, k == 7)
            pz, pkz = psum()
            for k in range(8):
                mm(pz, zTl[b][:, k, :], Wg[:, k, 1024 + half * 512:1024 + (half + 1) * 512], k == 0, k == 7, [f"zTl{b}", "Wg"], pkz, k == 7)
            A_(lambda e, pz=pz: e.activation(out=sig, in_=pz, func=AF.Sigmoid), [pkz], ["sig"])
            V(lambda e, pa=pa, half=half: e.tensor_tensor(out=ssm[:, half * 512:(half + 1) * 512], in0=pa, in1=sig, op=ALU.mult), [pka, "sig"], ["ssm"])
        rms_scale(ssm, 2, Bn, ["ssm"], ["Bn"])
        G_(lambda e, b=b: e.tensor_tensor(out=Bn, in0=Bn, in1=gl_[b][:, 1024:2048], op=ALU.mult), ["Bn", f"gl{b}"], ["Bn"])
        V(lambda e: e.tensor_tensor(out=mg, in0=An, in1=Bn, op=ALU.add), ["An", "Bn"], ["mg"])
        transpose8(mg, mgT, "mg", "mgT")
        for half in range(2):
            p, pk = psum()
            for k in range(8):
                mm(p, mgT[:, k, :], Wo[:, k, half * 512:(half + 1) * 512], k == 0, k == 7, ["mgT", "Wo"], pk, k == 7)
            V(lambda e, p=p, half=half, b=b: e.tensor_tensor(out=h1[b][:, half * 512:(half + 1) * 512], in0=p, in1=xr[b][:, half * 512:(half + 1) * 512], op=ALU.add),
              [pk, f"xr{b}"], [f"h1{b}"])
        fw.dma("sync", h1_s[i * 128:(i + 1) * 128, :], h1[b], reads=[f"h1{b}"], writes=[("h1_s", i)])
    fw.barrier()
    ar.reset(base_persist)

    if upto <= 4:
        fw.emit()
        return nc
    W1 = ar.alloc("W1", [128, 8, 5632], BF16); W2 = ar.alloc("W2", [128, 22, 1024], BF16)
    load_weights(W1, w_f1, 11, 8, "W1")
    load_weights(W2, w_f2, 2, 22, "W2")
    hl = [ar.alloc("hl", [128, D], F32) for _ in range(2)]
    hn = ar.alloc("hn", [128, D], BF16); hnT = ar.alloc("hnT", [128, 8, 128], BF16)
    sg = [ar.alloc("sg", [128, 256], F32) for _ in range(2)]
    actT = ar.alloc("actT", [128, 22, 128], BF16)
    ob = [ar.alloc("ob", [128, D], F32) for _ in range(2)]
    for i in range(TM_):
        b = i % 2
        fw.dma("sync", hl[b], h1_s[i * 128:(i + 1) * 128, :], reads=[("h1_s", i)], writes=[f"hl{b}"])
        rms_scale(hl[b], 3, hn, [f"hl{b}"], ["hn"])
        transpose8(hn, hnT, "hn", "hnT")
        for fp in range(11):
            p, pk = psum()
            for q4 in range(4):
                for k in range(8):
                    mm(p[:, q4 * 128:(q4 + 1) * 128], W1[:, k, fp * 512 + q4 * 128:fp * 512 + (q4 + 1) * 128], hnT[:, k, :], k == 0, k == 7,
                       ["hnT", "W1"], pk, (q4 == 3 and k == 7))
            s_ = sg[fp % 2]
            p4 = p.rearrange("p (a c) -> p a c", c=128)
            A_(lambda e, p4=p4, s_=s_: e.activation(out=s_.rearrange("p (a c) -> p a c", c=128), in_=p4[:, 0::2, :], func=AF.Sigmoid), [pk], [f"sg{fp % 2}"])
            V(lambda e, p4=p4, s_=s_: e.tensor_tensor(out=s_.rearrange("p (a c) -> p a c", c=128), in0=p4[:, 0::2, :], in1=s_.rearrange("p (a c) -> p a c", c=128), op=ALU.mult),
              [pk, f"sg{fp % 2}"], [f"sg{fp % 2}"])
            V(lambda e, p4=p4, s_=s_, fp=fp: e.tensor_tensor(out=actT[:, 2 * fp:2 * fp + 2, :], in0=p4[:, 1::2, :], in1=s_.rearrange("p (a c) -> p a c", c=128), op=ALU.mult),
              [pk, f"sg{fp % 2}"], ["actT"])
        for half in range(2):
            p, pk = psum()
            for k in range(22):
                mm(p, actT[:, k, :], W2[:, k, half * 512:(half + 1) * 512], k == 0, k == 21, ["actT", "W2"], pk, k == 21)
            V(lambda e, p=p, half=half, b=b: e.tensor_tensor(out=ob[b][:, half * 512:(half + 1) * 512], in0=p, in1=hl[b][:, half * 512:(half + 1) * 512], op=ALU.add),
              [pk, f"hl{b}"], [f"ob{b}"])
        fw.dma("sync", out[i * 128:(i + 1) * 128, :], ob[b], reads=[f"ob{b}"], writes=[("out", i)])
    fw.emit()
    return nc


def _panels(w, kk):
    n = w.shape[1] // 512
    return np.ascontiguousarray(w.reshape(kk, 128, n, 512).transpose(2, 1, 0, 3))


def prep_shared(inp):
    f = lambda a: np.asarray(a, dtype=np.float32)
    w_in = f(inp["w_in"])[0]
    qcols = []
    for j in range(8):
        for s in range(2):
            head = ((j // 4) * 2 + s) * 4 + (j % 4)
            qcols.extend(range(head * 64, head * 64 + 64))
    w_in_r = np.concatenate([w_in[:, qcols], w_in[:, 1024:1536], w_in[:, 1536:]], axis=1)
    wf1 = f(inp["w_ffn_in"])[0]
    cols = []
    for c in range(22):
        cols.extend(range(c * 128, (c + 1) * 128))
        cols.extend(range(DFF + c * 128, DFF + (c + 1) * 128))
    wf1_r = wf1[:, cols]
    rep = lambda v, n: np.ascontiguousarray(np.broadcast_to(f(v).reshape(1, -1), (128, n)))
    gains = np.stack([rep(inp["norm_mix"][0], D), rep(inp["attn_branch_norm"][0], D), rep(inp["ssm_branch_norm"][0], D), rep(inp["norm_ffn"][0], D)])

    def sp(a):
        return np.ascontiguousarray(f(a).reshape(32, 2, 64).transpose(1, 2, 0).reshape(128, 32))

    lam = np.stack([sp(inp["lam_re"][0]), sp(inp["lam_im"][0]), sp(np.broadcast_to(f(inp["log_dt"])[0][:, None], (64, 64)))])
    bre, bim = f(inp["ssm_b_re"])[0], f(inp["ssm_b_im"])[0]
    bpad = np.zeros((128, 32, 2, 128), np.float32)
    for g in range(64):
        gp, g2, g8 = g // 2, g % 2, g % 8
        bpad[g8 * 16:(g8 + 1) * 16, gp, 0, g2 * 64:(g2 + 1) * 64] = bre[g].T
        bpad[g8 * 16:(g8 + 1) * 16, gp, 1, g2 * 64:(g2 + 1) * 64] = bim[g].T
    cre, cim = f(inp["ssm_c_re"])[0], f(inp["ssm_c_im"])[0]
    cpad = np.zeros((2, 128, 32, 128), np.float32)
    for g in range(64):
        gp, g2, g8 = g // 2, g % 2, g % 8
        cpad[0, g2 * 64:(g2 + 1) * 64, gp, g8 * 16:(g8 + 1) * 16] = cre[g].T
        cpad[1, g2 * 64:(g2 + 1) * 64, gp, g8 * 16:(g8 + 1) * 16] = cim[g].T
    kk, qq = np.arange(128)[:, None], np.arange(128)[None, :]
    mcur = np.where(kk <= qq, 1.0, 0.0).astype(np.float32)
    mprev = np.where(kk > qq, 1.0, 0.0).astype(np.float32)
    return dict(
        w_in=_panels(w_in_r, 8), w_glu=_panels(f(inp["w_glu"])[0], 8), w_out=_panels(f(inp["w_out"])[0], 8),
        w_f1=_panels(wf1_r, 8), w_f2=_panels(f(inp["w_ffn_out"])[0], 22), gains=gains,
        gq=rep(np.tile(f(inp["q_norm"])[0], 4), 256), gk=rep(np.tile(f(inp["k_norm"])[0], 4), 256),
        sinks=rep(inp["attn_sinks"][0], 16), ident=np.eye(128, dtype=np.float32), lam=lam, bpad=bpad, cpad=cpad,
        dcol=np.ascontiguousarray(f(inp["ssm_d"])[0].reshape(8, 128).T),
    ), mcur, mprev


def prep_core(x_b, meta, h, NM, NP, mcur, mprev):
    xmain = np.ascontiguousarray(x_b[h * NM:(h + 1) * NM])
    xpre = np.zeros((NP, D), np.float32)
    xctx = np.zeros((256, D), np.float32)
    xctx[0:16] = meta
    if h == 0:
        xpre[NP - 16:] = meta
        m0 = np.zeros((128, 128), np.float32)
    else:
        xpre[112:128] = meta
        xpre[128:] = x_b[0:NM]
        xctx[128:256] = x_b[NM - 128:NM]
        m0 = mprev
    masks = np.stack([np.tile(mcur, (1, 4)), np.tile(mprev, (1, 4)), np.tile(m0, (1, 4))]).astype(np.float32)
    return dict(xmain=xmain, xpre=xpre, xctx=xctx, masks=masks)


_NC_CACHE = {}


def kernel(**inputs):
    x = np.asarray(inputs["x"], dtype=np.float32)
    Bsz, S, _ = x.shape
    NM = S // 2
    NP = NM + 128
    meta = np.asarray(inputs["meta_tokens"], dtype=np.float32)
    shared, mcur, mprev = prep_shared(inputs)
    in_maps = []
    for b in range(Bsz):
        for h in range(2):
            d = dict(shared)
            d.update(prep_core(x[b], meta, h, NM, NP, mcur, mprev))
            in_maps.append(d)
    nc = build(NM, NP)
    res = run_bass_kernel_spmd(nc, in_maps, core_ids=list(range(len(in_maps))))
    outp = np.zeros((Bsz, S, D), np.float32)
    for b in range(Bsz):
        for h in range(2):
            outp[b, h * NM:(h + 1) * NM] = res.results[2 * b + h]["out"]
    return outp
```

```python
import math
import contextlib
import numpy as np
import concourse.bass as bass
import concourse.mybir as mybir
from concourse.bass_utils import run_bass_kernel_spmd

F32 = mybir.dt.float32
BF16 = mybir.dt.bfloat16
AF = mybir.ActivationFunctionType
ALU = mybir.AluOpType
AX = mybir.AxisListType
ENGS = ("tensor", "vector", "scalar", "gpsimd", "sync")
D = 1024
DFF = 2816
NEG = -30000.0


class FW:
    def __init__(self, nc, n_dma_sems=40):
        self.nc = nc
        self.ops = {e: [] for e in ENGS}
        self.cnt = {e: 0 for e in ENGS}
        self.known = {e: {} for e in ENGS}
        self.last_w = {}
        self.readers = {}
        self.n_dma_sems = n_dma_sems
        self.dma_gen = [0] * n_dma_sems
        self.dma_rr = 0
        self.sem_names = [f"s_{e}" for e in ENGS] + [f"d_{i}" for i in range(n_dma_sems)]

    def _deps(self, reads, writes):
        evs = []
        for k in reads:
            if k in self.last_w:
                evs.append(self.last_w[k])
        for k in writes:
            if k in self.last_w:
                evs.append(self.last_w[k])
            evs.extend(self.readers.get(k, ()))
        return evs

    def _commit(self, ev, reads, writes):
        for k in reads:
            self.readers.setdefault(k, []).append(ev)
        for k in writes:
            self.last_w[k] = ev
            self.readers[k] = []

    def _waits(self, eng, evs):
        best = {}
        for (s, v) in evs:
            if v > best.get(s, 0):
                best[s] = v
        out = []
        kn = self.known[eng]
        for s, v in best.items():
            if eng == "tensor" and s == "s_tensor":
                continue
            if kn.get(s, 0) >= v:
                continue
            kn[s] = v
            out.append((s, v))
        return out

    def op(self, eng, fn, reads=(), writes=(), inc=True):
        evs = self._deps(reads, writes)
        waits = self._waits(eng, evs)
        sname = f"s_{eng}"
        ev = (sname, self.cnt[eng] + 1)
        if inc:
            self.cnt[eng] += 1
        self.ops[eng].append((waits, fn, (sname, 1) if inc else None))
        self._commit(ev, reads, writes)
        return ev

    def dma(self, queue, out, in_, reads=(), writes=(), **kw):
        i = self.dma_rr
        self.dma_rr = (self.dma_rr + 1) % self.n_dma_sems
        sname = f"d_{i}"
        evs = self._deps(reads, writes)
        if self.dma_gen[i] > 0:
            evs.append((sname, 16 * self.dma_gen[i]))
        waits = self._waits(queue, evs)
        self.dma_gen[i] += 1
        ev = (sname, 16 * self.dma_gen[i])
        self.ops[queue].append((waits, lambda e: e.dma_start(out=out, in_=in_, **kw), (sname, 16)))
        self._commit(ev, reads, writes)
        return ev

    def barrier(self):
        fin = []
        for e in ENGS:
            if self.cnt[e] > 0:
                fin.append((f"s_{e}", self.cnt[e]))
        for i in range(self.n_dma_sems):
            if self.dma_gen[i] > 0:
                fin.append((f"d_{i}", 16 * self.dma_gen[i]))
        for e in ENGS:
            w = self._waits(e, fin)
            if w:
                self.ops[e].append((w, None, None))
        self.last_w = {}
        self.readers = {}

    def emit(self):
        nc = self.nc
        self.barrier()
        with contextlib.ExitStack() as st:
            sems = {n: st.enter_context(nc.semaphore(n)) for n in self.sem_names}
            block = st.enter_context(nc.Block())

            def mk(engname):
                lst = self.ops[engname]

                def body(eng):
                    for (waits, fn, inc) in lst:
                        for (s, v) in waits:
                            eng.wait_ge(sems[s], v)
                        if fn is None:
                            continue
                        ins = fn(eng)
                        if inc is not None:
                            ins.then_inc(sems[inc[0]], inc[1])
                return body

            block.tensor(mk("tensor"))
            block.vector(mk("vector"))
            block.scalar(mk("scalar"))
            block.gpsimd(mk("gpsimd"))
            block.sync(mk("sync"))


class Arena:
    def __init__(self, nc, base=16640, limit=224 * 1024):
        self.nc, self.off, self.limit, self.n = nc, base, limit, 0

    def alloc(self, name, shape, dt):
        per = int(np.prod(shape[1:])) * (4 if dt == F32 else 2)
        per = (per + 63) // 64 * 64
        assert self.off + per <= self.limit, (name, self.off, per)
        self.n += 1
        t = self.nc.alloc_sbuf_tensor_at(f"{name}_{self.n}_{self.off}", list(shape), dt, offset=self.off)
        self.off += per
        return t.ap()

    def mark(self):
        return self.off

    def reset(self, off):
        self.off = off


def build(NM, NP, upto=9):
    nc = bass.Bass("TRN2", target_bir_lowering=False)
    fw = FW(nc)
    TM_, TP_ = NM // 128, NP // 128
    NS = TP_ + TM_
    NK = 2 + TM_

    def din(name, shape, dt=F32):
        return nc.dram_tensor(name, list(shape), dt, kind="ExternalInput").ap()

    xmain = din("xmain", [NM, D]); xpre = din("xpre", [NP, D]); xctx = din("xctx", [256, D])
    w_in = din("w_in", [9, 128, 8, 512]); w_glu = din("w_glu", [4, 128, 8, 512]); w_out = din("w_out", [2, 128, 8, 512])
    w_f1 = din("w_f1", [11, 128, 8, 512]); w_f2 = din("w_f2", [2, 128, 22, 512])
    gains = din("gains", [4, 128, D])
    gq = din("gq", [128, 256]); gk = din("gk", [128, 256]); sinks = din("sinks", [128, 16])
    masks = din("masks", [3, 128, 512])
    ident = din("ident", [128, 128])
    lam = din("lam", [3, 128, 32])
    btc = din("btc", [128, 32, 2, 16]); cc = din("cc", [2, 128, 32, 16]); dcol = din("dcol", [128, 8])
    out = nc.dram_tensor("out", [NM, D], F32, kind="ExternalOutput").ap()

    def dscr(name, shape, dt):
        return nc.dram_tensor(name, list(shape), dt, kind="Internal").ap()

    uT_s = dscr("uT_s", [NS, 128, 8, 128], BF16); qT_s = dscr("qT_s", [TM_, 128, 8, 128], BF16)
    kT_s = dscr("kT_s", [NK, 128, 2, 128], BF16); v_s = dscr("v_s", [NK, 128, 4, 65], BF16)
    g_s = dscr("g_s", [TM_, 128, 2048], BF16); zT_s = dscr("zT_s", [TM_, 128, 8, 128], BF16)
    h1_s = dscr("h1_s", [NM, D], F32)
    DB_s = dscr("DB_s", [8, 128, 8, 4, 2, 128], BF16); EC_s = dscr("EC_s", [8, 128, 8, 4, 2, 128], BF16)

    ar = Arena(nc)
    identf = ar.alloc("identf", [128, 128], F32); identb = ar.alloc("identb", [128, 128], BF16)
    gsb = ar.alloc("gsb", [128, 4, D], F32)
    fw.dma("sync", identf, ident, writes=["identf"])
    fw.op("vector", lambda e: e.tensor_copy(out=identb, in_=identf), reads=["identf"], writes=["identb"])
    fw.dma("sync", gsb, gains.rearrange("a p d -> p a d"), writes=["gsb"])
    pbank = [nc.alloc_psum_tensor(f"pb{i}", [128, 512], F32).ap() for i in range(8)]
    pcnt = [0]

    def psum():
        i = pcnt[0] % 8
        pcnt[0] += 1
        return pbank[i], f"pb{i}"

    rr = [0]

    def alt():
        rr[0] += 1
        return "vector" if rr[0] % 2 else "gpsimd"

    base0 = ar.mark()

    def load_weights(dst, src, npan, kk, key):
        m = ar.mark()
        st = [ar.alloc("wst", [128, 8, 512], F32) for _ in range(2)]
        n = 0
        for pi in range(npan):
            for k0 in range(0, kk, 8):
                kc = min(8, kk - k0)
                s = st[n % 2]
                fw.dma("sync", s[:, :kc, :], src[pi][:, k0:k0 + kc, :], writes=[f"wst{n % 2}"])
                eng = alt()
                fw.op(eng, lambda e, s=s, pi=pi, k0=k0, kc=kc: e.tensor_copy(out=dst[:, k0:k0 + kc, pi * 512:(pi + 1) * 512], in_=s[:, :kc, :]),
                      reads=[f"wst{n % 2}"], writes=[key])
                n += 1
        fw.barrier()
        ar.reset(m)

    def rms_scale(xin, gidx, xn_out, rkeys, wkeys, ncol=D):
        fw.op("scalar", lambda e: e.activation(out=junk[:, :ncol], in_=xin, func=AF.Square, accum_out=ssq),
              reads=rkeys, writes=["junk", "ssq"])
        fw.op("vector", lambda e: e.tensor_scalar(out=ssq, in0=ssq, scalar1=1.0 / ncol, scalar2=1e-6, op0=ALU.mult, op1=ALU.add),
              reads=["ssq"], writes=["ssq"])
        fw.op("scalar", lambda e: e.activation(out=ssq, in_=ssq, func=AF.Sqrt), reads=["ssq"], writes=["ssq"])
        fw.op("vector", lambda e: e.reciprocal(out=ssq, in_=ssq), reads=["ssq"], writes=["ssq"])
        fw.op("vector", lambda e: e.scalar_tensor_tensor(out=xn_out, in0=xin, scalar=ssq, in1=gsb[:, gidx, :ncol],
                                                         op0=ALU.mult, op1=ALU.mult),
              reads=list(rkeys) + ["ssq", "gsb"], writes=wkeys)

    def transpose8(src_bf, dstT, rkey, wkey, n=8):
        for half in range((n + 3) // 4):
            p, pk = psum()
            m = min(4, n - half * 4)
            for j in range(m):
                c = half * 4 + j
                fw.op("tensor", lambda e, c=c, j=j, p=p: e.matmul(p[:, j * 128:(j + 1) * 128], lhsT=src_bf[:, c * 128:(c + 1) * 128],
                                                                 rhs=identb, start=True, stop=True),
                      reads=[rkey, "identb"], writes=[pk], inc=(j == m - 1))
            fw.op("vector", lambda e, p=p, half=half, m=m: e.tensor_copy(
                out=dstT[:, half * 4:half * 4 + m, :], in_=p[:, :m * 128].rearrange("p (a b) -> p a b", b=128)),
                reads=[pk], writes=[wkey])

    junk = ar.alloc("junk", [128, D], F32); ssq = ar.alloc("ssq", [128, 1], F32)
    base1 = ar.mark()

    dbg = {}
    V = lambda fn, r, w: fw.op("vector", fn, reads=r, writes=w)
    A_ = lambda fn, r, w: fw.op("scalar", fn, reads=r, writes=w)
    G_ = lambda fn, r, w: fw.op("gpsimd", fn, reads=r, writes=w)

    def mm(out_ap, lhsT, rhs, start, stop, reads, pk, inc):
        fw.op("tensor", lambda e: e.matmul(out_ap, lhsT=lhsT, rhs=rhs, start=start, stop=stop), reads=reads, writes=[pk], inc=inc)

    dcs = ar.alloc("dcs", [128, 8], F32)
    esink = ar.alloc("esink", [128, 16], F32); gqk = ar.alloc("gqk", [128, 256], F32)
    maskb = ar.alloc("maskb", [128, 3, 512], BF16)
    base_persist = ar.mark()
    st32 = ar.alloc("st32", [128, 8192], F32)
    fw.dma("sync", dcs, dcol, writes=["dcs"])
    fw.dma("sync", st32[:, 0:16], sinks, writes=["a"])
    A_(lambda e: e.activation(out=esink, in_=st32[:, 0:16], func=AF.Exp), ["a"], ["esink"])
    fw.dma("sync", st32[:, 1024:1280], gq, writes=["b"])
    fw.dma("sync", st32[:, 2048:2304], gk, writes=["c"])
    V(lambda e: e.tensor_tensor(out=gqk, in0=st32[:, 1024:1280], in1=st32[:, 2048:2304], op=ALU.mult), ["b", "c"], ["gqk"])
    fw.dma("sync", st32[:, 4096:5632].rearrange("p (a c) -> p a c", a=3), masks.rearrange("a p c -> p a c"), writes=["d"])
    V(lambda e: e.tensor_copy(out=maskb, in_=st32[:, 4096:5632].rearrange("p (a c) -> p a c", a=3)), ["d"], ["maskb"])
    fw.barrier()
    ar.reset(base_persist)

    m1 = ar.mark()
    Win = ar.alloc("Win", [128, 8, 4608], BF16)
    load_weights(Win, w_in, 9, 8, "Win")
    QO, KVO, UO, GO = 0, 1024, 1536, 2560
    xt = [ar.alloc("xt", [128, D], F32) for _ in range(2)]
    xnb = [ar.alloc("xnb", [128, D], BF16) for _ in range(2)]
    xnT = [ar.alloc("xnT", [128, 8, 128], BF16) for _ in range(2)]
    uTb = [ar.alloc("uTb", [128, 8, 128], BF16) for _ in range(2)]
    qsq = ar.alloc("qsq", [128, 512], F32)
    qss = ar.alloc("qss", [128, 8], F32)
    qn = [ar.alloc("qn", [128, D], BF16) for _ in range(2)]
    qTb = [ar.alloc("qTb", [128, 8, 128], BF16) for _ in range(2)]
    kf = ar.alloc("kf", [128, 256], F32)
    kn = [ar.alloc("kn", [128, 256], BF16) for _ in range(2)]
    kTb = [ar.alloc("kTb", [128, 2, 128], BF16) for _ in range(2)]
    vab = [ar.alloc("vab", [128, 4, 65], BF16) for _ in range(2)]
    gb = [ar.alloc("gb", [128, 2048], BF16) for _ in range(2)]
    for b in range(2):
        V(lambda e, b=b: e.memset(vab[b][:, :, 64:65], 1.0), [], [f"vab{b}"])

    tiles = [("ctx", xctx[0:128, :], 0, None), ("ctx", xctx[128:256, :], 1, None)]
    for t in range(TP_):
        tiles.append(("pre", xpre[t * 128:(t + 1) * 128, :], t, None))
    for t in range(TM_):
        tiles.append(("main", xmain[t * 128:(t + 1) * 128, :], TP_ + t, t))

    def headnorm(p, pk, ncol, nh, dst, dkey, gain=None):
        A_(lambda e: e.activation(out=qsq[:, :ncol], in_=p[:, :ncol], func=AF.Square), [pk], ["qsq"])
        V(lambda e: e.tensor_reduce(out=qss[:, :nh], in_=qsq[:, :ncol].rearrange("p (h d) -> p h d", d=64), axis=AX.X, op=ALU.add), ["qsq"], ["qss"])
        V(lambda e: e.tensor_scalar(out=qss[:, :nh], in0=qss[:, :nh], scalar1=1.0 / 64, scalar2=1e-6, op0=ALU.mult, op1=ALU.add), ["qss"], ["qss"])
        A_(lambda e: e.activation(out=qss[:, :nh], in_=qss[:, :nh], func=AF.Sqrt), ["qss"], ["qss"])
        V(lambda e: e.reciprocal(out=qss[:, :nh], in_=qss[:, :nh]), ["qss"], ["qss"])
        rb = qss[:, :nh].unsqueeze(2).broadcast_to([128, nh, 64])
        if gain is None:
            V(lambda e: e.tensor_tensor(out=dst.rearrange("p (h d) -> p h d", d=64), in0=p[:, :ncol].rearrange("p (h d) -> p h d", d=64), in1=rb, op=ALU.mult),
              [pk, "qss"], [dkey])
        else:
            V(lambda e: e.tensor_tensor(out=kf.rearrange("p (h d) -> p h d", d=64), in0=p[:, :ncol].rearrange("p (h d) -> p h d", d=64), in1=rb, op=ALU.mult),
              [pk, "qss"], ["kf"])
            V(lambda e: e.tensor_tensor(out=dst, in0=kf, in1=gain, op=ALU.mult), ["kf", "gqk"], [dkey])

    for ti, (kind, src, sidx, midx) in enumerate(tiles):
        b = ti % 2
        fw.dma("sync", xt[b], src, writes=[f"xt{b}"])
        rms_scale(xt[b], 0, xnb[b], [f"xt{b}"], [f"xnb{b}"])
        transpose8(xnb[b], xnT[b], f"xnb{b}", f"xnT{b}")
        if kind in ("pre", "main"):
            for half in range(2):
                p, pk = psum()
                for cl in range(4):
                    ct = half * 4 + cl
                    for k in range(8):
                        mm(p[:, cl * 128:(cl + 1) * 128], Win[:, k, UO + ct * 128:UO + (ct + 1) * 128], xnT[b][:, k, :], k == 0, k == 7,
                           [f"xnT{b}", "Win"], pk, (cl == 3 and k == 7))
                V(lambda e, p=p, half=half, b=b: e.tensor_copy(out=uTb[b][:, half * 4:(half + 1) * 4, :], in_=p.rearrange("p (a c) -> p a c", c=128)),
                  [pk], [f"uTb{b}"])
            fw.dma("sync", uT_s[sidx], uTb[b], reads=[f"uTb{b}"], writes=[("uT_s", sidx)])
        if kind == "main":
            for half in range(2):
                p, pk = psum()
                for k in range(8):
                    mm(p, xnT[b][:, k, :], Win[:, k, QO + half * 512:QO + (half + 1) * 512], k == 0, k == 7, [f"xnT{b}", "Win"], pk, k == 7)
                headnorm(p, pk, 512, 8, qn[b][:, half * 512:(half + 1) * 512], f"qn{b}")
            transpose8(qn[b], qTb[b], f"qn{b}", f"qTb{b}")
            fw.dma("sync", qT_s[midx], qTb[b], reads=[f"qTb{b}"], writes=[("qT_s", midx)])
            for j in range(4):
                p, pk = psum()
                for k in range(8):
                    mm(p, xnT[b][:, k, :], Win[:, k, GO + j * 512:GO + (j + 1) * 512], k == 0, k == 7, [f"xnT{b}", "Win"], pk, k == 7)
                A_(lambda e, p=p, j=j, b=b: e.activation(out=gb[b][:, j * 512:(j + 1) * 512], in_=p, func=AF.Sigmoid), [pk], [f"gb{b}"])
            fw.dma("sync", g_s[midx], gb[b], reads=[f"gb{b}"], writes=[("g_s", midx)])
        if kind in ("ctx", "main"):
            kidx = sidx if kind == "ctx" else 2 + midx
            p, pk = psum()
            for k in range(8):
                mm(p, xnT[b][:, k, :], Win[:, k, KVO:KVO + 512], k == 0, k == 7, [f"xnT{b}", "Win"], pk, k == 7)
            headnorm(p, pk, 256, 4, kn[b], f"kn{b}", gain=gqk)
            V(lambda e, p=p, b=b: e.tensor_copy(out=vab[b][:, :, 0:64], in_=p[:, 256:512].rearrange("p (h d) -> p h d", d=64)), [pk], [f"vab{b}"])
            transpose8(kn[b], kTb[b], f"kn{b}", f"kTb{b}", n=2)
            fw.dma("sync", kT_s[kidx], kTb[b], reads=[f"kTb{b}"], writes=[("kT_s", kidx)])
            fw.dma("sync", v_s[kidx], vab[b], reads=[f"vab{b}"], writes=[("v_s", kidx)])
    fw.barrier()
    ar.reset(m1)

    if upto <= 1:
        fw.emit()
        return nc
    lamsb = ar.alloc("lamsb", [128, 3, 32], F32)
    fw.dma("sync", lamsb, lam.rearrange("a p g -> p a g"), writes=["lam"])
    smn = ["dt", "th", "rho", "sn", "cs", "ar", "ai", "fr", "fi", "t1", "t2", "t3", "den", "wr", "wi", "w128r", "w128i", "mk", "x2", "lrdt"]
    sm = {n: ar.alloc(n, [128, 32], F32) for n in smn}
    pwr = [ar.alloc("pwr", [128, 32], F32) for _ in range(9)]; pwi = [ar.alloc("pwi", [128, 32], F32) for _ in range(9)]
    Er = ar.alloc("Er", [128, 32, 128], F32); Ei = ar.alloc("Ei", [128, 32, 128], F32)
    Kpad = ar.alloc("Kpad", [128, 8, 8, 128], BF16)
    cR = ar.alloc("cR", [128, 2, 32], F32); SL = ar.alloc("SL", [128, 2, 32], F32)
    base_ssm = ar.mark()
    lr, li, ld = lamsb[:, 0, :], lamsb[:, 1, :], lamsb[:, 2, :]
    K = ["ssm0"]
    A_(lambda e: e.activation(out=sm["dt"], in_=ld, func=AF.Exp), ["lam"], K)
    V(lambda e: e.tensor_tensor(out=sm["th"], in0=li, in1=sm["dt"], op=ALU.mult), K, K)
    V(lambda e: e.tensor_tensor(out=sm["lrdt"], in0=lr, in1=sm["dt"], op=ALU.mult), K, K)
    A_(lambda e: e.activation(out=sm["rho"], in_=sm["lrdt"], func=AF.Exp), K, K)
    for _ in range(5):
        V(lambda e: e.tensor_single_scalar(out=sm["mk"], in_=sm["th"], scalar=math.pi, op=ALU.is_gt), K, K)
        V(lambda e: e.scalar_tensor_tensor(out=sm["th"], in0=sm["mk"], scalar=-2.0 * math.pi, in1=sm["th"], op0=ALU.mult, op1=ALU.add), K, K)
    V(lambda e: e.tensor_scalar(out=sm["t3"], in0=sm["th"], scalar1=0.125, scalar2=None, op0=ALU.mult), K, K)
    V(lambda e: e.tensor_tensor(out=sm["x2"], in0=sm["t3"], in1=sm["t3"], op=ALU.mult), K, K)

    def horner(o, coefs):
        V(lambda e: e.memset(o, coefs[0]), K, K)
        for c in coefs[1:]:
            V(lambda e: e.tensor_tensor(out=o, in0=o, in1=sm["x2"], op=ALU.mult), K, K)
            V(lambda e, c=c: e.tensor_scalar(out=o, in0=o, scalar1=float(c), scalar2=None, op0=ALU.add), K, K)

    def cdouble(sn_, cs_):
        V(lambda e: e.tensor_tensor(out=sm["t1"], in0=sn_, in1=cs_, op=ALU.mult), K, K)
        V(lambda e: e.tensor_tensor(out=sm["t2"], in0=cs_, in1=cs_, op=ALU.mult), K, K)
        V(lambda e: e.tensor_tensor(out=sm["t3"], in0=sn_, in1=sn_, op=ALU.mult), K, K)
        V(lambda e: e.tensor_scalar(out=sn_, in0=sm["t1"], scalar1=2.0, scalar2=None, op0=ALU.mult), K, K)
        V(lambda e: e.tensor_tensor(out=cs_, in0=sm["t2"], in1=sm["t3"], op=ALU.subtract), K, K)

    horner(sm["sn"], [-1 / 39916800.0, 1 / 362880.0, -1 / 5040.0, 1 / 120.0, -1 / 6.0, 1.0])
    V(lambda e: e.tensor_tensor(out=sm["sn"], in0=sm["sn"], in1=sm["t3"], op=ALU.mult), K, K)
    horner(sm["cs"], [-1 / 3628800.0, 1 / 40320.0, -1 / 720.0, 1 / 24.0, -0.5, 1.0])
    for _ in range(3):
        cdouble(sm["sn"], sm["cs"])
    V(lambda e: e.tensor_tensor(out=sm["ar"], in0=sm["rho"], in1=sm["cs"], op=ALU.mult), K, K)
    V(lambda e: e.tensor_tensor(out=sm["ai"], in0=sm["rho"], in1=sm["sn"], op=ALU.mult), K, K)
    V(lambda e: e.tensor_scalar(out=sm["t1"], in0=sm["ar"], scalar1=-1.0, scalar2=None, op0=ALU.add), K, K)
    V(lambda e: e.tensor_tensor(out=sm["den"], in0=lr, in1=lr, op=ALU.mult), K, K)
    V(lambda e: e.tensor_tensor(out=sm["t2"], in0=li, in1=li, op=ALU.mult), K, K)
    V(lambda e: e.tensor_tensor(out=sm["den"], in0=sm["den"], in1=sm["t2"], op=ALU.add), K, K)
    V(lambda e: e.reciprocal(out=sm["den"], in_=sm["den"]), K, K)
    V(lambda e: e.tensor_tensor(out=sm["t2"], in0=sm["t1"], in1=lr, op=ALU.mult), K, K)
    V(lambda e: e.tensor_tensor(out=sm["t3"], in0=sm["ai"], in1=li, op=ALU.mult), K, K)
    V(lambda e: e.tensor_tensor(out=sm["t2"], in0=sm["t2"], in1=sm["t3"], op=ALU.add), K, K)
    V(lambda e: e.tensor_tensor(out=sm["fr"], in0=sm["t2"], in1=sm["den"], op=ALU.mult), K, K)
    V(lambda e: e.tensor_tensor(out=sm["t2"], in0=sm["ai"], in1=lr, op=ALU.mult), K, K)
    V(lambda e: e.tensor_tensor(out=sm["t3"], in0=sm["t1"], in1=li, op=ALU.mult), K, K)
    V(lambda e: e.tensor_tensor(out=sm["t2"], in0=sm["t2"], in1=sm["t3"], op=ALU.subtract), K, K)
    V(lambda e: e.tensor_tensor(out=sm["fi"], in0=sm["t2"], in1=sm["den"], op=ALU.mult), K, K)
    V(lambda e: e.memset(pwr[0], 1.0), K, K)
    V(lambda e: e.memset(pwi[0], 0.0), K, K)
    for k in range(1, 9):
        V(lambda e, k=k: e.tensor_tensor(out=sm["t1"], in0=pwr[k - 1], in1=sm["ar"], op=ALU.mult), K, K)
        V(lambda e, k=k: e.tensor_tensor(out=sm["t2"], in0=pwi[k - 1], in1=sm["ai"], op=ALU.mult), K, K)
        V(lambda e, k=k: e.tensor_tensor(out=pwr[k], in0=sm["t1"], in1=sm["t2"], op=ALU.subtract), K, K)
        V(lambda e, k=k: e.tensor_tensor(out=sm["t1"], in0=pwr[k - 1], in1=sm["ai"], op=ALU.mult), K, K)
        V(lambda e, k=k: e.tensor_tensor(out=sm["t2"], in0=pwi[k - 1], in1=sm["ar"], op=ALU.mult), K, K)
        V(lambda e, k=k: e.tensor_tensor(out=pwi[k], in0=sm["t1"], in1=sm["t2"], op=ALU.add), K, K)
    V(lambda e: e.tensor_scalar(out=sm["t1"], in0=sm["lrdt"], scalar1=8.0, scalar2=None, op0=ALU.mult), K, K)
    A_(lambda e: e.activation(out=sm["rho"], in_=sm["t1"], func=AF.Exp), K, K)
    for _ in range(3):
        cdouble(sm["sn"], sm["cs"])
    V(lambda e: e.memset(Er[:, :, 0:1], 1.0), K, K)
    V(lambda e: e.memset(Ei[:, :, 0:1], 0.0), K, K)
    V(lambda e: e.tensor_copy(out=sm["wr"], in_=sm["cs"]), K, K)
    V(lambda e: e.tensor_copy(out=sm["wi"], in_=sm["sn"]), K, K)
    m0 = ar.mark()
    tA = ar.alloc("tA", [128, 32, 64], F32); tB = ar.alloc("tB", [128, 32, 64], F32)
    for k in range(7):
        n = 1 << k
        wrb = sm["wr"].unsqueeze(2).broadcast_to([128, 32, n]); wib = sm["wi"].unsqueeze(2).broadcast_to([128, 32, n])
        V(lambda e, n=n, wrb=wrb: e.tensor_tensor(out=tA[:, :, :n], in0=Er[:, :, :n], in1=wrb, op=ALU.mult), K, K)
        V(lambda e, n=n, wib=wib: e.tensor_tensor(out=tB[:, :, :n], in0=Ei[:, :, :n], in1=wib, op=ALU.mult), K, K)
        V(lambda e, n=n: e.tensor_tensor(out=Er[:, :, n:2 * n], in0=tA[:, :, :n], in1=tB[:, :, :n], op=ALU.subtract), K, K)
        V(lambda e, n=n, wib=wib: e.tensor_tensor(out=tA[:, :, :n], in0=Er[:, :, :n], in1=wib, op=ALU.mult), K, K)
        V(lambda e, n=n, wrb=wrb: e.tensor_tensor(out=tB[:, :, :n], in0=Ei[:, :, :n], in1=wrb, op=ALU.mult), K, K)
        V(lambda e, n=n: e.tensor_tensor(out=Ei[:, :, n:2 * n], in0=tA[:, :, :n], in1=tB[:, :, :n], op=ALU.add), K, K)
        V(lambda e: e.tensor_tensor(out=sm["t1"], in0=sm["wr"], in1=sm["wr"], op=ALU.mult), K, K)
        V(lambda e: e.tensor_tensor(out=sm["t2"], in0=sm["wi"], in1=sm["wi"], op=ALU.mult), K, K)
        V(lambda e: e.tensor_tensor(out=sm["t3"], in0=sm["wr"], in1=sm["wi"], op=ALU.mult), K, K)
        V(lambda e: e.tensor_tensor(out=sm["wr"], in0=sm["t1"], in1=sm["t2"], op=ALU.subtract), K, K)
        V(lambda e: e.tensor_scalar(out=sm["wi"], in0=sm["t3"], scalar1=2.0, scalar2=None, op0=ALU.mult), K, K)
    V(lambda e: e.tensor_copy(out=sm["w128r"], in_=sm["wr"]), K, K)
    V(lambda e: e.tensor_copy(out=sm["w128i"], in_=sm["wi"]), K, K)
    V(lambda e: e.memset(cR, 0.0), K, K)
    V(lambda e: e.memset(SL, 0.0), K, K)
    fw.barrier()
    ar.reset(m0)
    BTc = ar.alloc("BTc", [128, 32, 2, 16], F32); Cc = ar.alloc("Cc", [128, 2, 32, 16], F32)
    Cfc = ar.alloc("Cfc", [128, 32, 2, 16], F32); Xc = ar.alloc("Xc", [128, 32, 2, 16], F32)
    c1 = ar.alloc("c1", [128, 32, 16], F32); c2 = ar.alloc("c2", [128, 32, 16], F32)
    padf = ar.alloc("padf", [128, 32, 2, 128], F32); Cfp = ar.alloc("Cfp", [128, 32, 2, 128], F32)
    padb = ar.alloc("padb", [128, 32, 2, 128], BF16); DBsb = ar.alloc("DBsb", [128, 32, 2, 128], BF16)
    fw.dma("sync", BTc, btc, writes=["BTc"])
    fw.dma("sync", Cc, cc.rearrange("a p g c -> p a g c"), writes=["Cc"])
    G_(lambda e: e.memset(padf, 0.0), [], ["padf"])
    G_(lambda e: e.memset(Cfp, 0.0), [], ["Cfp"])

    def cmul_compact(dst, src_r, src_i, sr, si, rk, wk, neg_im=False):
        srb = sr.unsqueeze(2).broadcast_to([128, 32, 16]); sib = si.unsqueeze(2).broadcast_to([128, 32, 16])
        V(lambda e: e.tensor_tensor(out=c1, in0=src_r, in1=srb, op=ALU.mult), rk, ["c1"])
        V(lambda e: e.tensor_tensor(out=c2, in0=src_i, in1=sib, op=ALU.mult), rk, ["c2"])
        V(lambda e: e.tensor_tensor(out=dst[:, :, 0, :], in0=c1, in1=c2, op=ALU.subtract), ["c1", "c2"], wk)
        V(lambda e: e.tensor_tensor(out=c1, in0=src_r, in1=sib, op=ALU.mult), rk + wk, ["c1"])
        V(lambda e: e.tensor_tensor(out=c2, in0=src_i, in1=srb, op=ALU.mult), rk + wk, ["c2"])
        if neg_im:
            V(lambda e: e.scalar_tensor_tensor(out=dst[:, :, 1, :], in0=c1, scalar=-1.0, in1=c2, op0=ALU.mult, op1=ALU.subtract), ["c1", "c2"], wk)
        else:
            V(lambda e: e.tensor_tensor(out=dst[:, :, 1, :], in0=c1, in1=c2, op=ALU.add), ["c1", "c2"], wk)

    def scatter(dst_pad, src_c, rk, wk):
        for g2 in range(2):
            for q in range(4):
                blk = 2 * q + g2
                G_(lambda e, g2=g2, q=q, blk=blk: e.tensor_copy(out=dst_pad[g2 * 64:(g2 + 1) * 64, q::4, :, blk * 16:(blk + 1) * 16],
                                                                 in_=src_c[g2 * 64:(g2 + 1) * 64, q::4, :, :]), rk, wk)

    cmul_compact(Cfc, Cc[:, 0], Cc[:, 1], sm["fr"], sm["fi"], ["Cc"], ["Cfc"])
    cmul_compact(Xc, Cc[:, 0], Cc[:, 1], sm["fr"], sm["fi"], ["Cc"], ["Xc"], neg_im=True)
    scatter(Cfp, Xc, ["Xc"], ["Cfp"])
    for k in range(8):
        j = 7 - k
        cmul_compact(Xc, BTc[:, :, 0, :], BTc[:, :, 1, :], pwr[k], pwi[k], ["BTc"], ["Xc"])
        scatter(padf, Xc, ["Xc"], ["padf"])
        for r in range(8):
            p, pk = psum()
            n_ = 0
            for gl in range(4):
                for ri in range(2):
                    mm(p[:, 0:128], padf[:, 4 * r + gl, ri, :], Cfp[:, 4 * r + gl, ri, :], n_ == 0, n_ == 7, ["padf", "Cfp"], pk, n_ == 7)
                    n_ += 1
            if k == 0:
                V(lambda e, p=p, r=r: e.scalar_tensor_tensor(out=Kpad[:, r, 0, :], in0=identf, scalar=dcs[:, r:r + 1], in1=p[:, 0:128], op0=ALU.mult, op1=ALU.add),
                  [pk, "identf", "dcs"], ["Kpad"])
            else:
                V(lambda e, p=p, r=r, k=k: e.tensor_copy(out=Kpad[:, r, k, :], in_=p[:, 0:128]), [pk], ["Kpad"])
        V(lambda e: e.tensor_copy(out=padb, in_=padf), ["padf"], ["padb"])
        for g4 in range(16):
            p, pk = psum()
            for q in range(4):
                gi = g4 * 4 + q
                mm(p[:, q * 128:(q + 1) * 128], padb[:, gi // 2, gi % 2, :], identb, True, True, ["padb", "identb"], pk, q == 3)
            V(lambda e, p=p, g4=g4: e.tensor_copy(out=DBsb.rearrange("p g r s -> p (g r) s")[:, g4 * 4:(g4 + 1) * 4, :], in_=p.rearrange("p (a c) -> p a c", c=128)),
              [pk], ["DBsb"])
        fw.dma("sync", DB_s[:, :, j].rearrange("r p g a s -> p r g a s"), DBsb.rearrange("p (r g) a s -> p r g a s", r=8), reads=["DBsb"], writes=[("DB_s", j)])
    for j in range(8):
        cmul_compact(Xc, Cfc[:, :, 0, :], Cfc[:, :, 1, :], pwr[j + 1], pwi[j + 1], ["Cfc"], ["Xc"])
        scatter(padf, Xc, ["Xc"], ["padf"])
        V(lambda e: e.tensor_copy(out=padb, in_=padf), ["padf"], ["padb"])
        fw.dma("sync", EC_s[:, :, j].rearrange("r p g a s -> p r g a s"), padb.rearrange("p (r g) a s -> p r g a s", r=8), reads=["padb"], writes=[("EC_s", j)])
    fw.barrier()
    ar.reset(base_ssm)

    if upto <= 2:
        fw.emit()
        return nc
    NPS, NMS = NP // 1024, NM // 1024
    DBr = ar.alloc("DBr", [128, 8, 4, 2, 128], BF16); ECr = ar.alloc("ECr", [128, 8, 4, 2, 128], BF16)
    uTr = [ar.alloc("uTr", [128, 1024], BF16) for _ in range(2)]
    zTr = [ar.alloc("zTr", [128, 1024], BF16) for _ in range(2)]
    RB = []
    for b in range(2):
        d = {n: ar.alloc(n, [128, 4, 128], F32) for n in ["t1", "t2", "t3", "t4", "Xr", "Xi", "Rr", "Ri"]}
        d["Sr"] = ar.alloc("Sr", [128, 4, 130], BF16); d["Si"] = ar.alloc("Si", [128, 4, 130], BF16)
        for n in ["c1", "c2", "c3", "c4"]:
            d[n] = ar.alloc(n, [128, 4], F32)
        RB.append(d)
    ysb = [ar.alloc("ysb", [128, 512], F32) for _ in range(2)]
    g1b = [ar.alloc("g1b", [128, 512], F32) for _ in range(2)]
    g2b = [ar.alloc("g2b", [128, 512], F32) for _ in range(2)]
    rcount = 0
    hcount = 0
    for r in range(8):
        gsl = slice(4 * r, 4 * r + 4)
        fw.dma("sync", DBr, DB_s[r], writes=["DBr"])
        fw.dma("sync", ECr, EC_s[r], writes=["ECr"])
        for st in range(NPS + NMS):
            is_main = st >= NPS
            ub = st % 2
            fw.dma("sync", uTr[ub].rearrange("p (n t) -> p n t", t=128), uT_s[8 * st:8 * st + 8, :, r, :].rearrange("n p t -> p n t"),
                   writes=[f"uTr{ub}"])
            b = rcount % 2
            rcount += 1
            B = RB[b]
            kb = lambda n, b=b: f"{n}{b}"
            pXr, pkr = psum()
            pXi, pki = psum()
            for ri, (pX, pk) in enumerate(((pXr, pkr), (pXi, pki))):
                for gl in range(4):
                    for j in range(8):
                        mm(pX[:, gl * 128:(gl + 1) * 128], DBr[:, j, gl, ri, :], uTr[ub][:, j::8], j == 0, j == 7,
                           [f"uTr{ub}", "DBr"], pk, (gl == 3 and j == 7))
            pXr3 = pXr.rearrange("p (a c) -> p a c", c=128); pXi3 = pXi.rearrange("p (a c) -> p a c", c=128)
            Erg, Eig = Er[:, gsl, :], Ei[:, gsl, :]
            V(lambda e, B=B, a=pXr3, t=Erg: e.tensor_tensor(out=B["t1"], in0=a, in1=t, op=ALU.mult), [pkr], [kb("t1")])
            V(lambda e, B=B, a=pXi3, t=Eig: e.tensor_tensor(out=B["t2"], in0=a, in1=t, op=ALU.mult), [pki], [kb("t2")])
            V(lambda e, B=B, a=pXi3, t=Erg: e.tensor_tensor(out=B["t3"], in0=a, in1=t, op=ALU.mult), [pki], [kb("t3")])
            V(lambda e, B=B, a=pXr3, t=Eig: e.tensor_tensor(out=B["t4"], in0=a, in1=t, op=ALU.mult), [pkr], [kb("t4")])
            G_(lambda e, B=B: e.tensor_tensor(out=B["Xr"], in0=B["t1"], in1=B["t2"], op=ALU.add), [kb("t1"), kb("t2")], [kb("Xr")])
            G_(lambda e, B=B: e.tensor_tensor(out=B["Xi"], in0=B["t3"], in1=B["t4"], op=ALU.subtract), [kb("t3"), kb("t4")], [kb("Xi")])
            for gl in range(4):
                gp = 4 * r + gl
                for nm, xs, ci in (("Rr", "Xr", 0), ("Ri", "Xi", 1)):
                    V(lambda e, B=B, gl=gl, gp=gp, nm=nm, xs=xs, ci=ci: e.tensor_tensor_scan(
                        out=B[nm][:, gl, :], data0=sm["rho"][:, gp:gp + 1].broadcast_to([128, 128]), data1=B[xs][:, gl, :],
                        initial=cR[:, ci, gp:gp + 1], op0=ALU.mult, op1=ALU.add), [kb(xs), ("cR", r)], [kb(nm)])
            if is_main:
                G_(lambda e, B=B, gsl=gsl: e.tensor_copy(out=B["Sr"][:, :, 0], in_=SL[:, 0, gsl]), [("SL", r)], [kb("Sr")])
                G_(lambda e, B=B, gsl=gsl: e.tensor_copy(out=B["Si"][:, :, 0], in_=SL[:, 1, gsl]), [("SL", r)], [kb("Si")])
            wr4, wi4 = sm["w128r"][:, gsl], sm["w128i"][:, gsl]
            er7, ei7 = Er[:, gsl, 127], Ei[:, gsl, 127]
            Rr7, Ri7 = B["Rr"][:, :, 127], B["Ri"][:, :, 127]
            for (xr_, xi_, dst, negim, key) in ((wr4, wi4, cR, False, "cR"), (er7, ei7, SL, True, "SL")):
                G_(lambda e, B=B, a=Rr7, w=xr_: e.tensor_tensor(out=B["c1"], in0=a, in1=w, op=ALU.mult), [kb("Rr")], [kb("c1")])
                G_(lambda e, B=B, a=Ri7, w=xi_: e.tensor_tensor(out=B["c2"], in0=a, in1=w, op=ALU.mult), [kb("Ri")], [kb("c2")])
                G_(lambda e, B=B, a=Ri7, w=xr_: e.tensor_tensor(out=B["c3"], in0=a, in1=w, op=ALU.mult), [kb("Ri")], [kb("c3")])
                G_(lambda e, B=B, a=Rr7, w=xi_: e.tensor_tensor(out=B["c4"], in0=a, in1=w, op=ALU.mult), [kb("Rr")], [kb("c4")])
                G_(lambda e, B=B, dst=dst, gsl=gsl: e.tensor_tensor(out=dst[:, 0, gsl], in0=B["c1"], in1=B["c2"], op=ALU.subtract),
                   [kb("c1"), kb("c2")], [(key, r)])
                if negim:
                    V(lambda e, B=B, dst=dst, gsl=gsl: e.scalar_tensor_tensor(out=dst[:, 1, gsl], in0=B["c3"], scalar=-1.0, in1=B["c4"], op0=ALU.mult, op1=ALU.subtract),
                       [kb("c3"), kb("c4")], [(key, r)])
                else:
                    G_(lambda e, B=B, dst=dst, gsl=gsl: e.tensor_tensor(out=dst[:, 1, gsl], in0=B["c3"], in1=B["c4"], op=ALU.add),
                       [kb("c3"), kb("c4")], [(key, r)])
            if not is_main:
                continue
            G_(lambda e, B=B, t=Erg: e.tensor_tensor(out=B["t1"], in0=B["Rr"], in1=t, op=ALU.mult), [kb("Rr")], [kb("t1")])
            G_(lambda e, B=B, t=Eig: e.tensor_tensor(out=B["t2"], in0=B["Ri"], in1=t, op=ALU.mult), [kb("Ri")], [kb("t2")])
            V(lambda e, B=B, t=Erg: e.tensor_tensor(out=B["t3"], in0=B["Ri"], in1=t, op=ALU.mult), [kb("Ri")], [kb("t3")])
            V(lambda e, B=B, t=Eig: e.tensor_tensor(out=B["t4"], in0=B["Rr"], in1=t, op=ALU.mult), [kb("Rr")], [kb("t4")])
            V(lambda e, B=B: e.tensor_tensor(out=B["Sr"][:, :, 1:129], in0=B["t1"], in1=B["t2"], op=ALU.subtract), [kb("t1"), kb("t2")], [kb("Sr")])
            V(lambda e, B=B: e.scalar_tensor_tensor(out=B["Si"][:, :, 1:129], in0=B["t3"], scalar=-1.0, in1=B["t4"], op0=ALU.mult, op1=ALU.subtract),
              [kb("t3"), kb("t4")], [kb("Si")])
            zb = (st - NPS) % 2
            for half in range(2):
                py, pky = psum()
                for j in range(8):
                    o = py[:, j::8]
                    nmm = (j + 1) + 8
                    n_ = 0
                    for k in range(j + 1):
                        s0 = half * 512 + (j - k)
                        mm(o, Kpad[:, r, k, :], uTr[ub][:, s0:s0 + 505:8], n_ == 0, n_ == nmm - 1, [f"uTr{ub}", "Kpad"], pky, False)
                        n_ += 1
                    for gl in range(4):
                        for ri, Sn in enumerate(("Sr", "Si")):
                            mm(o, ECr[:, j, gl, ri, :], B[Sn][:, gl, half * 64:half * 64 + 64], n_ == 0, n_ == nmm - 1, [kb(Sn), "ECr"], pky,
                               (j == 7 and n_ == nmm - 1))
                            n_ += 1
                hb = hcount % 2
                hcount += 1
                A_(lambda e, py=py, hb=hb: e.activation(out=ysb[hb], in_=py, func=AF.Identity), [pky], [f"ysb{hb}"])
                G_(lambda e, hb=hb: e.tensor_tensor(out=g1b[hb], in0=ysb[hb], in1=ysb[hb], op=ALU.mult), [f"ysb{hb}"], [f"g1b{hb}"])
                G_(lambda e, hb=hb: e.tensor_scalar(out=g1b[hb], in0=g1b[hb], scalar1=0.044715, scalar2=1.0, op0=ALU.mult, op1=ALU.add), [f"g1b{hb}"], [f"g1b{hb}"])
                G_(lambda e, hb=hb: e.tensor_tensor(out=g1b[hb], in0=g1b[hb], in1=ysb[hb], op=ALU.mult), [f"g1b{hb}", f"ysb{hb}"], [f"g1b{hb}"])
                A_(lambda e, hb=hb: e.activation(out=g2b[hb], in_=g1b[hb], func=AF.Sigmoid, scale=1.5957691216057308), [f"g1b{hb}"], [f"g2b{hb}"])
                V(lambda e, hb=hb, zb=zb, half=half: e.tensor_tensor(out=zTr[zb][:, half * 512:(half + 1) * 512], in0=ysb[hb], in1=g2b[hb], op=ALU.mult),
                  [f"ysb{hb}", f"g2b{hb}"], [f"zTr{zb}"])
            m8 = 8 * (st - NPS)
            fw.dma("sync", zT_s[m8:m8 + 8, :, r, :].rearrange("n p t -> p n t"), zTr[zb].rearrange("p (n t) -> p n t", t=128),
                   reads=[f"zTr{zb}"], writes=[("zT_s", st, r)])
    fw.barrier()
    ar.reset(base_persist)

    if upto <= 3:
        fw.emit()
        return nc
    Wg = ar.alloc("Wg", [128, 8, 2048], BF16); Wo = ar.alloc("Wo", [128, 8, 1024], BF16)
    load_weights(Wg, w_glu, 4, 8, "Wg")
    load_weights(Wo, w_out, 2, 8, "Wo")
    kme = ar.alloc("kme", [128, 2, 128], BF16); vme = ar.alloc("vme", [128, 4, 65], BF16)
    fw.dma("sync", kme, kT_s[0], writes=["kme"]); fw.dma("sync", vme, v_s[0], writes=["vme"])
    qTl = [ar.alloc("qTl", [128, 8, 128], BF16) for _ in range(2)]
    kTl = [ar.alloc("kTl", [128, 2, 128], BF16) for _ in range(3)]
    vl = [ar.alloc("vl", [128, 4, 65], BF16) for _ in range(3)]
    gl_ = [ar.alloc("gl", [128, 2048], BF16) for _ in range(2)]
    zTl = [ar.alloc("zTl", [128, 8, 128], BF16) for _ in range(2)]
    xr = [ar.alloc("xr", [128, D], F32) for _ in range(2)]
    Pc = [ar.alloc("Pc", [128, 512], BF16) for _ in range(2)]
    Pp = [ar.alloc("Pp", [128, 512], BF16) for _ in range(2)]
    Pm = [ar.alloc("Pm", [128, 512], BF16) for _ in range(2)]
    den = ar.alloc("den", [128, 4], F32)
    for b_ in range(2):
        V(lambda e, b_=b_: e.memset(Pm[b_], 0.0), [], [f"Pm{b_}"])
    attn = ar.alloc("attn", [128, D], F32); An = ar.alloc("An", [128, D], F32)
    sig = ar.alloc("sig", [128, 512], F32); ssm = ar.alloc("ssm", [128, D], F32); Bn = ar.alloc("Bn", [128, D], F32)
    mg = ar.alloc("mg", [128, D], BF16); mgT = ar.alloc("mgT", [128, 8, 128], BF16)
    h1 = [ar.alloc("h1", [128, D], F32) for _ in range(2)]
    fw.dma("sync", kTl[1], kT_s[1], writes=["kTl1"]); fw.dma("sync", vl[1], v_s[1], writes=["vl1"])
    pcount = 0
    for i in range(TM_):
        b = i % 2
        jc, jp = 2 + i, 1 + i
        sc, sp = jc % 3, jp % 3
        fw.dma("sync", kTl[sc], kT_s[jc], reads=[("kT_s", jc)], writes=[f"kTl{sc}"])
        fw.dma("sync", vl[sc], v_s[jc], reads=[("v_s", jc)], writes=[f"vl{sc}"])
        fw.dma("sync", qTl[b], qT_s[i], writes=[f"qTl{b}"])
        fw.dma("sync", gl_[b], g_s[i], writes=[f"gl{b}"])
        fw.dma("sync", zTl[b], zT_s[i], writes=[f"zTl{b}"])
        fw.dma("sync", xr[b], xmain[i * 128:(i + 1) * 128, :], writes=[f"xr{b}"])
        for grp in range(4):
            pb = pcount % 2
            pcount += 1
            bs = (grp % 2) * 64
            kc = grp // 2
            qsel = qTl[b][bs:bs + 64, kc * 4:(kc + 1) * 4, :]
            pS, pkS = psum()
            mm(pS, kTl[sc][bs:bs + 64, kc, :], qsel, True, True, [f"kTl{sc}", f"qTl{b}"], pkS, True)
            A_(lambda e, pS=pS, pb=pb: e.activation(out=Pc[pb], in_=pS, func=AF.Exp, scale=0.125), [pkS], [f"Pc{pb}"])
            G_(lambda e, pb=pb: e.tensor_tensor(out=Pc[pb], in0=Pc[pb], in1=maskb[:, 0, :], op=ALU.mult), [f"Pc{pb}", "maskb"], [f"Pc{pb}"])
            pS2, pkS2 = psum()
            mm(pS2, kTl[sp][bs:bs + 64, kc, :], qsel, True, True, [f"kTl{sp}", f"qTl{b}"], pkS2, True)
            A_(lambda e, pS2=pS2, pb=pb: e.activation(out=Pp[pb], in_=pS2, func=AF.Exp, scale=0.125), [pkS2], [f"Pp{pb}"])
            mi = 2 if i == 0 else 1
            G_(lambda e, pb=pb, mi=mi: e.tensor_tensor(out=Pp[pb], in0=Pp[pb], in1=maskb[:, mi, :], op=ALU.mult), [f"Pp{pb}", "maskb"], [f"Pp{pb}"])
            pS3, pkS3 = psum()
            mm(pS3[0:16, :], kme[bs:bs + 64, kc, 0:16], qsel, True, True, ["kme", f"qTl{b}"], pkS3, True)
            A_(lambda e, pS3=pS3, pb=pb: e.activation(out=Pm[pb][0:16, :], in_=pS3[0:16, :], func=AF.Exp, scale=0.125), [pkS3], [f"Pm{pb}"])
            pO, pkO = psum()
            for r in range(4):
                o = pO[:, r * 65:(r + 1) * 65]
                mm(o, Pm[pb][:, r * 128:(r + 1) * 128], vme[:, grp, :], True, False, [f"Pm{pb}", "vme"], pkO, False)
                mm(o, Pp[pb][:, r * 128:(r + 1) * 128], vl[sp][:, grp, :], False, False, [f"Pp{pb}", f"vl{sp}"], pkO, False)
                mm(o, Pc[pb][:, r * 128:(r + 1) * 128], vl[sc][:, grp, :], False, True, [f"Pc{pb}", f"vl{sc}"], pkO, r == 3)
            pO3 = pO[:, 0:260].rearrange("p (r c) -> p r c", c=65)
            V(lambda e, pO3=pO3, grp=grp: e.tensor_tensor(out=den, in0=pO3[:, :, 64], in1=esink[:, grp * 4:(grp + 1) * 4], op=ALU.add), [pkO, "esink"], ["den"])
            V(lambda e: e.reciprocal(out=den, in_=den), ["den"], ["den"])
            V(lambda e, pO3=pO3, grp=grp: e.tensor_tensor(out=attn[:, grp * 256:(grp + 1) * 256].rearrange("p (r d) -> p r d", d=64), in0=pO3[:, :, 0:64],
                                                         in1=den.unsqueeze(2).broadcast_to([128, 4, 64]), op=ALU.mult), [pkO, "den"], ["attn"])
        rms_scale(attn, 1, An, ["attn"], ["An"])
        G_(lambda e, b=b: e.tensor_tensor(out=An, in0=An, in1=gl_[b][:, 0:1024], op=ALU.mult), ["An", f"gl{b}"], ["An"])
        for half in range(2):
            pa, pka = psum()
            for k in range(8):
                mm(pa, zTl[b][:, k, :], Wg[:, k, half * 512:(half + 1) * 512], k == 0, k == 7, [f"zTl{b}", "Wg"], pka, k == 7)
            pz, pkz = psum()
            for k in range(8):
                mm(pz, zTl[b][:, k, :], Wg[:, k, 1024 + half * 512:1024 + (half + 1) * 512], k == 0, k == 7, [f"zTl{b}", "Wg"], pkz, k == 7)
            A_(lambda e, pz=pz: e.activation(out=sig, in_=pz, func=AF.Sigmoid), [pkz], ["sig"])
            V(lambda e, pa=pa, half=half: e.tensor_tensor(out=ssm[:, half * 512:(half + 1) * 512], in0=pa, in1=sig, op=ALU.mult), [pka, "sig"], ["ssm"])
        rms_scale(ssm, 2, Bn, ["ssm"], ["Bn"])
        G_(lambda e, b=b: e.tensor_tensor(out=Bn, in0=Bn, in1=gl_[b][:, 1024:2048], op=ALU.mult), ["Bn", f"gl{b}"], ["Bn"])
        V(lambda e: e.tensor_tensor(out=mg, in0=An, in1=Bn, op=ALU.add), ["An", "Bn"], ["mg"])
        transpose8(mg, mgT, "mg", "mgT")
        for half in range(2):
            p, pk = psum()
            for k in range(8):
                mm(p, mgT[:, k, :], Wo[:, k, half * 512:(half + 1) * 512], k == 0, k == 7, ["mgT", "Wo"], pk, k == 7)
            V(lambda e, p=p, half=half, b=b: e.tensor_tensor(out=h1[b][:, half * 512:(half + 1) * 512], in0=p, in1=xr[b][:, half * 512:(half + 1) * 512], op=ALU.add),
              [pk, f"xr{b}"], [f"h1{b}"])
        fw.dma("sync", h1_s[i * 128:(i + 1) * 128, :], h1[b], reads=[f"h1{b}"], writes=[("h1_s", i)])
    fw.barrier()
    ar.reset(base_persist)

    if upto <= 4:
        fw.emit()
        return nc
    W1 = ar.alloc("W1", [128, 8, 5632], BF16); W2 = ar.alloc("W2", [128, 22, 1024], BF16)
    load_weights(W1, w_f1, 11, 8, "W1")
    load_weights(W2, w_f2, 2, 22, "W2")
    hl = [ar.alloc("hl", [128, D], F32) for _ in range(2)]
    hn = ar.alloc("hn", [128, D], BF16); hnT = ar.alloc("hnT", [128, 8, 128], BF16)
    sg = [ar.alloc("sg", [128, 256], F32) for _ in range(2)]
    actT = ar.alloc("actT", [128, 22, 128], BF16)
    ob = [ar.alloc("ob", [128, D], F32) for _ in range(2)]
    for i in range(TM_):
        b = i % 2
        fw.dma("sync", hl[b], h1_s[i * 128:(i + 1) * 128, :], reads=[("h1_s", i)], writes=[f"hl{b}"])
        rms_scale(hl[b], 3, hn, [f"hl{b}"], ["hn"])
        transpose8(hn, hnT, "hn", "hnT")
        for fp in range(11):
            p, pk = psum()
            for q4 in range(4):
                for k in range(8):
                    mm(p[:, q4 * 128:(q4 + 1) * 128], W1[:, k, fp * 512 + q4 * 128:fp * 512 + (q4 + 1) * 128], hnT[:, k, :], k == 0, k == 7,
                       ["hnT", "W1"], pk, (q4 == 3 and k == 7))
            s_ = sg[fp % 2]
            p4 = p.rearrange("p (a c) -> p a c", c=128)
            A_(lambda e, p4=p4, s_=s_: e.activation(out=s_.rearrange("p (a c) -> p a c", c=128), in_=p4[:, 0::2, :], func=AF.Sigmoid), [pk], [f"sg{fp % 2}"])
            V(lambda e, p4=p4, s_=s_: e.tensor_tensor(out=s_.rearrange("p (a c) -> p a c", c=128), in0=p4[:, 0::2, :], in1=s_.rearrange("p (a c) -> p a c", c=128), op=ALU.mult),
              [pk, f"sg{fp % 2}"], [f"sg{fp % 2}"])
            V(lambda e, p4=p4, s_=s_, fp=fp: e.tensor_tensor(out=actT[:, 2 * fp:2 * fp + 2, :], in0=p4[:, 1::2, :], in1=s_.rearrange("p (a c) -> p a c", c=128), op=ALU.mult),
              [pk, f"sg{fp % 2}"], ["actT"])
        for half in range(2):
            p, pk = psum()
            for k in range(22):
                mm(p, actT[:, k, :], W2[:, k, half * 512:(half + 1) * 512], k == 0, k == 21, ["actT", "W2"], pk, k == 21)
            V(lambda e, p=p, half=half, b=b: e.tensor_tensor(out=ob[b][:, half * 512:(half + 1) * 512], in0=p, in1=hl[b][:, half * 512:(half + 1) * 512], op=ALU.add),
              [pk, f"hl{b}"], [f"ob{b}"])
        fw.dma("sync", out[i * 128:(i + 1) * 128, :], ob[b], reads=[f"ob{b}"], writes=[("out", i)])
    fw.emit()
    return nc


def _panels(w, kk):
    n = w.shape[1] // 512
    return np.ascontiguousarray(w.reshape(kk, 128, n, 512).transpose(2, 1, 0, 3))


def prep_shared(inp):
    f = lambda a: np.asarray(a, dtype=np.float32)
    w_in = f(inp["w_in"])[0]
    qcols = []
    for j in range(8):
        for s in range(2):
            head = ((j // 4) * 2 + s) * 4 + (j % 4)
            qcols.extend(range(head * 64, head * 64 + 64))
    w_in_r = np.concatenate([w_in[:, qcols], w_in[:, 1024:1536], w_in[:, 1536:]], axis=1)
    wf1 = f(inp["w_ffn_in"])[0]
    cols = []
    for c in range(22):
        cols.extend(range(c * 128, (c + 1) * 128))
        cols.extend(range(DFF + c * 128, DFF + (c + 1) * 128))
    wf1_r = wf1[:, cols]
    rep = lambda v, n: np.ascontiguousarray(np.broadcast_to(f(v).reshape(1, -1), (128, n)))
    gains = np.stack([rep(inp["norm_mix"][0], D), rep(inp["attn_branch_norm"][0], D), rep(inp["ssm_branch_norm"][0], D), rep(inp["norm_ffn"][0], D)])

    def sp(a):
        return np.ascontiguousarray(f(a).reshape(32, 2, 64).transpose(1, 2, 0).reshape(128, 32))

    lam = np.stack([sp(inp["lam_re"][0]), sp(inp["lam_im"][0]), sp(np.broadcast_to(f(inp["log_dt"])[0][:, None], (64, 64)))])
    bre, bim = f(inp["ssm_b_re"])[0], f(inp["ssm_b_im"])[0]
    def spc(a):
        return a.reshape(32, 2, 64, a.shape[-1]).transpose(1, 2, 0, 3).reshape(128, 32, a.shape[-1])
    btc = np.ascontiguousarray(np.stack([spc(bre), spc(bim)], axis=2))
    cre, cim = f(inp["ssm_c_re"])[0], f(inp["ssm_c_im"])[0]
    cc = np.ascontiguousarray(np.stack([spc(cre.transpose(0, 2, 1)), spc(cim.transpose(0, 2, 1))]))
    kk, qq = np.arange(128)[:, None], np.arange(128)[None, :]
    mcur = np.where(kk <= qq, 1.0, 0.0).astype(np.float32)
    mprev = np.where(kk > qq, 1.0, 0.0).astype(np.float32)
    return dict(
        w_in=_panels(w_in_r, 8), w_glu=_panels(f(inp["w_glu"])[0], 8), w_out=_panels(f(inp["w_out"])[0], 8),
        w_f1=_panels(wf1_r, 8), w_f2=_panels(f(inp["w_ffn_out"])[0], 22), gains=gains,
        gq=rep(np.tile(f(inp["q_norm"])[0], 4), 256), gk=rep(np.tile(f(inp["k_norm"])[0], 4), 256),
        sinks=rep(inp["attn_sinks"][0], 16), ident=np.eye(128, dtype=np.float32), lam=lam, btc=btc, cc=cc,
        dcol=np.ascontiguousarray(f(inp["ssm_d"])[0].reshape(8, 128).T),
    ), mcur, mprev


def prep_core(x_b, meta, h, NM, NP, mcur, mprev):
    xmain = np.ascontiguousarray(x_b[h * NM:(h + 1) * NM])
    xpre = np.zeros((NP, D), np.float32)
    xctx = np.zeros((256, D), np.float32)
    xctx[0:16] = meta
    if h == 0:
        xpre[NP - 16:] = meta
        m0 = np.zeros((128, 128), np.float32)
    else:
        xpre[1008:1024] = meta
        xpre[1024:] = x_b[0:NM]
        xctx[128:256] = x_b[NM - 128:NM]
        m0 = mprev
    masks = np.stack([np.tile(mcur, (1, 4)), np.tile(mprev, (1, 4)), np.tile(m0, (1, 4))]).astype(np.float32)
    return dict(xmain=xmain, xpre=xpre, xctx=xctx, masks=masks)


_NC_CACHE = {}


def kernel(**inputs):
    x = np.asarray(inputs["x"], dtype=np.float32)
    Bsz, S, _ = x.shape
    NM = S // 2
    NP = NM + 1024
    meta = np.asarray(inputs["meta_tokens"], dtype=np.float32)
    shared, mcur, mprev = prep_shared(inputs)
    in_maps = []
    for b in range(Bsz):
        for h in range(2):
            d = dict(shared)
            d.update(prep_core(x[b], meta, h, NM, NP, mcur, mprev))
            in_maps.append(d)
    nc = build(NM, NP)
    res = run_bass_kernel_spmd(nc, in_maps, core_ids=list(range(len(in_maps))))
    outp = np.zeros((Bsz, S, D), np.float32)
    for b in range(Bsz):
        for h in range(2):
            outp[b, h * NM:(h + 1) * NM] = res.results[2 * b + h]["out"]
    return outp
```

```python
import math
import contextlib
import numpy as np
import concourse.bass as bass
import concourse.mybir as mybir
from concourse.bass_utils import run_bass_kernel_spmd

F32 = mybir.dt.float32
BF16 = mybir.dt.bfloat16
AF = mybir.ActivationFunctionType
ALU = mybir.AluOpType
AX = mybir.AxisListType
ENGS = ("tensor", "vector", "scalar", "gpsimd", "sync")
D = 1024
DFF = 2816
NEG = -30000.0


class FW:
    def __init__(self, nc, n_dma_sems=40):
        self.nc = nc
        self.ops = {e: [] for e in ENGS}
        self.cnt = {e: 0 for e in ENGS}
        self.known = {e: {} for e in ENGS}
        self.last_w = {}
        self.readers = {}
        self.n_dma_sems = n_dma_sems
        self.dma_gen = [0] * n_dma_sems
        self.dma_rr = 0
        self.sem_names = [f"s_{e}" for e in ENGS] + [f"d_{i}" for i in range(n_dma_sems)]

    def _deps(self, reads, writes):
        evs = []
        for k in reads:
            if k in self.last_w:
                evs.append(self.last_w[k])
        for k in writes:
            if k in self.last_w:
                evs.append(self.last_w[k])
            evs.extend(self.readers.get(k, ()))
        return evs

    def _commit(self, ev, reads, writes):
        for k in reads:
            self.readers.setdefault(k, []).append(ev)
        for k in writes:
            self.last_w[k] = ev
            self.readers[k] = []

    def _waits(self, eng, evs):
        best = {}
        for (s, v) in evs:
            if v > best.get(s, 0):
                best[s] = v
        out = []
        kn = self.known[eng]
        for s, v in best.items():
            if eng == "tensor" and s == "s_tensor":
                continue
            if kn.get(s, 0) >= v:
                continue
            kn[s] = v
            out.append((s, v))
        return out

    def op(self, eng, fn, reads=(), writes=(), inc=True):
        evs = self._deps(reads, writes)
        waits = self._waits(eng, evs)
        sname = f"s_{eng}"
        ev = (sname, self.cnt[eng] + 1)
        if inc:
            self.cnt[eng] += 1
        self.ops[eng].append((waits, fn, (sname, 1) if inc else None))
        self._commit(ev, reads, writes)
        return ev

    def dma(self, queue, out, in_, reads=(), writes=(), **kw):
        i = self.dma_rr
        self.dma_rr = (self.dma_rr + 1) % self.n_dma_sems
        sname = f"d_{i}"
        evs = self._deps(reads, writes)
        if self.dma_gen[i] > 0:
            evs.append((sname, 16 * self.dma_gen[i]))
        waits = self._waits(queue, evs)
        self.dma_gen[i] += 1
        ev = (sname, 16 * self.dma_gen[i])
        self.ops[queue].append((waits, lambda e: e.dma_start(out=out, in_=in_, **kw), (sname, 16)))
        self._commit(ev, reads, writes)
        return ev

    def defer_dma(self, *a, **kw):
        if not hasattr(self, "_deferred"):
            self._deferred = []
        self._deferred.append((a, kw))

    def flush(self):
        for a, kw in getattr(self, "_deferred", []):
            self.dma(*a, **kw)
        self._deferred = []

    def barrier(self):
        self.flush()
        fin = []
        for e in ENGS:
            if self.cnt[e] > 0:
                fin.append((f"s_{e}", self.cnt[e]))
        for i in range(self.n_dma_sems):
            if self.dma_gen[i] > 0:
                fin.append((f"d_{i}", 16 * self.dma_gen[i]))
        for e in ENGS:
            w = self._waits(e, fin)
            if w:
                self.ops[e].append((w, None, None))
        self.last_w = {}
        self.readers = {}

    def emit(self):
        nc = self.nc
        self.barrier()
        with contextlib.ExitStack() as st:
            sems = {n: st.enter_context(nc.semaphore(n)) for n in self.sem_names}
            block = st.enter_context(nc.Block())

            def mk(engname):
                lst = self.ops[engname]

                def body(eng):
                    for (waits, fn, inc) in lst:
                        for (s, v) in waits:
                            eng.wait_ge(sems[s], v)
                        if fn is None:
                            continue
                        ins = fn(eng)
                        if inc is not None:
                            ins.then_inc(sems[inc[0]], inc[1])
                return body

            block.tensor(mk("tensor"))
            block.vector(mk("vector"))
            block.scalar(mk("scalar"))
            block.gpsimd(mk("gpsimd"))
            block.sync(mk("sync"))


class Arena:
    def __init__(self, nc, base=16640, limit=224 * 1024):
        self.nc, self.off, self.limit, self.n = nc, base, limit, 0

    def alloc(self, name, shape, dt):
        per = int(np.prod(shape[1:])) * (4 if dt == F32 else 2)
        per = (per + 63) // 64 * 64
        assert self.off + per <= self.limit, (name, self.off, per)
        self.n += 1
        t = self.nc.alloc_sbuf_tensor_at(f"{name}_{self.n}_{self.off}", list(shape), dt, offset=self.off)
        self.off += per
        return t.ap()

    def mark(self):
        return self.off

    def reset(self, off):
        self.off = off


def build(NM, NP, upto=9):
    nc = bass.Bass("TRN2", target_bir_lowering=False)
    fw = FW(nc)
    TM_, TP_ = NM // 128, NP // 128
    NS = TP_ + TM_
    NK = 2 + TM_

    def din(name, shape, dt=F32):
        return nc.dram_tensor(name, list(shape), dt, kind="ExternalInput").ap()

    xmain = din("xmain", [NM, D]); xpre = din("xpre", [NP, D]); xctx = din("xctx", [256, D])
    w_in = din("w_in", [9, 128, 8, 512]); w_glu = din("w_glu", [4, 128, 8, 512]); w_out = din("w_out", [2, 128, 8, 512])
    w_f1 = din("w_f1", [11, 128, 8, 512]); w_f2 = din("w_f2", [2, 128, 22, 512])
    gains = din("gains", [4, 128, D])
    gq = din("gq", [128, 256]); gk = din("gk", [128, 256]); sinks = din("sinks", [128, 16])
    masks = din("masks", [3, 128, 512])
    ident = din("ident", [128, 128])
    lam = din("lam", [3, 128, 32])
    btc = din("btc", [128, 32, 2, 16]); cc = din("cc", [2, 128, 32, 16]); dcol = din("dcol", [128, 8])
    out = nc.dram_tensor("out", [NM, D], F32, kind="ExternalOutput").ap()

    def dscr(name, shape, dt):
        return nc.dram_tensor(name, list(shape), dt, kind="Internal").ap()

    uT_s = dscr("uT_s", [NS, 128, 8, 128], BF16); qT_s = dscr("qT_s", [TM_, 128, 8, 128], BF16)
    kT_s = dscr("kT_s", [NK, 128, 2, 128], BF16); v_s = dscr("v_s", [NK, 128, 4, 65], BF16)
    g_s = dscr("g_s", [TM_, 128, 2048], BF16); zT_s = dscr("zT_s", [TM_, 128, 8, 128], BF16)
    h1_s = dscr("h1_s", [NM, D], F32)
    DB_s = dscr("DB_s", [8, 128, 8, 4, 2, 128], BF16); EC_s = dscr("EC_s", [8, 128, 8, 4, 2, 128], BF16)

    ar = Arena(nc)
    identf = ar.alloc("identf", [128, 128], F32); identb = ar.alloc("identb", [128, 128], BF16)
    gsb = ar.alloc("gsb", [128, 4, D], F32)
    fw.dma("sync", identf, ident, writes=["identf"])
    fw.op("vector", lambda e: e.tensor_copy(out=identb, in_=identf), reads=["identf"], writes=["identb"])
    fw.dma("sync", gsb, gains.rearrange("a p d -> p a d"), writes=["gsb"])
    pbank = [nc.alloc_psum_tensor(f"pb{i}", [128, 512], F32).ap() for i in range(8)]
    pcnt = [0]

    def psum():
        i = pcnt[0] % 8
        pcnt[0] += 1
        return pbank[i], f"pb{i}"

    rr = [0]

    def alt():
        rr[0] += 1
        return "vector" if rr[0] % 2 else "gpsimd"

    base0 = ar.mark()

    def load_weights(dst, src, npan, kk, key):
        m = ar.mark()
        st = [ar.alloc("wst", [128, 8, 512], F32) for _ in range(2)]
        n = 0
        for pi in range(npan):
            for k0 in range(0, kk, 8):
                kc = min(8, kk - k0)
                s = st[n % 2]
                fw.dma("sync", s[:, :kc, :], src[pi][:, k0:k0 + kc, :], writes=[f"wst{n % 2}"])
                eng = alt()
                fw.op(eng, lambda e, s=s, pi=pi, k0=k0, kc=kc: e.tensor_copy(out=dst[:, k0:k0 + kc, pi * 512:(pi + 1) * 512], in_=s[:, :kc, :]),
                      reads=[f"wst{n % 2}"], writes=[key])
                n += 1
        fw.barrier()
        ar.reset(m)

    rmsc = [0]

    def rms_scale(xin, gidx, xn_out, rkeys, wkeys, ncol=D):
        pr = rmsc[0] % 2
        rmsc[0] += 1
        jk, sq_ = junk[pr], ssq[pr]
        kj, ks = f"junk{pr}", f"ssq{pr}"
        fw.op("scalar", lambda e: e.activation(out=jk[:, :ncol], in_=xin, func=AF.Square, accum_out=sq_),
              reads=rkeys, writes=[kj, ks])
        fw.op("vector", lambda e: e.tensor_scalar(out=sq_, in0=sq_, scalar1=1.0 / ncol, scalar2=1e-6, op0=ALU.mult, op1=ALU.add),
              reads=[ks], writes=[ks])
        fw.op("scalar", lambda e: e.activation(out=sq_, in_=sq_, func=AF.Sqrt), reads=[ks], writes=[ks])
        fw.op("vector", lambda e: e.reciprocal(out=sq_, in_=sq_), reads=[ks], writes=[ks])
        fw.op("vector", lambda e: e.scalar_tensor_tensor(out=xn_out, in0=xin, scalar=sq_, in1=gsb[:, gidx, :ncol],
                                                         op0=ALU.mult, op1=ALU.mult),
              reads=list(rkeys) + [ks, "gsb"], writes=wkeys)

    def transpose8(src_bf, dstT, rkey, wkey, n=8):
        for half in range((n + 3) // 4):
            p, pk = psum()
            m = min(4, n - half * 4)
            for j in range(m):
                c = half * 4 + j
                fw.op("tensor", lambda e, c=c, j=j, p=p: e.matmul(p[:, j * 128:(j + 1) * 128], lhsT=src_bf[:, c * 128:(c + 1) * 128],
                                                                 rhs=identb, start=True, stop=True),
                      reads=[rkey, "identb"], writes=[pk], inc=(j == m - 1))
            fw.op("vector", lambda e, p=p, half=half, m=m: e.tensor_copy(
                out=dstT[:, half * 4:half * 4 + m, :], in_=p[:, :m * 128].rearrange("p (a b) -> p a b", b=128)),
                reads=[pk], writes=[wkey])

    junk = [ar.alloc("junk", [128, D], F32) for _ in range(2)]; ssq = [ar.alloc("ssq", [128, 1], F32) for _ in range(2)]
    base1 = ar.mark()

    dbg = {}
    V = lambda fn, r, w: fw.op("vector", fn, reads=r, writes=w)
    A_ = lambda fn, r, w: fw.op("scalar", fn, reads=r, writes=w)
    G_ = lambda fn, r, w: fw.op("gpsimd", fn, reads=r, writes=w)

    def mm(out_ap, lhsT, rhs, start, stop, reads, pk, inc):
        fw.op("tensor", lambda e: e.matmul(out_ap, lhsT=lhsT, rhs=rhs, start=start, stop=stop), reads=reads, writes=[pk], inc=inc)

    dcs = ar.alloc("dcs", [128, 8], F32)
    esink = ar.alloc("esink", [128, 16], F32); gqk = ar.alloc("gqk", [128, 256], F32)
    maskb = ar.alloc("maskb", [128, 3, 512], BF16)
    base_persist = ar.mark()
    st32 = ar.alloc("st32", [128, 8192], F32)
    fw.dma("sync", dcs, dcol, writes=["dcs"])
    fw.dma("sync", st32[:, 0:16], sinks, writes=["a"])
    A_(lambda e: e.activation(out=esink, in_=st32[:, 0:16], func=AF.Exp), ["a"], ["esink"])
    fw.dma("sync", st32[:, 1024:1280], gq, writes=["b"])
    fw.dma("sync", st32[:, 2048:2304], gk, writes=["c"])
    V(lambda e: e.tensor_tensor(out=gqk, in0=st32[:, 1024:1280], in1=st32[:, 2048:2304], op=ALU.mult), ["b", "c"], ["gqk"])
    fw.dma("sync", st32[:, 4096:5632].rearrange("p (a c) -> p a c", a=3), masks.rearrange("a p c -> p a c"), writes=["d"])
    V(lambda e: e.tensor_copy(out=maskb, in_=st32[:, 4096:5632].rearrange("p (a c) -> p a c", a=3)), ["d"], ["maskb"])
    fw.barrier()
    ar.reset(base_persist)

    m1 = ar.mark()
    Win = ar.alloc("Win", [128, 8, 4608], BF16)
    load_weights(Win, w_in, 9, 8, "Win")
    QO, KVO, UO, GO = 0, 1024, 1536, 2560
    xt = [ar.alloc("xt", [128, D], F32) for _ in range(2)]
    xnb = [ar.alloc("xnb", [128, D], BF16) for _ in range(2)]
    xnT = [ar.alloc("xnT", [128, 8, 128], BF16) for _ in range(2)]
    uTb = [ar.alloc("uTb", [128, 8, 128], BF16) for _ in range(2)]
    qsq_ = [ar.alloc("qsq", [128, 512], F32) for _ in range(2)]
    qss_ = [ar.alloc("qss", [128, 8], F32) for _ in range(2)]
    hnc = [0]
    qn = [ar.alloc("qn", [128, D], BF16) for _ in range(2)]
    qTb = [ar.alloc("qTb", [128, 8, 128], BF16) for _ in range(2)]
    kf_ = [ar.alloc("kf", [128, 256], F32) for _ in range(2)]
    kn = [ar.alloc("kn", [128, 256], BF16) for _ in range(2)]
    kTb = [ar.alloc("kTb", [128, 2, 128], BF16) for _ in range(2)]
    vab = [ar.alloc("vab", [128, 4, 65], BF16) for _ in range(2)]
    gb = [ar.alloc("gb", [128, 2048], BF16) for _ in range(2)]
    for b in range(2):
        V(lambda e, b=b: e.memset(vab[b][:, :, 64:65], 1.0), [], [f"vab{b}"])

    tiles = [("ctx", xctx[0:128, :], 0, None), ("ctx", xctx[128:256, :], 1, None)]
    for t in range(TP_):
        tiles.append(("pre", xpre[t * 128:(t + 1) * 128, :], t, None))
    for t in range(TM_):
        tiles.append(("main", xmain[t * 128:(t + 1) * 128, :], TP_ + t, t))

    def headnorm(p, pk, ncol, nh, dst, dkey, gain=None):
        pr = hnc[0] % 2
        hnc[0] += 1
        qsq, qss, kf = qsq_[pr], qss_[pr], kf_[pr]
        kq, ks, kk_ = f"qsq{pr}", f"qss{pr}", f"kf{pr}"
        A_(lambda e: e.activation(out=qsq[:, :ncol], in_=p[:, :ncol], func=AF.Square), [pk], [kq])
        V(lambda e: e.tensor_reduce(out=qss[:, :nh], in_=qsq[:, :ncol].rearrange("p (h d) -> p h d", d=64), axis=AX.X, op=ALU.add), [kq], [ks])
        V(lambda e: e.tensor_scalar(out=qss[:, :nh], in0=qss[:, :nh], scalar1=1.0 / 64, scalar2=1e-6, op0=ALU.mult, op1=ALU.add), [ks], [ks])
        A_(lambda e: e.activation(out=qss[:, :nh], in_=qss[:, :nh], func=AF.Sqrt), [ks], [ks])
        V(lambda e: e.reciprocal(out=qss[:, :nh], in_=qss[:, :nh]), [ks], [ks])
        rb = qss[:, :nh].unsqueeze(2).broadcast_to([128, nh, 64])
        if gain is None:
            V(lambda e: e.tensor_tensor(out=dst.rearrange("p (h d) -> p h d", d=64), in0=p[:, :ncol].rearrange("p (h d) -> p h d", d=64), in1=rb, op=ALU.mult),
              [pk, ks], [dkey])
        else:
            V(lambda e: e.tensor_tensor(out=kf.rearrange("p (h d) -> p h d", d=64), in0=p[:, :ncol].rearrange("p (h d) -> p h d", d=64), in1=rb, op=ALU.mult),
              [pk, ks], [kk_])
            V(lambda e: e.tensor_tensor(out=dst, in0=kf, in1=gain, op=ALU.mult), [kk_, "gqk"], [dkey])

    for ti, (kind, src, sidx, midx) in enumerate(tiles):
        b = ti % 2
        fw.dma("sync", xt[b], src, writes=[f"xt{b}"])
        fw.flush()
        rms_scale(xt[b], 0, xnb[b], [f"xt{b}"], [f"xnb{b}"])
        transpose8(xnb[b], xnT[b], f"xnb{b}", f"xnT{b}")
        if kind in ("pre", "main"):
            for half in range(2):
                p, pk = psum()
                for cl in range(4):
                    ct = half * 4 + cl
                    for k in range(8):
                        mm(p[:, cl * 128:(cl + 1) * 128], Win[:, k, UO + ct * 128:UO + (ct + 1) * 128], xnT[b][:, k, :], k == 0, k == 7,
                           [f"xnT{b}", "Win"], pk, (cl == 3 and k == 7))
                V(lambda e, p=p, half=half, b=b: e.tensor_copy(out=uTb[b][:, half * 4:(half + 1) * 4, :], in_=p.rearrange("p (a c) -> p a c", c=128)),
                  [pk], [f"uTb{b}"])
            fw.defer_dma("sync", uT_s[sidx], uTb[b], reads=[f"uTb{b}"], writes=[("uT_s", sidx)])
        if kind == "main":
            for half in range(2):
                p, pk = psum()
                for k in range(8):
                    mm(p, xnT[b][:, k, :], Win[:, k, QO + half * 512:QO + (half + 1) * 512], k == 0, k == 7, [f"xnT{b}", "Win"], pk, k == 7)
                headnorm(p, pk, 512, 8, qn[b][:, half * 512:(half + 1) * 512], f"qn{b}")
            transpose8(qn[b], qTb[b], f"qn{b}", f"qTb{b}")
            fw.defer_dma("sync", qT_s[midx], qTb[b], reads=[f"qTb{b}"], writes=[("qT_s", midx)])
            for j in range(4):
                p, pk = psum()
                for k in range(8):
                    mm(p, xnT[b][:, k, :], Win[:, k, GO + j * 512:GO + (j + 1) * 512], k == 0, k == 7, [f"xnT{b}", "Win"], pk, k == 7)
                A_(lambda e, p=p, j=j, b=b: e.activation(out=gb[b][:, j * 512:(j + 1) * 512], in_=p, func=AF.Sigmoid), [pk], [f"gb{b}"])
            fw.defer_dma("sync", g_s[midx], gb[b], reads=[f"gb{b}"], writes=[("g_s", midx)])
        if kind in ("ctx", "main"):
            kidx = sidx if kind == "ctx" else 2 + midx
            p, pk = psum()
            for k in range(8):
                mm(p, xnT[b][:, k, :], Win[:, k, KVO:KVO + 512], k == 0, k == 7, [f"xnT{b}", "Win"], pk, k == 7)
            headnorm(p, pk, 256, 4, kn[b], f"kn{b}", gain=gqk)
            V(lambda e, p=p, b=b: e.tensor_copy(out=vab[b][:, :, 0:64], in_=p[:, 256:512].rearrange("p (h d) -> p h d", d=64)), [pk], [f"vab{b}"])
            transpose8(kn[b], kTb[b], f"kn{b}", f"kTb{b}", n=2)
            fw.defer_dma("sync", kT_s[kidx], kTb[b], reads=[f"kTb{b}"], writes=[("kT_s", kidx)])
            fw.defer_dma("sync", v_s[kidx], vab[b], reads=[f"vab{b}"], writes=[("v_s", kidx)])
    fw.barrier()
    ar.reset(m1)

    if upto <= 1:
        fw.emit()
        return nc
    lamsb = ar.alloc("lamsb", [128, 3, 32], F32)
    fw.dma("sync", lamsb, lam.rearrange("a p g -> p a g"), writes=["lam"])
    smn = ["dt", "th", "rho", "sn", "cs", "ar", "ai", "fr", "fi", "t1", "t2", "t3", "den", "wr", "wi", "w128r", "w128i", "mk", "x2", "lrdt"]
    sm = {n: ar.alloc(n, [128, 32], F32) for n in smn}
    pwr = [ar.alloc("pwr", [128, 32], F32) for _ in range(9)]; pwi = [ar.alloc("pwi", [128, 32], F32) for _ in range(9)]
    Er = ar.alloc("Er", [128, 32, 128], F32); Ei = ar.alloc("Ei", [128, 32, 128], F32)
    Kpad = ar.alloc("Kpad", [128, 8, 8, 128], BF16)
    cR = ar.alloc("cR", [128, 2, 32], F32); SL = ar.alloc("SL", [128, 2, 32], F32)
    base_ssm = ar.mark()
    lr, li, ld = lamsb[:, 0, :], lamsb[:, 1, :], lamsb[:, 2, :]
    K = ["ssm0"]
    A_(lambda e: e.activation(out=sm["dt"], in_=ld, func=AF.Exp), ["lam"], K)
    V(lambda e: e.tensor_tensor(out=sm["th"], in0=li, in1=sm["dt"], op=ALU.mult), K, K)
    V(lambda e: e.tensor_tensor(out=sm["lrdt"], in0=lr, in1=sm["dt"], op=ALU.mult), K, K)
    A_(lambda e: e.activation(out=sm["rho"], in_=sm["lrdt"], func=AF.Exp), K, K)
    for _ in range(5):
        V(lambda e: e.tensor_single_scalar(out=sm["mk"], in_=sm["th"], scalar=math.pi, op=ALU.is_gt), K, K)
        V(lambda e: e.scalar_tensor_tensor(out=sm["th"], in0=sm["mk"], scalar=-2.0 * math.pi, in1=sm["th"], op0=ALU.mult, op1=ALU.add), K, K)
    V(lambda e: e.tensor_scalar(out=sm["t3"], in0=sm["th"], scalar1=0.125, scalar2=None, op0=ALU.mult), K, K)
    V(lambda e: e.tensor_tensor(out=sm["x2"], in0=sm["t3"], in1=sm["t3"], op=ALU.mult), K, K)

    def horner(o, coefs):
        V(lambda e: e.memset(o, coefs[0]), K, K)
        for c in coefs[1:]:
            V(lambda e: e.tensor_tensor(out=o, in0=o, in1=sm["x2"], op=ALU.mult), K, K)
            V(lambda e, c=c: e.tensor_scalar(out=o, in0=o, scalar1=float(c), scalar2=None, op0=ALU.add), K, K)

    def cdouble(sn_, cs_):
        V(lambda e: e.tensor_tensor(out=sm["t1"], in0=sn_, in1=cs_, op=ALU.mult), K, K)
        V(lambda e: e.tensor_tensor(out=sm["t2"], in0=cs_, in1=cs_, op=ALU.mult), K, K)
        V(lambda e: e.tensor_tensor(out=sm["t3"], in0=sn_, in1=sn_, op=ALU.mult), K, K)
        V(lambda e: e.tensor_scalar(out=sn_, in0=sm["t1"], scalar1=2.0, scalar2=None, op0=ALU.mult), K, K)
        V(lambda e: e.tensor_tensor(out=cs_, in0=sm["t2"], in1=sm["t3"], op=ALU.subtract), K, K)

    horner(sm["sn"], [-1 / 39916800.0, 1 / 362880.0, -1 / 5040.0, 1 / 120.0, -1 / 6.0, 1.0])
    V(lambda e: e.tensor_tensor(out=sm["sn"], in0=sm["sn"], in1=sm["t3"], op=ALU.mult), K, K)
    horner(sm["cs"], [-1 / 3628800.0, 1 / 40320.0, -1 / 720.0, 1 / 24.0, -0.5, 1.0])
    for _ in range(3):
        cdouble(sm["sn"], sm["cs"])
    V(lambda e: e.tensor_tensor(out=sm["ar"], in0=sm["rho"], in1=sm["cs"], op=ALU.mult), K, K)
    V(lambda e: e.tensor_tensor(out=sm["ai"], in0=sm["rho"], in1=sm["sn"], op=ALU.mult), K, K)
    V(lambda e: e.tensor_scalar(out=sm["t1"], in0=sm["ar"], scalar1=-1.0, scalar2=None, op0=ALU.add), K, K)
    V(lambda e: e.tensor_tensor(out=sm["den"], in0=lr, in1=lr, op=ALU.mult), K, K)
    V(lambda e: e.tensor_tensor(out=sm["t2"], in0=li, in1=li, op=ALU.mult), K, K)
    V(lambda e: e.tensor_tensor(out=sm["den"], in0=sm["den"], in1=sm["t2"], op=ALU.add), K, K)
    V(lambda e: e.reciprocal(out=sm["den"], in_=sm["den"]), K, K)
    V(lambda e: e.tensor_tensor(out=sm["t2"], in0=sm["t1"], in1=lr, op=ALU.mult), K, K)
    V(lambda e: e.tensor_tensor(out=sm["t3"], in0=sm["ai"], in1=li, op=ALU.mult), K, K)
    V(lambda e: e.tensor_tensor(out=sm["t2"], in0=sm["t2"], in1=sm["t3"], op=ALU.add), K, K)
    V(lambda e: e.tensor_tensor(out=sm["fr"], in0=sm["t2"], in1=sm["den"], op=ALU.mult), K, K)
    V(lambda e: e.tensor_tensor(out=sm["t2"], in0=sm["ai"], in1=lr, op=ALU.mult), K, K)
    V(lambda e: e.tensor_tensor(out=sm["t3"], in0=sm["t1"], in1=li, op=ALU.mult), K, K)
    V(lambda e: e.tensor_tensor(out=sm["t2"], in0=sm["t2"], in1=sm["t3"], op=ALU.subtract), K, K)
    V(lambda e: e.tensor_tensor(out=sm["fi"], in0=sm["t2"], in1=sm["den"], op=ALU.mult), K, K)
    V(lambda e: e.memset(pwr[0], 1.0), K, K)
    V(lambda e: e.memset(pwi[0], 0.0), K, K)
    for k in range(1, 9):
        V(lambda e, k=k: e.tensor_tensor(out=sm["t1"], in0=pwr[k - 1], in1=sm["ar"], op=ALU.mult), K, K)
        V(lambda e, k=k: e.tensor_tensor(out=sm["t2"], in0=pwi[k - 1], in1=sm["ai"], op=ALU.mult), K, K)
        V(lambda e, k=k: e.tensor_tensor(out=pwr[k], in0=sm["t1"], in1=sm["t2"], op=ALU.subtract), K, K)
        V(lambda e, k=k: e.tensor_tensor(out=sm["t1"], in0=pwr[k - 1], in1=sm["ai"], op=ALU.mult), K, K)
        V(lambda e, k=k: e.tensor_tensor(out=sm["t2"], in0=pwi[k - 1], in1=sm["ar"], op=ALU.mult), K, K)
        V(lambda e, k=k: e.tensor_tensor(out=pwi[k], in0=sm["t1"], in1=sm["t2"], op=ALU.add), K, K)
    V(lambda e: e.tensor_scalar(out=sm["t1"], in0=sm["lrdt"], scalar1=8.0, scalar2=None, op0=ALU.mult), K, K)
    A_(lambda e: e.activation(out=sm["rho"], in_=sm["t1"], func=AF.Exp), K, K)
    for _ in range(3):
        cdouble(sm["sn"], sm["cs"])
    V(lambda e: e.memset(Er[:, :, 0:1], 1.0), K, K)
    V(lambda e: e.memset(Ei[:, :, 0:1], 0.0), K, K)
    V(lambda e: e.tensor_copy(out=sm["wr"], in_=sm["cs"]), K, K)
    V(lambda e: e.tensor_copy(out=sm["wi"], in_=sm["sn"]), K, K)
    m0 = ar.mark()
    tA = ar.alloc("tA", [128, 32, 64], F32); tB = ar.alloc("tB", [128, 32, 64], F32)
    for k in range(7):
        n = 1 << k
        wrb = sm["wr"].unsqueeze(2).broadcast_to([128, 32, n]); wib = sm["wi"].unsqueeze(2).broadcast_to([128, 32, n])
        V(lambda e, n=n, wrb=wrb: e.tensor_tensor(out=tA[:, :, :n], in0=Er[:, :, :n], in1=wrb, op=ALU.mult), K, K)
        V(lambda e, n=n, wib=wib: e.tensor_tensor(out=tB[:, :, :n], in0=Ei[:, :, :n], in1=wib, op=ALU.mult), K, K)
        V(lambda e, n=n: e.tensor_tensor(out=Er[:, :, n:2 * n], in0=tA[:, :, :n], in1=tB[:, :, :n], op=ALU.subtract), K, K)
        V(lambda e, n=n, wib=wib: e.tensor_tensor(out=tA[:, :, :n], in0=Er[:, :, :n], in1=wib, op=ALU.mult), K, K)
        V(lambda e, n=n, wrb=wrb: e.tensor_tensor(out=tB[:, :, :n], in0=Ei[:, :, :n], in1=wrb, op=ALU.mult), K, K)
        V(lambda e, n=n: e.tensor_tensor(out=Ei[:, :, n:2 * n], in0=tA[:, :, :n], in1=tB[:, :, :n], op=ALU.add), K, K)
        V(lambda e: e.tensor_tensor(out=sm["t1"], in0=sm["wr"], in1=sm["wr"], op=ALU.mult), K, K)
        V(lambda e: e.tensor_tensor(out=sm["t2"], in0=sm["wi"], in1=sm["wi"], op=ALU.mult), K, K)
        V(lambda e: e.tensor_tensor(out=sm["t3"], in0=sm["wr"], in1=sm["wi"], op=ALU.mult), K, K)
        V(lambda e: e.tensor_tensor(out=sm["wr"], in0=sm["t1"], in1=sm["t2"], op=ALU.subtract), K, K)
        V(lambda e: e.tensor_scalar(out=sm["wi"], in0=sm["t3"], scalar1=2.0, scalar2=None, op0=ALU.mult), K, K)
    V(lambda e: e.tensor_copy(out=sm["w128r"], in_=sm["wr"]), K, K)
    V(lambda e: e.tensor_copy(out=sm["w128i"], in_=sm["wi"]), K, K)
    V(lambda e: e.memset(cR, 0.0), K, K)
    V(lambda e: e.memset(SL, 0.0), K, K)
    fw.barrier()
    ar.reset(m0)
    BTc = ar.alloc("BTc", [128, 32, 2, 16], F32); Cc = ar.alloc("Cc", [128, 2, 32, 16], F32)
    Cfc = ar.alloc("Cfc", [128, 32, 2, 16], F32); Xc = ar.alloc("Xc", [128, 32, 2, 16], F32)
    c1 = ar.alloc("c1", [128, 32, 16], F32); c2 = ar.alloc("c2", [128, 32, 16], F32)
    padf = ar.alloc("padf", [128, 32, 2, 128], F32); Cfp = ar.alloc("Cfp", [128, 32, 2, 128], F32)
    padb = ar.alloc("padb", [128, 32, 2, 128], BF16); DBsb = ar.alloc("DBsb", [128, 32, 2, 128], BF16)
    fw.dma("sync", BTc, btc, writes=["BTc"])
    fw.dma("sync", Cc, cc.rearrange("a p g c -> p a g c"), writes=["Cc"])
    G_(lambda e: e.memset(padf, 0.0), [], ["padf"])
    G_(lambda e: e.memset(Cfp, 0.0), [], ["Cfp"])

    def cmul_compact(dst, src_r, src_i, sr, si, rk, wk, neg_im=False):
        srb = sr.unsqueeze(2).broadcast_to([128, 32, 16]); sib = si.unsqueeze(2).broadcast_to([128, 32, 16])
        V(lambda e: e.tensor_tensor(out=c1, in0=src_r, in1=srb, op=ALU.mult), rk, ["c1"])
        V(lambda e: e.tensor_tensor(out=c2, in0=src_i, in1=sib, op=ALU.mult), rk, ["c2"])
        V(lambda e: e.tensor_tensor(out=dst[:, :, 0, :], in0=c1, in1=c2, op=ALU.subtract), ["c1", "c2"], wk)
        V(lambda e: e.tensor_tensor(out=c1, in0=src_r, in1=sib, op=ALU.mult), rk + wk, ["c1"])
        V(lambda e: e.tensor_tensor(out=c2, in0=src_i, in1=srb, op=ALU.mult), rk + wk, ["c2"])
        if neg_im:
            V(lambda e: e.scalar_tensor_tensor(out=dst[:, :, 1, :], in0=c1, scalar=-1.0, in1=c2, op0=ALU.mult, op1=ALU.subtract), ["c1", "c2"], wk)
        else:
            V(lambda e: e.tensor_tensor(out=dst[:, :, 1, :], in0=c1, in1=c2, op=ALU.add), ["c1", "c2"], wk)

    def scatter(dst_pad, src_c, rk, wk):
        for g2 in range(2):
            for q in range(4):
                blk = 2 * q + g2
                G_(lambda e, g2=g2, q=q, blk=blk: e.tensor_copy(out=dst_pad[g2 * 64:(g2 + 1) * 64, q::4, :, blk * 16:(blk + 1) * 16],
                                                                 in_=src_c[g2 * 64:(g2 + 1) * 64, q::4, :, :]), rk, wk)

    cmul_compact(Cfc, Cc[:, 0], Cc[:, 1], sm["fr"], sm["fi"], ["Cc"], ["Cfc"])
    cmul_compact(Xc, Cc[:, 0], Cc[:, 1], sm["fr"], sm["fi"], ["Cc"], ["Xc"], neg_im=True)
    scatter(Cfp, Xc, ["Xc"], ["Cfp"])
    for k in range(8):
        j = 7 - k
        cmul_compact(Xc, BTc[:, :, 0, :], BTc[:, :, 1, :], pwr[k], pwi[k], ["BTc"], ["Xc"])
        scatter(padf, Xc, ["Xc"], ["padf"])
        for r in range(8):
            p, pk = psum()
            n_ = 0
            for gl in range(4):
                for ri in range(2):
                    mm(p[:, 0:128], padf[:, 4 * r + gl, ri, :], Cfp[:, 4 * r + gl, ri, :], n_ == 0, n_ == 7, ["padf", "Cfp"], pk, n_ == 7)
                    n_ += 1
            if k == 0:
                V(lambda e, p=p, r=r: e.scalar_tensor_tensor(out=Kpad[:, r, 0, :], in0=identf, scalar=dcs[:, r:r + 1], in1=p[:, 0:128], op0=ALU.mult, op1=ALU.add),
                  [pk, "identf", "dcs"], ["Kpad"])
            else:
                V(lambda e, p=p, r=r, k=k: e.tensor_copy(out=Kpad[:, r, k, :], in_=p[:, 0:128]), [pk], ["Kpad"])
        V(lambda e: e.tensor_copy(out=padb, in_=padf), ["padf"], ["padb"])
        for g4 in range(16):
            p, pk = psum()
            for q in range(4):
                gi = g4 * 4 + q
                mm(p[:, q * 128:(q + 1) * 128], padb[:, gi // 2, gi % 2, :], identb, True, True, ["padb", "identb"], pk, q == 3)
            V(lambda e, p=p, g4=g4: e.tensor_copy(out=DBsb.rearrange("p g r s -> p (g r) s")[:, g4 * 4:(g4 + 1) * 4, :], in_=p.rearrange("p (a c) -> p a c", c=128)),
              [pk], ["DBsb"])
        fw.dma("sync", DB_s[:, :, j].rearrange("r p g a s -> p r g a s"), DBsb.rearrange("p (r g) a s -> p r g a s", r=8), reads=["DBsb"], writes=[("DB_s", j)])
    for j in range(8):
        cmul_compact(Xc, Cfc[:, :, 0, :], Cfc[:, :, 1, :], pwr[j + 1], pwi[j + 1], ["Cfc"], ["Xc"])
        scatter(padf, Xc, ["Xc"], ["padf"])
        V(lambda e: e.tensor_copy(out=padb, in_=padf), ["padf"], ["padb"])
        fw.dma("sync", EC_s[:, :, j].rearrange("r p g a s -> p r g a s"), padb.rearrange("p (r g) a s -> p r g a s", r=8), reads=["padb"], writes=[("EC_s", j)])
    fw.barrier()
    ar.reset(base_ssm)

    if upto <= 2:
        fw.emit()
        return nc
    NPS, NMS = NP // 1024, NM // 1024
    DBr = ar.alloc("DBr", [128, 8, 4, 2, 128], BF16); ECr = ar.alloc("ECr", [128, 8, 4, 2, 128], BF16)
    uTr = [ar.alloc("uTr", [128, 1024], BF16) for _ in range(2)]
    zTr = [ar.alloc("zTr", [128, 1024], BF16) for _ in range(2)]
    RB = []
    for b in range(2):
        d = {n: ar.alloc(n, [128, 4, 128], F32) for n in ["t1", "t2", "t3", "t4", "Xr", "Xi", "Rr", "Ri"]}
        d["Sr"] = ar.alloc("Sr", [128, 4, 130], BF16); d["Si"] = ar.alloc("Si", [128, 4, 130], BF16)
        for n in ["c1", "c2", "c3", "c4"]:
            d[n] = ar.alloc(n, [128, 4], F32)
        RB.append(d)
    ysb = [ar.alloc("ysb", [128, 1024], F32) for _ in range(2)]
    g1b = [ar.alloc("g1b", [128, 1024], F32) for _ in range(2)]
    g2b = [ar.alloc("g2b", [128, 1024], F32) for _ in range(2)]
    rcount = 0
    hcount = 0
    for r in range(8):
        gsl = slice(4 * r, 4 * r + 4)
        fw.dma("sync", DBr, DB_s[r], writes=["DBr"])
        fw.dma("sync", ECr, EC_s[r], writes=["ECr"])
        for st in range(NPS + NMS):
            is_main = st >= NPS
            ub = st % 2
            fw.dma("sync", uTr[ub].rearrange("p (n t) -> p n t", t=128), uT_s[8 * st:8 * st + 8, :, r, :].rearrange("n p t -> p n t"),
                   writes=[f"uTr{ub}"])
            fw.flush()
            b = rcount % 2
            rcount += 1
            B = RB[b]
            kb = lambda n, b=b: f"{n}{b}"
            pXr, pkr = psum()
            pXi, pki = psum()
            for ri, (pX, pk) in enumerate(((pXr, pkr), (pXi, pki))):
                for gl in range(4):
                    for j in range(8):
                        mm(pX[:, gl * 128:(gl + 1) * 128], DBr[:, j, gl, ri, :], uTr[ub][:, j::8], j == 0, j == 7,
                           [f"uTr{ub}", "DBr"], pk, (gl == 3 and j == 7))
            pXr3 = pXr.rearrange("p (a c) -> p a c", c=128); pXi3 = pXi.rearrange("p (a c) -> p a c", c=128)
            Erg, Eig = Er[:, gsl, :], Ei[:, gsl, :]
            V(lambda e, B=B, a=pXr3, t=Erg: e.tensor_tensor(out=B["t1"], in0=a, in1=t, op=ALU.mult), [pkr], [kb("t1")])
            V(lambda e, B=B, a=pXi3, t=Eig: e.tensor_tensor(out=B["t2"], in0=a, in1=t, op=ALU.mult), [pki], [kb("t2")])
            V(lambda e, B=B, a=pXi3, t=Erg: e.tensor_tensor(out=B["t3"], in0=a, in1=t, op=ALU.mult), [pki], [kb("t3")])
            V(lambda e, B=B, a=pXr3, t=Eig: e.tensor_tensor(out=B["t4"], in0=a, in1=t, op=ALU.mult), [pkr], [kb("t4")])
            G_(lambda e, B=B: e.tensor_tensor(out=B["Xr"], in0=B["t1"], in1=B["t2"], op=ALU.add), [kb("t1"), kb("t2")], [kb("Xr")])
            G_(lambda e, B=B: e.tensor_tensor(out=B["Xi"], in0=B["t3"], in1=B["t4"], op=ALU.subtract), [kb("t3"), kb("t4")], [kb("Xi")])
            for gl in range(4):
                gp = 4 * r + gl
                for nm, xs, ci in (("Rr", "Xr", 0), ("Ri", "Xi", 1)):
                    V(lambda e, B=B, gl=gl, gp=gp, nm=nm, xs=xs, ci=ci: e.tensor_tensor_scan(
                        out=B[nm][:, gl, :], data0=sm["rho"][:, gp:gp + 1].broadcast_to([128, 128]), data1=B[xs][:, gl, :],
                        initial=cR[:, ci, gp:gp + 1], op0=ALU.mult, op1=ALU.add), [kb(xs), ("cR", r)], [kb(nm)])
            if is_main:
                G_(lambda e, B=B, gsl=gsl: e.tensor_copy(out=B["Sr"][:, :, 0], in_=SL[:, 0, gsl]), [("SL", r)], [kb("Sr")])
                G_(lambda e, B=B, gsl=gsl: e.tensor_copy(out=B["Si"][:, :, 0], in_=SL[:, 1, gsl]), [("SL", r)], [kb("Si")])
            wr4, wi4 = sm["w128r"][:, gsl], sm["w128i"][:, gsl]
            er7, ei7 = Er[:, gsl, 127], Ei[:, gsl, 127]
            Rr7, Ri7 = B["Rr"][:, :, 127], B["Ri"][:, :, 127]
            for (xr_, xi_, dst, negim, key) in ((wr4, wi4, cR, False, "cR"), (er7, ei7, SL, True, "SL")):
                G_(lambda e, B=B, a=Rr7, w=xr_: e.tensor_tensor(out=B["c1"], in0=a, in1=w, op=ALU.mult), [kb("Rr")], [kb("c1")])
                G_(lambda e, B=B, a=Ri7, w=xi_: e.tensor_tensor(out=B["c2"], in0=a, in1=w, op=ALU.mult), [kb("Ri")], [kb("c2")])
                G_(lambda e, B=B, a=Ri7, w=xr_: e.tensor_tensor(out=B["c3"], in0=a, in1=w, op=ALU.mult), [kb("Ri")], [kb("c3")])
                G_(lambda e, B=B, a=Rr7, w=xi_: e.tensor_tensor(out=B["c4"], in0=a, in1=w, op=ALU.mult), [kb("Rr")], [kb("c4")])
                G_(lambda e, B=B, dst=dst, gsl=gsl: e.tensor_tensor(out=dst[:, 0, gsl], in0=B["c1"], in1=B["c2"], op=ALU.subtract),
                   [kb("c1"), kb("c2")], [(key, r)])
                if negim:
                    V(lambda e, B=B, dst=dst, gsl=gsl: e.scalar_tensor_tensor(out=dst[:, 1, gsl], in0=B["c3"], scalar=-1.0, in1=B["c4"], op0=ALU.mult, op1=ALU.subtract),
                       [kb("c3"), kb("c4")], [(key, r)])
                else:
                    G_(lambda e, B=B, dst=dst, gsl=gsl: e.tensor_tensor(out=dst[:, 1, gsl], in0=B["c3"], in1=B["c4"], op=ALU.add),
                       [kb("c3"), kb("c4")], [(key, r)])
            if not is_main:
                continue
            G_(lambda e, B=B, t=Erg: e.tensor_tensor(out=B["t1"], in0=B["Rr"], in1=t, op=ALU.mult), [kb("Rr")], [kb("t1")])
            G_(lambda e, B=B, t=Eig: e.tensor_tensor(out=B["t2"], in0=B["Ri"], in1=t, op=ALU.mult), [kb("Ri")], [kb("t2")])
            V(lambda e, B=B, t=Erg: e.tensor_tensor(out=B["t3"], in0=B["Ri"], in1=t, op=ALU.mult), [kb("Ri")], [kb("t3")])
            V(lambda e, B=B, t=Eig: e.tensor_tensor(out=B["t4"], in0=B["Rr"], in1=t, op=ALU.mult), [kb("Rr")], [kb("t4")])
            V(lambda e, B=B: e.tensor_tensor(out=B["Sr"][:, :, 1:129], in0=B["t1"], in1=B["t2"], op=ALU.subtract), [kb("t1"), kb("t2")], [kb("Sr")])
            V(lambda e, B=B: e.scalar_tensor_tensor(out=B["Si"][:, :, 1:129], in0=B["t3"], scalar=-1.0, in1=B["t4"], op0=ALU.mult, op1=ALU.subtract),
              [kb("t3"), kb("t4")], [kb("Si")])
            zb = (st - NPS) % 2
            hb = hcount % 2
            hcount += 1
            for h2 in range(2):
                py, pky = psum()
                for j4 in range(4):
                    j = 4 * h2 + j4
                    o = py[:, j4 * 128:(j4 + 1) * 128]
                    nmm = (j + 1) + 8
                    n_ = 0
                    for k in range(j + 1):
                        mm(o, Kpad[:, r, k, :], uTr[ub][:, (j - k)::8], n_ == 0, n_ == nmm - 1, [f"uTr{ub}", "Kpad"], pky, False)
                        n_ += 1
                    for gl in range(4):
                        for ri, Sn in enumerate(("Sr", "Si")):
                            mm(o, ECr[:, j, gl, ri, :], B[Sn][:, gl, 0:128], n_ == 0, n_ == nmm - 1, [kb(Sn), "ECr"], pky,
                               (j4 == 3 and n_ == nmm - 1))
                            n_ += 1
                A_(lambda e, py=py, hb=hb, h2=h2: e.activation(out=ysb[hb].rearrange("p (c j) -> p c j", j=8)[:, :, 4 * h2:4 * h2 + 4],
                                                               in_=py.rearrange("p (j c) -> p c j", c=128), func=AF.Identity), [pky], [f"ysb{hb}"])
            G_(lambda e, hb=hb: e.tensor_tensor(out=g1b[hb], in0=ysb[hb], in1=ysb[hb], op=ALU.mult), [f"ysb{hb}"], [f"g1b{hb}"])
            G_(lambda e, hb=hb: e.tensor_scalar(out=g1b[hb], in0=g1b[hb], scalar1=0.044715, scalar2=1.0, op0=ALU.mult, op1=ALU.add), [f"g1b{hb}"], [f"g1b{hb}"])
            G_(lambda e, hb=hb: e.tensor_tensor(out=g1b[hb], in0=g1b[hb], in1=ysb[hb], op=ALU.mult), [f"g1b{hb}", f"ysb{hb}"], [f"g1b{hb}"])
            A_(lambda e, hb=hb: e.activation(out=g2b[hb], in_=g1b[hb], func=AF.Sigmoid, scale=1.5957691216057308), [f"g1b{hb}"], [f"g2b{hb}"])
            V(lambda e, hb=hb, zb=zb: e.tensor_tensor(out=zTr[zb], in0=ysb[hb], in1=g2b[hb], op=ALU.mult), [f"ysb{hb}", f"g2b{hb}"], [f"zTr{zb}"])
            m8 = 8 * (st - NPS)
            fw.defer_dma("sync", zT_s[m8:m8 + 8, :, r, :].rearrange("n p t -> p n t"), zTr[zb].rearrange("p (n t) -> p n t", t=128),
                   reads=[f"zTr{zb}"], writes=[("zT_s", st, r)])
    fw.barrier()
    ar.reset(base_persist)

    if upto <= 3:
        fw.emit()
        return nc
    Wg = ar.alloc("Wg", [128, 8, 2048], BF16); Wo = ar.alloc("Wo", [128, 8, 1024], BF16)
    load_weights(Wg, w_glu, 4, 8, "Wg")
    load_weights(Wo, w_out, 2, 8, "Wo")
    kme = ar.alloc("kme", [128, 2, 128], BF16); vme = ar.alloc("vme", [128, 4, 65], BF16)
    fw.dma("sync", kme, kT_s[0], writes=["kme"]); fw.dma("sync", vme, v_s[0], writes=["vme"])
    qTl = [ar.alloc("qTl", [128, 8, 128], BF16) for _ in range(2)]
    kTl = [ar.alloc("kTl", [128, 2, 128], BF16) for _ in range(3)]
    vl = [ar.alloc("vl", [128, 4, 65], BF16) for _ in range(3)]
    gl_ = [ar.alloc("gl", [128, 2048], BF16) for _ in range(2)]
    zTl = [ar.alloc("zTl", [128, 8, 128], BF16) for _ in range(2)]
    xr = [ar.alloc("xr", [128, D], F32) for _ in range(2)]
    Pc = [ar.alloc("Pc", [128, 512], BF16) for _ in range(2)]
    Pp = [ar.alloc("Pp", [128, 512], BF16) for _ in range(2)]
    Pm = [ar.alloc("Pm", [128, 512], BF16) for _ in range(2)]
    den_ = [ar.alloc("den", [128, 4], F32) for _ in range(2)]
    for b_ in range(2):
        V(lambda e, b_=b_: e.memset(Pm[b_], 0.0), [], [f"Pm{b_}"])
    attn_ = [ar.alloc("attn", [128, D], F32) for _ in range(2)]; An_ = [ar.alloc("An", [128, D], F32) for _ in range(2)]
    sig_ = [ar.alloc("sig", [128, 512], F32) for _ in range(2)]; ssm_ = [ar.alloc("ssm", [128, D], F32) for _ in range(2)]
    Bn_ = [ar.alloc("Bn", [128, D], F32) for _ in range(2)]
    mg_ = [ar.alloc("mg", [128, D], BF16) for _ in range(2)]; mgT_ = [ar.alloc("mgT", [128, 8, 128], BF16) for _ in range(2)]
    h1 = [ar.alloc("h1", [128, D], F32) for _ in range(2)]
    fw.dma("sync", kTl[1], kT_s[1], writes=["kTl1"]); fw.dma("sync", vl[1], v_s[1], writes=["vl1"])
    pcount = 0
    for i in range(TM_):
        b = i % 2
        jc, jp = 2 + i, 1 + i
        sc, sp = jc % 3, jp % 3
        fw.dma("sync", kTl[sc], kT_s[jc], reads=[("kT_s", jc)], writes=[f"kTl{sc}"])
        fw.dma("sync", vl[sc], v_s[jc], reads=[("v_s", jc)], writes=[f"vl{sc}"])
        fw.dma("sync", qTl[b], qT_s[i], writes=[f"qTl{b}"])
        fw.dma("sync", gl_[b], g_s[i], writes=[f"gl{b}"])
        fw.dma("sync", zTl[b], zT_s[i], writes=[f"zTl{b}"])
        fw.dma("sync", xr[b], xmain[i * 128:(i + 1) * 128, :], writes=[f"xr{b}"])
        fw.flush()
        attn, An, ssm, Bn, mg, mgT = attn_[b], An_[b], ssm_[b], Bn_[b], mg_[b], mgT_[b]
        kA, kAn, kss, kBn, kmg, kmT = f"attn{b}", f"An{b}", f"ssm{b}", f"Bn{b}", f"mg{b}", f"mgT{b}"
        for grp in range(4):
            pb = pcount % 2
            pcount += 1
            bs = (grp % 2) * 64
            kc = grp // 2
            qsel = qTl[b][bs:bs + 64, kc * 4:(kc + 1) * 4, :]
            pS, pkS = psum()
            mm(pS, kTl[sc][bs:bs + 64, kc, :], qsel, True, True, [f"kTl{sc}", f"qTl{b}"], pkS, True)
            A_(lambda e, pS=pS, pb=pb: e.activation(out=Pc[pb], in_=pS, func=AF.Exp, scale=0.125), [pkS], [f"Pc{pb}"])
            G_(lambda e, pb=pb: e.tensor_tensor(out=Pc[pb], in0=Pc[pb], in1=maskb[:, 0, :], op=ALU.mult), [f"Pc{pb}", "maskb"], [f"Pc{pb}"])
            pS2, pkS2 = psum()
            mm(pS2, kTl[sp][bs:bs + 64, kc, :], qsel, True, True, [f"kTl{sp}", f"qTl{b}"], pkS2, True)
            A_(lambda e, pS2=pS2, pb=pb: e.activation(out=Pp[pb], in_=pS2, func=AF.Exp, scale=0.125), [pkS2], [f"Pp{pb}"])
            mi = 2 if i == 0 else 1
            G_(lambda e, pb=pb, mi=mi: e.tensor_tensor(out=Pp[pb], in0=Pp[pb], in1=maskb[:, mi, :], op=ALU.mult), [f"Pp{pb}", "maskb"], [f"Pp{pb}"])
            pS3, pkS3 = psum()
            mm(pS3[0:16, :], kme[bs:bs + 64, kc, 0:16], qsel, True, True, ["kme", f"qTl{b}"], pkS3, True)
            A_(lambda e, pS3=pS3, pb=pb: e.activation(out=Pm[pb][0:16, :], in_=pS3[0:16, :], func=AF.Exp, scale=0.125), [pkS3], [f"Pm{pb}"])
            pO, pkO = psum()
            for r in range(4):
                o = pO[:, r * 65:(r + 1) * 65]
                mm(o, Pm[pb][:, r * 128:(r + 1) * 128], vme[:, grp, :], True, False, [f"Pm{pb}", "vme"], pkO, False)
                mm(o, Pp[pb][:, r * 128:(r + 1) * 128], vl[sp][:, grp, :], False, False, [f"Pp{pb}", f"vl{sp}"], pkO, False)
                mm(o, Pc[pb][:, r * 128:(r + 1) * 128], vl[sc][:, grp, :], False, True, [f"Pc{pb}", f"vl{sc}"], pkO, r == 3)
            pO3 = pO[:, 0:260].rearrange("p (r c) -> p r c", c=65)
            den = den_[pb]
            kd = f"den{pb}"
            V(lambda e, pO3=pO3, grp=grp, den=den: e.tensor_tensor(out=den, in0=pO3[:, :, 64], in1=esink[:, grp * 4:(grp + 1) * 4], op=ALU.add), [pkO, "esink"], [kd])
            V(lambda e, den=den: e.reciprocal(out=den, in_=den), [kd], [kd])
            V(lambda e, pO3=pO3, grp=grp, den=den, attn=attn: e.tensor_tensor(out=attn[:, grp * 256:(grp + 1) * 256].rearrange("p (r d) -> p r d", d=64), in0=pO3[:, :, 0:64],
                                                                          in1=den.unsqueeze(2).broadcast_to([128, 4, 64]), op=ALU.mult), [pkO, kd], [kA])
        rms_scale(attn, 1, An, [kA], [kAn])
        G_(lambda e, b=b, An=An: e.tensor_tensor(out=An, in0=An, in1=gl_[b][:, 0:1024], op=ALU.mult), [kAn, f"gl{b}"], [kAn])
        for half in range(2):
            pa, pka = psum()
            for k in range(8):
                mm(pa, zTl[b][:, k, :], Wg[:, k, half * 512:(half + 1) * 512], k == 0, k == 7, [f"zTl{b}", "Wg"], pka, k == 7)
            pz, pkz = psum()
            for k in range(8):
                mm(pz, zTl[b][:, k, :], Wg[:, k, 1024 + half * 512:1024 + (half + 1) * 512], k == 0, k == 7, [f"zTl{b}", "Wg"], pkz, k == 7)
            sig = sig_[half]
            A_(lambda e, pz=pz, sig=sig: e.activation(out=sig, in_=pz, func=AF.Sigmoid), [pkz], [f"sig{half}"])
            V(lambda e, pa=pa, half=half, sig=sig, ssm=ssm: e.tensor_tensor(out=ssm[:, half * 512:(half + 1) * 512], in0=pa, in1=sig, op=ALU.mult), [pka, f"sig{half}"], [kss])
        rms_scale(ssm, 2, Bn, [kss], [kBn])
        G_(lambda e, b=b, Bn=Bn: e.tensor_tensor(out=Bn, in0=Bn, in1=gl_[b][:, 1024:2048], op=ALU.mult), [kBn, f"gl{b}"], [kBn])
        V(lambda e, mg=mg, An=An, Bn=Bn: e.tensor_tensor(out=mg, in0=An, in1=Bn, op=ALU.add), [kAn, kBn], [kmg])
        transpose8(mg, mgT, kmg, kmT)
        for half in range(2):
            p, pk = psum()
            for k in range(8):
                mm(p, mgT[:, k, :], Wo[:, k, half * 512:(half + 1) * 512], k == 0, k == 7, [kmT, "Wo"], pk, k == 7)
            V(lambda e, p=p, half=half, b=b: e.tensor_tensor(out=h1[b][:, half * 512:(half + 1) * 512], in0=p, in1=xr[b][:, half * 512:(half + 1) * 512], op=ALU.add),
              [pk, f"xr{b}"], [f"h1{b}"])
        fw.defer_dma("sync", h1_s[i * 128:(i + 1) * 128, :], h1[b], reads=[f"h1{b}"], writes=[("h1_s", i)])
    fw.barrier()
    ar.reset(base_persist)

    if upto <= 4:
        fw.emit()
        return nc
    W1 = ar.alloc("W1", [128, 8, 5632], BF16); W2 = ar.alloc("W2", [128, 22, 1024], BF16)
    load_weights(W1, w_f1, 11, 8, "W1")
    load_weights(W2, w_f2, 2, 22, "W2")
    GT = 4
    hl = [ar.alloc("hl", [128, D], F32) for _ in range(2)]
    hn = [ar.alloc("hn", [128, D], BF16) for _ in range(2)]
    hnT = ar.alloc("hnT", [128, 8, GT * 128], BF16)
    sg = [ar.alloc("sg", [128, 512], F32) for _ in range(2)]
    actT = ar.alloc("actT", [128, 22, GT * 128], BF16)
    hres = hl
    ob = junk
    tcount = 0
    for g in range(TM_ // GT):
        for t4 in range(GT):
            i = g * GT + t4
            b = tcount % 2
            tcount += 1
            fw.dma("sync", hl[b], h1_s[i * 128:(i + 1) * 128, :], reads=[("h1_s", i)], writes=[f"hl{b}"])
            fw.flush()
            rms_scale(hl[b], 3, hn[b], [f"hl{b}"], [f"hn{b}"])
            for half in range(2):
                p, pk = psum()
                for jj in range(4):
                    c = half * 4 + jj
                    mm(p[:, jj * 128:(jj + 1) * 128], hn[b][:, c * 128:(c + 1) * 128], identb, True, True, [f"hn{b}", "identb"], pk, jj == 3)
                V(lambda e, p=p, half=half, t4=t4: e.tensor_copy(out=hnT[:, half * 4:half * 4 + 4, t4 * 128:(t4 + 1) * 128],
                                                               in_=p.rearrange("p (a c) -> p a c", c=128)), [pk], [("hnT", t4)])
        hk = [("hnT", t4) for t4 in range(GT)]
        for fc in range(22):
            fp, q2 = fc // 2, fc % 2
            pg, pkg = psum()
            for k in range(8):
                mm(pg, W1[:, k, fp * 512 + q2 * 256:fp * 512 + q2 * 256 + 128], hnT[:, k, :], k == 0, k == 7, hk + ["W1"], pkg, k == 7)
            pu, pku = psum()
            for k in range(8):
                mm(pu, W1[:, k, fp * 512 + q2 * 256 + 128:fp * 512 + q2 * 256 + 256], hnT[:, k, :], k == 0, k == 7, hk + ["W1"], pku, k == 7)
            s_ = sg[fc % 2]
            ks_ = f"sg{fc % 2}"
            A_(lambda e, pg=pg, s_=s_: e.activation(out=s_, in_=pg, func=AF.Sigmoid), [pkg], [ks_])
            V(lambda e, pg=pg, s_=s_: e.tensor_tensor(out=s_, in0=pg, in1=s_, op=ALU.mult), [pkg, ks_], [ks_])
            V(lambda e, pu=pu, s_=s_, fc=fc: e.tensor_tensor(out=actT[:, fc, :], in0=pu, in1=s_, op=ALU.mult), [pku, ks_], [("actT", fc)])
        ak = [("actT", fc) for fc in range(22)]
        for t4 in range(GT):
            i = g * GT + t4
            b = t4 % 2
            fw.dma("sync", hres[b], h1_s[i * 128:(i + 1) * 128, :], reads=[("h1_s", i)], writes=[f"hl{b}"])
            fw.flush()
            for half in range(2):
                p, pk = psum()
                for k in range(22):
                    mm(p, actT[:, k, t4 * 128:(t4 + 1) * 128], W2[:, k, half * 512:(half + 1) * 512], k == 0, k == 21, ak + ["W2"], pk, k == 21)
                V(lambda e, p=p, half=half, b=b: e.tensor_tensor(out=ob[b][:, half * 512:(half + 1) * 512], in0=p, in1=hres[b][:, half * 512:(half + 1) * 512], op=ALU.add),
                  [pk, f"hl{b}"], [f"junk{b}"])
            fw.defer_dma("sync", out[i * 128:(i + 1) * 128, :], ob[b], reads=[f"junk{b}"], writes=[("out", i)])
    fw.emit()
    return nc


def _panels(w, kk):
    n = w.shape[1] // 512
    return np.ascontiguousarray(w.reshape(kk, 128, n, 512).transpose(2, 1, 0, 3))


def prep_shared(inp):
    f = lambda a: np.asarray(a, dtype=np.float32)
    w_in = f(inp["w_in"])[0]
    qcols = []
    for j in range(8):
        for s in range(2):
            head = ((j // 4) * 2 + s) * 4 + (j % 4)
            qcols.extend(range(head * 64, head * 64 + 64))
    w_in_r = np.concatenate([w_in[:, qcols], w_in[:, 1024:1536], w_in[:, 1536:]], axis=1)
    wf1 = f(inp["w_ffn_in"])[0]
    cols = []
    for c in range(22):
        cols.extend(range(c * 128, (c + 1) * 128))
        cols.extend(range(DFF + c * 128, DFF + (c + 1) * 128))
    wf1_r = wf1[:, cols]
    rep = lambda v, n: np.ascontiguousarray(np.broadcast_to(f(v).reshape(1, -1), (128, n)))
    gains = np.stack([rep(inp["norm_mix"][0], D), rep(inp["attn_branch_norm"][0], D), rep(inp["ssm_branch_norm"][0], D), rep(inp["norm_ffn"][0], D)])

    def sp(a):
        return np.ascontiguousarray(f(a).reshape(32, 2, 64).transpose(1, 2, 0).reshape(128, 32))

    lam = np.stack([sp(inp["lam_re"][0]), sp(inp["lam_im"][0]), sp(np.broadcast_to(f(inp["log_dt"])[0][:, None], (64, 64)))])
    bre, bim = f(inp["ssm_b_re"])[0], f(inp["ssm_b_im"])[0]
    def spc(a):
        return a.reshape(32, 2, 64, a.shape[-1]).transpose(1, 2, 0, 3).reshape(128, 32, a.shape[-1])
    btc = np.ascontiguousarray(np.stack([spc(bre), spc(bim)], axis=2))
    cre, cim = f(inp["ssm_c_re"])[0], f(inp["ssm_c_im"])[0]
    cc = np.ascontiguousarray(np.stack([spc(cre.transpose(0, 2, 1)), spc(cim.transpose(0, 2, 1))]))
    kk, qq = np.arange(128)[:, None], np.arange(128)[None, :]
    mcur = np.where(kk <= qq, 1.0, 0.0).astype(np.float32)
    mprev = np.where(kk > qq, 1.0, 0.0).astype(np.float32)
    return dict(
        w_in=_panels(w_in_r, 8), w_glu=_panels(f(inp["w_glu"])[0], 8), w_out=_panels(f(inp["w_out"])[0], 8),
        w_f1=_panels(wf1_r, 8), w_f2=_panels(f(inp["w_ffn_out"])[0], 22), gains=gains,
        gq=rep(np.tile(f(inp["q_norm"])[0], 4), 256), gk=rep(np.tile(f(inp["k_norm"])[0], 4), 256),
        sinks=rep(inp["attn_sinks"][0], 16), ident=np.eye(128, dtype=np.float32), lam=lam, btc=btc, cc=cc,
        dcol=np.ascontiguousarray(f(inp["ssm_d"])[0].reshape(8, 128).T),
    ), mcur, mprev


def prep_core(x_b, meta, h, NM, NP, mcur, mprev):
    xmain = np.ascontiguousarray(x_b[h * NM:(h + 1) * NM])
    xpre = np.zeros((NP, D), np.float32)
    xctx = np.zeros((256, D), np.float32)
    xctx[0:16] = meta
    if h == 0:
        xpre[NP - 16:] = meta
        m0 = np.zeros((128, 128), np.float32)
    else:
        xpre[1008:1024] = meta
        xpre[1024:] = x_b[0:NM]
        xctx[128:256] = x_b[NM - 128:NM]
        m0 = mprev
    masks = np.stack([np.tile(mcur, (1, 4)), np.tile(mprev, (1, 4)), np.tile(m0, (1, 4))]).astype(np.float32)
    return dict(xmain=xmain, xpre=xpre, xctx=xctx, masks=masks)


_NC_CACHE = {}


def kernel(**inputs):
    x = np.asarray(inputs["x"], dtype=np.float32)
    Bsz, S, _ = x.shape
    NM = S // 2
    NP = NM + 1024
    meta = np.asarray(inputs["meta_tokens"], dtype=np.float32)
    shared, mcur, mprev = prep_shared(inputs)
    in_maps = []
    for b in range(Bsz):
        for h in range(2):
            d = dict(shared)
            d.update(prep_core(x[b], meta, h, NM, NP, mcur, mprev))
            in_maps.append(d)
    nc = build(NM, NP)
    res = run_bass_kernel_spmd(nc, in_maps, core_ids=list(range(len(in_maps))))
    outp = np.zeros((Bsz, S, D), np.float32)
    for b in range(Bsz):
        for h in range(2):
            outp[b, h * NM:(h + 1) * NM] = res.results[2 * b + h]["out"]
    return outp
```

```python
import math
import contextlib
import numpy as np
import concourse.bass as bass
import concourse.mybir as mybir
from concourse.bass_utils import run_bass_kernel_spmd

F32 = mybir.dt.float32
BF16 = mybir.dt.bfloat16
AF = mybir.ActivationFunctionType
ALU = mybir.AluOpType
AX = mybir.AxisListType
ENGS = ("tensor", "vector", "scalar", "gpsimd", "sync")
D = 1024
DFF = 2816
NEG = -30000.0


class FW:
    def __init__(self, nc, n_dma_sems=40):
        self.nc = nc
        self.ops = {e: [] for e in ENGS}
        self.cnt = {e: 0 for e in ENGS}
        self.known = {e: {} for e in ENGS}
        self.last_w = {}
        self.readers = {}
        self.n_dma_sems = n_dma_sems
        self.dma_gen = [0] * n_dma_sems
        self.dma_rr = 0
        self.sem_names = [f"s_{e}" for e in ENGS] + [f"d_{i}" for i in range(n_dma_sems)]

    def _deps(self, reads, writes):
        evs = []
        for k in reads:
            if k in self.last_w:
                evs.append(self.last_w[k])
        for k in writes:
            if k in self.last_w:
                evs.append(self.last_w[k])
            evs.extend(self.readers.get(k, ()))
        return evs

    def _commit(self, ev, reads, writes):
        for k in reads:
            self.readers.setdefault(k, []).append(ev)
        for k in writes:
            self.last_w[k] = ev
            self.readers[k] = []

    def _waits(self, eng, evs):
        best = {}
        for (s, v) in evs:
            if v > best.get(s, 0):
                best[s] = v
        out = []
        kn = self.known[eng]
        for s, v in best.items():
            if eng == "tensor" and s == "s_tensor":
                continue
            if kn.get(s, 0) >= v:
                continue
            kn[s] = v
            out.append((s, v))
        return out

    def op(self, eng, fn, reads=(), writes=(), inc=True):
        evs = self._deps(reads, writes)
        waits = self._waits(eng, evs)
        sname = f"s_{eng}"
        ev = (sname, self.cnt[eng] + 1)
        if inc:
            self.cnt[eng] += 1
        self.ops[eng].append((waits, fn, (sname, 1) if inc else None))
        self._commit(ev, reads, writes)
        return ev

    def dma(self, queue, out, in_, reads=(), writes=(), **kw):
        i = self.dma_rr
        self.dma_rr = (self.dma_rr + 1) % self.n_dma_sems
        sname = f"d_{i}"
        evs = self._deps(reads, writes)
        if self.dma_gen[i] > 0:
            evs.append((sname, 16 * self.dma_gen[i]))
        waits = self._waits(queue, evs)
        self.dma_gen[i] += 1
        ev = (sname, 16 * self.dma_gen[i])
        self.ops[queue].append((waits, lambda e: e.dma_start(out=out, in_=in_, **kw), (sname, 16)))
        self._commit(ev, reads, writes)
        return ev

    def defer_dma(self, *a, **kw):
        if not hasattr(self, "_deferred"):
            self._deferred = []
        self._deferred.append((a, kw))

    def flush(self):
        for a, kw in getattr(self, "_deferred", []):
            self.dma(*a, **kw)
        self._deferred = []

    def barrier(self):
        self.flush()
        fin = []
        for e in ENGS:
            if self.cnt[e] > 0:
                fin.append((f"s_{e}", self.cnt[e]))
        for i in range(self.n_dma_sems):
            if self.dma_gen[i] > 0:
                fin.append((f"d_{i}", 16 * self.dma_gen[i]))
        for e in ENGS:
            w = self._waits(e, fin)
            if w:
                self.ops[e].append((w, None, None))
        self.last_w = {}
        self.readers = {}

    def emit(self):
        nc = self.nc
        self.barrier()
        with contextlib.ExitStack() as st:
            sems = {n: st.enter_context(nc.semaphore(n)) for n in self.sem_names}
            block = st.enter_context(nc.Block())

            def mk(engname):
                lst = self.ops[engname]

                def body(eng):
                    for (waits, fn, inc) in lst:
                        for (s, v) in waits:
                            eng.wait_ge(sems[s], v)
                        if fn is None:
                            continue
                        ins = fn(eng)
                        if inc is not None:
                            ins.then_inc(sems[inc[0]], inc[1])
                return body

            block.tensor(mk("tensor"))
            block.vector(mk("vector"))
            block.scalar(mk("scalar"))
            block.gpsimd(mk("gpsimd"))
            block.sync(mk("sync"))


class Arena:
    def __init__(self, nc, base=16640, limit=224 * 1024):
        self.nc, self.off, self.limit, self.n = nc, base, limit, 0

    def alloc(self, name, shape, dt):
        per = int(np.prod(shape[1:])) * (4 if dt == F32 else 2)
        per = (per + 63) // 64 * 64
        assert self.off + per <= self.limit, (name, self.off, per)
        self.n += 1
        t = self.nc.alloc_sbuf_tensor_at(f"{name}_{self.n}_{self.off}", list(shape), dt, offset=self.off)
        self.off += per
        return t.ap()

    def mark(self):
        return self.off

    def reset(self, off):
        self.off = off


def build(NM, NP, upto=9):
    nc = bass.Bass("TRN2", target_bir_lowering=False)
    fw = FW(nc)
    TM_, TP_ = NM // 128, NP // 128
    NS = TP_ + TM_
    NK = 2 + TM_

    def din(name, shape, dt=F32):
        return nc.dram_tensor(name, list(shape), dt, kind="ExternalInput").ap()

    xmain = din("xmain", [NM, D]); xpre = din("xpre", [NP, D]); xctx = din("xctx", [256, D])
    w_in = din("w_in", [9, 128, 8, 512]); w_glu = din("w_glu", [4, 128, 8, 512]); w_out = din("w_out", [2, 128, 8, 512])
    w_f1 = din("w_f1", [11, 128, 8, 512]); w_f2 = din("w_f2", [2, 128, 22, 512])
    gains = din("gains", [4, 128, D])
    gq = din("gq", [128, 256]); gk = din("gk", [128, 256]); sinks = din("sinks", [128, 16])
    masks = din("masks", [3, 128, 512])
    ident = din("ident", [128, 128])
    lam = din("lam", [3, 128, 32])
    btc = din("btc", [128, 32, 2, 16]); cc = din("cc", [2, 128, 32, 16]); dcol = din("dcol", [128, 8])
    out = nc.dram_tensor("out", [NM, D], F32, kind="ExternalOutput").ap()

    def dscr(name, shape, dt):
        return nc.dram_tensor(name, list(shape), dt, kind="Internal").ap()

    uT_s = dscr("uT_s", [NS, 128, 8, 128], BF16); qT_s = dscr("qT_s", [TM_, 128, 8, 128], BF16)
    kT_s = dscr("kT_s", [NK, 128, 2, 128], BF16); v_s = dscr("v_s", [NK, 128, 4, 65], BF16)
    g_s = dscr("g_s", [TM_, 128, 2048], BF16); zT_s = dscr("zT_s", [TM_, 128, 8, 128], BF16)
    h1_s = dscr("h1_s", [NM, D], F32)
    DB_s = dscr("DB_s", [8, 128, 8, 4, 2, 128], BF16); EC_s = dscr("EC_s", [8, 128, 8, 4, 2, 128], BF16)

    ar = Arena(nc)
    identf = ar.alloc("identf", [128, 128], F32); identb = ar.alloc("identb", [128, 128], BF16)
    gsb = ar.alloc("gsb", [128, 4, D], F32)
    fw.dma("sync", identf, ident, writes=["identf"])
    fw.op("vector", lambda e: e.tensor_copy(out=identb, in_=identf), reads=["identf"], writes=["identb"])
    fw.dma("sync", gsb, gains.rearrange("a p d -> p a d"), writes=["gsb"])
    pbank = [nc.alloc_psum_tensor(f"pb{i}", [128, 512], F32).ap() for i in range(8)]
    pcnt = [0]

    def psum():
        i = pcnt[0] % 8
        pcnt[0] += 1
        return pbank[i], f"pb{i}"

    rr = [0]

    def alt():
        rr[0] += 1
        return "vector" if rr[0] % 2 else "gpsimd"

    base0 = ar.mark()

    def load_weights(dst, src, npan, kk, key):
        m = ar.mark()
        nst = 3 if ar.off + 3 * 16384 <= ar.limit else 2
        st = [ar.alloc("wst", [128, 8, 512], F32) for _ in range(nst)]
        cyc = ["vector", "scalar", "gpsimd", "vector", "scalar"]
        n = 0
        for pi in range(npan):
            for k0 in range(0, kk, 8):
                kc = min(8, kk - k0)
                s = st[n % nst]
                fw.dma("sync", s[:, :kc, :], src[pi][:, k0:k0 + kc, :], writes=[f"wst{n % nst}"])
                eng = cyc[n % len(cyc)]
                o_ = dst[:, k0:k0 + kc, pi * 512:(pi + 1) * 512]
                if eng == "scalar":
                    fw.op(eng, lambda e, s=s, kc=kc, o_=o_: e.activation(out=o_, in_=s[:, :kc, :], func=AF.Copy), reads=[f"wst{n % nst}"], writes=[key])
                else:
                    fw.op(eng, lambda e, s=s, kc=kc, o_=o_: e.tensor_copy(out=o_, in_=s[:, :kc, :]), reads=[f"wst{n % nst}"], writes=[key])
                n += 1
        fw.barrier()
        ar.reset(m)

    rmsc = [0]

    def rms_scale(xin, gidx, xn_out, rkeys, wkeys, ncol=D):
        pr = rmsc[0] % 2
        rmsc[0] += 1
        jk, sq_ = junk[pr], ssq[pr]
        kj, ks = f"junk{pr}", f"ssq{pr}"
        fw.op("scalar", lambda e: e.activation(out=jk[:, :ncol], in_=xin, func=AF.Square, accum_out=sq_),
              reads=rkeys, writes=[kj, ks])
        fw.op("vector", lambda e: e.tensor_scalar(out=sq_, in0=sq_, scalar1=1.0 / ncol, scalar2=1e-6, op0=ALU.mult, op1=ALU.add),
              reads=[ks], writes=[ks])
        fw.op("scalar", lambda e: e.activation(out=sq_, in_=sq_, func=AF.Sqrt), reads=[ks], writes=[ks])
        fw.op("vector", lambda e: e.reciprocal(out=sq_, in_=sq_), reads=[ks], writes=[ks])
        fw.op("vector", lambda e: e.scalar_tensor_tensor(out=xn_out, in0=xin, scalar=sq_, in1=gsb[:, gidx, :ncol],
                                                         op0=ALU.mult, op1=ALU.mult),
              reads=list(rkeys) + [ks, "gsb"], writes=wkeys)

    def transpose8(src_bf, dstT, rkey, wkey, n=8):
        for half in range((n + 3) // 4):
            p, pk = psum()
            m = min(4, n - half * 4)
            for j in range(m):
                c = half * 4 + j
                fw.op("tensor", lambda e, c=c, j=j, p=p: e.matmul(p[:, j * 128:(j + 1) * 128], lhsT=src_bf[:, c * 128:(c + 1) * 128],
                                                                 rhs=identb, start=True, stop=True),
                      reads=[rkey, "identb"], writes=[pk], inc=(j == m - 1))
            fw.op("vector", lambda e, p=p, half=half, m=m: e.tensor_copy(
                out=dstT[:, half * 4:half * 4 + m, :], in_=p[:, :m * 128].rearrange("p (a b) -> p a b", b=128)),
                reads=[pk], writes=[wkey])

    junk = [ar.alloc("junk", [128, D], F32) for _ in range(2)]; ssq = [ar.alloc("ssq", [128, 1], F32) for _ in range(2)]
    base1 = ar.mark()

    dbg = {}
    V = lambda fn, r, w: fw.op("vector", fn, reads=r, writes=w)
    A_ = lambda fn, r, w: fw.op("scalar", fn, reads=r, writes=w)
    G_ = lambda fn, r, w: fw.op("gpsimd", fn, reads=r, writes=w)

    def mm(out_ap, lhsT, rhs, start, stop, reads, pk, inc):
        fw.op("tensor", lambda e: e.matmul(out_ap, lhsT=lhsT, rhs=rhs, start=start, stop=stop), reads=reads, writes=[pk], inc=inc)

    dcs = ar.alloc("dcs", [128, 8], F32)
    esink = ar.alloc("esink", [128, 16], F32); gqk = ar.alloc("gqk", [128, 256], F32)
    maskb = ar.alloc("maskb", [128, 3, 512], BF16)
    base_persist = ar.mark()
    st32 = ar.alloc("st32", [128, 8192], F32)
    fw.dma("sync", dcs, dcol, writes=["dcs"])
    fw.dma("sync", st32[:, 0:16], sinks, writes=["a"])
    A_(lambda e: e.activation(out=esink, in_=st32[:, 0:16], func=AF.Exp), ["a"], ["esink"])
    fw.dma("sync", st32[:, 1024:1280], gq, writes=["b"])
    fw.dma("sync", st32[:, 2048:2304], gk, writes=["c"])
    V(lambda e: e.tensor_tensor(out=gqk, in0=st32[:, 1024:1280], in1=st32[:, 2048:2304], op=ALU.mult), ["b", "c"], ["gqk"])
    fw.dma("sync", st32[:, 4096:5632].rearrange("p (a c) -> p a c", a=3), masks.rearrange("a p c -> p a c"), writes=["d"])
    V(lambda e: e.tensor_copy(out=maskb, in_=st32[:, 4096:5632].rearrange("p (a c) -> p a c", a=3)), ["d"], ["maskb"])
    fw.barrier()
    ar.reset(base_persist)

    m1 = ar.mark()
    Win = ar.alloc("Win", [128, 8, 4608], BF16)
    load_weights(Win, w_in, 9, 8, "Win")
    QO, KVO, UO, GO = 0, 1024, 1536, 2560
    xt = [ar.alloc("xt", [128, D], F32) for _ in range(2)]
    xnb = [ar.alloc("xnb", [128, D], BF16) for _ in range(2)]
    xnT4 = [ar.alloc("xnT4", [128, 8, 512], BF16) for _ in range(2)]
    uTb4 = [ar.alloc("uTb4", [128, 8, 512], BF16) for _ in range(2)]
    qsq_ = [ar.alloc("qsq", [128, 512], F32) for _ in range(2)]
    qss_ = [ar.alloc("qss", [128, 8], F32) for _ in range(2)]
    hnc = [0]
    qn = [ar.alloc("qn", [128, D], BF16) for _ in range(2)]
    qTb = [ar.alloc("qTb", [128, 8, 128], BF16) for _ in range(2)]
    kf_ = [ar.alloc("kf", [128, 256], F32) for _ in range(2)]
    kn = [ar.alloc("kn", [128, 256], BF16) for _ in range(2)]
    kTb = [ar.alloc("kTb", [128, 2, 128], BF16) for _ in range(2)]
    vab = [ar.alloc("vab", [128, 4, 65], BF16) for _ in range(2)]
    gb = [ar.alloc("gb", [128, 2048], BF16) for _ in range(2)]
    for b in range(2):
        V(lambda e, b=b: e.memset(vab[b][:, :, 64:65], 1.0), [], [f"vab{b}"])

    groups = [[("ctx", xctx[0:128, :], 0, None), ("ctx", xctx[128:256, :], 1, None)]]
    for t in range(0, TP_, 4):
        groups.append([("pre", xpre[(t + q) * 128:(t + q + 1) * 128, :], t + q, None) for q in range(4)])
    for t in range(0, TM_, 4):
        groups.append([("main", xmain[(t + q) * 128:(t + q + 1) * 128, :], TP_ + t + q, t + q) for q in range(4)])

    def headnorm(p, pk, ncol, nh, dst, dkey, gain=None):
        pr = hnc[0] % 2
        hnc[0] += 1
        qsq, qss, kf = qsq_[pr], qss_[pr], kf_[pr]
        kq, ks, kk_ = f"qsq{pr}", f"qss{pr}", f"kf{pr}"
        A_(lambda e: e.activation(out=qsq[:, :ncol], in_=p[:, :ncol], func=AF.Square), [pk], [kq])
        V(lambda e: e.tensor_reduce(out=qss[:, :nh], in_=qsq[:, :ncol].rearrange("p (h d) -> p h d", d=64), axis=AX.X, op=ALU.add), [kq], [ks])
        V(lambda e: e.tensor_scalar(out=qss[:, :nh], in0=qss[:, :nh], scalar1=1.0 / 64, scalar2=1e-6, op0=ALU.mult, op1=ALU.add), [ks], [ks])
        A_(lambda e: e.activation(out=qss[:, :nh], in_=qss[:, :nh], func=AF.Sqrt), [ks], [ks])
        V(lambda e: e.reciprocal(out=qss[:, :nh], in_=qss[:, :nh]), [ks], [ks])
        rb = qss[:, :nh].unsqueeze(2).broadcast_to([128, nh, 64])
        if gain is None:
            V(lambda e: e.tensor_tensor(out=dst.rearrange("p (h d) -> p h d", d=64), in0=p[:, :ncol].rearrange("p (h d) -> p h d", d=64), in1=rb, op=ALU.mult),
              [pk, ks], [dkey])
        else:
            V(lambda e: e.tensor_tensor(out=kf.rearrange("p (h d) -> p h d", d=64), in0=p[:, :ncol].rearrange("p (h d) -> p h d", d=64), in1=rb, op=ALU.mult),
              [pk, ks], [kk_])
            V(lambda e: e.tensor_tensor(out=dst, in0=kf, in1=gain, op=ALU.mult), [kk_, "gqk"], [dkey])

    ti = 0
    for gi, grp_tiles in enumerate(groups):
        gpar = gi % 2
        X4 = xnT4[gpar]
        xkeys = []
        for t4, (kind, src, sidx, midx) in enumerate(grp_tiles):
            b = ti % 2
            ti += 1
            fw.dma("sync", xt[b], src, writes=[f"xt{b}"])
            fw.flush()
            rms_scale(xt[b], 0, xnb[b], [f"xt{b}"], [f"xnb{b}"])
            xk = f"xnT{gpar}_{t4}"
            xkeys.append(xk)
            XT = X4[:, :, t4 * 128:(t4 + 1) * 128]
            transpose8(xnb[b], XT, f"xnb{b}", xk)
            if kind == "main":
                for half in range(2):
                    p, pk = psum()
                    for k in range(8):
                        mm(p, XT[:, k, :], Win[:, k, QO + half * 512:QO + (half + 1) * 512], k == 0, k == 7, [xk, "Win"], pk, k == 7)
                    headnorm(p, pk, 512, 8, qn[b][:, half * 512:(half + 1) * 512], f"qn{b}")
                transpose8(qn[b], qTb[b], f"qn{b}", f"qTb{b}")
                fw.defer_dma("sync", qT_s[midx], qTb[b], reads=[f"qTb{b}"], writes=[("qT_s", midx)])
                for j in range(4):
                    p, pk = psum()
                    for k in range(8):
                        mm(p, XT[:, k, :], Win[:, k, GO + j * 512:GO + (j + 1) * 512], k == 0, k == 7, [xk, "Win"], pk, k == 7)
                    A_(lambda e, p=p, j=j, b=b: e.activation(out=gb[b][:, j * 512:(j + 1) * 512], in_=p, func=AF.Sigmoid), [pk], [f"gb{b}"])
                fw.defer_dma("sync", g_s[midx], gb[b], reads=[f"gb{b}"], writes=[("g_s", midx)])
            if kind in ("ctx", "main"):
                kidx = sidx if kind == "ctx" else 2 + midx
                p, pk = psum()
                for k in range(8):
                    mm(p, XT[:, k, :], Win[:, k, KVO:KVO + 512], k == 0, k == 7, [xk, "Win"], pk, k == 7)
                headnorm(p, pk, 256, 4, kn[b], f"kn{b}", gain=gqk)
                V(lambda e, p=p, b=b: e.tensor_copy(out=vab[b][:, :, 0:64], in_=p[:, 256:512].rearrange("p (h d) -> p h d", d=64)), [pk], [f"vab{b}"])
                transpose8(kn[b], kTb[b], f"kn{b}", f"kTb{b}", n=2)
                fw.defer_dma("sync", kT_s[kidx], kTb[b], reads=[f"kTb{b}"], writes=[("kT_s", kidx)])
                fw.defer_dma("sync", v_s[kidx], vab[b], reads=[f"vab{b}"], writes=[("v_s", kidx)])
        if grp_tiles[0][0] in ("pre", "main"):
            s0 = grp_tiles[0][2]
            for ct in range(8):
                p, pk = psum()
                for k in range(8):
                    mm(p, Win[:, k, UO + ct * 128:UO + (ct + 1) * 128], X4[:, k, :], k == 0, k == 7, xkeys + ["Win"], pk, k == 7)
                if ct % 2 == 0:
                    V(lambda e, p=p, ct=ct, gpar=gpar: e.tensor_copy(out=uTb4[gpar][:, ct, :], in_=p), [pk], [f"uTb4{gpar}"])
                else:
                    A_(lambda e, p=p, ct=ct, gpar=gpar: e.activation(out=uTb4[gpar][:, ct, :], in_=p, func=AF.Copy), [pk], [f"uTb4{gpar}"])
            fw.defer_dma("sync", uT_s[s0:s0 + 4].rearrange("n p c t -> p c n t"), uTb4[gpar].rearrange("p c (n t) -> p c n t", t=128),
                         reads=[f"uTb4{gpar}"], writes=[("uT_s", s0)])
    fw.barrier()
    ar.reset(m1)

    if upto <= 1:
        fw.emit()
        return nc
    lamsb = ar.alloc("lamsb", [128, 3, 32], F32)
    fw.dma("sync", lamsb, lam.rearrange("a p g -> p a g"), writes=["lam"])
    smn = ["dt", "th", "rho", "sn", "cs", "ar", "ai", "fr", "fi", "t1", "t2", "t3", "den", "wr", "wi", "w128r", "w128i", "mk", "x2", "lrdt"]
    sm = {n: ar.alloc(n, [128, 32], F32) for n in smn}
    pwr = [ar.alloc("pwr", [128, 32], F32) for _ in range(9)]; pwi = [ar.alloc("pwi", [128, 32], F32) for _ in range(9)]
    Er = ar.alloc("Er", [128, 32, 128], F32); Ei = ar.alloc("Ei", [128, 32, 128], F32)
    Kpad = ar.alloc("Kpad", [128, 8, 8, 128], BF16)
    cR = ar.alloc("cR", [128, 2, 32], F32); SL = ar.alloc("SL", [128, 2, 32], F32)
    base_ssm = ar.mark()
    lr, li, ld = lamsb[:, 0, :], lamsb[:, 1, :], lamsb[:, 2, :]
    K = ["ssm0"]
    A_(lambda e: e.activation(out=sm["dt"], in_=ld, func=AF.Exp), ["lam"], K)
    V(lambda e: e.tensor_tensor(out=sm["th"], in0=li, in1=sm["dt"], op=ALU.mult), K, K)
    V(lambda e: e.tensor_tensor(out=sm["lrdt"], in0=lr, in1=sm["dt"], op=ALU.mult), K, K)
    A_(lambda e: e.activation(out=sm["rho"], in_=sm["lrdt"], func=AF.Exp), K, K)
    for _ in range(5):
        V(lambda e: e.tensor_single_scalar(out=sm["mk"], in_=sm["th"], scalar=math.pi, op=ALU.is_gt), K, K)
        V(lambda e: e.scalar_tensor_tensor(out=sm["th"], in0=sm["mk"], scalar=-2.0 * math.pi, in1=sm["th"], op0=ALU.mult, op1=ALU.add), K, K)
    V(lambda e: e.tensor_scalar(out=sm["t3"], in0=sm["th"], scalar1=0.125, scalar2=None, op0=ALU.mult), K, K)
    V(lambda e: e.tensor_tensor(out=sm["x2"], in0=sm["t3"], in1=sm["t3"], op=ALU.mult), K, K)

    def horner(o, coefs):
        V(lambda e: e.memset(o, coefs[0]), K, K)
        for c in coefs[1:]:
            V(lambda e: e.tensor_tensor(out=o, in0=o, in1=sm["x2"], op=ALU.mult), K, K)
            V(lambda e, c=c: e.tensor_scalar(out=o, in0=o, scalar1=float(c), scalar2=None, op0=ALU.add), K, K)

    def cdouble(sn_, cs_):
        V(lambda e: e.tensor_tensor(out=sm["t1"], in0=sn_, in1=cs_, op=ALU.mult), K, K)
        V(lambda e: e.tensor_tensor(out=sm["t2"], in0=cs_, in1=cs_, op=ALU.mult), K, K)
        V(lambda e: e.tensor_tensor(out=sm["t3"], in0=sn_, in1=sn_, op=ALU.mult), K, K)
        V(lambda e: e.tensor_scalar(out=sn_, in0=sm["t1"], scalar1=2.0, scalar2=None, op0=ALU.mult), K, K)
        V(lambda e: e.tensor_tensor(out=cs_, in0=sm["t2"], in1=sm["t3"], op=ALU.subtract), K, K)

    horner(sm["sn"], [-1 / 39916800.0, 1 / 362880.0, -1 / 5040.0, 1 / 120.0, -1 / 6.0, 1.0])
    V(lambda e: e.tensor_tensor(out=sm["sn"], in0=sm["sn"], in1=sm["t3"], op=ALU.mult), K, K)
    horner(sm["cs"], [-1 / 3628800.0, 1 / 40320.0, -1 / 720.0, 1 / 24.0, -0.5, 1.0])
    for _ in range(3):
        cdouble(sm["sn"], sm["cs"])
    V(lambda e: e.tensor_tensor(out=sm["ar"], in0=sm["rho"], in1=sm["cs"], op=ALU.mult), K, K)
    V(lambda e: e.tensor_tensor(out=sm["ai"], in0=sm["rho"], in1=sm["sn"], op=ALU.mult), K, K)
    V(lambda e: e.tensor_scalar(out=sm["t1"], in0=sm["ar"], scalar1=-1.0, scalar2=None, op0=ALU.add), K, K)
    V(lambda e: e.tensor_tensor(out=sm["den"], in0=lr, in1=lr, op=ALU.mult), K, K)
    V(lambda e: e.tensor_tensor(out=sm["t2"], in0=li, in1=li, op=ALU.mult), K, K)
    V(lambda e: e.tensor_tensor(out=sm["den"], in0=sm["den"], in1=sm["t2"], op=ALU.add), K, K)
    V(lambda e: e.reciprocal(out=sm["den"], in_=sm["den"]), K, K)
    V(lambda e: e.tensor_tensor(out=sm["t2"], in0=sm["t1"], in1=lr, op=ALU.mult), K, K)
    V(lambda e: e.tensor_tensor(out=sm["t3"], in0=sm["ai"], in1=li, op=ALU.mult), K, K)
    V(lambda e: e.tensor_tensor(out=sm["t2"], in0=sm["t2"], in1=sm["t3"], op=ALU.add), K, K)
    V(lambda e: e.tensor_tensor(out=sm["fr"], in0=sm["t2"], in1=sm["den"], op=ALU.mult), K, K)
    V(lambda e: e.tensor_tensor(out=sm["t2"], in0=sm["ai"], in1=lr, op=ALU.mult), K, K)
    V(lambda e: e.tensor_tensor(out=sm["t3"], in0=sm["t1"], in1=li, op=ALU.mult), K, K)
    V(lambda e: e.tensor_tensor(out=sm["t2"], in0=sm["t2"], in1=sm["t3"], op=ALU.subtract), K, K)
    V(lambda e: e.tensor_tensor(out=sm["fi"], in0=sm["t2"], in1=sm["den"], op=ALU.mult), K, K)
    V(lambda e: e.memset(pwr[0], 1.0), K, K)
    V(lambda e: e.memset(pwi[0], 0.0), K, K)
    for k in range(1, 9):
        V(lambda e, k=k: e.tensor_tensor(out=sm["t1"], in0=pwr[k - 1], in1=sm["ar"], op=ALU.mult), K, K)
        V(lambda e, k=k: e.tensor_tensor(out=sm["t2"], in0=pwi[k - 1], in1=sm["ai"], op=ALU.mult), K, K)
        V(lambda e, k=k: e.tensor_tensor(out=pwr[k], in0=sm["t1"], in1=sm["t2"], op=ALU.subtract), K, K)
        V(lambda e, k=k: e.tensor_tensor(out=sm["t1"], in0=pwr[k - 1], in1=sm["ai"], op=ALU.mult), K, K)
        V(lambda e, k=k: e.tensor_tensor(out=sm["t2"], in0=pwi[k - 1], in1=sm["ar"], op=ALU.mult), K, K)
        V(lambda e, k=k: e.tensor_tensor(out=pwi[k], in0=sm["t1"], in1=sm["t2"], op=ALU.add), K, K)
    V(lambda e: e.tensor_scalar(out=sm["t1"], in0=sm["lrdt"], scalar1=8.0, scalar2=None, op0=ALU.mult), K, K)
    A_(lambda e: e.activation(out=sm["rho"], in_=sm["t1"], func=AF.Exp), K, K)
    for _ in range(3):
        cdouble(sm["sn"], sm["cs"])
    V(lambda e: e.memset(Er[:, :, 0:1], 1.0), K, K)
    V(lambda e: e.memset(Ei[:, :, 0:1], 0.0), K, K)
    V(lambda e: e.tensor_copy(out=sm["wr"], in_=sm["cs"]), K, K)
    V(lambda e: e.tensor_copy(out=sm["wi"], in_=sm["sn"]), K, K)
    m0 = ar.mark()
    tA = ar.alloc("tA", [128, 32, 64], F32); tB = ar.alloc("tB", [128, 32, 64], F32)
    for k in range(7):
        n = 1 << k
        wrb = sm["wr"].unsqueeze(2).broadcast_to([128, 32, n]); wib = sm["wi"].unsqueeze(2).broadcast_to([128, 32, n])
        V(lambda e, n=n, wrb=wrb: e.tensor_tensor(out=tA[:, :, :n], in0=Er[:, :, :n], in1=wrb, op=ALU.mult), K, K)
        V(lambda e, n=n, wib=wib: e.tensor_tensor(out=tB[:, :, :n], in0=Ei[:, :, :n], in1=wib, op=ALU.mult), K, K)
        V(lambda e, n=n: e.tensor_tensor(out=Er[:, :, n:2 * n], in0=tA[:, :, :n], in1=tB[:, :, :n], op=ALU.subtract), K, K)
        V(lambda e, n=n, wib=wib: e.tensor_tensor(out=tA[:, :, :n], in0=Er[:, :, :n], in1=wib, op=ALU.mult), K, K)
        V(lambda e, n=n, wrb=wrb: e.tensor_tensor(out=tB[:, :, :n], in0=Ei[:, :, :n], in1=wrb, op=ALU.mult), K, K)
        V(lambda e, n=n: e.tensor_tensor(out=Ei[:, :, n:2 * n], in0=tA[:, :, :n], in1=tB[:, :, :n], op=ALU.add), K, K)
        V(lambda e: e.tensor_tensor(out=sm["t1"], in0=sm["wr"], in1=sm["wr"], op=ALU.mult), K, K)
        V(lambda e: e.tensor_tensor(out=sm["t2"], in0=sm["wi"], in1=sm["wi"], op=ALU.mult), K, K)
        V(lambda e: e.tensor_tensor(out=sm["t3"], in0=sm["wr"], in1=sm["wi"], op=ALU.mult), K, K)
        V(lambda e: e.tensor_tensor(out=sm["wr"], in0=sm["t1"], in1=sm["t2"], op=ALU.subtract), K, K)
        V(lambda e: e.tensor_scalar(out=sm["wi"], in0=sm["t3"], scalar1=2.0, scalar2=None, op0=ALU.mult), K, K)
    V(lambda e: e.tensor_copy(out=sm["w128r"], in_=sm["wr"]), K, K)
    V(lambda e: e.tensor_copy(out=sm["w128i"], in_=sm["wi"]), K, K)
    V(lambda e: e.memset(cR, 0.0), K, K)
    V(lambda e: e.memset(SL, 0.0), K, K)
    fw.barrier()
    ar.reset(m0)
    BTc = ar.alloc("BTc", [128, 32, 2, 16], F32); Cc = ar.alloc("Cc", [128, 2, 32, 16], F32)
    Cfc = ar.alloc("Cfc", [128, 32, 2, 16], F32); Xc = ar.alloc("Xc", [128, 32, 2, 16], F32)
    c1 = ar.alloc("c1", [128, 32, 16], F32); c2 = ar.alloc("c2", [128, 32, 16], F32)
    padf = ar.alloc("padf", [128, 32, 2, 128], F32); Cfp = ar.alloc("Cfp", [128, 32, 2, 128], F32)
    padb = ar.alloc("padb", [128, 32, 2, 128], BF16); DBsb = ar.alloc("DBsb", [128, 32, 2, 128], BF16)
    fw.dma("sync", BTc, btc, writes=["BTc"])
    fw.dma("sync", Cc, cc.rearrange("a p g c -> p a g c"), writes=["Cc"])
    G_(lambda e: e.memset(padf, 0.0), [], ["padf"])
    G_(lambda e: e.memset(Cfp, 0.0), [], ["Cfp"])

    def cmul_compact(dst, src_r, src_i, sr, si, rk, wk, neg_im=False):
        srb = sr.unsqueeze(2).broadcast_to([128, 32, 16]); sib = si.unsqueeze(2).broadcast_to([128, 32, 16])
        V(lambda e: e.tensor_tensor(out=c1, in0=src_r, in1=srb, op=ALU.mult), rk, ["c1"])
        V(lambda e: e.tensor_tensor(out=c2, in0=src_i, in1=sib, op=ALU.mult), rk, ["c2"])
        V(lambda e: e.tensor_tensor(out=dst[:, :, 0, :], in0=c1, in1=c2, op=ALU.subtract), ["c1", "c2"], wk)
        V(lambda e: e.tensor_tensor(out=c1, in0=src_r, in1=sib, op=ALU.mult), rk + wk, ["c1"])
        V(lambda e: e.tensor_tensor(out=c2, in0=src_i, in1=srb, op=ALU.mult), rk + wk, ["c2"])
        if neg_im:
            V(lambda e: e.scalar_tensor_tensor(out=dst[:, :, 1, :], in0=c1, scalar=-1.0, in1=c2, op0=ALU.mult, op1=ALU.subtract), ["c1", "c2"], wk)
        else:
            V(lambda e: e.tensor_tensor(out=dst[:, :, 1, :], in0=c1, in1=c2, op=ALU.add), ["c1", "c2"], wk)

    def scatter(dst_pad, src_c, rk, wk):
        for g2 in range(2):
            for q in range(4):
                blk = 2 * q + g2
                G_(lambda e, g2=g2, q=q, blk=blk: e.tensor_copy(out=dst_pad[g2 * 64:(g2 + 1) * 64, q::4, :, blk * 16:(blk + 1) * 16],
                                                                 in_=src_c[g2 * 64:(g2 + 1) * 64, q::4, :, :]), rk, wk)

    cmul_compact(Cfc, Cc[:, 0], Cc[:, 1], sm["fr"], sm["fi"], ["Cc"], ["Cfc"])
    cmul_compact(Xc, Cc[:, 0], Cc[:, 1], sm["fr"], sm["fi"], ["Cc"], ["Xc"], neg_im=True)
    scatter(Cfp, Xc, ["Xc"], ["Cfp"])
    for k in range(8):
        j = 7 - k
        cmul_compact(Xc, BTc[:, :, 0, :], BTc[:, :, 1, :], pwr[k], pwi[k], ["BTc"], ["Xc"])
        scatter(padf, Xc, ["Xc"], ["padf"])
        for r in range(8):
            p, pk = psum()
            n_ = 0
            for gl in range(4):
                for ri in range(2):
                    mm(p[:, 0:128], padf[:, 4 * r + gl, ri, :], Cfp[:, 4 * r + gl, ri, :], n_ == 0, n_ == 7, ["padf", "Cfp"], pk, n_ == 7)
                    n_ += 1
            if k == 0:
                V(lambda e, p=p, r=r: e.scalar_tensor_tensor(out=Kpad[:, r, 0, :], in0=identf, scalar=dcs[:, r:r + 1], in1=p[:, 0:128], op0=ALU.mult, op1=ALU.add),
                  [pk, "identf", "dcs"], ["Kpad"])
            else:
                V(lambda e, p=p, r=r, k=k: e.tensor_copy(out=Kpad[:, r, k, :], in_=p[:, 0:128]), [pk], ["Kpad"])
        V(lambda e: e.tensor_copy(out=padb, in_=padf), ["padf"], ["padb"])
        for g4 in range(16):
            p, pk = psum()
            for q in range(4):
                gi = g4 * 4 + q
                mm(p[:, q * 128:(q + 1) * 128], padb[:, gi // 2, gi % 2, :], identb, True, True, ["padb", "identb"], pk, q == 3)
            V(lambda e, p=p, g4=g4: e.tensor_copy(out=DBsb.rearrange("p g r s -> p (g r) s")[:, g4 * 4:(g4 + 1) * 4, :], in_=p.rearrange("p (a c) -> p a c", c=128)),
              [pk], ["DBsb"])
        fw.dma("sync", DB_s[:, :, j].rearrange("r p g a s -> p r g a s"), DBsb.rearrange("p (r g) a s -> p r g a s", r=8), reads=["DBsb"], writes=[("DB_s", j)])
    for j in range(8):
        cmul_compact(Xc, Cfc[:, :, 0, :], Cfc[:, :, 1, :], pwr[j + 1], pwi[j + 1], ["Cfc"], ["Xc"])
        scatter(padf, Xc, ["Xc"], ["padf"])
        V(lambda e: e.tensor_copy(out=padb, in_=padf), ["padf"], ["padb"])
        fw.dma("sync", EC_s[:, :, j].rearrange("r p g a s -> p r g a s"), padb.rearrange("p (r g) a s -> p r g a s", r=8), reads=["padb"], writes=[("EC_s", j)])
    fw.barrier()
    ar.reset(base_ssm)

    if upto <= 2:
        fw.emit()
        return nc
    NPS, NMS = NP // 1024, NM // 1024
    DBr = ar.alloc("DBr", [128, 8, 4, 2, 128], BF16); ECr = ar.alloc("ECr", [128, 8, 4, 2, 128], BF16)
    uTr = [ar.alloc("uTr", [128, 1024], BF16) for _ in range(2)]
    zTr = [ar.alloc("zTr", [128, 1024], BF16) for _ in range(2)]
    RB = []
    for b in range(2):
        d = {n: ar.alloc(n, [128, 4, 128], F32) for n in ["t1", "t2", "t3", "t4", "Xr", "Xi", "Rr", "Ri"]}
        d["Sr"] = ar.alloc("Sr", [128, 4, 130], BF16); d["Si"] = ar.alloc("Si", [128, 4, 130], BF16)
        for n in ["c1", "c2", "c3", "c4"]:
            d[n] = ar.alloc(n, [128, 4], F32)
        RB.append(d)
    ysb = [ar.alloc("ysb", [128, 1024], F32) for _ in range(2)]
    g1b = [ar.alloc("g1b", [128, 1024], F32) for _ in range(2)]
    g2b = [ar.alloc("g2b", [128, 1024], F32) for _ in range(2)]
    rcount = 0
    hcount = 0
    for r in range(8):
        gsl = slice(4 * r, 4 * r + 4)
        fw.dma("sync", DBr, DB_s[r], writes=["DBr"])
        fw.dma("sync", ECr, EC_s[r], writes=["ECr"])
        for st in range(NPS + NMS):
            is_main = st >= NPS
            ub = st % 2
            fw.dma("sync", uTr[ub].rearrange("p (n t) -> p n t", t=128), uT_s[8 * st:8 * st + 8, :, r, :].rearrange("n p t -> p n t"),
                   writes=[f"uTr{ub}"])
            fw.flush()
            b = rcount % 2
            rcount += 1
            B = RB[b]
            kb = lambda n, b=b: f"{n}{b}"
            pXr, pkr = psum()
            pXi, pki = psum()
            for ri, (pX, pk) in enumerate(((pXr, pkr), (pXi, pki))):
                for gl in range(4):
                    for j in range(8):
                        mm(pX[:, gl * 128:(gl + 1) * 128], DBr[:, j, gl, ri, :], uTr[ub][:, j::8], j == 0, j == 7,
                           [f"uTr{ub}", "DBr"], pk, (gl == 3 and j == 7))
            pXr3 = pXr.rearrange("p (a c) -> p a c", c=128); pXi3 = pXi.rearrange("p (a c) -> p a c", c=128)
            Erg, Eig = Er[:, gsl, :], Ei[:, gsl, :]
            V(lambda e, B=B, a=pXr3, t=Erg: e.tensor_tensor(out=B["t1"], in0=a, in1=t, op=ALU.mult), [pkr], [kb("t1")])
            V(lambda e, B=B, a=pXi3, t=Eig: e.tensor_tensor(out=B["t2"], in0=a, in1=t, op=ALU.mult), [pki], [kb("t2")])
            V(lambda e, B=B, a=pXi3, t=Erg: e.tensor_tensor(out=B["t3"], in0=a, in1=t, op=ALU.mult), [pki], [kb("t3")])
            V(lambda e, B=B, a=pXr3, t=Eig: e.tensor_tensor(out=B["t4"], in0=a, in1=t, op=ALU.mult), [pkr], [kb("t4")])
            G_(lambda e, B=B: e.tensor_tensor(out=B["Xr"], in0=B["t1"], in1=B["t2"], op=ALU.add), [kb("t1"), kb("t2")], [kb("Xr")])
            G_(lambda e, B=B: e.tensor_tensor(out=B["Xi"], in0=B["t3"], in1=B["t4"], op=ALU.subtract), [kb("t3"), kb("t4")], [kb("Xi")])
            for gl in range(4):
                gp = 4 * r + gl
                for nm, xs, ci in (("Rr", "Xr", 0), ("Ri", "Xi", 1)):
                    V(lambda e, B=B, gl=gl, gp=gp, nm=nm, xs=xs, ci=ci: e.tensor_tensor_scan(
                        out=B[nm][:, gl, :], data0=sm["rho"][:, gp:gp + 1].broadcast_to([128, 128]), data1=B[xs][:, gl, :],
                        initial=cR[:, ci, gp:gp + 1], op0=ALU.mult, op1=ALU.add), [kb(xs), ("cR", r)], [kb(nm)])
            if is_main:
                G_(lambda e, B=B, gsl=gsl: e.tensor_copy(out=B["Sr"][:, :, 0], in_=SL[:, 0, gsl]), [("SL", r)], [kb("Sr")])
                G_(lambda e, B=B, gsl=gsl: e.tensor_copy(out=B["Si"][:, :, 0], in_=SL[:, 1, gsl]), [("SL", r)], [kb("Si")])
            wr4, wi4 = sm["w128r"][:, gsl], sm["w128i"][:, gsl]
            er7, ei7 = Er[:, gsl, 127], Ei[:, gsl, 127]
            Rr7, Ri7 = B["Rr"][:, :, 127], B["Ri"][:, :, 127]
            for (xr_, xi_, dst, negim, key) in ((wr4, wi4, cR, False, "cR"), (er7, ei7, SL, True, "SL")):
                G_(lambda e, B=B, a=Rr7, w=xr_: e.tensor_tensor(out=B["c1"], in0=a, in1=w, op=ALU.mult), [kb("Rr")], [kb("c1")])
                G_(lambda e, B=B, a=Ri7, w=xi_: e.tensor_tensor(out=B["c2"], in0=a, in1=w, op=ALU.mult), [kb("Ri")], [kb("c2")])
                G_(lambda e, B=B, a=Ri7, w=xr_: e.tensor_tensor(out=B["c3"], in0=a, in1=w, op=ALU.mult), [kb("Ri")], [kb("c3")])
                G_(lambda e, B=B, a=Rr7, w=xi_: e.tensor_tensor(out=B["c4"], in0=a, in1=w, op=ALU.mult), [kb("Rr")], [kb("c4")])
                G_(lambda e, B=B, dst=dst, gsl=gsl: e.tensor_tensor(out=dst[:, 0, gsl], in0=B["c1"], in1=B["c2"], op=ALU.subtract),
                   [kb("c1"), kb("c2")], [(key, r)])
                if negim:
                    V(lambda e, B=B, dst=dst, gsl=gsl: e.scalar_tensor_tensor(out=dst[:, 1, gsl], in0=B["c3"], scalar=-1.0, in1=B["c4"], op0=ALU.mult, op1=ALU.subtract),
                       [kb("c3"), kb("c4")], [(key, r)])
                else:
                    G_(lambda e, B=B, dst=dst, gsl=gsl: e.tensor_tensor(out=dst[:, 1, gsl], in0=B["c3"], in1=B["c4"], op=ALU.add),
                       [kb("c3"), kb("c4")], [(key, r)])
            if not is_main:
                continue
            G_(lambda e, B=B, t=Erg: e.tensor_tensor(out=B["t1"], in0=B["Rr"], in1=t, op=ALU.mult), [kb("Rr")], [kb("t1")])
            G_(lambda e, B=B, t=Eig: e.tensor_tensor(out=B["t2"], in0=B["Ri"], in1=t, op=ALU.mult), [kb("Ri")], [kb("t2")])
            V(lambda e, B=B, t=Erg: e.tensor_tensor(out=B["t3"], in0=B["Ri"], in1=t, op=ALU.mult), [kb("Ri")], [kb("t3")])
            V(lambda e, B=B, t=Eig: e.tensor_tensor(out=B["t4"], in0=B["Rr"], in1=t, op=ALU.mult), [kb("Rr")], [kb("t4")])
            V(lambda e, B=B: e.tensor_tensor(out=B["Sr"][:, :, 1:129], in0=B["t1"], in1=B["t2"], op=ALU.subtract), [kb("t1"), kb("t2")], [kb("Sr")])
            V(lambda e, B=B: e.scalar_tensor_tensor(out=B["Si"][:, :, 1:129], in0=B["t3"], scalar=-1.0, in1=B["t4"], op0=ALU.mult, op1=ALU.subtract),
              [kb("t3"), kb("t4")], [kb("Si")])
            zb = (st - NPS) % 2
            hb = hcount % 2
            hcount += 1
            for h2 in range(2):
                py, pky = psum()
                for j4 in range(4):
                    j = 4 * h2 + j4
                    o = py[:, j4 * 128:(j4 + 1) * 128]
                    nmm = (j + 1) + 8
                    n_ = 0
                    for k in range(j + 1):
                        mm(o, Kpad[:, r, k, :], uTr[ub][:, (j - k)::8], n_ == 0, n_ == nmm - 1, [f"uTr{ub}", "Kpad"], pky, False)
                        n_ += 1
                    for gl in range(4):
                        for ri, Sn in enumerate(("Sr", "Si")):
                            mm(o, ECr[:, j, gl, ri, :], B[Sn][:, gl, 0:128], n_ == 0, n_ == nmm - 1, [kb(Sn), "ECr"], pky,
                               (j4 == 3 and n_ == nmm - 1))
                            n_ += 1
                A_(lambda e, py=py, hb=hb, h2=h2: e.activation(out=ysb[hb].rearrange("p (c j) -> p c j", j=8)[:, :, 4 * h2:4 * h2 + 4],
                                                               in_=py.rearrange("p (j c) -> p c j", c=128), func=AF.Identity), [pky], [f"ysb{hb}"])
            G_(lambda e, hb=hb: e.tensor_tensor(out=g1b[hb], in0=ysb[hb], in1=ysb[hb], op=ALU.mult), [f"ysb{hb}"], [f"g1b{hb}"])
            G_(lambda e, hb=hb: e.tensor_scalar(out=g1b[hb], in0=g1b[hb], scalar1=0.044715, scalar2=1.0, op0=ALU.mult, op1=ALU.add), [f"g1b{hb}"], [f"g1b{hb}"])
            G_(lambda e, hb=hb: e.tensor_tensor(out=g1b[hb], in0=g1b[hb], in1=ysb[hb], op=ALU.mult), [f"g1b{hb}", f"ysb{hb}"], [f"g1b{hb}"])
            A_(lambda e, hb=hb: e.activation(out=g2b[hb], in_=g1b[hb], func=AF.Sigmoid, scale=1.5957691216057308), [f"g1b{hb}"], [f"g2b{hb}"])
            V(lambda e, hb=hb, zb=zb: e.tensor_tensor(out=zTr[zb], in0=ysb[hb], in1=g2b[hb], op=ALU.mult), [f"ysb{hb}", f"g2b{hb}"], [f"zTr{zb}"])
            m8 = 8 * (st - NPS)
            fw.defer_dma("sync", zT_s[m8:m8 + 8, :, r, :].rearrange("n p t -> p n t"), zTr[zb].rearrange("p (n t) -> p n t", t=128),
                   reads=[f"zTr{zb}"], writes=[("zT_s", st, r)])
    fw.barrier()
    ar.reset(base_persist)

    if upto <= 3:
        fw.emit()
        return nc
    Wg = ar.alloc("Wg", [128, 8, 2048], BF16); Wo = ar.alloc("Wo", [128, 8, 1024], BF16)
    load_weights(Wg, w_glu, 4, 8, "Wg")
    load_weights(Wo, w_out, 2, 8, "Wo")
    kme = ar.alloc("kme", [128, 2, 128], BF16); vme = ar.alloc("vme", [128, 4, 65], BF16)
    fw.dma("sync", kme, kT_s[0], writes=["kme"]); fw.dma("sync", vme, v_s[0], writes=["vme"])
    qTl = [ar.alloc("qTl", [128, 8, 128], BF16) for _ in range(2)]
    kTl = [ar.alloc("kTl", [128, 2, 128], BF16) for _ in range(3)]
    vl = [ar.alloc("vl", [128, 4, 65], BF16) for _ in range(3)]
    gl_ = [ar.alloc("gl", [128, 2048], BF16) for _ in range(2)]
    zTl = [ar.alloc("zTl", [128, 8, 128], BF16) for _ in range(2)]
    xr = [ar.alloc("xr", [128, D], F32) for _ in range(2)]
    Pc = [ar.alloc("Pc", [128, 512], BF16) for _ in range(2)]
    Pp = [ar.alloc("Pp", [128, 512], BF16) for _ in range(2)]
    Pm = [ar.alloc("Pm", [128, 512], BF16) for _ in range(2)]
    den_ = [ar.alloc("den", [128, 4], F32) for _ in range(2)]
    for b_ in range(2):
        V(lambda e, b_=b_: e.memset(Pm[b_], 0.0), [], [f"Pm{b_}"])
    attn_ = [ar.alloc("attn", [128, D], F32) for _ in range(2)]; An_ = [ar.alloc("An", [128, D], F32) for _ in range(2)]
    sig_ = [ar.alloc("sig", [128, 512], F32) for _ in range(2)]; ssm_ = [ar.alloc("ssm", [128, D], F32) for _ in range(2)]
    Bn_ = [ar.alloc("Bn", [128, D], F32) for _ in range(2)]
    mg_ = [ar.alloc("mg", [128, D], BF16) for _ in range(2)]; mgT_ = [ar.alloc("mgT", [128, 8, 128], BF16) for _ in range(2)]
    h1 = [ar.alloc("h1", [128, D], F32) for _ in range(2)]
    fw.dma("sync", kTl[1], kT_s[1], writes=["kTl1"]); fw.dma("sync", vl[1], v_s[1], writes=["vl1"])
    def s3_loads(i):
        b = i % 2
        jc = 2 + i
        sc = jc % 3
        fw.dma("sync", kTl[sc], kT_s[jc], reads=[("kT_s", jc)], writes=[f"kTl{sc}"])
        fw.dma("sync", vl[sc], v_s[jc], reads=[("v_s", jc)], writes=[f"vl{sc}"])
        fw.dma("sync", qTl[b], qT_s[i], writes=[f"qTl{b}"])
        fw.dma("sync", gl_[b], g_s[i], writes=[f"gl{b}"])
        fw.dma("sync", zTl[b], zT_s[i], writes=[f"zTl{b}"])
        fw.dma("sync", xr[b], xmain[i * 128:(i + 1) * 128, :], writes=[f"xr{b}"])
        fw.flush()

    def s3_A(n):
        i, grp = n // 4, n % 4
        b, pb = i % 2, n % 2
        sc, sp = (2 + i) % 3, (1 + i) % 3
        bs, kc = (grp % 2) * 64, grp // 2
        qsel = qTl[b][bs:bs + 64, kc * 4:(kc + 1) * 4, :]
        pS, pkS = psum()
        mm(pS, kTl[sc][bs:bs + 64, kc, :], qsel, True, True, [f"kTl{sc}", f"qTl{b}"], pkS, True)
        A_(lambda e: e.activation(out=Pc[pb], in_=pS, func=AF.Exp, scale=0.125), [pkS], [f"Pc{pb}"])
        G_(lambda e: e.tensor_tensor(out=Pc[pb], in0=Pc[pb], in1=maskb[:, 0, :], op=ALU.mult), [f"Pc{pb}", "maskb"], [f"Pc{pb}"])
        pS2, pkS2 = psum()
        mm(pS2, kTl[sp][bs:bs + 64, kc, :], qsel, True, True, [f"kTl{sp}", f"qTl{b}"], pkS2, True)
        A_(lambda e: e.activation(out=Pp[pb], in_=pS2, func=AF.Exp, scale=0.125), [pkS2], [f"Pp{pb}"])
        mi = 2 if i == 0 else 1
        G_(lambda e: e.tensor_tensor(out=Pp[pb], in0=Pp[pb], in1=maskb[:, mi, :], op=ALU.mult), [f"Pp{pb}", "maskb"], [f"Pp{pb}"])
        pS3, pkS3 = psum()
        mm(pS3[0:16, :], kme[bs:bs + 64, kc, 0:16], qsel, True, True, ["kme", f"qTl{b}"], pkS3, True)
        A_(lambda e: e.activation(out=Pm[pb][0:16, :], in_=pS3[0:16, :], func=AF.Exp, scale=0.125), [pkS3], [f"Pm{pb}"])

    def s3_B(n):
        i, grp = n // 4, n % 4
        b, pb = i % 2, n % 2
        sc, sp = (2 + i) % 3, (1 + i) % 3
        attn, kA = attn_[b], f"attn{b}"
        pO, pkO = psum()
        for r in range(4):
            o = pO[:, r * 65:(r + 1) * 65]
            mm(o, Pm[pb][:, r * 128:(r + 1) * 128], vme[:, grp, :], True, False, [f"Pm{pb}", "vme"], pkO, False)
            mm(o, Pp[pb][:, r * 128:(r + 1) * 128], vl[sp][:, grp, :], False, False, [f"Pp{pb}", f"vl{sp}"], pkO, False)
            mm(o, Pc[pb][:, r * 128:(r + 1) * 128], vl[sc][:, grp, :], False, True, [f"Pc{pb}", f"vl{sc}"], pkO, r == 3)
        pO3 = pO[:, 0:260].rearrange("p (r c) -> p r c", c=65)
        den = den_[pb]
        kd = f"den{pb}"
        V(lambda e: e.tensor_tensor(out=den, in0=pO3[:, :, 64], in1=esink[:, grp * 4:(grp + 1) * 4], op=ALU.add), [pkO, "esink"], [kd])
        V(lambda e: e.reciprocal(out=den, in_=den), [kd], [kd])
        V(lambda e: e.tensor_tensor(out=attn[:, grp * 256:(grp + 1) * 256].rearrange("p (r d) -> p r d", d=64), in0=pO3[:, :, 0:64],
                                    in1=den.unsqueeze(2).broadcast_to([128, 4, 64]), op=ALU.mult), [pkO, kd], [kA])

    def s3_tail(i):
        b = i % 2
        attn, An, ssm, Bn, mg, mgT = attn_[b], An_[b], ssm_[b], Bn_[b], mg_[b], mgT_[b]
        kA, kAn, kss, kBn, kmg, kmT = f"attn{b}", f"An{b}", f"ssm{b}", f"Bn{b}", f"mg{b}", f"mgT{b}"
        rms_scale(attn, 1, An, [kA], [kAn])
        G_(lambda e: e.tensor_tensor(out=An, in0=An, in1=gl_[b][:, 0:1024], op=ALU.mult), [kAn, f"gl{b}"], [kAn])
        for half in range(2):
            pa, pka = psum()
            for k in range(8):
                mm(pa, zTl[b][:, k, :], Wg[:, k, half * 512:(half + 1) * 512], k == 0, k == 7, [f"zTl{b}", "Wg"], pka, k == 7)
            pz, pkz = psum()
            for k in range(8):
                mm(pz, zTl[b][:, k, :], Wg[:, k, 1024 + half * 512:1024 + (half + 1) * 512], k == 0, k == 7, [f"zTl{b}", "Wg"], pkz, k == 7)
            sig = sig_[half]
            A_(lambda e, pz=pz, sig=sig: e.activation(out=sig, in_=pz, func=AF.Sigmoid), [pkz], [f"sig{half}"])
            V(lambda e, pa=pa, half=half, sig=sig: e.tensor_tensor(out=ssm[:, half * 512:(half + 1) * 512], in0=pa, in1=sig, op=ALU.mult), [pka, f"sig{half}"], [kss])
        rms_scale(ssm, 2, Bn, [kss], [kBn])
        G_(lambda e: e.tensor_tensor(out=Bn, in0=Bn, in1=gl_[b][:, 1024:2048], op=ALU.mult), [kBn, f"gl{b}"], [kBn])
        V(lambda e: e.tensor_tensor(out=mg, in0=An, in1=Bn, op=ALU.add), [kAn, kBn], [kmg])
        transpose8(mg, mgT, kmg, kmT)
        for half in range(2):
            p, pk = psum()
            for k in range(8):
                mm(p, mgT[:, k, :], Wo[:, k, half * 512:(half + 1) * 512], k == 0, k == 7, [kmT, "Wo"], pk, k == 7)
            V(lambda e, p=p, half=half: e.tensor_tensor(out=h1[b][:, half * 512:(half + 1) * 512], in0=p, in1=xr[b][:, half * 512:(half + 1) * 512], op=ALU.add),
              [pk, f"xr{b}"], [f"h1{b}"])
        fw.defer_dma("sync", h1_s[i * 128:(i + 1) * 128, :], h1[b], reads=[f"h1{b}"], writes=[("h1_s", i)])

    NG = 4 * TM_
    s3_loads(0)
    s3_A(0)
    for n in range(NG):
        if n + 1 < NG:
            if (n + 1) % 4 == 0:
                s3_loads((n + 1) // 4)
            s3_A(n + 1)
        s3_B(n)
        if n % 4 == 3:
            s3_tail(n // 4)
    fw.barrier()
    ar.reset(base_persist)

    if upto <= 4:
        fw.emit()
        return nc
    W1 = ar.alloc("W1", [128, 8, 5632], BF16); W2 = ar.alloc("W2", [128, 22, 1024], BF16)
    load_weights(W1, w_f1, 11, 8, "W1")
    load_weights(W2, w_f2, 2, 22, "W2")
    GT = 4
    hl = [ar.alloc("hl", [128, D], F32) for _ in range(2)]
    hn = [ar.alloc("hn", [128, D], BF16) for _ in range(2)]
    hnT = ar.alloc("hnT", [128, 8, GT * 128], BF16)
    sg = [ar.alloc("sg", [128, 512], F32) for _ in range(2)]
    actT = ar.alloc("actT", [128, 22, GT * 128], BF16)
    hres = hl
    ob = junk
    tcount = 0
    for g in range(TM_ // GT):
        for t4 in range(GT):
            i = g * GT + t4
            b = tcount % 2
            tcount += 1
            fw.dma("sync", hl[b], h1_s[i * 128:(i + 1) * 128, :], reads=[("h1_s", i)], writes=[f"hl{b}"])
            fw.flush()
            rms_scale(hl[b], 3, hn[b], [f"hl{b}"], [f"hn{b}"])
            for half in range(2):
                p, pk = psum()
                for jj in range(4):
                    c = half * 4 + jj
                    mm(p[:, jj * 128:(jj + 1) * 128], hn[b][:, c * 128:(c + 1) * 128], identb, True, True, [f"hn{b}", "identb"], pk, jj == 3)
                V(lambda e, p=p, half=half, t4=t4: e.tensor_copy(out=hnT[:, half * 4:half * 4 + 4, t4 * 128:(t4 + 1) * 128],
                                                               in_=p.rearrange("p (a c) -> p a c", c=128)), [pk], [("hnT", t4)])
        hk = [("hnT", t4) for t4 in range(GT)]
        for fc in range(22):
            fp, q2 = fc // 2, fc % 2
            pg, pkg = psum()
            for k in range(8):
                mm(pg, W1[:, k, fp * 512 + q2 * 256:fp * 512 + q2 * 256 + 128], hnT[:, k, :], k == 0, k == 7, hk + ["W1"], pkg, k == 7)
            pu, pku = psum()
            for k in range(8):
                mm(pu, W1[:, k, fp * 512 + q2 * 256 + 128:fp * 512 + q2 * 256 + 256], hnT[:, k, :], k == 0, k == 7, hk + ["W1"], pku, k == 7)
            s_ = sg[fc % 2]
            ks_ = f"sg{fc % 2}"
            A_(lambda e, pg=pg, s_=s_: e.activation(out=s_, in_=pg, func=AF.Sigmoid), [pkg], [ks_])
            V(lambda e, pg=pg, s_=s_: e.tensor_tensor(out=s_, in0=pg, in1=s_, op=ALU.mult), [pkg, ks_], [ks_])
            V(lambda e, pu=pu, s_=s_, fc=fc: e.tensor_tensor(out=actT[:, fc, :], in0=pu, in1=s_, op=ALU.mult), [pku, ks_], [("actT", fc)])
        ak = [("actT", fc) for fc in range(22)]
        for t4 in range(GT):
            i = g * GT + t4
            b = t4 % 2
            fw.dma("sync", hres[b], h1_s[i * 128:(i + 1) * 128, :], reads=[("h1_s", i)], writes=[f"hl{b}"])
            fw.flush()
            for half in range(2):
                p, pk = psum()
                for k in range(22):
                    mm(p, actT[:, k, t4 * 128:(t4 + 1) * 128], W2[:, k, half * 512:(half + 1) * 512], k == 0, k == 21, ak + ["W2"], pk, k == 21)
                V(lambda e, p=p, half=half, b=b: e.tensor_tensor(out=ob[b][:, half * 512:(half + 1) * 512], in0=p, in1=hres[b][:, half * 512:(half + 1) * 512], op=ALU.add),
                  [pk, f"hl{b}"], [f"junk{b}"])
            fw.defer_dma("sync", out[i * 128:(i + 1) * 128, :], ob[b], reads=[f"junk{b}"], writes=[("out", i)])
    fw.emit()
    return nc


def _panels(w, kk):
    n = w.shape[1] // 512
    return np.ascontiguousarray(w.reshape(kk, 128, n, 512).transpose(2, 1, 0, 3))


def prep_shared(inp):
    f = lambda a: np.asarray(a, dtype=np.float32)
    w_in = f(inp["w_in"])[0]
    qcols = []
    for j in range(8):
        for s in range(2):
            head = ((j // 4) * 2 + s) * 4 + (j % 4)
            qcols.extend(range(head * 64, head * 64 + 64))
    w_in_r = np.concatenate([w_in[:, qcols], w_in[:, 1024:1536], w_in[:, 1536:]], axis=1)
    wf1 = f(inp["w_ffn_in"])[0]
    cols = []
    for c in range(22):
        cols.extend(range(c * 128, (c + 1) * 128))
        cols.extend(range(DFF + c * 128, DFF + (c + 1) * 128))
    wf1_r = wf1[:, cols]
    rep = lambda v, n: np.ascontiguousarray(np.broadcast_to(f(v).reshape(1, -1), (128, n)))
    gains = np.stack([rep(inp["norm_mix"][0], D), rep(inp["attn_branch_norm"][0], D), rep(inp["ssm_branch_norm"][0], D), rep(inp["norm_ffn"][0], D)])

    def sp(a):
        return np.ascontiguousarray(f(a).reshape(32, 2, 64).transpose(1, 2, 0).reshape(128, 32))

    lam = np.stack([sp(inp["lam_re"][0]), sp(inp["lam_im"][0]), sp(np.broadcast_to(f(inp["log_dt"])[0][:, None], (64, 64)))])
    bre, bim = f(inp["ssm_b_re"])[0], f(inp["ssm_b_im"])[0]
    def spc(a):
        return a.reshape(32, 2, 64, a.shape[-1]).transpose(1, 2, 0, 3).reshape(128, 32, a.shape[-1])
    btc = np.ascontiguousarray(np.stack([spc(bre), spc(bim)], axis=2))
    cre, cim = f(inp["ssm_c_re"])[0], f(inp["ssm_c_im"])[0]
    cc = np.ascontiguousarray(np.stack([spc(cre.transpose(0, 2, 1)), spc(cim.transpose(0, 2, 1))]))
    kk, qq = np.arange(128)[:, None], np.arange(128)[None, :]
    mcur = np.where(kk <= qq, 1.0, 0.0).astype(np.float32)
    mprev = np.where(kk > qq, 1.0, 0.0).astype(np.float32)
    return dict(
        w_in=_panels(w_in_r, 8), w_glu=_panels(f(inp["w_glu"])[0], 8), w_out=_panels(f(inp["w_out"])[0], 8),
        w_f1=_panels(wf1_r, 8), w_f2=_panels(f(inp["w_ffn_out"])[0], 22), gains=gains,
        gq=rep(np.tile(f(inp["q_norm"])[0], 4), 256), gk=rep(np.tile(f(inp["k_norm"])[0], 4), 256),
        sinks=rep(inp["attn_sinks"][0], 16), ident=np.eye(128, dtype=np.float32), lam=lam, btc=btc, cc=cc,
        dcol=np.ascontiguousarray(f(inp["ssm_d"])[0].reshape(8, 128).T),
    ), mcur, mprev


def prep_core(x_b, meta, h, NM, NP, mcur, mprev):
    xmain = np.ascontiguousarray(x_b[h * NM:(h + 1) * NM])
    xpre = np.zeros((NP, D), np.float32)
    xctx = np.zeros((256, D), np.float32)
    xctx[0:16] = meta
    if h == 0:
        xpre[NP - 16:] = meta
        m0 = np.zeros((128, 128), np.float32)
    else:
        xpre[1008:1024] = meta
        xpre[1024:] = x_b[0:NM]
        xctx[128:256] = x_b[NM - 128:NM]
        m0 = mprev
    masks = np.stack([np.tile(mcur, (1, 4)), np.tile(mprev, (1, 4)), np.tile(m0, (1, 4))]).astype(np.float32)
    return dict(xmain=xmain, xpre=xpre, xctx=xctx, masks=masks)


_NC_CACHE = {}


def kernel(**inputs):
    x = np.asarray(inputs["x"], dtype=np.float32)
    Bsz, S, _ = x.shape
    NM = S // 2
    NP = NM + 1024
    meta = np.asarray(inputs["meta_tokens"], dtype=np.float32)
    shared, mcur, mprev = prep_shared(inputs)
    in_maps = []
    for b in range(Bsz):
        for h in range(2):
            d = dict(shared)
            d.update(prep_core(x[b], meta, h, NM, NP, mcur, mprev))
            in_maps.append(d)
    nc = build(NM, NP)
    res = run_bass_kernel_spmd(nc, in_maps, core_ids=list(range(len(in_maps))))
    outp = np.zeros((Bsz, S, D), np.float32)
    for b in range(Bsz):
        for h in range(2):
            outp[b, h * NM:(h + 1) * NM] = res.results[2 * b + h]["out"]
    return outp
```

```python
import math
import contextlib
import numpy as np
import concourse.bass as bass
import concourse.mybir as mybir
from concourse.bass_utils import run_bass_kernel_spmd

F32 = mybir.dt.float32
BF16 = mybir.dt.bfloat16
AF = mybir.ActivationFunctionType
ALU = mybir.AluOpType
AX = mybir.AxisListType
ENGS = ("tensor", "vector", "scalar", "gpsimd", "sync")
D = 1024
DFF = 2816
NEG = -30000.0


class FW:
    def __init__(self, nc, n_dma_sems=40):
        self.nc = nc
        self.ops = {e: [] for e in ENGS}
        self.cnt = {e: 0 for e in ENGS}
        self.known = {e: {} for e in ENGS}
        self.last_w = {}
        self.readers = {}
        self.n_dma_sems = n_dma_sems
        self.dma_gen = [0] * n_dma_sems
        self.dma_rr = 0
        self.sem_names = [f"s_{e}" for e in ENGS] + [f"d_{i}" for i in range(n_dma_sems)]

    def _deps(self, reads, writes):
        evs = []
        for k in reads:
            if k in self.last_w:
                evs.append(self.last_w[k])
        for k in writes:
            if k in self.last_w:
                evs.append(self.last_w[k])
            evs.extend(self.readers.get(k, ()))
        return evs

    def _commit(self, ev, reads, writes):
        for k in reads:
            self.readers.setdefault(k, []).append(ev)
        for k in writes:
            self.last_w[k] = ev
            self.readers[k] = []

    def _waits(self, eng, evs):
        best = {}
        for (s, v) in evs:
            if v > best.get(s, 0):
                best[s] = v
        out = []
        kn = self.known[eng]
        for s, v in best.items():
            if eng == "tensor" and s == "s_tensor":
                continue
            if kn.get(s, 0) >= v:
                continue
            kn[s] = v
            out.append((s, v))
        return out

    def op(self, eng, fn, reads=(), writes=(), inc=True):
        evs = self._deps(reads, writes)
        waits = self._waits(eng, evs)
        sname = f"s_{eng}"
        ev = (sname, self.cnt[eng] + 1)
        if inc:
            self.cnt[eng] += 1
        self.ops[eng].append((waits, fn, (sname, 1) if inc else None))
        self._commit(ev, reads, writes)
        return ev

    def dma(self, queue, out, in_, reads=(), writes=(), **kw):
        i = self.dma_rr
        self.dma_rr = (self.dma_rr + 1) % self.n_dma_sems
        sname = f"d_{i}"
        evs = self._deps(reads, writes)
        if self.dma_gen[i] > 0:
            evs.append((sname, 16 * self.dma_gen[i]))
        waits = self._waits(queue, evs)
        self.dma_gen[i] += 1
        ev = (sname, 16 * self.dma_gen[i])
        self.ops[queue].append((waits, lambda e: e.dma_start(out=out, in_=in_, **kw), (sname, 16)))
        self._commit(ev, reads, writes)
        return ev

    def defer_dma(self, *a, **kw):
        if not hasattr(self, "_deferred"):
            self._deferred = []
        self._deferred.append((a, kw))

    def flush(self):
        for a, kw in getattr(self, "_deferred", []):
            self.dma(*a, **kw)
        self._deferred = []

    def barrier(self):
        self.flush()
        fin = []
        for e in ENGS:
            if self.cnt[e] > 0:
                fin.append((f"s_{e}", self.cnt[e]))
        for i in range(self.n_dma_sems):
            if self.dma_gen[i] > 0:
                fin.append((f"d_{i}", 16 * self.dma_gen[i]))
        for e in ENGS:
            w = self._waits(e, fin)
            if w:
                self.ops[e].append((w, None, None))
        self.last_w = {}
        self.readers = {}

    def emit(self):
        nc = self.nc
        self.barrier()
        with contextlib.ExitStack() as st:
            sems = {n: st.enter_context(nc.semaphore(n)) for n in self.sem_names}
            block = st.enter_context(nc.Block())

            def mk(engname):
                lst = self.ops[engname]

                def body(eng):
                    for (waits, fn, inc) in lst:
                        for (s, v) in waits:
                            eng.wait_ge(sems[s], v)
                        if fn is None:
                            continue
                        ins = fn(eng)
                        if inc is not None:
                            ins.then_inc(sems[inc[0]], inc[1])
                return body

            block.tensor(mk("tensor"))
            block.vector(mk("vector"))
            block.scalar(mk("scalar"))
            block.gpsimd(mk("gpsimd"))
            block.sync(mk("sync"))


class Arena:
    def __init__(self, nc, base=16640, limit=224 * 1024):
        self.nc, self.off, self.limit, self.n = nc, base, limit, 0

    def alloc(self, name, shape, dt):
        per = int(np.prod(shape[1:])) * (4 if dt == F32 else 2)
        per = (per + 63) // 64 * 64
        assert self.off + per <= self.limit, (name, self.off, per)
        self.n += 1
        t = self.nc.alloc_sbuf_tensor_at(f"{name}_{self.n}_{self.off}", list(shape), dt, offset=self.off)
        self.off += per
        return t.ap()

    def mark(self):
        return self.off

    def reset(self, off):
        self.off = off


def build(NM, NP, upto=9):
    nc = bass.Bass("TRN2", target_bir_lowering=False)
    fw = FW(nc)
    TM_, TP_ = NM // 128, NP // 128
    NS = TP_ + TM_
    NK = 2 + TM_

    def din(name, shape, dt=F32):
        return nc.dram_tensor(name, list(shape), dt, kind="ExternalInput").ap()

    xmain = din("xmain", [NM, D]); xpre = din("xpre", [NP, D]); xctx = din("xctx", [256, D])
    w_in = din("w_in", [9, 128, 8, 512]); w_glu = din("w_glu", [4, 128, 8, 512]); w_out = din("w_out", [2, 128, 8, 512])
    w_f1 = din("w_f1", [11, 128, 8, 512]); w_f2 = din("w_f2", [2, 128, 22, 512])
    gains = din("gains", [4, 128, D])
    gq = din("gq", [128, 256]); gk = din("gk", [128, 256]); sinks = din("sinks", [128, 16])
    masks = din("masks", [3, 128, 512])
    ident = din("ident", [128, 128])
    lam = din("lam", [3, 128, 32])
    btc = din("btc", [128, 32, 2, 16]); cc = din("cc", [2, 128, 32, 16]); dcol = din("dcol", [128, 8])
    out = nc.dram_tensor("out", [NM, D], F32, kind="ExternalOutput").ap()

    def dscr(name, shape, dt):
        return nc.dram_tensor(name, list(shape), dt, kind="Internal").ap()

    uT_s = dscr("uT_s", [NS, 128, 8, 128], BF16); qT_s = dscr("qT_s", [TM_, 128, 8, 128], BF16)
    kT_s = dscr("kT_s", [NK, 128, 2, 128], BF16); v_s = dscr("v_s", [NK, 128, 4, 65], BF16)
    g_s = dscr("g_s", [TM_, 128, 2048], BF16); zT_s = dscr("zT_s", [TM_, 128, 8, 128], BF16)
    h1_s = dscr("h1_s", [NM, D], F32)
    DB_s = dscr("DB_s", [8, 128, 8, 4, 2, 128], BF16); EC_s = dscr("EC_s", [8, 128, 8, 4, 2, 128], BF16)

    ar = Arena(nc)
    identf = ar.alloc("identf", [128, 128], F32); identb = ar.alloc("identb", [128, 128], BF16)
    gsb = ar.alloc("gsb", [128, 4, D], F32)
    fw.dma("sync", identf, ident, writes=["identf"])
    fw.op("vector", lambda e: e.tensor_copy(out=identb, in_=identf), reads=["identf"], writes=["identb"])
    fw.dma("sync", gsb, gains.rearrange("a p d -> p a d"), writes=["gsb"])
    pbank = [nc.alloc_psum_tensor(f"pb{i}", [128, 512], F32).ap() for i in range(8)]
    pcnt = [0]

    def psum():
        i = pcnt[0] % 8
        pcnt[0] += 1
        return pbank[i], f"pb{i}"

    rr = [0]

    def alt():
        rr[0] += 1
        return "vector" if rr[0] % 2 else "gpsimd"

    base0 = ar.mark()

    def load_weights(dst, src, npan, kk, key):
        m = ar.mark()
        nst = 3 if ar.off + 3 * 16384 <= ar.limit else 2
        st = [ar.alloc("wst", [128, 8, 512], F32) for _ in range(nst)]
        cyc = ["vector", "scalar", "gpsimd", "vector", "scalar"]
        n = 0
        for pi in range(npan):
            for k0 in range(0, kk, 8):
                kc = min(8, kk - k0)
                s = st[n % nst]
                fw.dma("sync", s[:, :kc, :], src[pi][:, k0:k0 + kc, :], writes=[f"wst{n % nst}"])
                eng = cyc[n % len(cyc)]
                o_ = dst[:, k0:k0 + kc, pi * 512:(pi + 1) * 512]
                if eng == "scalar":
                    fw.op(eng, lambda e, s=s, kc=kc, o_=o_: e.activation(out=o_, in_=s[:, :kc, :], func=AF.Copy), reads=[f"wst{n % nst}"], writes=[key])
                else:
                    fw.op(eng, lambda e, s=s, kc=kc, o_=o_: e.tensor_copy(out=o_, in_=s[:, :kc, :]), reads=[f"wst{n % nst}"], writes=[key])
                n += 1
        fw.barrier()
        ar.reset(m)

    rmsc = [0]

    def rms_scale(xin, gidx, xn_out, rkeys, wkeys, ncol=D):
        pr = rmsc[0] % 2
        rmsc[0] += 1
        jk, sq_ = junk[pr], ssq[pr]
        kj, ks = f"junk{pr}", f"ssq{pr}"
        fw.op("scalar", lambda e: e.activation(out=jk[:, :ncol], in_=xin, func=AF.Square, accum_out=sq_),
              reads=rkeys, writes=[kj, ks])
        fw.op("vector", lambda e: e.tensor_scalar(out=sq_, in0=sq_, scalar1=1.0 / ncol, scalar2=1e-6, op0=ALU.mult, op1=ALU.add),
              reads=[ks], writes=[ks])
        fw.op("scalar", lambda e: e.activation(out=sq_, in_=sq_, func=AF.Sqrt), reads=[ks], writes=[ks])
        fw.op("vector", lambda e: e.reciprocal(out=sq_, in_=sq_), reads=[ks], writes=[ks])
        fw.op("vector", lambda e: e.scalar_tensor_tensor(out=xn_out, in0=xin, scalar=sq_, in1=gsb[:, gidx, :ncol],
                                                         op0=ALU.mult, op1=ALU.mult),
              reads=list(rkeys) + [ks, "gsb"], writes=wkeys)

    def transpose8(src_bf, dstT, rkey, wkey, n=8):
        for half in range((n + 3) // 4):
            p, pk = psum()
            m = min(4, n - half * 4)
            for j in range(m):
                c = half * 4 + j
                fw.op("tensor", lambda e, c=c, j=j, p=p: e.matmul(p[:, j * 128:(j + 1) * 128], lhsT=src_bf[:, c * 128:(c + 1) * 128],
                                                                 rhs=identb, start=True, stop=True),
                      reads=[rkey, "identb"], writes=[pk], inc=(j == m - 1))
            fw.op("vector", lambda e, p=p, half=half, m=m: e.tensor_copy(
                out=dstT[:, half * 4:half * 4 + m, :], in_=p[:, :m * 128].rearrange("p (a b) -> p a b", b=128)),
                reads=[pk], writes=[wkey])

    junk = [ar.alloc("junk", [128, D], F32) for _ in range(2)]; ssq = [ar.alloc("ssq", [128, 1], F32) for _ in range(2)]
    base1 = ar.mark()

    dbg = {}
    V = lambda fn, r, w: fw.op("vector", fn, reads=r, writes=w)
    A_ = lambda fn, r, w: fw.op("scalar", fn, reads=r, writes=w)
    G_ = lambda fn, r, w: fw.op("gpsimd", fn, reads=r, writes=w)

    def mm(out_ap, lhsT, rhs, start, stop, reads, pk, inc):
        fw.op("tensor", lambda e: e.matmul(out_ap, lhsT=lhsT, rhs=rhs, start=start, stop=stop), reads=reads, writes=[pk], inc=inc)

    dcs = ar.alloc("dcs", [128, 8], F32)
    esink = ar.alloc("esink", [128, 16], F32); gqk = ar.alloc("gqk", [128, 256], F32)
    maskb = ar.alloc("maskb", [128, 3, 512], BF16)
    base_persist = ar.mark()
    st32 = ar.alloc("st32", [128, 8192], F32)
    fw.dma("sync", dcs, dcol, writes=["dcs"])
    fw.dma("sync", st32[:, 0:16], sinks, writes=["a"])
    A_(lambda e: e.activation(out=esink, in_=st32[:, 0:16], func=AF.Exp), ["a"], ["esink"])
    fw.dma("sync", st32[:, 1024:1280], gq, writes=["b"])
    fw.dma("sync", st32[:, 2048:2304], gk, writes=["c"])
    V(lambda e: e.tensor_tensor(out=gqk, in0=st32[:, 1024:1280], in1=st32[:, 2048:2304], op=ALU.mult), ["b", "c"], ["gqk"])
    fw.dma("sync", st32[:, 4096:5632].rearrange("p (a c) -> p a c", a=3), masks.rearrange("a p c -> p a c"), writes=["d"])
    V(lambda e: e.tensor_copy(out=maskb, in_=st32[:, 4096:5632].rearrange("p (a c) -> p a c", a=3)), ["d"], ["maskb"])
    fw.barrier()
    ar.reset(base_persist)

    m1 = ar.mark()
    Win = ar.alloc("Win", [128, 8, 4608], BF16)
    load_weights(Win, w_in, 9, 8, "Win")
    QO, KVO, UO, GO = 0, 1024, 1536, 2560
    xt = [ar.alloc("xt", [128, D], F32) for _ in range(2)]
    xnb = [ar.alloc("xnb", [128, D], BF16) for _ in range(2)]
    xnT4 = [ar.alloc("xnT4", [128, 8, 512], BF16) for _ in range(2)]
    uTb4 = [ar.alloc("uTb4", [128, 8, 512], BF16) for _ in range(2)]
    qsq_ = [ar.alloc("qsq", [128, 512], F32) for _ in range(2)]
    qss_ = [ar.alloc("qss", [128, 8], F32) for _ in range(2)]
    hnc = [0]
    qn = [ar.alloc("qn", [128, D], BF16) for _ in range(2)]
    qTb = [ar.alloc("qTb", [128, 8, 128], BF16) for _ in range(2)]
    kf_ = [ar.alloc("kf", [128, 256], F32) for _ in range(2)]
    kn = [ar.alloc("kn", [128, 256], BF16) for _ in range(2)]
    kTb = [ar.alloc("kTb", [128, 2, 128], BF16) for _ in range(2)]
    vab = [ar.alloc("vab", [128, 4, 65], BF16) for _ in range(2)]
    gb = [ar.alloc("gb", [128, 2048], BF16) for _ in range(2)]
    for b in range(2):
        V(lambda e, b=b: e.memset(vab[b][:, :, 64:65], 1.0), [], [f"vab{b}"])

    groups = [[("ctx", xctx[0:128, :], 0, None), ("ctx", xctx[128:256, :], 1, None)]]
    for t in range(0, TP_, 4):
        groups.append([("pre", xpre[(t + q) * 128:(t + q + 1) * 128, :], t + q, None) for q in range(4)])
    for t in range(0, TM_, 4):
        groups.append([("main", xmain[(t + q) * 128:(t + q + 1) * 128, :], TP_ + t + q, t + q) for q in range(4)])

    def headnorm(p, pk, ncol, nh, dst, dkey, gain=None):
        pr = hnc[0] % 2
        hnc[0] += 1
        qsq, qss, kf = qsq_[pr], qss_[pr], kf_[pr]
        kq, ks, kk_ = f"qsq{pr}", f"qss{pr}", f"kf{pr}"
        A_(lambda e: e.activation(out=qsq[:, :ncol], in_=p[:, :ncol], func=AF.Square), [pk], [kq])
        V(lambda e: e.tensor_reduce(out=qss[:, :nh], in_=qsq[:, :ncol].rearrange("p (h d) -> p h d", d=64), axis=AX.X, op=ALU.add), [kq], [ks])
        V(lambda e: e.tensor_scalar(out=qss[:, :nh], in0=qss[:, :nh], scalar1=1.0 / 64, scalar2=1e-6, op0=ALU.mult, op1=ALU.add), [ks], [ks])
        A_(lambda e: e.activation(out=qss[:, :nh], in_=qss[:, :nh], func=AF.Sqrt), [ks], [ks])
        V(lambda e: e.reciprocal(out=qss[:, :nh], in_=qss[:, :nh]), [ks], [ks])
        rb = qss[:, :nh].unsqueeze(2).broadcast_to([128, nh, 64])
        if gain is None:
            V(lambda e: e.tensor_tensor(out=dst.rearrange("p (h d) -> p h d", d=64), in0=p[:, :ncol].rearrange("p (h d) -> p h d", d=64), in1=rb, op=ALU.mult),
              [pk, ks], [dkey])
        else:
            V(lambda e: e.tensor_tensor(out=kf.rearrange("p (h d) -> p h d", d=64), in0=p[:, :ncol].rearrange("p (h d) -> p h d", d=64), in1=rb, op=ALU.mult),
              [pk, ks], [kk_])
            V(lambda e: e.tensor_tensor(out=dst, in0=kf, in1=gain, op=ALU.mult), [kk_, "gqk"], [dkey])

    ti = 0
    for gi, grp_tiles in enumerate(groups):
        gpar = gi % 2
        X4 = xnT4[gpar]
        xkeys = []
        for t4, (kind, src, sidx, midx) in enumerate(grp_tiles):
            b = ti % 2
            ti += 1
            fw.dma("sync", xt[b], src, writes=[f"xt{b}"])
            fw.flush()
            rms_scale(xt[b], 0, xnb[b], [f"xt{b}"], [f"xnb{b}"])
            xk = f"xnT{gpar}_{t4}"
            xkeys.append(xk)
            XT = X4[:, :, t4 * 128:(t4 + 1) * 128]
            transpose8(xnb[b], XT, f"xnb{b}", xk)
            if kind == "main":
                for half in range(2):
                    p, pk = psum()
                    for k in range(8):
                        mm(p, XT[:, k, :], Win[:, k, QO + half * 512:QO + (half + 1) * 512], k == 0, k == 7, [xk, "Win"], pk, k == 7)
                    headnorm(p, pk, 512, 8, qn[b][:, half * 512:(half + 1) * 512], f"qn{b}")
                transpose8(qn[b], qTb[b], f"qn{b}", f"qTb{b}")
                fw.defer_dma("sync", qT_s[midx], qTb[b], reads=[f"qTb{b}"], writes=[("qT_s", midx)])
                for j in range(4):
                    p, pk = psum()
                    for k in range(8):
                        mm(p, XT[:, k, :], Win[:, k, GO + j * 512:GO + (j + 1) * 512], k == 0, k == 7, [xk, "Win"], pk, k == 7)
                    A_(lambda e, p=p, j=j, b=b: e.activation(out=gb[b][:, j * 512:(j + 1) * 512], in_=p, func=AF.Sigmoid), [pk], [f"gb{b}"])
                fw.defer_dma("sync", g_s[midx], gb[b], reads=[f"gb{b}"], writes=[("g_s", midx)])
            if kind in ("ctx", "main"):
                kidx = sidx if kind == "ctx" else 2 + midx
                p, pk = psum()
                for k in range(8):
                    mm(p, XT[:, k, :], Win[:, k, KVO:KVO + 512], k == 0, k == 7, [xk, "Win"], pk, k == 7)
                headnorm(p, pk, 256, 4, kn[b], f"kn{b}", gain=gqk)
                V(lambda e, p=p, b=b: e.tensor_copy(out=vab[b][:, :, 0:64], in_=p[:, 256:512].rearrange("p (h d) -> p h d", d=64)), [pk], [f"vab{b}"])
                transpose8(kn[b], kTb[b], f"kn{b}", f"kTb{b}", n=2)
                fw.defer_dma("sync", kT_s[kidx], kTb[b], reads=[f"kTb{b}"], writes=[("kT_s", kidx)])
                fw.defer_dma("sync", v_s[kidx], vab[b], reads=[f"vab{b}"], writes=[("v_s", kidx)])
        if grp_tiles[0][0] in ("pre", "main"):
            s0 = grp_tiles[0][2]
            for ct in range(8):
                p, pk = psum()
                for k in range(8):
                    mm(p, Win[:, k, UO + ct * 128:UO + (ct + 1) * 128], X4[:, k, :], k == 0, k == 7, xkeys + ["Win"], pk, k == 7)
                if ct % 2 == 0:
                    V(lambda e, p=p, ct=ct, gpar=gpar: e.tensor_copy(out=uTb4[gpar][:, ct, :], in_=p), [pk], [f"uTb4{gpar}"])
                else:
                    A_(lambda e, p=p, ct=ct, gpar=gpar: e.activation(out=uTb4[gpar][:, ct, :], in_=p, func=AF.Copy), [pk], [f"uTb4{gpar}"])
            fw.defer_dma("sync", uT_s[s0:s0 + 4].rearrange("n p c t -> p c n t"), uTb4[gpar].rearrange("p c (n t) -> p c n t", t=128),
                         reads=[f"uTb4{gpar}"], writes=[("uT_s", s0)])
    fw.barrier()
    ar.reset(m1)

    if upto <= 1:
        fw.emit()
        return nc
    lamsb = ar.alloc("lamsb", [128, 3, 32], F32)
    fw.dma("sync", lamsb, lam.rearrange("a p g -> p a g"), writes=["lam"])
    smn = ["dt", "th", "rho", "sn", "cs", "ar", "ai", "fr", "fi", "t1", "t2", "t3", "den", "wr", "wi", "w128r", "w128i", "mk", "x2", "lrdt"]
    sm = {n: ar.alloc(n, [128, 32], F32) for n in smn}
    pwr = [ar.alloc("pwr", [128, 32], F32) for _ in range(9)]; pwi = [ar.alloc("pwi", [128, 32], F32) for _ in range(9)]
    Er = ar.alloc("Er", [128, 32, 128], F32); Ei = ar.alloc("Ei", [128, 32, 128], F32)
    Kpad = ar.alloc("Kpad", [128, 8, 8, 128], BF16)
    cR = ar.alloc("cR", [128, 2, 32], F32); SL = ar.alloc("SL", [128, 2, 32], F32)
    base_ssm = ar.mark()
    lr, li, ld = lamsb[:, 0, :], lamsb[:, 1, :], lamsb[:, 2, :]
    K = ["ssm0"]
    A_(lambda e: e.activation(out=sm["dt"], in_=ld, func=AF.Exp), ["lam"], K)
    V(lambda e: e.tensor_tensor(out=sm["th"], in0=li, in1=sm["dt"], op=ALU.mult), K, K)
    V(lambda e: e.tensor_tensor(out=sm["lrdt"], in0=lr, in1=sm["dt"], op=ALU.mult), K, K)
    A_(lambda e: e.activation(out=sm["rho"], in_=sm["lrdt"], func=AF.Exp), K, K)
    for _ in range(5):
        V(lambda e: e.tensor_single_scalar(out=sm["mk"], in_=sm["th"], scalar=math.pi, op=ALU.is_gt), K, K)
        V(lambda e: e.scalar_tensor_tensor(out=sm["th"], in0=sm["mk"], scalar=-2.0 * math.pi, in1=sm["th"], op0=ALU.mult, op1=ALU.add), K, K)
    V(lambda e: e.tensor_scalar(out=sm["t3"], in0=sm["th"], scalar1=0.125, scalar2=None, op0=ALU.mult), K, K)
    V(lambda e: e.tensor_tensor(out=sm["x2"], in0=sm["t3"], in1=sm["t3"], op=ALU.mult), K, K)

    def horner(o, coefs):
        V(lambda e: e.memset(o, coefs[0]), K, K)
        for c in coefs[1:]:
            V(lambda e: e.tensor_tensor(out=o, in0=o, in1=sm["x2"], op=ALU.mult), K, K)
            V(lambda e, c=c: e.tensor_scalar(out=o, in0=o, scalar1=float(c), scalar2=None, op0=ALU.add), K, K)

    def cdouble(sn_, cs_):
        V(lambda e: e.tensor_tensor(out=sm["t1"], in0=sn_, in1=cs_, op=ALU.mult), K, K)
        V(lambda e: e.tensor_tensor(out=sm["t2"], in0=cs_, in1=cs_, op=ALU.mult), K, K)
        V(lambda e: e.tensor_tensor(out=sm["t3"], in0=sn_, in1=sn_, op=ALU.mult), K, K)
        V(lambda e: e.tensor_scalar(out=sn_, in0=sm["t1"], scalar1=2.0, scalar2=None, op0=ALU.mult), K, K)
        V(lambda e: e.tensor_tensor(out=cs_, in0=sm["t2"], in1=sm["t3"], op=ALU.subtract), K, K)

    horner(sm["sn"], [-1 / 39916800.0, 1 / 362880.0, -1 / 5040.0, 1 / 120.0, -1 / 6.0, 1.0])
    V(lambda e: e.tensor_tensor(out=sm["sn"], in0=sm["sn"], in1=sm["t3"], op=ALU.mult), K, K)
    horner(sm["cs"], [-1 / 3628800.0, 1 / 40320.0, -1 / 720.0, 1 / 24.0, -0.5, 1.0])
    for _ in range(3):
        cdouble(sm["sn"], sm["cs"])
    V(lambda e: e.tensor_tensor(out=sm["ar"], in0=sm["rho"], in1=sm["cs"], op=ALU.mult), K, K)
    V(lambda e: e.tensor_tensor(out=sm["ai"], in0=sm["rho"], in1=sm["sn"], op=ALU.mult), K, K)
    V(lambda e: e.tensor_scalar(out=sm["t1"], in0=sm["ar"], scalar1=-1.0, scalar2=None, op0=ALU.add), K, K)
    V(lambda e: e.tensor_tensor(out=sm["den"], in0=lr, in1=lr, op=ALU.mult), K, K)
    V(lambda e: e.tensor_tensor(out=sm["t2"], in0=li, in1=li, op=ALU.mult), K, K)
    V(lambda e: e.tensor_tensor(out=sm["den"], in0=sm["den"], in1=sm["t2"], op=ALU.add), K, K)
    V(lambda e: e.reciprocal(out=sm["den"], in_=sm["den"]), K, K)
    V(lambda e: e.tensor_tensor(out=sm["t2"], in0=sm["t1"], in1=lr, op=ALU.mult), K, K)
    V(lambda e: e.tensor_tensor(out=sm["t3"], in0=sm["ai"], in1=li, op=ALU.mult), K, K)
    V(lambda e: e.tensor_tensor(out=sm["t2"], in0=sm["t2"], in1=sm["t3"], op=ALU.add), K, K)
    V(lambda e: e.tensor_tensor(out=sm["fr"], in0=sm["t2"], in1=sm["den"], op=ALU.mult), K, K)
    V(lambda e: e.tensor_tensor(out=sm["t2"], in0=sm["ai"], in1=lr, op=ALU.mult), K, K)
    V(lambda e: e.tensor_tensor(out=sm["t3"], in0=sm["t1"], in1=li, op=ALU.mult), K, K)
    V(lambda e: e.tensor_tensor(out=sm["t2"], in0=sm["t2"], in1=sm["t3"], op=ALU.subtract), K, K)
    V(lambda e: e.tensor_tensor(out=sm["fi"], in0=sm["t2"], in1=sm["den"], op=ALU.mult), K, K)
    V(lambda e: e.memset(pwr[0], 1.0), K, K)
    V(lambda e: e.memset(pwi[0], 0.0), K, K)
    for k in range(1, 9):
        V(lambda e, k=k: e.tensor_tensor(out=sm["t1"], in0=pwr[k - 1], in1=sm["ar"], op=ALU.mult), K, K)
        V(lambda e, k=k: e.tensor_tensor(out=sm["t2"], in0=pwi[k - 1], in1=sm["ai"], op=ALU.mult), K, K)
        V(lambda e, k=k: e.tensor_tensor(out=pwr[k], in0=sm["t1"], in1=sm["t2"], op=ALU.subtract), K, K)
        V(lambda e, k=k: e.tensor_tensor(out=sm["t1"], in0=pwr[k - 1], in1=sm["ai"], op=ALU.mult), K, K)
        V(lambda e, k=k: e.tensor_tensor(out=sm["t2"], in0=pwi[k - 1], in1=sm["ar"], op=ALU.mult), K, K)
        V(lambda e, k=k: e.tensor_tensor(out=pwi[k], in0=sm["t1"], in1=sm["t2"], op=ALU.add), K, K)
    V(lambda e: e.tensor_scalar(out=sm["t1"], in0=sm["lrdt"], scalar1=8.0, scalar2=None, op0=ALU.mult), K, K)
    A_(lambda e: e.activation(out=sm["rho"], in_=sm["t1"], func=AF.Exp), K, K)
    for _ in range(3):
        cdouble(sm["sn"], sm["cs"])
    V(lambda e: e.memset(Er[:, :, 0:1], 1.0), K, K)
    V(lambda e: e.memset(Ei[:, :, 0:1], 0.0), K, K)
    V(lambda e: e.tensor_copy(out=sm["wr"], in_=sm["cs"]), K, K)
    V(lambda e: e.tensor_copy(out=sm["wi"], in_=sm["sn"]), K, K)
    m0 = ar.mark()
    tA = ar.alloc("tA", [128, 32, 64], F32); tB = ar.alloc("tB", [128, 32, 64], F32)
    for k in range(7):
        n = 1 << k
        wrb = sm["wr"].unsqueeze(2).broadcast_to([128, 32, n]); wib = sm["wi"].unsqueeze(2).broadcast_to([128, 32, n])
        V(lambda e, n=n, wrb=wrb: e.tensor_tensor(out=tA[:, :, :n], in0=Er[:, :, :n], in1=wrb, op=ALU.mult), K, K)
        V(lambda e, n=n, wib=wib: e.tensor_tensor(out=tB[:, :, :n], in0=Ei[:, :, :n], in1=wib, op=ALU.mult), K, K)
        V(lambda e, n=n: e.tensor_tensor(out=Er[:, :, n:2 * n], in0=tA[:, :, :n], in1=tB[:, :, :n], op=ALU.subtract), K, K)
        V(lambda e, n=n, wib=wib: e.tensor_tensor(out=tA[:, :, :n], in0=Er[:, :, :n], in1=wib, op=ALU.mult), K, K)
        V(lambda e, n=n, wrb=wrb: e.tensor_tensor(out=tB[:, :, :n], in0=Ei[:, :, :n], in1=wrb, op=ALU.mult), K, K)
        V(lambda e, n=n: e.tensor_tensor(out=Ei[:, :, n:2 * n], in0=tA[:, :, :n], in1=tB[:, :, :n], op=ALU.add), K, K)
        V(lambda e: e.tensor_tensor(out=sm["t1"], in0=sm["wr"], in1=sm["wr"], op=ALU.mult), K, K)
        V(lambda e: e.tensor_tensor(out=sm["t2"], in0=sm["wi"], in1=sm["wi"], op=ALU.mult), K, K)
        V(lambda e: e.tensor_tensor(out=sm["t3"], in0=sm["wr"], in1=sm["wi"], op=ALU.mult), K, K)
        V(lambda e: e.tensor_tensor(out=sm["wr"], in0=sm["t1"], in1=sm["t2"], op=ALU.subtract), K, K)
        V(lambda e: e.tensor_scalar(out=sm["wi"], in0=sm["t3"], scalar1=2.0, scalar2=None, op0=ALU.mult), K, K)
    V(lambda e: e.tensor_copy(out=sm["w128r"], in_=sm["wr"]), K, K)
    V(lambda e: e.tensor_copy(out=sm["w128i"], in_=sm["wi"]), K, K)
    V(lambda e: e.memset(cR, 0.0), K, K)
    V(lambda e: e.memset(SL, 0.0), K, K)
    fw.barrier()
    ar.reset(m0)
    BTc = ar.alloc("BTc", [128, 32, 2, 16], F32); Cc = ar.alloc("Cc", [128, 2, 32, 16], F32)
    Cfc = ar.alloc("Cfc", [128, 32, 2, 16], F32); Xc = ar.alloc("Xc", [128, 32, 2, 16], F32)
    c1 = ar.alloc("c1", [128, 32, 16], F32); c2 = ar.alloc("c2", [128, 32, 16], F32)
    Cfp = ar.alloc("Cfp", [128, 32, 2, 128], BF16)
    padb = ar.alloc("padb", [128, 32, 2, 128], BF16); DBsb = ar.alloc("DBsb", [128, 32, 2, 128], BF16)
    fw.dma("sync", BTc, btc, writes=["BTc"])
    fw.dma("sync", Cc, cc.rearrange("a p g c -> p a g c"), writes=["Cc"])
    G_(lambda e: e.memset(padb, 0.0), [], ["padb"])
    G_(lambda e: e.memset(Cfp, 0.0), [], ["Cfp"])

    def cmul_compact(dst, src_r, src_i, sr, si, rk, wk, neg_im=False):
        srb = sr.unsqueeze(2).broadcast_to([128, 32, 16]); sib = si.unsqueeze(2).broadcast_to([128, 32, 16])
        V(lambda e: e.tensor_tensor(out=c1, in0=src_r, in1=srb, op=ALU.mult), rk, ["c1"])
        V(lambda e: e.tensor_tensor(out=c2, in0=src_i, in1=sib, op=ALU.mult), rk, ["c2"])
        V(lambda e: e.tensor_tensor(out=dst[:, :, 0, :], in0=c1, in1=c2, op=ALU.subtract), ["c1", "c2"], wk)
        V(lambda e: e.tensor_tensor(out=c1, in0=src_r, in1=sib, op=ALU.mult), rk + wk, ["c1"])
        V(lambda e: e.tensor_tensor(out=c2, in0=src_i, in1=srb, op=ALU.mult), rk + wk, ["c2"])
        if neg_im:
            V(lambda e: e.scalar_tensor_tensor(out=dst[:, :, 1, :], in0=c1, scalar=-1.0, in1=c2, op0=ALU.mult, op1=ALU.subtract), ["c1", "c2"], wk)
        else:
            V(lambda e: e.tensor_tensor(out=dst[:, :, 1, :], in0=c1, in1=c2, op=ALU.add), ["c1", "c2"], wk)

    def scatter(dst_pad, src_c, rk, wk):
        for g2 in range(2):
            for q in range(4):
                blk = 2 * q + g2
                G_(lambda e, g2=g2, q=q, blk=blk: e.tensor_copy(out=dst_pad[g2 * 64:(g2 + 1) * 64, q::4, :, blk * 16:(blk + 1) * 16],
                                                                 in_=src_c[g2 * 64:(g2 + 1) * 64, q::4, :, :]), rk, wk)

    cmul_compact(Cfc, Cc[:, 0], Cc[:, 1], sm["fr"], sm["fi"], ["Cc"], ["Cfc"])
    cmul_compact(Xc, Cc[:, 0], Cc[:, 1], sm["fr"], sm["fi"], ["Cc"], ["Xc"], neg_im=True)
    scatter(Cfp, Xc, ["Xc"], ["Cfp"])
    for k in range(8):
        j = 7 - k
        cmul_compact(Xc, BTc[:, :, 0, :], BTc[:, :, 1, :], pwr[k], pwi[k], ["BTc"], ["Xc"])
        scatter(padb, Xc, ["Xc"], ["padb"])
        for r in range(8):
            p, pk = psum()
            n_ = 0
            for gl in range(4):
                for ri in range(2):
                    mm(p[:, 0:128], padb[:, 4 * r + gl, ri, :], Cfp[:, 4 * r + gl, ri, :], n_ == 0, n_ == 7, ["padb", "Cfp"], pk, n_ == 7)
                    n_ += 1
            if k == 0:
                V(lambda e, p=p, r=r: e.scalar_tensor_tensor(out=Kpad[:, r, 0, :], in0=identf, scalar=dcs[:, r:r + 1], in1=p[:, 0:128], op0=ALU.mult, op1=ALU.add),
                  [pk, "identf", "dcs"], ["Kpad"])
            else:
                V(lambda e, p=p, r=r, k=k: e.tensor_copy(out=Kpad[:, r, k, :], in_=p[:, 0:128]), [pk], ["Kpad"])
        for g4 in range(16):
            p, pk = psum()
            for q in range(4):
                gi = g4 * 4 + q
                mm(p[:, q * 128:(q + 1) * 128], padb[:, gi // 2, gi % 2, :], identb, True, True, ["padb", "identb"], pk, q == 3)
            V(lambda e, p=p, g4=g4: e.tensor_copy(out=DBsb.rearrange("p g r s -> p (g r) s")[:, g4 * 4:(g4 + 1) * 4, :], in_=p.rearrange("p (a c) -> p a c", c=128)),
              [pk], ["DBsb"])
        fw.dma("sync", DB_s[:, :, j].rearrange("r p g a s -> p r g a s"), DBsb.rearrange("p (r g) a s -> p r g a s", r=8), reads=["DBsb"], writes=[("DB_s", j)])
    for j in range(8):
        cmul_compact(Xc, Cfc[:, :, 0, :], Cfc[:, :, 1, :], pwr[j + 1], pwi[j + 1], ["Cfc"], ["Xc"])
        scatter(padb, Xc, ["Xc"], ["padb"])
        fw.dma("sync", EC_s[:, :, j].rearrange("r p g a s -> p r g a s"), padb.rearrange("p (r g) a s -> p r g a s", r=8), reads=["padb"], writes=[("EC_s", j)])
    fw.barrier()
    ar.reset(base_ssm)

    if upto <= 2:
        fw.emit()
        return nc
    NPS, NMS = NP // 1024, NM // 1024
    DBr = ar.alloc("DBr", [128, 8, 4, 2, 128], BF16); ECr = ar.alloc("ECr", [128, 8, 4, 2, 128], BF16)
    uTr = [ar.alloc("uTr", [128, 1024], BF16) for _ in range(2)]
    uTj = [ar.alloc("uTj", [128, 8, 128], BF16) for _ in range(2)]
    zTr = [ar.alloc("zTr", [128, 1024], BF16) for _ in range(2)]
    RB = []
    for b in range(2):
        d = {n: ar.alloc(n, [128, 4, 128], F32) for n in ["t1", "t2", "t3", "t4", "Xr", "Xi", "Rr", "Ri"]}
        d["Sr"] = ar.alloc("Sr", [128, 4, 130], BF16); d["Si"] = ar.alloc("Si", [128, 4, 130], BF16)
        for n in ["c1", "c2", "c3", "c4"]:
            d[n] = ar.alloc(n, [128, 4], F32)
        RB.append(d)
    ysb = [ar.alloc("ysb", [128, 1024], F32) for _ in range(2)]
    g1b = [ar.alloc("g1b", [128, 1024], F32) for _ in range(2)]
    g2b = [ar.alloc("g2b", [128, 1024], F32) for _ in range(2)]
    rcount = 0
    hcount = 0
    for r in range(8):
        gsl = slice(4 * r, 4 * r + 4)
        fw.dma("sync", DBr, DB_s[r], writes=["DBr"])
        fw.dma("sync", ECr, EC_s[r], writes=["ECr"])
        for st in range(NPS + NMS):
            is_main = st >= NPS
            ub = st % 2
            fw.dma("sync", uTr[ub].rearrange("p (n t) -> p n t", t=128), uT_s[8 * st:8 * st + 8, :, r, :].rearrange("n p t -> p n t"),
                   writes=[f"uTr{ub}"])
            fw.flush()
            A_(lambda e, ub=ub: e.activation(out=uTj[ub], in_=uTr[ub].rearrange("p (c j) -> p j c", j=8), func=AF.Copy), [f"uTr{ub}"], [f"uTj{ub}"])
            b = rcount % 2
            rcount += 1
            B = RB[b]
            kb = lambda n, b=b: f"{n}{b}"
            pXr, pkr = psum()
            pXi, pki = psum()
            for ri, (pX, pk) in enumerate(((pXr, pkr), (pXi, pki))):
                for gl in range(4):
                    for j in range(8):
                        mm(pX[:, gl * 128:(gl + 1) * 128], DBr[:, j, gl, ri, :], uTj[ub][:, j, :], j == 0, j == 7,
                           [f"uTj{ub}", "DBr"], pk, (gl == 3 and j == 7))
            pXr3 = pXr.rearrange("p (a c) -> p a c", c=128); pXi3 = pXi.rearrange("p (a c) -> p a c", c=128)
            Erg, Eig = Er[:, gsl, :], Ei[:, gsl, :]
            V(lambda e, B=B, a=pXr3, t=Erg: e.tensor_tensor(out=B["t1"], in0=a, in1=t, op=ALU.mult), [pkr], [kb("t1")])
            V(lambda e, B=B, a=pXi3, t=Eig: e.tensor_tensor(out=B["t2"], in0=a, in1=t, op=ALU.mult), [pki], [kb("t2")])
            V(lambda e, B=B, a=pXi3, t=Erg: e.tensor_tensor(out=B["t3"], in0=a, in1=t, op=ALU.mult), [pki], [kb("t3")])
            V(lambda e, B=B, a=pXr3, t=Eig: e.tensor_tensor(out=B["t4"], in0=a, in1=t, op=ALU.mult), [pkr], [kb("t4")])
            G_(lambda e, B=B: e.tensor_tensor(out=B["Xr"], in0=B["t1"], in1=B["t2"], op=ALU.add), [kb("t1"), kb("t2")], [kb("Xr")])
            G_(lambda e, B=B: e.tensor_tensor(out=B["Xi"], in0=B["t3"], in1=B["t4"], op=ALU.subtract), [kb("t3"), kb("t4")], [kb("Xi")])
            for gl in range(4):
                gp = 4 * r + gl
                for nm, xs, ci in (("Rr", "Xr", 0), ("Ri", "Xi", 1)):
                    V(lambda e, B=B, gl=gl, gp=gp, nm=nm, xs=xs, ci=ci: e.tensor_tensor_scan(
                        out=B[nm][:, gl, :], data0=sm["rho"][:, gp:gp + 1].broadcast_to([128, 128]), data1=B[xs][:, gl, :],
                        initial=cR[:, ci, gp:gp + 1], op0=ALU.mult, op1=ALU.add), [kb(xs), ("cR", r)], [kb(nm)])
            if is_main:
                G_(lambda e, B=B, gsl=gsl: e.tensor_copy(out=B["Sr"][:, :, 0], in_=SL[:, 0, gsl]), [("SL", r)], [kb("Sr")])
                G_(lambda e, B=B, gsl=gsl: e.tensor_copy(out=B["Si"][:, :, 0], in_=SL[:, 1, gsl]), [("SL", r)], [kb("Si")])
            wr4, wi4 = sm["w128r"][:, gsl], sm["w128i"][:, gsl]
            er7, ei7 = Er[:, gsl, 127], Ei[:, gsl, 127]
            Rr7, Ri7 = B["Rr"][:, :, 127], B["Ri"][:, :, 127]
            for (xr_, xi_, dst, negim, key) in ((wr4, wi4, cR, False, "cR"), (er7, ei7, SL, True, "SL")):
                G_(lambda e, B=B, a=Rr7, w=xr_: e.tensor_tensor(out=B["c1"], in0=a, in1=w, op=ALU.mult), [kb("Rr")], [kb("c1")])
                G_(lambda e, B=B, a=Ri7, w=xi_: e.tensor_tensor(out=B["c2"], in0=a, in1=w, op=ALU.mult), [kb("Ri")], [kb("c2")])
                G_(lambda e, B=B, a=Ri7, w=xr_: e.tensor_tensor(out=B["c3"], in0=a, in1=w, op=ALU.mult), [kb("Ri")], [kb("c3")])
                G_(lambda e, B=B, a=Rr7, w=xi_: e.tensor_tensor(out=B["c4"], in0=a, in1=w, op=ALU.mult), [kb("Rr")], [kb("c4")])
                G_(lambda e, B=B, dst=dst, gsl=gsl: e.tensor_tensor(out=dst[:, 0, gsl], in0=B["c1"], in1=B["c2"], op=ALU.subtract),
                   [kb("c1"), kb("c2")], [(key, r)])
                if negim:
                    V(lambda e, B=B, dst=dst, gsl=gsl: e.scalar_tensor_tensor(out=dst[:, 1, gsl], in0=B["c3"], scalar=-1.0, in1=B["c4"], op0=ALU.mult, op1=ALU.subtract),
                       [kb("c3"), kb("c4")], [(key, r)])
                else:
                    G_(lambda e, B=B, dst=dst, gsl=gsl: e.tensor_tensor(out=dst[:, 1, gsl], in0=B["c3"], in1=B["c4"], op=ALU.add),
                       [kb("c3"), kb("c4")], [(key, r)])
            if not is_main:
                continue
            G_(lambda e, B=B, t=Erg: e.tensor_tensor(out=B["t1"], in0=B["Rr"], in1=t, op=ALU.mult), [kb("Rr")], [kb("t1")])
            G_(lambda e, B=B, t=Eig: e.tensor_tensor(out=B["t2"], in0=B["Ri"], in1=t, op=ALU.mult), [kb("Ri")], [kb("t2")])
            V(lambda e, B=B, t=Erg: e.tensor_tensor(out=B["t3"], in0=B["Ri"], in1=t, op=ALU.mult), [kb("Ri")], [kb("t3")])
            V(lambda e, B=B, t=Eig: e.tensor_tensor(out=B["t4"], in0=B["Rr"], in1=t, op=ALU.mult), [kb("Rr")], [kb("t4")])
            V(lambda e, B=B: e.tensor_tensor(out=B["Sr"][:, :, 1:129], in0=B["t1"], in1=B["t2"], op=ALU.subtract), [kb("t1"), kb("t2")], [kb("Sr")])
            V(lambda e, B=B: e.scalar_tensor_tensor(out=B["Si"][:, :, 1:129], in0=B["t3"], scalar=-1.0, in1=B["t4"], op0=ALU.mult, op1=ALU.subtract),
              [kb("t3"), kb("t4")], [kb("Si")])
            zb = (st - NPS) % 2
            hb = hcount % 2
            hcount += 1
            for h2 in range(2):
                py, pky = psum()
                for j4 in range(4):
                    j = 4 * h2 + j4
                    o = py[:, j4 * 128:(j4 + 1) * 128]
                    nmm = (j + 1) + 8
                    n_ = 0
                    for k in range(j + 1):
                        mm(o, Kpad[:, r, k, :], uTj[ub][:, j - k, :], n_ == 0, n_ == nmm - 1, [f"uTj{ub}", "Kpad"], pky, False)
                        n_ += 1
                    for gl in range(4):
                        for ri, Sn in enumerate(("Sr", "Si")):
                            mm(o, ECr[:, j, gl, ri, :], B[Sn][:, gl, 0:128], n_ == 0, n_ == nmm - 1, [kb(Sn), "ECr"], pky,
                               (j4 == 3 and n_ == nmm - 1))
                            n_ += 1
                A_(lambda e, py=py, hb=hb, h2=h2: e.activation(out=ysb[hb].rearrange("p (c j) -> p c j", j=8)[:, :, 4 * h2:4 * h2 + 4],
                                                               in_=py.rearrange("p (j c) -> p c j", c=128), func=AF.Identity), [pky], [f"ysb{hb}"])
            G_(lambda e, hb=hb: e.tensor_tensor(out=g1b[hb], in0=ysb[hb], in1=ysb[hb], op=ALU.mult), [f"ysb{hb}"], [f"g1b{hb}"])
            G_(lambda e, hb=hb: e.tensor_scalar(out=g1b[hb], in0=g1b[hb], scalar1=0.044715, scalar2=1.0, op0=ALU.mult, op1=ALU.add), [f"g1b{hb}"], [f"g1b{hb}"])
            G_(lambda e, hb=hb: e.tensor_tensor(out=g1b[hb], in0=g1b[hb], in1=ysb[hb], op=ALU.mult), [f"g1b{hb}", f"ysb{hb}"], [f"g1b{hb}"])
            A_(lambda e, hb=hb: e.activation(out=g2b[hb], in_=g1b[hb], func=AF.Sigmoid, scale=1.5957691216057308), [f"g1b{hb}"], [f"g2b{hb}"])
            V(lambda e, hb=hb, zb=zb: e.tensor_tensor(out=zTr[zb], in0=ysb[hb], in1=g2b[hb], op=ALU.mult), [f"ysb{hb}", f"g2b{hb}"], [f"zTr{zb}"])
            m8 = 8 * (st - NPS)
            fw.defer_dma("sync", zT_s[m8:m8 + 8, :, r, :].rearrange("n p t -> p n t"), zTr[zb].rearrange("p (n t) -> p n t", t=128),
                   reads=[f"zTr{zb}"], writes=[("zT_s", st, r)])
    fw.barrier()
    ar.reset(base_persist)

    if upto <= 3:
        fw.emit()
        return nc
    Wg = ar.alloc("Wg", [128, 8, 2048], BF16); Wo = ar.alloc("Wo", [128, 8, 1024], BF16)
    load_weights(Wg, w_glu, 4, 8, "Wg")
    load_weights(Wo, w_out, 2, 8, "Wo")
    kme = ar.alloc("kme", [128, 2, 128], BF16); vme = ar.alloc("vme", [128, 4, 65], BF16)
    fw.dma("sync", kme, kT_s[0], writes=["kme"]); fw.dma("sync", vme, v_s[0], writes=["vme"])
    qTl = [ar.alloc("qTl", [128, 8, 128], BF16) for _ in range(2)]
    kTl = [ar.alloc("kTl", [128, 2, 128], BF16) for _ in range(3)]
    vl = [ar.alloc("vl", [128, 4, 65], BF16) for _ in range(3)]
    gl_ = [ar.alloc("gl", [128, 2048], BF16) for _ in range(2)]
    zTl = [ar.alloc("zTl", [128, 8, 128], BF16) for _ in range(2)]
    xr = [ar.alloc("xr", [128, D], F32) for _ in range(2)]
    Pc = [ar.alloc("Pc", [128, 512], BF16) for _ in range(2)]
    Pp = [ar.alloc("Pp", [128, 512], BF16) for _ in range(2)]
    Pm = [ar.alloc("Pm", [128, 512], BF16) for _ in range(2)]
    den_ = [ar.alloc("den", [128, 4], F32) for _ in range(2)]
    for b_ in range(2):
        V(lambda e, b_=b_: e.memset(Pm[b_], 0.0), [], [f"Pm{b_}"])
    attn_ = [ar.alloc("attn", [128, D], F32) for _ in range(2)]; An_ = [ar.alloc("An", [128, D], F32) for _ in range(2)]
    sig_ = [ar.alloc("sig", [128, 512], F32) for _ in range(2)]; ssm_ = [ar.alloc("ssm", [128, D], F32) for _ in range(2)]
    Bn_ = [ar.alloc("Bn", [128, D], F32) for _ in range(2)]
    mg_ = [ar.alloc("mg", [128, D], BF16) for _ in range(2)]; mgT_ = [ar.alloc("mgT", [128, 8, 128], BF16) for _ in range(2)]
    h1 = [ar.alloc("h1", [128, D], F32) for _ in range(2)]
    fw.dma("sync", kTl[1], kT_s[1], writes=["kTl1"]); fw.dma("sync", vl[1], v_s[1], writes=["vl1"])
    def s3_loads(i):
        b = i % 2
        jc = 2 + i
        sc = jc % 3
        fw.dma("sync", kTl[sc], kT_s[jc], reads=[("kT_s", jc)], writes=[f"kTl{sc}"])
        fw.dma("sync", vl[sc], v_s[jc], reads=[("v_s", jc)], writes=[f"vl{sc}"])
        fw.dma("sync", qTl[b], qT_s[i], writes=[f"qTl{b}"])
        fw.dma("sync", gl_[b], g_s[i], writes=[f"gl{b}"])
        fw.dma("sync", zTl[b], zT_s[i], writes=[f"zTl{b}"])
        fw.dma("sync", xr[b], xmain[i * 128:(i + 1) * 128, :], writes=[f"xr{b}"])
        fw.flush()

    def s3_A(n):
        i, grp = n // 4, n % 4
        b, pb = i % 2, n % 2
        sc, sp = (2 + i) % 3, (1 + i) % 3
        bs, kc = (grp % 2) * 64, grp // 2
        qsel = qTl[b][bs:bs + 64, kc * 4:(kc + 1) * 4, :]
        pS, pkS = psum()
        mm(pS, kTl[sc][bs:bs + 64, kc, :], qsel, True, True, [f"kTl{sc}", f"qTl{b}"], pkS, True)
        A_(lambda e: e.activation(out=Pc[pb], in_=pS, func=AF.Exp, scale=0.125), [pkS], [f"Pc{pb}"])
        G_(lambda e: e.tensor_tensor(out=Pc[pb], in0=Pc[pb], in1=maskb[:, 0, :], op=ALU.mult), [f"Pc{pb}", "maskb"], [f"Pc{pb}"])
        pS2, pkS2 = psum()
        mm(pS2, kTl[sp][bs:bs + 64, kc, :], qsel, True, True, [f"kTl{sp}", f"qTl{b}"], pkS2, True)
        A_(lambda e: e.activation(out=Pp[pb], in_=pS2, func=AF.Exp, scale=0.125), [pkS2], [f"Pp{pb}"])
        mi = 2 if i == 0 else 1
        G_(lambda e: e.tensor_tensor(out=Pp[pb], in0=Pp[pb], in1=maskb[:, mi, :], op=ALU.mult), [f"Pp{pb}", "maskb"], [f"Pp{pb}"])
        pS3, pkS3 = psum()
        mm(pS3[0:16, :], kme[bs:bs + 64, kc, 0:16], qsel, True, True, ["kme", f"qTl{b}"], pkS3, True)
        A_(lambda e: e.activation(out=Pm[pb][0:16, :], in_=pS3[0:16, :], func=AF.Exp, scale=0.125), [pkS3], [f"Pm{pb}"])

    def s3_B(n):
        i, grp = n // 4, n % 4
        b, pb = i % 2, n % 2
        sc, sp = (2 + i) % 3, (1 + i) % 3
        attn, kA = attn_[b], f"attn{b}"
        pO, pkO = psum()
        for r in range(4):
            o = pO[:, r * 65:(r + 1) * 65]
            mm(o, Pm[pb][:, r * 128:(r + 1) * 128], vme[:, grp, :], True, False, [f"Pm{pb}", "vme"], pkO, False)
            mm(o, Pp[pb][:, r * 128:(r + 1) * 128], vl[sp][:, grp, :], False, False, [f"Pp{pb}", f"vl{sp}"], pkO, False)
            mm(o, Pc[pb][:, r * 128:(r + 1) * 128], vl[sc][:, grp, :], False, True, [f"Pc{pb}", f"vl{sc}"], pkO, r == 3)
        pO3 = pO[:, 0:260].rearrange("p (r c) -> p r c", c=65)
        den = den_[pb]
        kd = f"den{pb}"
        V(lambda e: e.tensor_tensor(out=den, in0=pO3[:, :, 64], in1=esink[:, grp * 4:(grp + 1) * 4], op=ALU.add), [pkO, "esink"], [kd])
        V(lambda e: e.reciprocal(out=den, in_=den), [kd], [kd])
        V(lambda e: e.tensor_tensor(out=attn[:, grp * 256:(grp + 1) * 256].rearrange("p (r d) -> p r d", d=64), in0=pO3[:, :, 0:64],
                                    in1=den.unsqueeze(2).broadcast_to([128, 4, 64]), op=ALU.mult), [pkO, kd], [kA])

    def s3_tail(i):
        b = i % 2
        attn, An, ssm, Bn, mg, mgT = attn_[b], An_[b], ssm_[b], Bn_[b], mg_[b], mgT_[b]
        kA, kAn, kss, kBn, kmg, kmT = f"attn{b}", f"An{b}", f"ssm{b}", f"Bn{b}", f"mg{b}", f"mgT{b}"
        rms_scale(attn, 1, An, [kA], [kAn])
        G_(lambda e: e.tensor_tensor(out=An, in0=An, in1=gl_[b][:, 0:1024], op=ALU.mult), [kAn, f"gl{b}"], [kAn])
        for half in range(2):
            pa, pka = psum()
            for k in range(8):
                mm(pa, zTl[b][:, k, :], Wg[:, k, half * 512:(half + 1) * 512], k == 0, k == 7, [f"zTl{b}", "Wg"], pka, k == 7)
            pz, pkz = psum()
            for k in range(8):
                mm(pz, zTl[b][:, k, :], Wg[:, k, 1024 + half * 512:1024 + (half + 1) * 512], k == 0, k == 7, [f"zTl{b}", "Wg"], pkz, k == 7)
            sig = sig_[half]
            A_(lambda e, pz=pz, sig=sig: e.activation(out=sig, in_=pz, func=AF.Sigmoid), [pkz], [f"sig{half}"])
            V(lambda e, pa=pa, half=half, sig=sig: e.tensor_tensor(out=ssm[:, half * 512:(half + 1) * 512], in0=pa, in1=sig, op=ALU.mult), [pka, f"sig{half}"], [kss])
        rms_scale(ssm, 2, Bn, [kss], [kBn])
        G_(lambda e: e.tensor_tensor(out=Bn, in0=Bn, in1=gl_[b][:, 1024:2048], op=ALU.mult), [kBn, f"gl{b}"], [kBn])
        V(lambda e: e.tensor_tensor(out=mg, in0=An, in1=Bn, op=ALU.add), [kAn, kBn], [kmg])
        transpose8(mg, mgT, kmg, kmT)
        for half in range(2):
            p, pk = psum()
            for k in range(8):
                mm(p, mgT[:, k, :], Wo[:, k, half * 512:(half + 1) * 512], k == 0, k == 7, [kmT, "Wo"], pk, k == 7)
            V(lambda e, p=p, half=half: e.tensor_tensor(out=h1[b][:, half * 512:(half + 1) * 512], in0=p, in1=xr[b][:, half * 512:(half + 1) * 512], op=ALU.add),
              [pk, f"xr{b}"], [f"h1{b}"])
        fw.defer_dma("sync", h1_s[i * 128:(i + 1) * 128, :], h1[b], reads=[f"h1{b}"], writes=[("h1_s", i)])

    NG = 4 * TM_
    s3_loads(0)
    s3_A(0)
    for n in range(NG):
        if n + 1 < NG:
            if (n + 1) % 4 == 0:
                s3_loads((n + 1) // 4)
            s3_A(n + 1)
        s3_B(n)
        if n % 4 == 3:
            s3_tail(n // 4)
    fw.barrier()
    ar.reset(base_persist)

    if upto <= 4:
        fw.emit()
        return nc
    W1 = ar.alloc("W1", [128, 8, 5632], BF16); W2 = ar.alloc("W2", [128, 22, 1024], BF16)
    load_weights(W1, w_f1, 11, 8, "W1")
    load_weights(W2, w_f2, 2, 22, "W2")
    GT = 4
    hl = [ar.alloc("hl", [128, D], F32) for _ in range(2)]
    hn = [ar.alloc("hn", [128, D], BF16) for _ in range(2)]
    hnT = ar.alloc("hnT", [128, 8, GT * 128], BF16)
    sg = [ar.alloc("sg", [128, 512], F32) for _ in range(2)]
    actT = ar.alloc("actT", [128, 22, GT * 128], BF16)
    hres = hl
    ob = junk
    tcount = 0
    for g in range(TM_ // GT):
        for t4 in range(GT):
            i = g * GT + t4
            b = tcount % 2
            tcount += 1
            fw.dma("sync", hl[b], h1_s[i * 128:(i + 1) * 128, :], reads=[("h1_s", i)], writes=[f"hl{b}"])
            fw.flush()
            rms_scale(hl[b], 3, hn[b], [f"hl{b}"], [f"hn{b}"])
            for half in range(2):
                p, pk = psum()
                for jj in range(4):
                    c = half * 4 + jj
                    mm(p[:, jj * 128:(jj + 1) * 128], hn[b][:, c * 128:(c + 1) * 128], identb, True, True, [f"hn{b}", "identb"], pk, jj == 3)
                V(lambda e, p=p, half=half, t4=t4: e.tensor_copy(out=hnT[:, half * 4:half * 4 + 4, t4 * 128:(t4 + 1) * 128],
                                                               in_=p.rearrange("p (a c) -> p a c", c=128)), [pk], [("hnT", t4)])
        hk = [("hnT", t4) for t4 in range(GT)]
        for fc in range(22):
            fp, q2 = fc // 2, fc % 2
            pg, pkg = psum()
            for k in range(8):
                mm(pg, W1[:, k, fp * 512 + q2 * 256:fp * 512 + q2 * 256 + 128], hnT[:, k, :], k == 0, k == 7, hk + ["W1"], pkg, k == 7)
            pu, pku = psum()
            for k in range(8):
                mm(pu, W1[:, k, fp * 512 + q2 * 256 + 128:fp * 512 + q2 * 256 + 256], hnT[:, k, :], k == 0, k == 7, hk + ["W1"], pku, k == 7)
            s_ = sg[fc % 2]
            ks_ = f"sg{fc % 2}"
            A_(lambda e, pg=pg, s_=s_: e.activation(out=s_, in_=pg, func=AF.Sigmoid), [pkg], [ks_])
            V(lambda e, pg=pg, s_=s_: e.tensor_tensor(out=s_, in0=pg, in1=s_, op=ALU.mult), [pkg, ks_], [ks_])
            V(lambda e, pu=pu, s_=s_, fc=fc: e.tensor_tensor(out=actT[:, fc, :], in0=pu, in1=s_, op=ALU.mult), [pku, ks_], [("actT", fc)])
        ak = [("actT", fc) for fc in range(22)]
        for t4 in range(GT):
            i = g * GT + t4
            b = t4 % 2
            fw.dma("sync", hres[b], h1_s[i * 128:(i + 1) * 128, :], reads=[("h1_s", i)], writes=[f"hl{b}"])
            fw.flush()
            for half in range(2):
                p, pk = psum()
                for k in range(22):
                    mm(p, actT[:, k, t4 * 128:(t4 + 1) * 128], W2[:, k, half * 512:(half + 1) * 512], k == 0, k == 21, ak + ["W2"], pk, k == 21)
                V(lambda e, p=p, half=half, b=b: e.tensor_tensor(out=ob[b][:, half * 512:(half + 1) * 512], in0=p, in1=hres[b][:, half * 512:(half + 1) * 512], op=ALU.add),
                  [pk, f"hl{b}"], [f"junk{b}"])
            fw.defer_dma("sync", out[i * 128:(i + 1) * 128, :], ob[b], reads=[f"junk{b}"], writes=[("out", i)])
    fw.emit()
    return nc


def _panels(w, kk):
    n = w.shape[1] // 512
    return np.ascontiguousarray(w.reshape(kk, 128, n, 512).transpose(2, 1, 0, 3))


def prep_shared(inp):
    f = lambda a: np.asarray(a, dtype=np.float32)
    w_in = f(inp["w_in"])[0]
    qcols = []
    for j in range(8):
        for s in range(2):
            head = ((j // 4) * 2 + s) * 4 + (j % 4)
            qcols.extend(range(head * 64, head * 64 + 64))
    w_in_r = np.concatenate([w_in[:, qcols], w_in[:, 1024:1536], w_in[:, 1536:]], axis=1)
    wf1 = f(inp["w_ffn_in"])[0]
    cols = []
    for c in range(22):
        cols.extend(range(c * 128, (c + 1) * 128))
        cols.extend(range(DFF + c * 128, DFF + (c + 1) * 128))
    wf1_r = wf1[:, cols]
    rep = lambda v, n: np.ascontiguousarray(np.broadcast_to(f(v).reshape(1, -1), (128, n)))
    gains = np.stack([rep(inp["norm_mix"][0], D), rep(inp["attn_branch_norm"][0], D), rep(inp["ssm_branch_norm"][0], D), rep(inp["norm_ffn"][0], D)])

    def sp(a):
        return np.ascontiguousarray(f(a).reshape(32, 2, 64).transpose(1, 2, 0).reshape(128, 32))

    lam = np.stack([sp(inp["lam_re"][0]), sp(inp["lam_im"][0]), sp(np.broadcast_to(f(inp["log_dt"])[0][:, None], (64, 64)))])
    bre, bim = f(inp["ssm_b_re"])[0], f(inp["ssm_b_im"])[0]
    def spc(a):
        return a.reshape(32, 2, 64, a.shape[-1]).transpose(1, 2, 0, 3).reshape(128, 32, a.shape[-1])
    btc = np.ascontiguousarray(np.stack([spc(bre), spc(bim)], axis=2))
    cre, cim = f(inp["ssm_c_re"])[0], f(inp["ssm_c_im"])[0]
    cc = np.ascontiguousarray(np.stack([spc(cre.transpose(0, 2, 1)), spc(cim.transpose(0, 2, 1))]))
    kk, qq = np.arange(128)[:, None], np.arange(128)[None, :]
    mcur = np.where(kk <= qq, 1.0, 0.0).astype(np.float32)
    mprev = np.where(kk > qq, 1.0, 0.0).astype(np.float32)
    return dict(
        w_in=_panels(w_in_r, 8), w_glu=_panels(f(inp["w_glu"])[0], 8), w_out=_panels(f(inp["w_out"])[0], 8),
        w_f1=_panels(wf1_r, 8), w_f2=_panels(f(inp["w_ffn_out"])[0], 22), gains=gains,
        gq=rep(np.tile(f(inp["q_norm"])[0], 4), 256), gk=rep(np.tile(f(inp["k_norm"])[0], 4), 256),
        sinks=rep(inp["attn_sinks"][0], 16), ident=np.eye(128, dtype=np.float32), lam=lam, btc=btc, cc=cc,
        dcol=np.ascontiguousarray(f(inp["ssm_d"])[0].reshape(8, 128).T),
    ), mcur, mprev


def prep_core(x_b, meta, h, NM, NP, mcur, mprev):
    xmain = np.ascontiguousarray(x_b[h * NM:(h + 1) * NM])
    xpre = np.zeros((NP, D), np.float32)
    xctx = np.zeros((256, D), np.float32)
    xctx[0:16] = meta
    if h == 0:
        xpre[NP - 16:] = meta
        m0 = np.zeros((128, 128), np.float32)
    else:
        xpre[1008:1024] = meta
        xpre[1024:] = x_b[0:NM]
        xctx[128:256] = x_b[NM - 128:NM]
        m0 = mprev
    masks = np.stack([np.tile(mcur, (1, 4)), np.tile(mprev, (1, 4)), np.tile(m0, (1, 4))]).astype(np.float32)
    return dict(xmain=xmain, xpre=xpre, xctx=xctx, masks=masks)


_NC_CACHE = {}


def kernel(**inputs):
    x = np.asarray(inputs["x"], dtype=np.float32)
    Bsz, S, _ = x.shape
    NM = S // 2
    NP = NM + 1024
    meta = np.asarray(inputs["meta_tokens"], dtype=np.float32)
    shared, mcur, mprev = prep_shared(inputs)
    in_maps = []
    for b in range(Bsz):
        for h in range(2):
            d = dict(shared)
            d.update(prep_core(x[b], meta, h, NM, NP, mcur, mprev))
            in_maps.append(d)
    nc = build(NM, NP)
    res = run_bass_kernel_spmd(nc, in_maps, core_ids=list(range(len(in_maps))))
    outp = np.zeros((Bsz, S, D), np.float32)
    for b in range(Bsz):
        for h in range(2):
            outp[b, h * NM:(h + 1) * NM] = res.results[2 * b + h]["out"]
    return outp
```

```python
import math
import contextlib
import numpy as np
import concourse.bass as bass
import concourse.mybir as mybir
from concourse.bass_utils import run_bass_kernel_spmd

F32 = mybir.dt.float32
BF16 = mybir.dt.bfloat16
AF = mybir.ActivationFunctionType
ALU = mybir.AluOpType
AX = mybir.AxisListType
ENGS = ("tensor", "vector", "scalar", "gpsimd", "sync")
D = 1024
DFF = 2816
NEG = -30000.0


class FW:
    def __init__(self, nc, n_dma_sems=40):
        self.nc = nc
        self.ops = {e: [] for e in ENGS}
        self.cnt = {e: 0 for e in ENGS}
        self.known = {e: {} for e in ENGS}
        self.last_w = {}
        self.readers = {}
        self.n_dma_sems = n_dma_sems
        self.dma_gen = [0] * n_dma_sems
        self.dma_rr = 0
        self.sem_names = [f"s_{e}" for e in ENGS] + [f"d_{i}" for i in range(n_dma_sems)]

    def _deps(self, reads, writes):
        evs = []
        for k in reads:
            if k in self.last_w:
                evs.append(self.last_w[k])
        for k in writes:
            if k in self.last_w:
                evs.append(self.last_w[k])
            evs.extend(self.readers.get(k, ()))
        return evs

    def _commit(self, ev, reads, writes):
        for k in reads:
            self.readers.setdefault(k, []).append(ev)
        for k in writes:
            self.last_w[k] = ev
            self.readers[k] = []

    def _waits(self, eng, evs):
        best = {}
        for (s, v) in evs:
            if v > best.get(s, 0):
                best[s] = v
        out = []
        kn = self.known[eng]
        for s, v in best.items():
            if eng == "tensor" and s == "s_tensor":
                continue
            if kn.get(s, 0) >= v:
                continue
            kn[s] = v
            out.append((s, v))
        return out

    def op(self, eng, fn, reads=(), writes=(), inc=True):
        evs = self._deps(reads, writes)
        waits = self._waits(eng, evs)
        sname = f"s_{eng}"
        ev = (sname, self.cnt[eng] + 1)
        if inc:
            self.cnt[eng] += 1
        self.ops[eng].append((waits, fn, (sname, 1) if inc else None))
        self._commit(ev, reads, writes)
        return ev

    def dma(self, queue, out, in_, reads=(), writes=(), **kw):
        i = self.dma_rr
        self.dma_rr = (self.dma_rr + 1) % self.n_dma_sems
        sname = f"d_{i}"
        evs = self._deps(reads, writes)
        if self.dma_gen[i] > 0:
            evs.append((sname, 16 * self.dma_gen[i]))
        waits = self._waits(queue, evs)
        self.dma_gen[i] += 1
        ev = (sname, 16 * self.dma_gen[i])
        self.ops[queue].append((waits, lambda e: e.dma_start(out=out, in_=in_, **kw), (sname, 16)))
        self._commit(ev, reads, writes)
        return ev

    def defer_dma(self, *a, **kw):
        if not hasattr(self, "_deferred"):
            self._deferred = []
        self._deferred.append((a, kw))

    def flush(self):
        for a, kw in getattr(self, "_deferred", []):
            self.dma(*a, **kw)
        self._deferred = []

    def barrier(self):
        self.flush()
        fin = []
        for e in ENGS:
            if self.cnt[e] > 0:
                fin.append((f"s_{e}", self.cnt[e]))
        for i in range(self.n_dma_sems):
            if self.dma_gen[i] > 0:
                fin.append((f"d_{i}", 16 * self.dma_gen[i]))
        for e in ENGS:
            w = self._waits(e, fin)
            if w:
                self.ops[e].append((w, None, None))
        self.last_w = {}
        self.readers = {}

    def emit(self):
        nc = self.nc
        self.barrier()
        with contextlib.ExitStack() as st:
            sems = {n: st.enter_context(nc.semaphore(n)) for n in self.sem_names}
            block = st.enter_context(nc.Block())

            def mk(engname):
                lst = self.ops[engname]

                def body(eng):
                    for (waits, fn, inc) in lst:
                        for (s, v) in waits:
                            eng.wait_ge(sems[s], v)
                        if fn is None:
                            continue
                        ins = fn(eng)
                        if inc is not None:
                            ins.then_inc(sems[inc[0]], inc[1])
                return body

            block.tensor(mk("tensor"))
            block.vector(mk("vector"))
            block.scalar(mk("scalar"))
            block.gpsimd(mk("gpsimd"))
            block.sync(mk("sync"))


class Arena:
    def __init__(self, nc, base=16640, limit=224 * 1024):
        self.nc, self.off, self.limit, self.n = nc, base, limit, 0

    def alloc(self, name, shape, dt):
        per = int(np.prod(shape[1:])) * (4 if dt == F32 else 2)
        per = (per + 63) // 64 * 64
        assert self.off + per <= self.limit, (name, self.off, per)
        self.n += 1
        t = self.nc.alloc_sbuf_tensor_at(f"{name}_{self.n}_{self.off}", list(shape), dt, offset=self.off)
        self.off += per
        return t.ap()

    def mark(self):
        return self.off

    def reset(self, off):
        self.off = off


def build(NM, NP, upto=9):
    nc = bass.Bass("TRN2", target_bir_lowering=False)
    fw = FW(nc)
    TM_, TP_ = NM // 128, NP // 128
    NS = TP_ + TM_
    NK = 2 + TM_

    def din(name, shape, dt=F32):
        return nc.dram_tensor(name, list(shape), dt, kind="ExternalInput").ap()

    xmain = din("xmain", [NM, D]); xpre = din("xpre", [NP, D]); xctx = din("xctx", [256, D])
    w_in = din("w_in", [9, 128, 8, 512]); w_glu = din("w_glu", [4, 128, 8, 512]); w_out = din("w_out", [2, 128, 8, 512])
    w_f1 = din("w_f1", [11, 128, 8, 512]); w_f2 = din("w_f2", [2, 128, 22, 512])
    gains = din("gains", [4, 128, D])
    gq = din("gq", [128, 256]); gk = din("gk", [128, 256]); sinks = din("sinks", [128, 16])
    masks = din("masks", [3, 128, 512])
    ident = din("ident", [128, 128])
    lam = din("lam", [3, 128, 32])
    btc = din("btc", [128, 32, 2, 16]); cc = din("cc", [2, 128, 32, 16]); dcol = din("dcol", [128, 8])
    out = nc.dram_tensor("out", [NM, D], F32, kind="ExternalOutput").ap()

    def dscr(name, shape, dt):
        return nc.dram_tensor(name, list(shape), dt, kind="Internal").ap()

    uT_s = dscr("uT_s", [NS, 128, 8, 128], BF16); qT_s = dscr("qT_s", [TM_, 128, 8, 128], BF16)
    kT_s = dscr("kT_s", [NK, 128, 2, 128], BF16); v_s = dscr("v_s", [NK, 128, 4, 65], BF16)
    g_s = dscr("g_s", [TM_, 128, 2048], BF16); zT_s = dscr("zT_s", [TM_, 128, 8, 128], BF16)
    h1_s = dscr("h1_s", [NM, D], F32)
    DB_s = dscr("DB_s", [8, 128, 8, 4, 2, 128], BF16); EC_s = dscr("EC_s", [8, 128, 8, 4, 2, 128], BF16)

    ar = Arena(nc)
    identf = ar.alloc("identf", [128, 128], F32); identb = ar.alloc("identb", [128, 128], BF16)
    gsb = ar.alloc("gsb", [128, 4, D], F32)
    fw.dma("sync", identf, ident, writes=["identf"])
    fw.op("vector", lambda e: e.tensor_copy(out=identb, in_=identf), reads=["identf"], writes=["identb"])
    fw.dma("sync", gsb, gains.rearrange("a p d -> p a d"), writes=["gsb"])
    pbank = [nc.alloc_psum_tensor(f"pb{i}", [128, 512], F32).ap() for i in range(8)]
    pcnt = [0]

    def psum():
        i = pcnt[0] % 8
        pcnt[0] += 1
        return pbank[i], f"pb{i}"

    rr = [0]

    def alt():
        rr[0] += 1
        return "vector" if rr[0] % 2 else "gpsimd"

    base0 = ar.mark()

    def load_weights(dst, src, npan, kk, key):
        m = ar.mark()
        nst = 3 if ar.off + 3 * 16384 <= ar.limit else 2
        st = [ar.alloc("wst", [128, 8, 512], F32) for _ in range(nst)]
        cyc = ["vector", "scalar", "gpsimd", "vector", "scalar"]
        n = 0
        for pi in range(npan):
            for k0 in range(0, kk, 8):
                kc = min(8, kk - k0)
                s = st[n % nst]
                fw.dma("sync", s[:, :kc, :], src[pi][:, k0:k0 + kc, :], writes=[f"wst{n % nst}"])
                eng = cyc[n % len(cyc)]
                o_ = dst[:, k0:k0 + kc, pi * 512:(pi + 1) * 512]
                if eng == "scalar":
                    fw.op(eng, lambda e, s=s, kc=kc, o_=o_: e.activation(out=o_, in_=s[:, :kc, :], func=AF.Copy), reads=[f"wst{n % nst}"], writes=[key])
                else:
                    fw.op(eng, lambda e, s=s, kc=kc, o_=o_: e.tensor_copy(out=o_, in_=s[:, :kc, :]), reads=[f"wst{n % nst}"], writes=[key])
                n += 1
        fw.barrier()
        ar.reset(m)

    rmsc = [0]

    def rms_scale(xin, gidx, xn_out, rkeys, wkeys, ncol=D):
        pr = rmsc[0] % 2
        rmsc[0] += 1
        jk, sq_ = junk[pr], ssq[pr]
        kj, ks = f"junk{pr}", f"ssq{pr}"
        fw.op("scalar", lambda e: e.activation(out=jk[:, :ncol], in_=xin, func=AF.Square, accum_out=sq_),
              reads=rkeys, writes=[kj, ks])
        fw.op("vector", lambda e: e.tensor_scalar(out=sq_, in0=sq_, scalar1=1.0 / ncol, scalar2=1e-6, op0=ALU.mult, op1=ALU.add),
              reads=[ks], writes=[ks])
        fw.op("scalar", lambda e: e.activation(out=sq_, in_=sq_, func=AF.Sqrt), reads=[ks], writes=[ks])
        fw.op("vector", lambda e: e.reciprocal(out=sq_, in_=sq_), reads=[ks], writes=[ks])
        fw.op("vector", lambda e: e.scalar_tensor_tensor(out=xn_out, in0=xin, scalar=sq_, in1=gsb[:, gidx, :ncol],
                                                         op0=ALU.mult, op1=ALU.mult),
              reads=list(rkeys) + [ks, "gsb"], writes=wkeys)

    def transpose8(src_bf, dstT, rkey, wkey, n=8):
        for half in range((n + 3) // 4):
            p, pk = psum()
            m = min(4, n - half * 4)
            for j in range(m):
                c = half * 4 + j
                fw.op("tensor", lambda e, c=c, j=j, p=p: e.matmul(p[:, j * 128:(j + 1) * 128], lhsT=src_bf[:, c * 128:(c + 1) * 128],
                                                                 rhs=identb, start=True, stop=True),
                      reads=[rkey, "identb"], writes=[pk], inc=(j == m - 1))
            fw.op("vector", lambda e, p=p, half=half, m=m: e.tensor_copy(
                out=dstT[:, half * 4:half * 4 + m, :], in_=p[:, :m * 128].rearrange("p (a b) -> p a b", b=128)),
                reads=[pk], writes=[wkey])

    junk = [ar.alloc("junk", [128, D], F32) for _ in range(2)]; ssq = [ar.alloc("ssq", [128, 1], F32) for _ in range(2)]
    base1 = ar.mark()

    dbg = {}
    V = lambda fn, r, w: fw.op("vector", fn, reads=r, writes=w)
    A_ = lambda fn, r, w: fw.op("scalar", fn, reads=r, writes=w)
    G_ = lambda fn, r, w: fw.op("gpsimd", fn, reads=r, writes=w)

    def mm(out_ap, lhsT, rhs, start, stop, reads, pk, inc):
        fw.op("tensor", lambda e: e.matmul(out_ap, lhsT=lhsT, rhs=rhs, start=start, stop=stop), reads=reads, writes=[pk], inc=inc)

    dcs = ar.alloc("dcs", [128, 8], F32)
    esink = ar.alloc("esink", [128, 16], F32); gqk = ar.alloc("gqk", [128, 256], F32)
    maskb = ar.alloc("maskb", [128, 3, 512], BF16)
    base_persist = ar.mark()
    st32 = ar.alloc("st32", [128, 8192], F32)
    fw.dma("sync", dcs, dcol, writes=["dcs"])
    fw.dma("sync", st32[:, 0:16], sinks, writes=["a"])
    A_(lambda e: e.activation(out=esink, in_=st32[:, 0:16], func=AF.Exp), ["a"], ["esink"])
    fw.dma("sync", st32[:, 1024:1280], gq, writes=["b"])
    fw.dma("sync", st32[:, 2048:2304], gk, writes=["c"])
    V(lambda e: e.tensor_tensor(out=gqk, in0=st32[:, 1024:1280], in1=st32[:, 2048:2304], op=ALU.mult), ["b", "c"], ["gqk"])
    fw.dma("sync", st32[:, 4096:5632].rearrange("p (a c) -> p a c", a=3), masks.rearrange("a p c -> p a c"), writes=["d"])
    V(lambda e: e.tensor_copy(out=maskb, in_=st32[:, 4096:5632].rearrange("p (a c) -> p a c", a=3)), ["d"], ["maskb"])
    fw.barrier()
    ar.reset(base_persist)

    m1 = ar.mark()
    Win = ar.alloc("Win", [128, 8, 4608], BF16)
    load_weights(Win, w_in, 9, 8, "Win")
    QO, KVO, UO, GO = 0, 1024, 1536, 2560
    xt = [ar.alloc("xt", [128, D], F32) for _ in range(2)]
    xnb = [ar.alloc("xnb", [128, D], BF16) for _ in range(2)]
    xnT4 = [ar.alloc("xnT4", [128, 8, 512], BF16) for _ in range(2)]
    uTb4 = [ar.alloc("uTb4", [128, 8, 512], BF16) for _ in range(2)]
    qsq_ = [ar.alloc("qsq", [128, 512], F32) for _ in range(2)]
    qss_ = [ar.alloc("qss", [128, 8], F32) for _ in range(2)]
    hnc = [0]
    qn = [ar.alloc("qn", [128, D], BF16) for _ in range(2)]
    qTb = [ar.alloc("qTb", [128, 8, 128], BF16) for _ in range(2)]
    kf_ = [ar.alloc("kf", [128, 256], F32) for _ in range(2)]
    kn = [ar.alloc("kn", [128, 256], BF16) for _ in range(2)]
    kTb = [ar.alloc("kTb", [128, 2, 128], BF16) for _ in range(2)]
    vab = [ar.alloc("vab", [128, 4, 65], BF16) for _ in range(2)]
    gb = [ar.alloc("gb", [128, 2048], BF16) for _ in range(2)]
    for b in range(2):
        V(lambda e, b=b: e.memset(vab[b][:, :, 64:65], 1.0), [], [f"vab{b}"])

    groups = [[("ctx", xctx[0:128, :], 0, None), ("ctx", xctx[128:256, :], 1, None)]]
    for t in range(0, TP_, 4):
        groups.append([("pre", xpre[(t + q) * 128:(t + q + 1) * 128, :], t + q, None) for q in range(4)])
    for t in range(0, TM_, 4):
        groups.append([("main", xmain[(t + q) * 128:(t + q + 1) * 128, :], TP_ + t + q, t + q) for q in range(4)])

    def headnorm(p, pk, ncol, nh, dst, dkey, gain=None):
        pr = hnc[0] % 2
        hnc[0] += 1
        qsq, qss, kf = qsq_[pr], qss_[pr], kf_[pr]
        kq, ks, kk_ = f"qsq{pr}", f"qss{pr}", f"kf{pr}"
        A_(lambda e: e.activation(out=qsq[:, :ncol], in_=p[:, :ncol], func=AF.Square), [pk], [kq])
        V(lambda e: e.tensor_reduce(out=qss[:, :nh], in_=qsq[:, :ncol].rearrange("p (h d) -> p h d", d=64), axis=AX.X, op=ALU.add), [kq], [ks])
        V(lambda e: e.tensor_scalar(out=qss[:, :nh], in0=qss[:, :nh], scalar1=1.0 / 64, scalar2=1e-6, op0=ALU.mult, op1=ALU.add), [ks], [ks])
        A_(lambda e: e.activation(out=qss[:, :nh], in_=qss[:, :nh], func=AF.Sqrt), [ks], [ks])
        V(lambda e: e.reciprocal(out=qss[:, :nh], in_=qss[:, :nh]), [ks], [ks])
        rb = qss[:, :nh].unsqueeze(2).broadcast_to([128, nh, 64])
        if gain is None:
            V(lambda e: e.tensor_tensor(out=dst.rearrange("p (h d) -> p h d", d=64), in0=p[:, :ncol].rearrange("p (h d) -> p h d", d=64), in1=rb, op=ALU.mult),
              [pk, ks], [dkey])
        else:
            V(lambda e: e.tensor_tensor(out=kf.rearrange("p (h d) -> p h d", d=64), in0=p[:, :ncol].rearrange("p (h d) -> p h d", d=64), in1=rb, op=ALU.mult),
              [pk, ks], [kk_])
            V(lambda e: e.tensor_tensor(out=dst, in0=kf, in1=gain, op=ALU.mult), [kk_, "gqk"], [dkey])

    ti = 0
    for gi, grp_tiles in enumerate(groups):
        gpar = gi % 2
        X4 = xnT4[gpar]
        xkeys = []
        for t4, (kind, src, sidx, midx) in enumerate(grp_tiles):
            b = ti % 2
            ti += 1
            fw.dma("sync", xt[b], src, writes=[f"xt{b}"])
            fw.flush()
            rms_scale(xt[b], 0, xnb[b], [f"xt{b}"], [f"xnb{b}"])
            xk = f"xnT{gpar}_{t4}"
            xkeys.append(xk)
            XT = X4[:, :, t4 * 128:(t4 + 1) * 128]
            transpose8(xnb[b], XT, f"xnb{b}", xk)
            if kind == "main":
                for half in range(2):
                    p, pk = psum()
                    for k in range(8):
                        mm(p, XT[:, k, :], Win[:, k, QO + half * 512:QO + (half + 1) * 512], k == 0, k == 7, [xk, "Win"], pk, k == 7)
                    headnorm(p, pk, 512, 8, qn[b][:, half * 512:(half + 1) * 512], f"qn{b}")
                transpose8(qn[b], qTb[b], f"qn{b}", f"qTb{b}")
                fw.defer_dma("sync", qT_s[midx], qTb[b], reads=[f"qTb{b}"], writes=[("qT_s", midx)])
                for j in range(4):
                    p, pk = psum()
                    for k in range(8):
                        mm(p, XT[:, k, :], Win[:, k, GO + j * 512:GO + (j + 1) * 512], k == 0, k == 7, [xk, "Win"], pk, k == 7)
                    A_(lambda e, p=p, j=j, b=b: e.activation(out=gb[b][:, j * 512:(j + 1) * 512], in_=p, func=AF.Sigmoid), [pk], [f"gb{b}"])
                fw.defer_dma("sync", g_s[midx], gb[b], reads=[f"gb{b}"], writes=[("g_s", midx)])
            if kind in ("ctx", "main"):
                kidx = sidx if kind == "ctx" else 2 + midx
                p, pk = psum()
                for k in range(8):
                    mm(p, XT[:, k, :], Win[:, k, KVO:KVO + 512], k == 0, k == 7, [xk, "Win"], pk, k == 7)
                headnorm(p, pk, 256, 4, kn[b], f"kn{b}", gain=gqk)
                V(lambda e, p=p, b=b: e.tensor_copy(out=vab[b][:, :, 0:64], in_=p[:, 256:512].rearrange("p (h d) -> p h d", d=64)), [pk], [f"vab{b}"])
                transpose8(kn[b], kTb[b], f"kn{b}", f"kTb{b}", n=2)
                fw.defer_dma("sync", kT_s[kidx], kTb[b], reads=[f"kTb{b}"], writes=[("kT_s", kidx)])
                fw.defer_dma("sync", v_s[kidx], vab[b], reads=[f"vab{b}"], writes=[("v_s", kidx)])
        if grp_tiles[0][0] in ("pre", "main"):
            s0 = grp_tiles[0][2]
            for ct in range(8):
                p, pk = psum()
                for k in range(8):
                    mm(p, Win[:, k, UO + ct * 128:UO + (ct + 1) * 128], X4[:, k, :], k == 0, k == 7, xkeys + ["Win"], pk, k == 7)
                if ct % 2 == 0:
                    V(lambda e, p=p, ct=ct, gpar=gpar: e.tensor_copy(out=uTb4[gpar][:, ct, :], in_=p), [pk], [f"uTb4{gpar}"])
                else:
                    A_(lambda e, p=p, ct=ct, gpar=gpar: e.activation(out=uTb4[gpar][:, ct, :], in_=p, func=AF.Copy), [pk], [f"uTb4{gpar}"])
            fw.defer_dma("sync", uT_s[s0:s0 + 4].rearrange("n p c t -> p c n t"), uTb4[gpar].rearrange("p c (n t) -> p c n t", t=128),
                         reads=[f"uTb4{gpar}"], writes=[("uT_s", s0)])
    fw.barrier()
    ar.reset(m1)

    if upto <= 1:
        fw.emit()
        return nc
    lamsb = ar.alloc("lamsb", [128, 3, 32], F32)
    fw.dma("sync", lamsb, lam.rearrange("a p g -> p a g"), writes=["lam"])
    smn = ["dt", "th", "rho", "sn", "cs", "ar", "ai", "fr", "fi", "t1", "t2", "t3", "den", "wr", "wi", "w128r", "w128i", "mk", "x2", "lrdt"]
    sm = {n: ar.alloc(n, [128, 32], F32) for n in smn}
    pwr = [ar.alloc("pwr", [128, 32], F32) for _ in range(9)]; pwi = [ar.alloc("pwi", [128, 32], F32) for _ in range(9)]
    Er = ar.alloc("Er", [128, 32, 128], F32); Ei = ar.alloc("Ei", [128, 32, 128], F32)
    Kpad = ar.alloc("Kpad", [128, 8, 8, 128], BF16)
    cR = ar.alloc("cR", [128, 2, 32], F32); SL = ar.alloc("SL", [128, 2, 32], F32)
    base_ssm = ar.mark()
    lr, li, ld = lamsb[:, 0, :], lamsb[:, 1, :], lamsb[:, 2, :]
    K = ["ssm0"]
    A_(lambda e: e.activation(out=sm["dt"], in_=ld, func=AF.Exp), ["lam"], K)
    V(lambda e: e.tensor_tensor(out=sm["th"], in0=li, in1=sm["dt"], op=ALU.mult), K, K)
    V(lambda e: e.tensor_tensor(out=sm["lrdt"], in0=lr, in1=sm["dt"], op=ALU.mult), K, K)
    A_(lambda e: e.activation(out=sm["rho"], in_=sm["lrdt"], func=AF.Exp), K, K)
    for _ in range(5):
        V(lambda e: e.tensor_single_scalar(out=sm["mk"], in_=sm["th"], scalar=math.pi, op=ALU.is_gt), K, K)
        V(lambda e: e.scalar_tensor_tensor(out=sm["th"], in0=sm["mk"], scalar=-2.0 * math.pi, in1=sm["th"], op0=ALU.mult, op1=ALU.add), K, K)
    V(lambda e: e.tensor_scalar(out=sm["t3"], in0=sm["th"], scalar1=0.125, scalar2=None, op0=ALU.mult), K, K)
    V(lambda e: e.tensor_tensor(out=sm["x2"], in0=sm["t3"], in1=sm["t3"], op=ALU.mult), K, K)

    def horner(o, coefs):
        V(lambda e: e.memset(o, coefs[0]), K, K)
        for c in coefs[1:]:
            V(lambda e: e.tensor_tensor(out=o, in0=o, in1=sm["x2"], op=ALU.mult), K, K)
            V(lambda e, c=c: e.tensor_scalar(out=o, in0=o, scalar1=float(c), scalar2=None, op0=ALU.add), K, K)

    def cdouble(sn_, cs_):
        V(lambda e: e.tensor_tensor(out=sm["t1"], in0=sn_, in1=cs_, op=ALU.mult), K, K)
        V(lambda e: e.tensor_tensor(out=sm["t2"], in0=cs_, in1=cs_, op=ALU.mult), K, K)
        V(lambda e: e.tensor_tensor(out=sm["t3"], in0=sn_, in1=sn_, op=ALU.mult), K, K)
        V(lambda e: e.tensor_scalar(out=sn_, in0=sm["t1"], scalar1=2.0, scalar2=None, op0=ALU.mult), K, K)
        V(lambda e: e.tensor_tensor(out=cs_, in0=sm["t2"], in1=sm["t3"], op=ALU.subtract), K, K)

    horner(sm["sn"], [-1 / 39916800.0, 1 / 362880.0, -1 / 5040.0, 1 / 120.0, -1 / 6.0, 1.0])
    V(lambda e: e.tensor_tensor(out=sm["sn"], in0=sm["sn"], in1=sm["t3"], op=ALU.mult), K, K)
    horner(sm["cs"], [-1 / 3628800.0, 1 / 40320.0, -1 / 720.0, 1 / 24.0, -0.5, 1.0])
    for _ in range(3):
        cdouble(sm["sn"], sm["cs"])
    V(lambda e: e.tensor_tensor(out=sm["ar"], in0=sm["rho"], in1=sm["cs"], op=ALU.mult), K, K)
    V(lambda e: e.tensor_tensor(out=sm["ai"], in0=sm["rho"], in1=sm["sn"], op=ALU.mult), K, K)
    V(lambda e: e.tensor_scalar(out=sm["t1"], in0=sm["ar"], scalar1=-1.0, scalar2=None, op0=ALU.add), K, K)
    V(lambda e: e.tensor_tensor(out=sm["den"], in0=lr, in1=lr, op=ALU.mult), K, K)
    V(lambda e: e.tensor_tensor(out=sm["t2"], in0=li, in1=li, op=ALU.mult), K, K)
    V(lambda e: e.tensor_tensor(out=sm["den"], in0=sm["den"], in1=sm["t2"], op=ALU.add), K, K)
    V(lambda e: e.reciprocal(out=sm["den"], in_=sm["den"]), K, K)
    V(lambda e: e.tensor_tensor(out=sm["t2"], in0=sm["t1"], in1=lr, op=ALU.mult), K, K)
    V(lambda e: e.tensor_tensor(out=sm["t3"], in0=sm["ai"], in1=li, op=ALU.mult), K, K)
    V(lambda e: e.tensor_tensor(out=sm["t2"], in0=sm["t2"], in1=sm["t3"], op=ALU.add), K, K)
    V(lambda e: e.tensor_tensor(out=sm["fr"], in0=sm["t2"], in1=sm["den"], op=ALU.mult), K, K)
    V(lambda e: e.tensor_tensor(out=sm["t2"], in0=sm["ai"], in1=lr, op=ALU.mult), K, K)
    V(lambda e: e.tensor_tensor(out=sm["t3"], in0=sm["t1"], in1=li, op=ALU.mult), K, K)
    V(lambda e: e.tensor_tensor(out=sm["t2"], in0=sm["t2"], in1=sm["t3"], op=ALU.subtract), K, K)
    V(lambda e: e.tensor_tensor(out=sm["fi"], in0=sm["t2"], in1=sm["den"], op=ALU.mult), K, K)
    V(lambda e: e.memset(pwr[0], 1.0), K, K)
    V(lambda e: e.memset(pwi[0], 0.0), K, K)
    for k in range(1, 9):
        V(lambda e, k=k: e.tensor_tensor(out=sm["t1"], in0=pwr[k - 1], in1=sm["ar"], op=ALU.mult), K, K)
        V(lambda e, k=k: e.tensor_tensor(out=sm["t2"], in0=pwi[k - 1], in1=sm["ai"], op=ALU.mult), K, K)
        V(lambda e, k=k: e.tensor_tensor(out=pwr[k], in0=sm["t1"], in1=sm["t2"], op=ALU.subtract), K, K)
        V(lambda e, k=k: e.tensor_tensor(out=sm["t1"], in0=pwr[k - 1], in1=sm["ai"], op=ALU.mult), K, K)
        V(lambda e, k=k: e.tensor_tensor(out=sm["t2"], in0=pwi[k - 1], in1=sm["ar"], op=ALU.mult), K, K)
        V(lambda e, k=k: e.tensor_tensor(out=pwi[k], in0=sm["t1"], in1=sm["t2"], op=ALU.add), K, K)
    V(lambda e: e.tensor_scalar(out=sm["t1"], in0=sm["lrdt"], scalar1=8.0, scalar2=None, op0=ALU.mult), K, K)
    A_(lambda e: e.activation(out=sm["rho"], in_=sm["t1"], func=AF.Exp), K, K)
    for _ in range(3):
        cdouble(sm["sn"], sm["cs"])
    V(lambda e: e.memset(Er[:, :, 0:1], 1.0), K, K)
    V(lambda e: e.memset(Ei[:, :, 0:1], 0.0), K, K)
    V(lambda e: e.tensor_copy(out=sm["wr"], in_=sm["cs"]), K, K)
    V(lambda e: e.tensor_copy(out=sm["wi"], in_=sm["sn"]), K, K)
    m0 = ar.mark()
    tA = ar.alloc("tA", [128, 32, 64], F32); tB = ar.alloc("tB", [128, 32, 64], F32)
    for k in range(7):
        n = 1 << k
        wrb = sm["wr"].unsqueeze(2).broadcast_to([128, 32, n]); wib = sm["wi"].unsqueeze(2).broadcast_to([128, 32, n])
        V(lambda e, n=n, wrb=wrb: e.tensor_tensor(out=tA[:, :, :n], in0=Er[:, :, :n], in1=wrb, op=ALU.mult), K, K)
        V(lambda e, n=n, wib=wib: e.tensor_tensor(out=tB[:, :, :n], in0=Ei[:, :, :n], in1=wib, op=ALU.mult), K, K)
        V(lambda e, n=n: e.tensor_tensor(out=Er[:, :, n:2 * n], in0=tA[:, :, :n], in1=tB[:, :, :n], op=ALU.subtract), K, K)
        V(lambda e, n=n, wib=wib: e.tensor_tensor(out=tA[:, :, :n], in0=Er[:, :, :n], in1=wib, op=ALU.mult), K, K)
        V(lambda e, n=n, wrb=wrb: e.tensor_tensor(out=tB[:, :, :n], in0=Ei[:, :, :n], in1=wrb, op=ALU.mult), K, K)
        V(lambda e, n=n: e.tensor_tensor(out=Ei[:, :, n:2 * n], in0=tA[:, :, :n], in1=tB[:, :, :n], op=ALU.add), K, K)
        V(lambda e: e.tensor_tensor(out=sm["t1"], in0=sm["wr"], in1=sm["wr"], op=ALU.mult), K, K)
        V(lambda e: e.tensor_tensor(out=sm["t2"], in0=sm["wi"], in1=sm["wi"], op=ALU.mult), K, K)
        V(lambda e: e.tensor_tensor(out=sm["t3"], in0=sm["wr"], in1=sm["wi"], op=ALU.mult), K, K)
        V(lambda e: e.tensor_tensor(out=sm["wr"], in0=sm["t1"], in1=sm["t2"], op=ALU.subtract), K, K)
        V(lambda e: e.tensor_scalar(out=sm["wi"], in0=sm["t3"], scalar1=2.0, scalar2=None, op0=ALU.mult), K, K)
    V(lambda e: e.tensor_copy(out=sm["w128r"], in_=sm["wr"]), K, K)
    V(lambda e: e.tensor_copy(out=sm["w128i"], in_=sm["wi"]), K, K)
    V(lambda e: e.memset(cR, 0.0), K, K)
    V(lambda e: e.memset(SL, 0.0), K, K)
    fw.barrier()
    ar.reset(m0)
    BTc = ar.alloc("BTc", [128, 32, 2, 16], F32); Cc = ar.alloc("Cc", [128, 2, 32, 16], F32)
    Cfc = ar.alloc("Cfc", [128, 32, 2, 16], F32); Xc = ar.alloc("Xc", [128, 32, 2, 16], F32)
    c1 = ar.alloc("c1", [128, 32, 16], F32); c2 = ar.alloc("c2", [128, 32, 16], F32)
    Cfp = ar.alloc("Cfp", [128, 32, 2, 128], BF16)
    padb = ar.alloc("padb", [128, 32, 2, 128], BF16); DBsb = ar.alloc("DBsb", [128, 32, 2, 128], BF16)
    fw.dma("sync", BTc, btc, writes=["BTc"])
    fw.dma("sync", Cc, cc.rearrange("a p g c -> p a g c"), writes=["Cc"])
    G_(lambda e: e.memset(padb, 0.0), [], ["padb"])
    G_(lambda e: e.memset(Cfp, 0.0), [], ["Cfp"])

    def cmul_compact(dst, src_r, src_i, sr, si, rk, wk, neg_im=False):
        srb = sr.unsqueeze(2).broadcast_to([128, 32, 16]); sib = si.unsqueeze(2).broadcast_to([128, 32, 16])
        V(lambda e: e.tensor_tensor(out=c1, in0=src_r, in1=srb, op=ALU.mult), rk, ["c1"])
        V(lambda e: e.tensor_tensor(out=c2, in0=src_i, in1=sib, op=ALU.mult), rk, ["c2"])
        V(lambda e: e.tensor_tensor(out=dst[:, :, 0, :], in0=c1, in1=c2, op=ALU.subtract), ["c1", "c2"], wk)
        V(lambda e: e.tensor_tensor(out=c1, in0=src_r, in1=sib, op=ALU.mult), rk + wk, ["c1"])
        V(lambda e: e.tensor_tensor(out=c2, in0=src_i, in1=srb, op=ALU.mult), rk + wk, ["c2"])
        if neg_im:
            V(lambda e: e.scalar_tensor_tensor(out=dst[:, :, 1, :], in0=c1, scalar=-1.0, in1=c2, op0=ALU.mult, op1=ALU.subtract), ["c1", "c2"], wk)
        else:
            V(lambda e: e.tensor_tensor(out=dst[:, :, 1, :], in0=c1, in1=c2, op=ALU.add), ["c1", "c2"], wk)

    def scatter(dst_pad, src_c, rk, wk):
        for g2 in range(2):
            for q in range(4):
                blk = 2 * q + g2
                G_(lambda e, g2=g2, q=q, blk=blk: e.tensor_copy(out=dst_pad[g2 * 64:(g2 + 1) * 64, q::4, :, blk * 16:(blk + 1) * 16],
                                                                 in_=src_c[g2 * 64:(g2 + 1) * 64, q::4, :, :]), rk, wk)

    cmul_compact(Cfc, Cc[:, 0], Cc[:, 1], sm["fr"], sm["fi"], ["Cc"], ["Cfc"])
    cmul_compact(Xc, Cc[:, 0], Cc[:, 1], sm["fr"], sm["fi"], ["Cc"], ["Xc"], neg_im=True)
    scatter(Cfp, Xc, ["Xc"], ["Cfp"])
    for k in range(8):
        j = 7 - k
        cmul_compact(Xc, BTc[:, :, 0, :], BTc[:, :, 1, :], pwr[k], pwi[k], ["BTc"], ["Xc"])
        scatter(padb, Xc, ["Xc"], ["padb"])
        for r in range(8):
            p, pk = psum()
            n_ = 0
            for gl in range(4):
                for ri in range(2):
                    mm(p[:, 0:128], padb[:, 4 * r + gl, ri, :], Cfp[:, 4 * r + gl, ri, :], n_ == 0, n_ == 7, ["padb", "Cfp"], pk, n_ == 7)
                    n_ += 1
            if k == 0:
                V(lambda e, p=p, r=r: e.scalar_tensor_tensor(out=Kpad[:, r, 0, :], in0=identf, scalar=dcs[:, r:r + 1], in1=p[:, 0:128], op0=ALU.mult, op1=ALU.add),
                  [pk, "identf", "dcs"], ["Kpad"])
            else:
                V(lambda e, p=p, r=r, k=k: e.tensor_copy(out=Kpad[:, r, k, :], in_=p[:, 0:128]), [pk], ["Kpad"])
        for g4 in range(16):
            p, pk = psum()
            for q in range(4):
                gi = g4 * 4 + q
                mm(p[:, q * 128:(q + 1) * 128], padb[:, gi // 2, gi % 2, :], identb, True, True, ["padb", "identb"], pk, q == 3)
            V(lambda e, p=p, g4=g4: e.tensor_copy(out=DBsb.rearrange("p g r s -> p (g r) s")[:, g4 * 4:(g4 + 1) * 4, :], in_=p.rearrange("p (a c) -> p a c", c=128)),
              [pk], ["DBsb"])
        fw.dma("sync", DB_s[:, :, j].rearrange("r p g a s -> p r g a s"), DBsb.rearrange("p (r g) a s -> p r g a s", r=8), reads=["DBsb"], writes=[("DB_s", j)])
    for j in range(8):
        cmul_compact(Xc, Cfc[:, :, 0, :], Cfc[:, :, 1, :], pwr[j + 1], pwi[j + 1], ["Cfc"], ["Xc"])
        scatter(padb, Xc, ["Xc"], ["padb"])
        fw.dma("sync", EC_s[:, :, j].rearrange("r p g a s -> p r g a s"), padb.rearrange("p (r g) a s -> p r g a s", r=8), reads=["padb"], writes=[("EC_s", j)])
    fw.barrier()
    ar.reset(base_ssm)

    if upto <= 2:
        fw.emit()
        return nc
    NPS, NMS = NP // 1024, NM // 1024
    DBr2 = [ar.alloc("DBr", [128, 8, 4, 2, 128], BF16) for _ in range(2)]
    ECr2 = [ar.alloc("ECr", [128, 8, 4, 2, 128], BF16) for _ in range(2)]
    uTr = [ar.alloc("uTr", [128, 1024], BF16) for _ in range(2)]
    uTj = [ar.alloc("uTj", [128, 8, 128], BF16) for _ in range(2)]
    zTr = [ar.alloc("zTr", [128, 1024], BF16) for _ in range(2)]
    RB = []
    for b in range(2):
        d = {n: ar.alloc(n, [128, 4, 128], F32) for n in ["t1", "t2", "t3", "t4", "Rr", "Ri"]}
        d["Xr"], d["Xi"] = d["t1"], d["t3"]
        d["Sr"] = ar.alloc("Sr", [128, 4, 130], BF16); d["Si"] = ar.alloc("Si", [128, 4, 130], BF16)
        for n in ["c1", "c2", "c3", "c4"]:
            d[n] = ar.alloc(n, [128, 4], F32)
        RB.append(d)
    ysb = [ar.alloc("ysb", [128, 1024], F32) for _ in range(2)]
    g1b = [ar.alloc("g1b", [128, 1024], F32) for _ in range(2)]
    g2b = g1b
    rcount = 0
    hcount = 0
    for rp in range(4):
      for q_ in range(2):
        fw.dma("sync", DBr2[q_], DB_s[2 * rp + q_], writes=[f"DBr{q_}"])
        fw.dma("sync", ECr2[q_], EC_s[2 * rp + q_], writes=[f"ECr{q_}"])
      for st in range(NPS + NMS):
        for q_ in range(2):
            r = 2 * rp + q_
            gsl = slice(4 * r, 4 * r + 4)
            DBr, ECr = DBr2[q_], ECr2[q_]
            kDB, kEC = f"DBr{q_}", f"ECr{q_}"
            is_main = st >= NPS
            ub = q_
            fw.dma("sync", uTr[ub].rearrange("p (n t) -> p n t", t=128), uT_s[8 * st:8 * st + 8, :, r, :].rearrange("n p t -> p n t"),
                   writes=[f"uTr{ub}"])
            fw.flush()
            A_(lambda e, ub=ub: e.activation(out=uTj[ub], in_=uTr[ub].rearrange("p (c j) -> p j c", j=8), func=AF.Copy), [f"uTr{ub}"], [f"uTj{ub}"])
            b = rcount % 2
            rcount += 1
            B = RB[b]
            kb = lambda n, b=b: f"{n}{b}"
            pXr, pkr = psum()
            pXi, pki = psum()
            for ri, (pX, pk) in enumerate(((pXr, pkr), (pXi, pki))):
                for gl in range(4):
                    for j in range(8):
                        mm(pX[:, gl * 128:(gl + 1) * 128], DBr[:, j, gl, ri, :], uTj[ub][:, j, :], j == 0, j == 7,
                           [f"uTj{ub}", kDB], pk, (gl == 3 and j == 7))
            pXr3 = pXr.rearrange("p (a c) -> p a c", c=128); pXi3 = pXi.rearrange("p (a c) -> p a c", c=128)
            Erg, Eig = Er[:, gsl, :], Ei[:, gsl, :]
            V(lambda e, B=B, a=pXr3, t=Erg: e.tensor_tensor(out=B["t1"], in0=a, in1=t, op=ALU.mult), [pkr], [kb("t1")])
            V(lambda e, B=B, a=pXi3, t=Eig: e.tensor_tensor(out=B["t2"], in0=a, in1=t, op=ALU.mult), [pki], [kb("t2")])
            V(lambda e, B=B, a=pXi3, t=Erg: e.tensor_tensor(out=B["t3"], in0=a, in1=t, op=ALU.mult), [pki], [kb("t3")])
            V(lambda e, B=B, a=pXr3, t=Eig: e.tensor_tensor(out=B["t4"], in0=a, in1=t, op=ALU.mult), [pkr], [kb("t4")])
            G_(lambda e, B=B: e.tensor_tensor(out=B["t1"], in0=B["t1"], in1=B["t2"], op=ALU.add), [kb("t1"), kb("t2")], [kb("t1")])
            G_(lambda e, B=B: e.tensor_tensor(out=B["t3"], in0=B["t3"], in1=B["t4"], op=ALU.subtract), [kb("t3"), kb("t4")], [kb("t3")])
            for gl in range(4):
                gp = 4 * r + gl
                for nm, xs, ci in (("Rr", "t1", 0), ("Ri", "t3", 1)):
                    V(lambda e, B=B, gl=gl, gp=gp, nm=nm, xs=xs, ci=ci: e.tensor_tensor_scan(
                        out=B[nm][:, gl, :], data0=sm["rho"][:, gp:gp + 1].broadcast_to([128, 128]), data1=B[xs][:, gl, :],
                        initial=cR[:, ci, gp:gp + 1], op0=ALU.mult, op1=ALU.add), [kb(xs), ("cR", r)], [kb(nm)])
            if is_main:
                G_(lambda e, B=B, gsl=gsl: e.tensor_copy(out=B["Sr"][:, :, 0], in_=SL[:, 0, gsl]), [("SL", r)], [kb("Sr")])
                G_(lambda e, B=B, gsl=gsl: e.tensor_copy(out=B["Si"][:, :, 0], in_=SL[:, 1, gsl]), [("SL", r)], [kb("Si")])
            wr4, wi4 = sm["w128r"][:, gsl], sm["w128i"][:, gsl]
            er7, ei7 = Er[:, gsl, 127], Ei[:, gsl, 127]
            Rr7, Ri7 = B["Rr"][:, :, 127], B["Ri"][:, :, 127]
            for (xr_, xi_, dst, negim, key) in ((wr4, wi4, cR, False, "cR"), (er7, ei7, SL, True, "SL")):
                G_(lambda e, B=B, a=Rr7, w=xr_: e.tensor_tensor(out=B["c1"], in0=a, in1=w, op=ALU.mult), [kb("Rr")], [kb("c1")])
                G_(lambda e, B=B, a=Ri7, w=xi_: e.tensor_tensor(out=B["c2"], in0=a, in1=w, op=ALU.mult), [kb("Ri")], [kb("c2")])
                G_(lambda e, B=B, a=Ri7, w=xr_: e.tensor_tensor(out=B["c3"], in0=a, in1=w, op=ALU.mult), [kb("Ri")], [kb("c3")])
                G_(lambda e, B=B, a=Rr7, w=xi_: e.tensor_tensor(out=B["c4"], in0=a, in1=w, op=ALU.mult), [kb("Rr")], [kb("c4")])
                G_(lambda e, B=B, dst=dst, gsl=gsl: e.tensor_tensor(out=dst[:, 0, gsl], in0=B["c1"], in1=B["c2"], op=ALU.subtract),
                   [kb("c1"), kb("c2")], [(key, r)])
                if negim:
                    V(lambda e, B=B, dst=dst, gsl=gsl: e.scalar_tensor_tensor(out=dst[:, 1, gsl], in0=B["c3"], scalar=-1.0, in1=B["c4"], op0=ALU.mult, op1=ALU.subtract),
                       [kb("c3"), kb("c4")], [(key, r)])
                else:
                    G_(lambda e, B=B, dst=dst, gsl=gsl: e.tensor_tensor(out=dst[:, 1, gsl], in0=B["c3"], in1=B["c4"], op=ALU.add),
                       [kb("c3"), kb("c4")], [(key, r)])
            if not is_main:
                continue
            G_(lambda e, B=B, t=Erg: e.tensor_tensor(out=B["t1"], in0=B["Rr"], in1=t, op=ALU.mult), [kb("Rr")], [kb("t1")])
            G_(lambda e, B=B, t=Eig: e.tensor_tensor(out=B["t2"], in0=B["Ri"], in1=t, op=ALU.mult), [kb("Ri")], [kb("t2")])
            V(lambda e, B=B, t=Erg: e.tensor_tensor(out=B["t3"], in0=B["Ri"], in1=t, op=ALU.mult), [kb("Ri")], [kb("t3")])
            V(lambda e, B=B, t=Eig: e.tensor_tensor(out=B["t4"], in0=B["Rr"], in1=t, op=ALU.mult), [kb("Rr")], [kb("t4")])
            V(lambda e, B=B: e.tensor_tensor(out=B["Sr"][:, :, 1:129], in0=B["t1"], in1=B["t2"], op=ALU.subtract), [kb("t1"), kb("t2")], [kb("Sr")])
            V(lambda e, B=B: e.scalar_tensor_tensor(out=B["Si"][:, :, 1:129], in0=B["t3"], scalar=-1.0, in1=B["t4"], op0=ALU.mult, op1=ALU.subtract),
              [kb("t3"), kb("t4")], [kb("Si")])
            zb = q_
            hb = q_
            for h2 in range(2):
                py, pky = psum()
                for j4 in range(4):
                    j = 4 * h2 + j4
                    o = py[:, j4 * 128:(j4 + 1) * 128]
                    nmm = (j + 1) + 8
                    n_ = 0
                    for k in range(j + 1):
                        mm(o, Kpad[:, r, k, :], uTj[ub][:, j - k, :], n_ == 0, n_ == nmm - 1, [f"uTj{ub}", "Kpad"], pky, False)
                        n_ += 1
                    for gl in range(4):
                        for ri, Sn in enumerate(("Sr", "Si")):
                            mm(o, ECr[:, j, gl, ri, :], B[Sn][:, gl, 0:128], n_ == 0, n_ == nmm - 1, [kb(Sn), kEC], pky,
                               (j4 == 3 and n_ == nmm - 1))
                            n_ += 1
                A_(lambda e, py=py, hb=hb, h2=h2: e.activation(out=ysb[hb].rearrange("p (c j) -> p c j", j=8)[:, :, 4 * h2:4 * h2 + 4],
                                                               in_=py.rearrange("p (j c) -> p c j", c=128), func=AF.Identity), [pky], [f"ysb{hb}"])
            G_(lambda e, hb=hb: e.tensor_tensor(out=g1b[hb], in0=ysb[hb], in1=ysb[hb], op=ALU.mult), [f"ysb{hb}"], [f"g1b{hb}"])
            G_(lambda e, hb=hb: e.tensor_scalar(out=g1b[hb], in0=g1b[hb], scalar1=0.044715, scalar2=1.0, op0=ALU.mult, op1=ALU.add), [f"g1b{hb}"], [f"g1b{hb}"])
            G_(lambda e, hb=hb: e.tensor_tensor(out=g1b[hb], in0=g1b[hb], in1=ysb[hb], op=ALU.mult), [f"g1b{hb}", f"ysb{hb}"], [f"g1b{hb}"])
            A_(lambda e, hb=hb: e.activation(out=g2b[hb], in_=g1b[hb], func=AF.Sigmoid, scale=1.5957691216057308), [f"g1b{hb}"], [f"g2b{hb}"])
            V(lambda e, hb=hb, zb=zb: e.tensor_tensor(out=zTr[zb], in0=ysb[hb], in1=g2b[hb], op=ALU.mult), [f"ysb{hb}", f"g2b{hb}"], [f"zTr{zb}"])
            m8 = 8 * (st - NPS)
            fw.defer_dma("sync", zT_s[m8:m8 + 8, :, r, :].rearrange("n p t -> p n t"), zTr[zb].rearrange("p (n t) -> p n t", t=128),
                   reads=[f"zTr{zb}"], writes=[("zT_s", st, r)])
    fw.barrier()
    ar.reset(base_persist)

    if upto <= 3:
        fw.emit()
        return nc
    Wg = ar.alloc("Wg", [128, 8, 2048], BF16); Wo = ar.alloc("Wo", [128, 8, 1024], BF16)
    load_weights(Wg, w_glu, 4, 8, "Wg")
    load_weights(Wo, w_out, 2, 8, "Wo")
    kme = ar.alloc("kme", [128, 2, 128], BF16); vme = ar.alloc("vme", [128, 4, 65], BF16)
    fw.dma("sync", kme, kT_s[0], writes=["kme"]); fw.dma("sync", vme, v_s[0], writes=["vme"])
    qTl = [ar.alloc("qTl", [128, 8, 128], BF16) for _ in range(2)]
    kTl = [ar.alloc("kTl", [128, 2, 128], BF16) for _ in range(3)]
    vl = [ar.alloc("vl", [128, 4, 65], BF16) for _ in range(3)]
    gl_ = [ar.alloc("gl", [128, 2048], BF16) for _ in range(2)]
    zTl = [ar.alloc("zTl", [128, 8, 128], BF16) for _ in range(2)]
    xr = [ar.alloc("xr", [128, D], F32) for _ in range(2)]
    Pc = [ar.alloc("Pc", [128, 512], BF16) for _ in range(2)]
    Pp = [ar.alloc("Pp", [128, 512], BF16) for _ in range(2)]
    Pm = [ar.alloc("Pm", [128, 512], BF16) for _ in range(2)]
    den_ = [ar.alloc("den", [128, 4], F32) for _ in range(2)]
    for b_ in range(2):
        V(lambda e, b_=b_: e.memset(Pm[b_], 0.0), [], [f"Pm{b_}"])
    attn_ = [ar.alloc("attn", [128, D], F32) for _ in range(2)]; An_ = [ar.alloc("An", [128, D], F32) for _ in range(2)]
    sig_ = [ar.alloc("sig", [128, 512], F32) for _ in range(2)]; ssm_ = [ar.alloc("ssm", [128, D], F32) for _ in range(2)]
    Bn_ = [ar.alloc("Bn", [128, D], F32) for _ in range(2)]
    mg_ = [ar.alloc("mg", [128, D], BF16) for _ in range(2)]; mgT_ = [ar.alloc("mgT", [128, 8, 128], BF16) for _ in range(2)]
    h1 = [ar.alloc("h1", [128, D], F32) for _ in range(2)]
    fw.dma("sync", kTl[1], kT_s[1], writes=["kTl1"]); fw.dma("sync", vl[1], v_s[1], writes=["vl1"])
    def s3_loads(i):
        b = i % 2
        jc = 2 + i
        sc = jc % 3
        fw.dma("sync", kTl[sc], kT_s[jc], reads=[("kT_s", jc)], writes=[f"kTl{sc}"])
        fw.dma("sync", vl[sc], v_s[jc], reads=[("v_s", jc)], writes=[f"vl{sc}"])
        fw.dma("sync", qTl[b], qT_s[i], writes=[f"qTl{b}"])
        fw.dma("sync", gl_[b], g_s[i], writes=[f"gl{b}"])
        fw.dma("sync", zTl[b], zT_s[i], writes=[f"zTl{b}"])
        fw.dma("sync", xr[b], xmain[i * 128:(i + 1) * 128, :], writes=[f"xr{b}"])
        fw.flush()

    def s3_A(n):
        i, grp = n // 4, n % 4
        b, pb = i % 2, n % 2
        sc, sp = (2 + i) % 3, (1 + i) % 3
        bs, kc = (grp % 2) * 64, grp // 2
        qsel = qTl[b][bs:bs + 64, kc * 4:(kc + 1) * 4, :]
        pS, pkS = psum()
        mm(pS, kTl[sc][bs:bs + 64, kc, :], qsel, True, True, [f"kTl{sc}", f"qTl{b}"], pkS, True)
        A_(lambda e: e.activation(out=Pc[pb], in_=pS, func=AF.Exp, scale=0.125), [pkS], [f"Pc{pb}"])
        G_(lambda e: e.tensor_tensor(out=Pc[pb], in0=Pc[pb], in1=maskb[:, 0, :], op=ALU.mult), [f"Pc{pb}", "maskb"], [f"Pc{pb}"])
        pS2, pkS2 = psum()
        mm(pS2, kTl[sp][bs:bs + 64, kc, :], qsel, True, True, [f"kTl{sp}", f"qTl{b}"], pkS2, True)
        A_(lambda e: e.activation(out=Pp[pb], in_=pS2, func=AF.Exp, scale=0.125), [pkS2], [f"Pp{pb}"])
        mi = 2 if i == 0 else 1
        G_(lambda e: e.tensor_tensor(out=Pp[pb], in0=Pp[pb], in1=maskb[:, mi, :], op=ALU.mult), [f"Pp{pb}", "maskb"], [f"Pp{pb}"])
        pS3, pkS3 = psum()
        mm(pS3[0:16, :], kme[bs:bs + 64, kc, 0:16], qsel, True, True, ["kme", f"qTl{b}"], pkS3, True)
        A_(lambda e: e.activation(out=Pm[pb][0:16, :], in_=pS3[0:16, :], func=AF.Exp, scale=0.125), [pkS3], [f"Pm{pb}"])

    def s3_B(n):
        i, grp = n // 4, n % 4
        b, pb = i % 2, n % 2
        sc, sp = (2 + i) % 3, (1 + i) % 3
        attn, kA = attn_[b], f"attn{b}"
        pO, pkO = psum()
        for r in range(4):
            o = pO[:, r * 65:(r + 1) * 65]
            mm(o, Pm[pb][:, r * 128:(r + 1) * 128], vme[:, grp, :], True, False, [f"Pm{pb}", "vme"], pkO, False)
            mm(o, Pp[pb][:, r * 128:(r + 1) * 128], vl[sp][:, grp, :], False, False, [f"Pp{pb}", f"vl{sp}"], pkO, False)
            mm(o, Pc[pb][:, r * 128:(r + 1) * 128], vl[sc][:, grp, :], False, True, [f"Pc{pb}", f"vl{sc}"], pkO, r == 3)
        pO3 = pO[:, 0:260].rearrange("p (r c) -> p r c", c=65)
        den = den_[pb]
        kd = f"den{pb}"
        V(lambda e: e.tensor_tensor(out=den, in0=pO3[:, :, 64], in1=esink[:, grp * 4:(grp + 1) * 4], op=ALU.add), [pkO, "esink"], [kd])
        V(lambda e: e.reciprocal(out=den, in_=den), [kd], [kd])
        V(lambda e: e.tensor_tensor(out=attn[:, grp * 256:(grp + 1) * 256].rearrange("p (r d) -> p r d", d=64), in0=pO3[:, :, 0:64],
                                    in1=den.unsqueeze(2).broadcast_to([128, 4, 64]), op=ALU.mult), [pkO, kd], [kA])

    def s3_tail(i):
        b = i % 2
        attn, An, ssm, Bn, mg, mgT = attn_[b], An_[b], ssm_[b], Bn_[b], mg_[b], mgT_[b]
        kA, kAn, kss, kBn, kmg, kmT = f"attn{b}", f"An{b}", f"ssm{b}", f"Bn{b}", f"mg{b}", f"mgT{b}"
        rms_scale(attn, 1, An, [kA], [kAn])
        G_(lambda e: e.tensor_tensor(out=An, in0=An, in1=gl_[b][:, 0:1024], op=ALU.mult), [kAn, f"gl{b}"], [kAn])
        for half in range(2):
            pa, pka = psum()
            for k in range(8):
                mm(pa, zTl[b][:, k, :], Wg[:, k, half * 512:(half + 1) * 512], k == 0, k == 7, [f"zTl{b}", "Wg"], pka, k == 7)
            pz, pkz = psum()
            for k in range(8):
                mm(pz, zTl[b][:, k, :], Wg[:, k, 1024 + half * 512:1024 + (half + 1) * 512], k == 0, k == 7, [f"zTl{b}", "Wg"], pkz, k == 7)
            sig = sig_[half]
            A_(lambda e, pz=pz, sig=sig: e.activation(out=sig, in_=pz, func=AF.Sigmoid), [pkz], [f"sig{half}"])
            V(lambda e, pa=pa, half=half, sig=sig: e.tensor_tensor(out=ssm[:, half * 512:(half + 1) * 512], in0=pa, in1=sig, op=ALU.mult), [pka, f"sig{half}"], [kss])
        rms_scale(ssm, 2, Bn, [kss], [kBn])
        G_(lambda e: e.tensor_tensor(out=Bn, in0=Bn, in1=gl_[b][:, 1024:2048], op=ALU.mult), [kBn, f"gl{b}"], [kBn])
        V(lambda e: e.tensor_tensor(out=mg, in0=An, in1=Bn, op=ALU.add), [kAn, kBn], [kmg])
        transpose8(mg, mgT, kmg, kmT)
        for half in range(2):
            p, pk = psum()
            for k in range(8):
                mm(p, mgT[:, k, :], Wo[:, k, half * 512:(half + 1) * 512], k == 0, k == 7, [kmT, "Wo"], pk, k == 7)
            V(lambda e, p=p, half=half: e.tensor_tensor(out=h1[b][:, half * 512:(half + 1) * 512], in0=p, in1=xr[b][:, half * 512:(half + 1) * 512], op=ALU.add),
              [pk, f"xr{b}"], [f"h1{b}"])
        fw.defer_dma("sync", h1_s[i * 128:(i + 1) * 128, :], h1[b], reads=[f"h1{b}"], writes=[("h1_s", i)])

    NG = 4 * TM_
    s3_loads(0)
    s3_A(0)
    for n in range(NG):
        if n + 1 < NG:
            if (n + 1) % 4 == 0:
                s3_loads((n + 1) // 4)
            s3_A(n + 1)
        s3_B(n)
        if n % 4 == 3:
            s3_tail(n // 4)
    fw.barrier()
    ar.reset(base_persist)

    if upto <= 4:
        fw.emit()
        return nc
    W1 = ar.alloc("W1", [128, 8, 5632], BF16); W2 = ar.alloc("W2", [128, 22, 1024], BF16)
    load_weights(W1, w_f1, 11, 8, "W1")
    load_weights(W2, w_f2, 2, 22, "W2")
    GT = 4
    hl = [ar.alloc("hl", [128, D], F32) for _ in range(2)]
    hn = [ar.alloc("hn", [128, D], BF16) for _ in range(2)]
    hnT = ar.alloc("hnT", [128, 8, GT * 128], BF16)
    sg = [ar.alloc("sg", [128, 512], F32) for _ in range(2)]
    actT = ar.alloc("actT", [128, 22, GT * 128], BF16)
    hres = hl
    ob = junk
    tcount = 0
    for g in range(TM_ // GT):
        for t4 in range(GT):
            i = g * GT + t4
            b = tcount % 2
            tcount += 1
            fw.dma("sync", hl[b], h1_s[i * 128:(i + 1) * 128, :], reads=[("h1_s", i)], writes=[f"hl{b}"])
            fw.flush()
            rms_scale(hl[b], 3, hn[b], [f"hl{b}"], [f"hn{b}"])
            for half in range(2):
                p, pk = psum()
                for jj in range(4):
                    c = half * 4 + jj
                    mm(p[:, jj * 128:(jj + 1) * 128], hn[b][:, c * 128:(c + 1) * 128], identb, True, True, [f"hn{b}", "identb"], pk, jj == 3)
                V(lambda e, p=p, half=half, t4=t4: e.tensor_copy(out=hnT[:, half * 4:half * 4 + 4, t4 * 128:(t4 + 1) * 128],
                                                               in_=p.rearrange("p (a c) -> p a c", c=128)), [pk], [("hnT", t4)])
        hk = [("hnT", t4) for t4 in range(GT)]
        for fc in range(22):
            fp, q2 = fc // 2, fc % 2
            pg, pkg = psum()
            for k in range(8):
                mm(pg, W1[:, k, fp * 512 + q2 * 256:fp * 512 + q2 * 256 + 128], hnT[:, k, :], k == 0, k == 7, hk + ["W1"], pkg, k == 7)
            pu, pku = psum()
            for k in range(8):
                mm(pu, W1[:, k, fp * 512 + q2 * 256 + 128:fp * 512 + q2 * 256 + 256], hnT[:, k, :], k == 0, k == 7, hk + ["W1"], pku, k == 7)
            s_ = sg[fc % 2]
            ks_ = f"sg{fc % 2}"
            A_(lambda e, pg=pg, s_=s_: e.activation(out=s_, in_=pg, func=AF.Sigmoid), [pkg], [ks_])
            V(lambda e, pg=pg, s_=s_: e.tensor_tensor(out=s_, in0=pg, in1=s_, op=ALU.mult), [pkg, ks_], [ks_])
            V(lambda e, pu=pu, s_=s_, fc=fc: e.tensor_tensor(out=actT[:, fc, :], in0=pu, in1=s_, op=ALU.mult), [pku, ks_], [("actT", fc)])
        ak = [("actT", fc) for fc in range(22)]
        for t4 in range(GT):
            i = g * GT + t4
            b = t4 % 2
            fw.dma("sync", hres[b], h1_s[i * 128:(i + 1) * 128, :], reads=[("h1_s", i)], writes=[f"hl{b}"])
            fw.flush()
            for half in range(2):
                p, pk = psum()
                for k in range(22):
                    mm(p, actT[:, k, t4 * 128:(t4 + 1) * 128], W2[:, k, half * 512:(half + 1) * 512], k == 0, k == 21, ak + ["W2"], pk, k == 21)
                V(lambda e, p=p, half=half, b=b: e.tensor_tensor(out=ob[b][:, half * 512:(half + 1) * 512], in0=p, in1=hres[b][:, half * 512:(half + 1) * 512], op=ALU.add),
                  [pk, f"hl{b}"], [f"junk{b}"])
            fw.defer_dma("sync", out[i * 128:(i + 1) * 128, :], ob[b], reads=[f"junk{b}"], writes=[("out", i)])
    fw.emit()
    return nc


def _panels(w, kk):
    n = w.shape[1] // 512
    return np.ascontiguousarray(w.reshape(kk, 128, n, 512).transpose(2, 1, 0, 3))


def prep_shared(inp):
    f = lambda a: np.asarray(a, dtype=np.float32)
    w_in = f(inp["w_in"])[0]
    qcols = []
    for j in range(8):
        for s in range(2):
            head = ((j // 4) * 2 + s) * 4 + (j % 4)
            qcols.extend(range(head * 64, head * 64 + 64))
    w_in_r = np.concatenate([w_in[:, qcols], w_in[:, 1024:1536], w_in[:, 1536:]], axis=1)
    wf1 = f(inp["w_ffn_in"])[0]
    cols = []
    for c in range(22):
        cols.extend(range(c * 128, (c + 1) * 128))
        cols.extend(range(DFF + c * 128, DFF + (c + 1) * 128))
    wf1_r = wf1[:, cols]
    rep = lambda v, n: np.ascontiguousarray(np.broadcast_to(f(v).reshape(1, -1), (128, n)))
    gains = np.stack([rep(inp["norm_mix"][0], D), rep(inp["attn_branch_norm"][0], D), rep(inp["ssm_branch_norm"][0], D), rep(inp["norm_ffn"][0], D)])

    def sp(a):
        return np.ascontiguousarray(f(a).reshape(32, 2, 64).transpose(1, 2, 0).reshape(128, 32))

    lam = np.stack([sp(inp["lam_re"][0]), sp(inp["lam_im"][0]), sp(np.broadcast_to(f(inp["log_dt"])[0][:, None], (64, 64)))])
    bre, bim = f(inp["ssm_b_re"])[0], f(inp["ssm_b_im"])[0]
    def spc(a):
        return a.reshape(32, 2, 64, a.shape[-1]).transpose(1, 2, 0, 3).reshape(128, 32, a.shape[-1])
    btc = np.ascontiguousarray(np.stack([spc(bre), spc(bim)], axis=2))
    cre, cim = f(inp["ssm_c_re"])[0], f(inp["ssm_c_im"])[0]
    cc = np.ascontiguousarray(np.stack([spc(cre.transpose(0, 2, 1)), spc(cim.transpose(0, 2, 1))]))
    kk, qq = np.arange(128)[:, None], np.arange(128)[None, :]
    mcur = np.where(kk <= qq, 1.0, 0.0).astype(np.float32)
    mprev = np.where(kk > qq, 1.0, 0.0).astype(np.float32)
    return dict(
        w_in=_panels(w_in_r, 8), w_glu=_panels(f(inp["w_glu"])[0], 8), w_out=_panels(f(inp["w_out"])[0], 8),
        w_f1=_panels(wf1_r, 8), w_f2=_panels(f(inp["w_ffn_out"])[0], 22), gains=gains,
        gq=rep(np.tile(f(inp["q_norm"])[0], 4), 256), gk=rep(np.tile(f(inp["k_norm"])[0], 4), 256),
        sinks=rep(inp["attn_sinks"][0], 16), ident=np.eye(128, dtype=np.float32), lam=lam, btc=btc, cc=cc,
        dcol=np.ascontiguousarray(f(inp["ssm_d"])[0].reshape(8, 128).T),
    ), mcur, mprev


def prep_core(x_b, meta, h, NM, NP, mcur, mprev):
    xmain = np.ascontiguousarray(x_b[h * NM:(h + 1) * NM])
    xpre = np.zeros((NP, D), np.float32)
    xctx = np.zeros((256, D), np.float32)
    xctx[0:16] = meta
    if h == 0:
        xpre[NP - 16:] = meta
        m0 = np.zeros((128, 128), np.float32)
    else:
        xpre[1008:1024] = meta
        xpre[1024:] = x_b[0:NM]
        xctx[128:256] = x_b[NM - 128:NM]
        m0 = mprev
    masks = np.stack([np.tile(mcur, (1, 4)), np.tile(mprev, (1, 4)), np.tile(m0, (1, 4))]).astype(np.float32)
    return dict(xmain=xmain, xpre=xpre, xctx=xctx, masks=masks)


_NC_CACHE = {}


def kernel(**inputs):
    x = np.asarray(inputs["x"], dtype=np.float32)
    Bsz, S, _ = x.shape
    NM = S // 2
    NP = NM + 1024
    meta = np.asarray(inputs["meta_tokens"], dtype=np.float32)
    shared, mcur, mprev = prep_shared(inputs)
    in_maps = []
    for b in range(Bsz):
        for h in range(2):
            d = dict(shared)
            d.update(prep_core(x[b], meta, h, NM, NP, mcur, mprev))
            in_maps.append(d)
    nc = build(NM, NP)
    res = run_bass_kernel_spmd(nc, in_maps, core_ids=list(range(len(in_maps))))
    outp = np.zeros((Bsz, S, D), np.float32)
    for b in range(Bsz):
        for h in range(2):
            outp[b, h * NM:(h + 1) * NM] = res.results[2 * b + h]["out"]
    return outp
```

```python
import math
import contextlib
import numpy as np
import concourse.bass as bass
import concourse.mybir as mybir
from concourse.bass_utils import run_bass_kernel_spmd

F32 = mybir.dt.float32
BF16 = mybir.dt.bfloat16
AF = mybir.ActivationFunctionType
ALU = mybir.AluOpType
AX = mybir.AxisListType
ENGS = ("tensor", "vector", "scalar", "gpsimd", "sync")
D = 1024
DFF = 2816
NEG = -30000.0


class FW:
    def __init__(self, nc, n_dma_sems=40):
        self.nc = nc
        self.ops = {e: [] for e in ENGS}
        self.cnt = {e: 0 for e in ENGS}
        self.known = {e: {} for e in ENGS}
        self.last_w = {}
        self.readers = {}
        self.n_dma_sems = n_dma_sems
        self.dma_gen = [0] * n_dma_sems
        self.dma_rr = 0
        self.sem_names = [f"s_{e}" for e in ENGS] + [f"d_{i}" for i in range(n_dma_sems)]

    def _deps(self, reads, writes):
        evs = []
        for k in reads:
            if k in self.last_w:
                evs.append(self.last_w[k])
        for k in writes:
            if k in self.last_w:
                evs.append(self.last_w[k])
            evs.extend(self.readers.get(k, ()))
        return evs

    def _commit(self, ev, reads, writes):
        for k in reads:
            self.readers.setdefault(k, []).append(ev)
        for k in writes:
            self.last_w[k] = ev
            self.readers[k] = []

    def _waits(self, eng, evs):
        best = {}
        for (s, v) in evs:
            if v > best.get(s, 0):
                best[s] = v
        out = []
        kn = self.known[eng]
        for s, v in best.items():
            if eng == "tensor" and s == "s_tensor":
                continue
            if kn.get(s, 0) >= v:
                continue
            kn[s] = v
            out.append((s, v))
        return out

    def op(self, eng, fn, reads=(), writes=(), inc=True):
        evs = self._deps(reads, writes)
        waits = self._waits(eng, evs)
        sname = f"s_{eng}"
        ev = (sname, self.cnt[eng] + 1)
        if inc:
            self.cnt[eng] += 1
        self.ops[eng].append((waits, fn, (sname, 1) if inc else None))
        self._commit(ev, reads, writes)
        return ev

    def dma(self, queue, out, in_, reads=(), writes=(), **kw):
        i = self.dma_rr
        self.dma_rr = (self.dma_rr + 1) % self.n_dma_sems
        sname = f"d_{i}"
        evs = self._deps(reads, writes)
        if self.dma_gen[i] > 0:
            evs.append((sname, 16 * self.dma_gen[i]))
        waits = self._waits(queue, evs)
        self.dma_gen[i] += 1
        ev = (sname, 16 * self.dma_gen[i])
        self.ops[queue].append((waits, lambda e: e.dma_start(out=out, in_=in_, **kw), (sname, 16)))
        self._commit(ev, reads, writes)
        return ev

    def defer_dma(self, *a, **kw):
        if not hasattr(self, "_deferred"):
            self._deferred = []
        self._deferred.append((a, kw))

    def flush(self):
        for a, kw in getattr(self, "_deferred", []):
            self.dma(*a, **kw)
        self._deferred = []

    def barrier(self):
        self.flush()
        fin = []
        for e in ENGS:
            if self.cnt[e] > 0:
                fin.append((f"s_{e}", self.cnt[e]))
        for i in range(self.n_dma_sems):
            if self.dma_gen[i] > 0:
                fin.append((f"d_{i}", 16 * self.dma_gen[i]))
        for e in ENGS:
            w = self._waits(e, fin)
            if w:
                self.ops[e].append((w, None, None))
        self.last_w = {}
        self.readers = {}

    def emit(self):
        nc = self.nc
        self.barrier()
        with contextlib.ExitStack() as st:
            sems = {n: st.enter_context(nc.semaphore(n)) for n in self.sem_names}
            block = st.enter_context(nc.Block())

            def mk(engname):
                lst = self.ops[engname]

                def body(eng):
                    for (waits, fn, inc) in lst:
                        for (s, v) in waits:
                            eng.wait_ge(sems[s], v)
                        if fn is None:
                            continue
                        ins = fn(eng)
                        if inc is not None:
                            ins.then_inc(sems[inc[0]], inc[1])
                return body

            block.tensor(mk("tensor"))
            block.vector(mk("vector"))
            block.scalar(mk("scalar"))
            block.gpsimd(mk("gpsimd"))
            block.sync(mk("sync"))


class Arena:
    def __init__(self, nc, base=16640, limit=224 * 1024):
        self.nc, self.off, self.limit, self.n = nc, base, limit, 0

    def alloc(self, name, shape, dt):
        per = int(np.prod(shape[1:])) * (4 if dt == F32 else 2)
        per = (per + 63) // 64 * 64
        assert self.off + per <= self.limit, (name, self.off, per)
        self.n += 1
        t = self.nc.alloc_sbuf_tensor_at(f"{name}_{self.n}_{self.off}", list(shape), dt, offset=self.off)
        self.off += per
        return t.ap()

    def mark(self):
        return self.off

    def reset(self, off):
        self.off = off


def build(NM, NP, upto=9):
    nc = bass.Bass("TRN2", target_bir_lowering=False)
    fw = FW(nc)
    TM_, TP_ = NM // 128, NP // 128
    NS = TP_ + TM_
    NK = 2 + TM_

    def din(name, shape, dt=F32):
        return nc.dram_tensor(name, list(shape), dt, kind="ExternalInput").ap()

    xmain = din("xmain", [NM, D]); xpre = din("xpre", [NP, D]); xctx = din("xctx", [256, D])
    w_in = din("w_in", [9, 128, 8, 512]); w_glu = din("w_glu", [4, 128, 8, 512]); w_out = din("w_out", [2, 128, 8, 512])
    w_f1 = din("w_f1", [11, 128, 8, 512]); w_f2 = din("w_f2", [2, 128, 22, 512])
    gains = din("gains", [4, 128, D])
    gq = din("gq", [128, 256]); gk = din("gk", [128, 256]); sinks = din("sinks", [128, 16])
    masks = din("masks", [3, 128, 512])
    ident = din("ident", [128, 128])
    lam = din("lam", [3, 128, 32])
    btc = din("btc", [128, 32, 2, 16]); cc = din("cc", [2, 128, 32, 16]); dcol = din("dcol", [128, 8])
    out = nc.dram_tensor("out", [NM, D], F32, kind="ExternalOutput").ap()

    def dscr(name, shape, dt):
        return nc.dram_tensor(name, list(shape), dt, kind="Internal").ap()

    uT_s = dscr("uT_s", [NS, 128, 8, 128], BF16); qT_s = dscr("qT_s", [TM_, 128, 8, 128], BF16)
    kT_s = dscr("kT_s", [NK, 128, 2, 128], BF16); v_s = dscr("v_s", [NK, 128, 4, 65], BF16)
    g_s = dscr("g_s", [TM_, 128, 2048], BF16); zT_s = dscr("zT_s", [TM_, 128, 8, 128], BF16)
    h1_s = dscr("h1_s", [NM, D], F32)
    DB_s = dscr("DB_s", [8, 128, 8, 4, 2, 128], BF16); EC_s = dscr("EC_s", [8, 128, 8, 4, 2, 128], BF16)

    ar = Arena(nc)
    identf = ar.alloc("identf", [128, 128], F32); identb = ar.alloc("identb", [128, 128], BF16)
    gsb = ar.alloc("gsb", [128, 4, D], F32)
    fw.dma("sync", identf, ident, writes=["identf"])
    fw.op("vector", lambda e: e.tensor_copy(out=identb, in_=identf), reads=["identf"], writes=["identb"])
    fw.dma("sync", gsb, gains.rearrange("a p d -> p a d"), writes=["gsb"])
    pbank = [nc.alloc_psum_tensor(f"pb{i}", [128, 512], F32).ap() for i in range(8)]
    pcnt = [0]

    def psum():
        i = pcnt[0] % 8
        pcnt[0] += 1
        return pbank[i], f"pb{i}"

    rr = [0]

    def alt():
        rr[0] += 1
        return "vector" if rr[0] % 2 else "gpsimd"

    base0 = ar.mark()

    def load_weights(dst, src, npan, kk, key):
        m = ar.mark()
        nst = 3 if ar.off + 3 * 16384 <= ar.limit else 2
        st = [ar.alloc("wst", [128, 8, 512], F32) for _ in range(nst)]
        cyc = ["vector", "scalar", "gpsimd", "vector", "scalar"]
        n = 0
        for pi in range(npan):
            for k0 in range(0, kk, 8):
                kc = min(8, kk - k0)
                s = st[n % nst]
                fw.dma("sync", s[:, :kc, :], src[pi][:, k0:k0 + kc, :], writes=[f"wst{n % nst}"])
                eng = cyc[n % len(cyc)]
                o_ = dst[:, k0:k0 + kc, pi * 512:(pi + 1) * 512]
                if eng == "scalar":
                    fw.op(eng, lambda e, s=s, kc=kc, o_=o_: e.activation(out=o_, in_=s[:, :kc, :], func=AF.Copy), reads=[f"wst{n % nst}"], writes=[key])
                else:
                    fw.op(eng, lambda e, s=s, kc=kc, o_=o_: e.tensor_copy(out=o_, in_=s[:, :kc, :]), reads=[f"wst{n % nst}"], writes=[key])
                n += 1
        fw.barrier()
        ar.reset(m)

    rmsc = [0]

    def rms_scale(xin, gidx, xn_out, rkeys, wkeys, ncol=D):
        pr = rmsc[0] % 2
        rmsc[0] += 1
        jk, sq_ = junk[pr], ssq[pr]
        kj, ks = f"junk{pr}", f"ssq{pr}"
        fw.op("scalar", lambda e: e.activation(out=jk[:, :ncol], in_=xin, func=AF.Square, accum_out=sq_),
              reads=rkeys, writes=[kj, ks])
        fw.op("vector", lambda e: e.tensor_scalar(out=sq_, in0=sq_, scalar1=1.0 / ncol, scalar2=1e-6, op0=ALU.mult, op1=ALU.add),
              reads=[ks], writes=[ks])
        fw.op("scalar", lambda e: e.activation(out=sq_, in_=sq_, func=AF.Sqrt), reads=[ks], writes=[ks])
        fw.op("vector", lambda e: e.reciprocal(out=sq_, in_=sq_), reads=[ks], writes=[ks])
        fw.op("vector", lambda e: e.scalar_tensor_tensor(out=xn_out, in0=xin, scalar=sq_, in1=gsb[:, gidx, :ncol],
                                                         op0=ALU.mult, op1=ALU.mult),
              reads=list(rkeys) + [ks, "gsb"], writes=wkeys)

    def transpose8(src_bf, dstT, rkey, wkey, n=8):
        for half in range((n + 3) // 4):
            p, pk = psum()
            m = min(4, n - half * 4)
            for j in range(m):
                c = half * 4 + j
                fw.op("tensor", lambda e, c=c, j=j, p=p: e.matmul(p[:, j * 128:(j + 1) * 128], lhsT=src_bf[:, c * 128:(c + 1) * 128],
                                                                 rhs=identb, start=True, stop=True),
                      reads=[rkey, "identb"], writes=[pk], inc=(j == m - 1))
            fw.op("vector", lambda e, p=p, half=half, m=m: e.tensor_copy(
                out=dstT[:, half * 4:half * 4 + m, :], in_=p[:, :m * 128].rearrange("p (a b) -> p a b", b=128)),
                reads=[pk], writes=[wkey])

    junk = [ar.alloc("junk", [128, D], F32) for _ in range(2)]; ssq = [ar.alloc("ssq", [128, 1], F32) for _ in range(2)]
    base1 = ar.mark()

    dbg = {}
    V = lambda fn, r, w: fw.op("vector", fn, reads=r, writes=w)
    A_ = lambda fn, r, w: fw.op("scalar", fn, reads=r, writes=w)
    G_ = lambda fn, r, w: fw.op("gpsimd", fn, reads=r, writes=w)

    def mm(out_ap, lhsT, rhs, start, stop, reads, pk, inc):
        fw.op("tensor", lambda e: e.matmul(out_ap, lhsT=lhsT, rhs=rhs, start=start, stop=stop), reads=reads, writes=[pk], inc=inc)

    dcs = ar.alloc("dcs", [128, 8], F32)
    esink = ar.alloc("esink", [128, 16], F32); gqk = ar.alloc("gqk", [128, 256], F32)
    maskb = ar.alloc("maskb", [128, 3, 512], BF16)
    base_persist = ar.mark()
    st32 = ar.alloc("st32", [128, 8192], F32)
    fw.dma("sync", dcs, dcol, writes=["dcs"])
    fw.dma("sync", st32[:, 0:16], sinks, writes=["a"])
    A_(lambda e: e.activation(out=esink, in_=st32[:, 0:16], func=AF.Exp), ["a"], ["esink"])
    fw.dma("sync", st32[:, 1024:1280], gq, writes=["b"])
    fw.dma("sync", st32[:, 2048:2304], gk, writes=["c"])
    V(lambda e: e.tensor_tensor(out=gqk, in0=st32[:, 1024:1280], in1=st32[:, 2048:2304], op=ALU.mult), ["b", "c"], ["gqk"])
    fw.dma("sync", st32[:, 4096:5632].rearrange("p (a c) -> p a c", a=3), masks.rearrange("a p c -> p a c"), writes=["d"])
    V(lambda e: e.tensor_copy(out=maskb, in_=st32[:, 4096:5632].rearrange("p (a c) -> p a c", a=3)), ["d"], ["maskb"])
    fw.barrier()
    ar.reset(base_persist)

    m1 = ar.mark()
    Win = ar.alloc("Win", [128, 8, 4608], BF16)
    load_weights(Win, w_in, 9, 8, "Win")
    QO, KVO, UO, GO = 0, 1024, 1536, 2560
    xt = [ar.alloc("xt", [128, D], F32) for _ in range(2)]
    xnb = [ar.alloc("xnb", [128, D], BF16) for _ in range(2)]
    xnT4 = [ar.alloc("xnT4", [128, 8, 512], BF16) for _ in range(2)]
    uTb4 = [ar.alloc("uTb4", [128, 8, 512], BF16) for _ in range(2)]
    qsq_ = [ar.alloc("qsq", [128, 512], F32) for _ in range(2)]
    qss_ = [ar.alloc("qss", [128, 8], F32) for _ in range(2)]
    hnc = [0]
    qn = [ar.alloc("qn", [128, D], BF16) for _ in range(2)]
    qTb = [ar.alloc("qTb", [128, 8, 128], BF16) for _ in range(2)]
    kf_ = [ar.alloc("kf", [128, 256], F32) for _ in range(2)]
    kn = [ar.alloc("kn", [128, 256], BF16) for _ in range(2)]
    kTb = [ar.alloc("kTb", [128, 2, 128], BF16) for _ in range(2)]
    vab = [ar.alloc("vab", [128, 4, 65], BF16) for _ in range(2)]
    gb = [ar.alloc("gb", [128, 2048], BF16) for _ in range(2)]
    for b in range(2):
        V(lambda e, b=b: e.memset(vab[b][:, :, 64:65], 1.0), [], [f"vab{b}"])

    groups = [[("ctx", xctx[0:128, :], 0, None), ("ctx", xctx[128:256, :], 1, None)]]
    for t in range(0, TP_, 4):
        groups.append([("pre", xpre[(t + q) * 128:(t + q + 1) * 128, :], t + q, None) for q in range(4)])
    for t in range(0, TM_, 4):
        groups.append([("main", xmain[(t + q) * 128:(t + q + 1) * 128, :], TP_ + t + q, t + q) for q in range(4)])

    def headnorm(p, pk, ncol, nh, dst, dkey, gain=None):
        pr = hnc[0] % 2
        hnc[0] += 1
        qsq, qss, kf = qsq_[pr], qss_[pr], kf_[pr]
        kq, ks, kk_ = f"qsq{pr}", f"qss{pr}", f"kf{pr}"
        A_(lambda e: e.activation(out=qsq[:, :ncol], in_=p[:, :ncol], func=AF.Square), [pk], [kq])
        V(lambda e: e.tensor_reduce(out=qss[:, :nh], in_=qsq[:, :ncol].rearrange("p (h d) -> p h d", d=64), axis=AX.X, op=ALU.add), [kq], [ks])
        V(lambda e: e.tensor_scalar(out=qss[:, :nh], in0=qss[:, :nh], scalar1=1.0 / 64, scalar2=1e-6, op0=ALU.mult, op1=ALU.add), [ks], [ks])
        A_(lambda e: e.activation(out=qss[:, :nh], in_=qss[:, :nh], func=AF.Sqrt), [ks], [ks])
        V(lambda e: e.reciprocal(out=qss[:, :nh], in_=qss[:, :nh]), [ks], [ks])
        rb = qss[:, :nh].unsqueeze(2).broadcast_to([128, nh, 64])
        if gain is None:
            V(lambda e: e.tensor_tensor(out=dst.rearrange("p (h d) -> p h d", d=64), in0=p[:, :ncol].rearrange("p (h d) -> p h d", d=64), in1=rb, op=ALU.mult),
              [pk, ks], [dkey])
        else:
            V(lambda e: e.tensor_tensor(out=kf.rearrange("p (h d) -> p h d", d=64), in0=p[:, :ncol].rearrange("p (h d) -> p h d", d=64), in1=rb, op=ALU.mult),
              [pk, ks], [kk_])
            V(lambda e: e.tensor_tensor(out=dst, in0=kf, in1=gain, op=ALU.mult), [kk_, "gqk"], [dkey])

    ti = 0
    for gi, grp_tiles in enumerate(groups):
        gpar = gi % 2
        X4 = xnT4[gpar]
        xkeys = []
        for t4, (kind, src, sidx, midx) in enumerate(grp_tiles):
            b = ti % 2
            ti += 1
            fw.dma("sync", xt[b], src, writes=[f"xt{b}"])
            fw.flush()
            rms_scale(xt[b], 0, xnb[b], [f"xt{b}"], [f"xnb{b}"])
            xk = f"xnT{gpar}_{t4}"
            xkeys.append(xk)
            XT = X4[:, :, t4 * 128:(t4 + 1) * 128]
            transpose8(xnb[b], XT, f"xnb{b}", xk)
            if kind == "main":
                for half in range(2):
                    p, pk = psum()
                    for k in range(8):
                        mm(p, XT[:, k, :], Win[:, k, QO + half * 512:QO + (half + 1) * 512], k == 0, k == 7, [xk, "Win"], pk, k == 7)
                    headnorm(p, pk, 512, 8, qn[b][:, half * 512:(half + 1) * 512], f"qn{b}")
                transpose8(qn[b], qTb[b], f"qn{b}", f"qTb{b}")
                fw.defer_dma("sync", qT_s[midx], qTb[b], reads=[f"qTb{b}"], writes=[("qT_s", midx)])
                for j in range(4):
                    p, pk = psum()
                    for k in range(8):
                        mm(p, XT[:, k, :], Win[:, k, GO + j * 512:GO + (j + 1) * 512], k == 0, k == 7, [xk, "Win"], pk, k == 7)
                    A_(lambda e, p=p, j=j, b=b: e.activation(out=gb[b][:, j * 512:(j + 1) * 512], in_=p, func=AF.Sigmoid), [pk], [f"gb{b}"])
                fw.defer_dma("sync", g_s[midx], gb[b], reads=[f"gb{b}"], writes=[("g_s", midx)])
            if kind in ("ctx", "main"):
                kidx = sidx if kind == "ctx" else 2 + midx
                p, pk = psum()
                for k in range(8):
                    mm(p, XT[:, k, :], Win[:, k, KVO:KVO + 512], k == 0, k == 7, [xk, "Win"], pk, k == 7)
                headnorm(p, pk, 256, 4, kn[b], f"kn{b}", gain=gqk)
                V(lambda e, p=p, b=b: e.tensor_copy(out=vab[b][:, :, 0:64], in_=p[:, 256:512].rearrange("p (h d) -> p h d", d=64)), [pk], [f"vab{b}"])
                transpose8(kn[b], kTb[b], f"kn{b}", f"kTb{b}", n=2)
                fw.defer_dma("sync", kT_s[kidx], kTb[b], reads=[f"kTb{b}"], writes=[("kT_s", kidx)])
                fw.defer_dma("sync", v_s[kidx], vab[b], reads=[f"vab{b}"], writes=[("v_s", kidx)])
        if grp_tiles[0][0] in ("pre", "main"):
            s0 = grp_tiles[0][2]
            for ct in range(8):
                p, pk = psum()
                for k in range(8):
                    mm(p, Win[:, k, UO + ct * 128:UO + (ct + 1) * 128], X4[:, k, :], k == 0, k == 7, xkeys + ["Win"], pk, k == 7)
                if ct % 2 == 0:
                    V(lambda e, p=p, ct=ct, gpar=gpar: e.tensor_copy(out=uTb4[gpar][:, ct, :], in_=p), [pk], [f"uTb4{gpar}"])
                else:
                    A_(lambda e, p=p, ct=ct, gpar=gpar: e.activation(out=uTb4[gpar][:, ct, :], in_=p, func=AF.Copy), [pk], [f"uTb4{gpar}"])
            fw.defer_dma("sync", uT_s[s0:s0 + 4].rearrange("n p c t -> p c n t"), uTb4[gpar].rearrange("p c (n t) -> p c n t", t=128),
                         reads=[f"uTb4{gpar}"], writes=[("uT_s", s0)])
    fw.barrier()
    ar.reset(m1)

    if upto <= 1:
        fw.emit()
        return nc
    lamsb = ar.alloc("lamsb", [128, 3, 32], F32)
    fw.dma("sync", lamsb, lam.rearrange("a p g -> p a g"), writes=["lam"])
    smn = ["dt", "th", "rho", "sn", "cs", "ar", "ai", "fr", "fi", "t1", "t2", "t3", "den", "wr", "wi", "w128r", "w128i", "mk", "x2", "lrdt"]
    sm = {n: ar.alloc(n, [128, 32], F32) for n in smn}
    pwr = [ar.alloc("pwr", [128, 32], F32) for _ in range(9)]; pwi = [ar.alloc("pwi", [128, 32], F32) for _ in range(9)]
    Er = ar.alloc("Er", [128, 32, 128], F32); Ei = ar.alloc("Ei", [128, 32, 128], F32)
    Kpad = ar.alloc("Kpad", [128, 8, 8, 128], BF16)
    cR = ar.alloc("cR", [128, 2, 32], F32); SL = ar.alloc("SL", [128, 2, 32], F32)
    base_ssm = ar.mark()
    lr, li, ld = lamsb[:, 0, :], lamsb[:, 1, :], lamsb[:, 2, :]
    K = ["ssm0"]
    A_(lambda e: e.activation(out=sm["dt"], in_=ld, func=AF.Exp), ["lam"], K)
    V(lambda e: e.tensor_tensor(out=sm["th"], in0=li, in1=sm["dt"], op=ALU.mult), K, K)
    V(lambda e: e.tensor_tensor(out=sm["lrdt"], in0=lr, in1=sm["dt"], op=ALU.mult), K, K)
    A_(lambda e: e.activation(out=sm["rho"], in_=sm["lrdt"], func=AF.Exp), K, K)
    for _ in range(5):
        V(lambda e: e.tensor_single_scalar(out=sm["mk"], in_=sm["th"], scalar=math.pi, op=ALU.is_gt), K, K)
        V(lambda e: e.scalar_tensor_tensor(out=sm["th"], in0=sm["mk"], scalar=-2.0 * math.pi, in1=sm["th"], op0=ALU.mult, op1=ALU.add), K, K)
    V(lambda e: e.tensor_scalar(out=sm["t3"], in0=sm["th"], scalar1=0.125, scalar2=None, op0=ALU.mult), K, K)
    V(lambda e: e.tensor_tensor(out=sm["x2"], in0=sm["t3"], in1=sm["t3"], op=ALU.mult), K, K)

    def horner(o, coefs):
        V(lambda e: e.memset(o, coefs[0]), K, K)
        for c in coefs[1:]:
            V(lambda e: e.tensor_tensor(out=o, in0=o, in1=sm["x2"], op=ALU.mult), K, K)
            V(lambda e, c=c: e.tensor_scalar(out=o, in0=o, scalar1=float(c), scalar2=None, op0=ALU.add), K, K)

    def cdouble(sn_, cs_):
        V(lambda e: e.tensor_tensor(out=sm["t1"], in0=sn_, in1=cs_, op=ALU.mult), K, K)
        V(lambda e: e.tensor_tensor(out=sm["t2"], in0=cs_, in1=cs_, op=ALU.mult), K, K)
        V(lambda e: e.tensor_tensor(out=sm["t3"], in0=sn_, in1=sn_, op=ALU.mult), K, K)
        V(lambda e: e.tensor_scalar(out=sn_, in0=sm["t1"], scalar1=2.0, scalar2=None, op0=ALU.mult), K, K)
        V(lambda e: e.tensor_tensor(out=cs_, in0=sm["t2"], in1=sm["t3"], op=ALU.subtract), K, K)

    horner(sm["sn"], [-1 / 39916800.0, 1 / 362880.0, -1 / 5040.0, 1 / 120.0, -1 / 6.0, 1.0])
    V(lambda e: e.tensor_tensor(out=sm["sn"], in0=sm["sn"], in1=sm["t3"], op=ALU.mult), K, K)
    horner(sm["cs"], [-1 / 3628800.0, 1 / 40320.0, -1 / 720.0, 1 / 24.0, -0.5, 1.0])
    for _ in range(3):
        cdouble(sm["sn"], sm["cs"])
    V(lambda e: e.tensor_tensor(out=sm["ar"], in0=sm["rho"], in1=sm["cs"], op=ALU.mult), K, K)
    V(lambda e: e.tensor_tensor(out=sm["ai"], in0=sm["rho"], in1=sm["sn"], op=ALU.mult), K, K)
    V(lambda e: e.tensor_scalar(out=sm["t1"], in0=sm["ar"], scalar1=-1.0, scalar2=None, op0=ALU.add), K, K)
    V(lambda e: e.tensor_tensor(out=sm["den"], in0=lr, in1=lr, op=ALU.mult), K, K)
    V(lambda e: e.tensor_tensor(out=sm["t2"], in0=li, in1=li, op=ALU.mult), K, K)
    V(lambda e: e.tensor_tensor(out=sm["den"], in0=sm["den"], in1=sm["t2"], op=ALU.add), K, K)
    V(lambda e: e.reciprocal(out=sm["den"], in_=sm["den"]), K, K)
    V(lambda e: e.tensor_tensor(out=sm["t2"], in0=sm["t1"], in1=lr, op=ALU.mult), K, K)
    V(lambda e: e.tensor_tensor(out=sm["t3"], in0=sm["ai"], in1=li, op=ALU.mult), K, K)
    V(lambda e: e.tensor_tensor(out=sm["t2"], in0=sm["t2"], in1=sm["t3"], op=ALU.add), K, K)
    V(lambda e: e.tensor_tensor(out=sm["fr"], in0=sm["t2"], in1=sm["den"], op=ALU.mult), K, K)
    V(lambda e: e.tensor_tensor(out=sm["t2"], in0=sm["ai"], in1=lr, op=ALU.mult), K, K)
    V(lambda e: e.tensor_tensor(out=sm["t3"], in0=sm["t1"], in1=li, op=ALU.mult), K, K)
    V(lambda e: e.tensor_tensor(out=sm["t2"], in0=sm["t2"], in1=sm["t3"], op=ALU.subtract), K, K)
    V(lambda e: e.tensor_tensor(out=sm["fi"], in0=sm["t2"], in1=sm["den"], op=ALU.mult), K, K)
    V(lambda e: e.memset(pwr[0], 1.0), K, K)
    V(lambda e: e.memset(pwi[0], 0.0), K, K)
    for k in range(1, 9):
        V(lambda e, k=k: e.tensor_tensor(out=sm["t1"], in0=pwr[k - 1], in1=sm["ar"], op=ALU.mult), K, K)
        V(lambda e, k=k: e.tensor_tensor(out=sm["t2"], in0=pwi[k - 1], in1=sm["ai"], op=ALU.mult), K, K)
        V(lambda e, k=k: e.tensor_tensor(out=pwr[k], in0=sm["t1"], in1=sm["t2"], op=ALU.subtract), K, K)
        V(lambda e, k=k: e.tensor_tensor(out=sm["t1"], in0=pwr[k - 1], in1=sm["ai"], op=ALU.mult), K, K)
        V(lambda e, k=k: e.tensor_tensor(out=sm["t2"], in0=pwi[k - 1], in1=sm["ar"], op=ALU.mult), K, K)
        V(lambda e, k=k: e.tensor_tensor(out=pwi[k], in0=sm["t1"], in1=sm["t2"], op=ALU.add), K, K)
    V(lambda e: e.tensor_scalar(out=sm["t1"], in0=sm["lrdt"], scalar1=8.0, scalar2=None, op0=ALU.mult), K, K)
    A_(lambda e: e.activation(out=sm["rho"], in_=sm["t1"], func=AF.Exp), K, K)
    for _ in range(3):
        cdouble(sm["sn"], sm["cs"])
    V(lambda e: e.memset(Er[:, :, 0:1], 1.0), K, K)
    V(lambda e: e.memset(Ei[:, :, 0:1], 0.0), K, K)
    V(lambda e: e.tensor_copy(out=sm["wr"], in_=sm["cs"]), K, K)
    V(lambda e: e.tensor_copy(out=sm["wi"], in_=sm["sn"]), K, K)
    m0 = ar.mark()
    tA = ar.alloc("tA", [128, 32, 64], F32); tB = ar.alloc("tB", [128, 32, 64], F32)
    for k in range(7):
        n = 1 << k
        wrb = sm["wr"].unsqueeze(2).broadcast_to([128, 32, n]); wib = sm["wi"].unsqueeze(2).broadcast_to([128, 32, n])
        V(lambda e, n=n, wrb=wrb: e.tensor_tensor(out=tA[:, :, :n], in0=Er[:, :, :n], in1=wrb, op=ALU.mult), K, K)
        V(lambda e, n=n, wib=wib: e.tensor_tensor(out=tB[:, :, :n], in0=Ei[:, :, :n], in1=wib, op=ALU.mult), K, K)
        V(lambda e, n=n: e.tensor_tensor(out=Er[:, :, n:2 * n], in0=tA[:, :, :n], in1=tB[:, :, :n], op=ALU.subtract), K, K)
        V(lambda e, n=n, wib=wib: e.tensor_tensor(out=tA[:, :, :n], in0=Er[:, :, :n], in1=wib, op=ALU.mult), K, K)
        V(lambda e, n=n, wrb=wrb: e.tensor_tensor(out=tB[:, :, :n], in0=Ei[:, :, :n], in1=wrb, op=ALU.mult), K, K)
        V(lambda e, n=n: e.tensor_tensor(out=Ei[:, :, n:2 * n], in0=tA[:, :, :n], in1=tB[:, :, :n], op=ALU.add), K, K)
        V(lambda e: e.tensor_tensor(out=sm["t1"], in0=sm["wr"], in1=sm["wr"], op=ALU.mult), K, K)
        V(lambda e: e.tensor_tensor(out=sm["t2"], in0=sm["wi"], in1=sm["wi"], op=ALU.mult), K, K)
        V(lambda e: e.tensor_tensor(out=sm["t3"], in0=sm["wr"], in1=sm["wi"], op=ALU.mult), K, K)
        V(lambda e: e.tensor_tensor(out=sm["wr"], in0=sm["t1"], in1=sm["t2"], op=ALU.subtract), K, K)
        V(lambda e: e.tensor_scalar(out=sm["wi"], in0=sm["t3"], scalar1=2.0, scalar2=None, op0=ALU.mult), K, K)
    V(lambda e: e.tensor_copy(out=sm["w128r"], in_=sm["wr"]), K, K)
    V(lambda e: e.tensor_copy(out=sm["w128i"], in_=sm["wi"]), K, K)
    V(lambda e: e.memset(cR, 0.0), K, K)
    V(lambda e: e.memset(SL, 0.0), K, K)
    fw.barrier()
    ar.reset(m0)
    BTc = ar.alloc("BTc", [128, 32, 2, 16], F32); Cc = ar.alloc("Cc", [128, 2, 32, 16], F32)
    Cfc = ar.alloc("Cfc", [128, 32, 2, 16], F32); Xc = ar.alloc("Xc", [128, 32, 2, 16], F32)
    c1 = ar.alloc("c1", [128, 32, 16], F32); c2 = ar.alloc("c2", [128, 32, 16], F32)
    Cfp = ar.alloc("Cfp", [128, 32, 2, 128], BF16)
    padb = ar.alloc("padb", [128, 32, 2, 128], BF16); DBsb = ar.alloc("DBsb", [128, 32, 2, 128], BF16)
    fw.dma("sync", BTc, btc, writes=["BTc"])
    fw.dma("sync", Cc, cc.rearrange("a p g c -> p a g c"), writes=["Cc"])
    G_(lambda e: e.memset(padb, 0.0), [], ["padb"])
    G_(lambda e: e.memset(Cfp, 0.0), [], ["Cfp"])

    def cmul_compact(dst, src_r, src_i, sr, si, rk, wk, neg_im=False):
        srb = sr.unsqueeze(2).broadcast_to([128, 32, 16]); sib = si.unsqueeze(2).broadcast_to([128, 32, 16])
        V(lambda e: e.tensor_tensor(out=c1, in0=src_r, in1=srb, op=ALU.mult), rk, ["c1"])
        V(lambda e: e.tensor_tensor(out=c2, in0=src_i, in1=sib, op=ALU.mult), rk, ["c2"])
        V(lambda e: e.tensor_tensor(out=dst[:, :, 0, :], in0=c1, in1=c2, op=ALU.subtract), ["c1", "c2"], wk)
        V(lambda e: e.tensor_tensor(out=c1, in0=src_r, in1=sib, op=ALU.mult), rk + wk, ["c1"])
        V(lambda e: e.tensor_tensor(out=c2, in0=src_i, in1=srb, op=ALU.mult), rk + wk, ["c2"])
        if neg_im:
            V(lambda e: e.scalar_tensor_tensor(out=dst[:, :, 1, :], in0=c1, scalar=-1.0, in1=c2, op0=ALU.mult, op1=ALU.subtract), ["c1", "c2"], wk)
        else:
            V(lambda e: e.tensor_tensor(out=dst[:, :, 1, :], in0=c1, in1=c2, op=ALU.add), ["c1", "c2"], wk)

    def scatter(dst_pad, src_c, rk, wk):
        for g2 in range(2):
            for q in range(4):
                blk = 2 * q + g2
                G_(lambda e, g2=g2, q=q, blk=blk: e.tensor_copy(out=dst_pad[g2 * 64:(g2 + 1) * 64, q::4, :, blk * 16:(blk + 1) * 16],
                                                                 in_=src_c[g2 * 64:(g2 + 1) * 64, q::4, :, :]), rk, wk)

    cmul_compact(Cfc, Cc[:, 0], Cc[:, 1], sm["fr"], sm["fi"], ["Cc"], ["Cfc"])
    cmul_compact(Xc, Cc[:, 0], Cc[:, 1], sm["fr"], sm["fi"], ["Cc"], ["Xc"], neg_im=True)
    scatter(Cfp, Xc, ["Xc"], ["Cfp"])
    for k in range(8):
        j = 7 - k
        cmul_compact(Xc, BTc[:, :, 0, :], BTc[:, :, 1, :], pwr[k], pwi[k], ["BTc"], ["Xc"])
        scatter(padb, Xc, ["Xc"], ["padb"])
        for r in range(8):
            p, pk = psum()
            n_ = 0
            for gl in range(4):
                for ri in range(2):
                    mm(p[:, 0:128], padb[:, 4 * r + gl, ri, :], Cfp[:, 4 * r + gl, ri, :], n_ == 0, n_ == 7, ["padb", "Cfp"], pk, n_ == 7)
                    n_ += 1
            if k == 0:
                V(lambda e, p=p, r=r: e.scalar_tensor_tensor(out=Kpad[:, r, 0, :], in0=identf, scalar=dcs[:, r:r + 1], in1=p[:, 0:128], op0=ALU.mult, op1=ALU.add),
                  [pk, "identf", "dcs"], ["Kpad"])
            else:
                V(lambda e, p=p, r=r, k=k: e.tensor_copy(out=Kpad[:, r, k, :], in_=p[:, 0:128]), [pk], ["Kpad"])
        for g4 in range(16):
            p, pk = psum()
            for q in range(4):
                gi = g4 * 4 + q
                mm(p[:, q * 128:(q + 1) * 128], padb[:, gi // 2, gi % 2, :], identb, True, True, ["padb", "identb"], pk, q == 3)
            V(lambda e, p=p, g4=g4: e.tensor_copy(out=DBsb.rearrange("p g r s -> p (g r) s")[:, g4 * 4:(g4 + 1) * 4, :], in_=p.rearrange("p (a c) -> p a c", c=128)),
              [pk], ["DBsb"])
        fw.dma("sync", DB_s[:, :, j].rearrange("r p g a s -> p r g a s"), DBsb.rearrange("p (r g) a s -> p r g a s", r=8), reads=["DBsb"], writes=[("DB_s", j)])
    for j in range(8):
        cmul_compact(Xc, Cfc[:, :, 0, :], Cfc[:, :, 1, :], pwr[j + 1], pwi[j + 1], ["Cfc"], ["Xc"])
        scatter(padb, Xc, ["Xc"], ["padb"])
        fw.dma("sync", EC_s[:, :, j].rearrange("r p g a s -> p r g a s"), padb.rearrange("p (r g) a s -> p r g a s", r=8), reads=["padb"], writes=[("EC_s", j)])
    fw.barrier()
    ar.reset(base_ssm)

    if upto <= 2:
        fw.emit()
        return nc
    NPS, NMS = NP // 1024, NM // 1024
    DBr2 = [ar.alloc("DBr", [128, 8, 4, 2, 128], BF16) for _ in range(2)]
    ECr2 = [ar.alloc("ECr", [128, 8, 4, 2, 128], BF16) for _ in range(2)]
    uTr = [ar.alloc("uTr", [128, 1024], BF16) for _ in range(2)]
    uTj = [ar.alloc("uTj", [128, 8, 128], BF16) for _ in range(2)]
    zTr = [ar.alloc("zTr", [128, 1024], BF16) for _ in range(2)]
    RB = []
    for b in range(2):
        d = {n: ar.alloc(n, [128, 4, 128], F32) for n in ["t1", "t2", "t3", "t4", "Rr", "Ri"]}
        d["Xr"], d["Xi"] = d["t1"], d["t3"]
        d["Sr"] = ar.alloc("Sr", [128, 4, 130], BF16); d["Si"] = ar.alloc("Si", [128, 4, 130], BF16)
        for n in ["c1", "c2", "c3", "c4"]:
            d[n] = ar.alloc(n, [128, 4], F32)
        RB.append(d)
    ysb = [ar.alloc("ysb", [128, 1024], F32) for _ in range(2)]
    g1b = [ar.alloc("g1b", [128, 1024], F32) for _ in range(2)]
    g2b = g1b
    rcount = 0
    hcount = 0
    def round_gen(rp, st, q_):
        r = 2 * rp + q_
        gsl = slice(4 * r, 4 * r + 4)
        DBr, ECr = DBr2[q_], ECr2[q_]
        kDB, kEC = f"DBr{q_}", f"ECr{q_}"
        is_main = st >= NPS
        ub = q_
        fw.dma("sync", uTr[ub].rearrange("p (n t) -> p n t", t=128), uT_s[8 * st:8 * st + 8, :, r, :].rearrange("n p t -> p n t"),
               writes=[f"uTr{ub}"])
        fw.flush()
        A_(lambda e, ub=ub: e.activation(out=uTj[ub], in_=uTr[ub].rearrange("p (c j) -> p j c", j=8), func=AF.Copy), [f"uTr{ub}"], [f"uTj{ub}"])
        yield
        b = q_
        B = RB[b]
        kb = lambda n, b=b: f"{n}{b}"
        pXr, pkr = psum()
        pXi, pki = psum()
        for ri, (pX, pk) in enumerate(((pXr, pkr), (pXi, pki))):
            for gl in range(4):
                for j in range(8):
                    mm(pX[:, gl * 128:(gl + 1) * 128], DBr[:, j, gl, ri, :], uTj[ub][:, j, :], j == 0, j == 7,
                       [f"uTj{ub}", kDB], pk, (gl == 3 and j == 7))
        yield
        pXr3 = pXr.rearrange("p (a c) -> p a c", c=128); pXi3 = pXi.rearrange("p (a c) -> p a c", c=128)
        Erg, Eig = Er[:, gsl, :], Ei[:, gsl, :]
        V(lambda e, B=B, a=pXr3, t=Erg: e.tensor_tensor(out=B["t1"], in0=a, in1=t, op=ALU.mult), [pkr], [kb("t1")])
        V(lambda e, B=B, a=pXi3, t=Eig: e.tensor_tensor(out=B["t2"], in0=a, in1=t, op=ALU.mult), [pki], [kb("t2")])
        V(lambda e, B=B, a=pXi3, t=Erg: e.tensor_tensor(out=B["t3"], in0=a, in1=t, op=ALU.mult), [pki], [kb("t3")])
        V(lambda e, B=B, a=pXr3, t=Eig: e.tensor_tensor(out=B["t4"], in0=a, in1=t, op=ALU.mult), [pkr], [kb("t4")])
        yield
        G_(lambda e, B=B: e.tensor_tensor(out=B["t1"], in0=B["t1"], in1=B["t2"], op=ALU.add), [kb("t1"), kb("t2")], [kb("t1")])
        G_(lambda e, B=B: e.tensor_tensor(out=B["t3"], in0=B["t3"], in1=B["t4"], op=ALU.subtract), [kb("t3"), kb("t4")], [kb("t3")])
        yield
        for gl in range(4):
            gp = 4 * r + gl
            for nm, xs, ci in (("Rr", "t1", 0), ("Ri", "t3", 1)):
                V(lambda e, B=B, gl=gl, gp=gp, nm=nm, xs=xs, ci=ci: e.tensor_tensor_scan(
                    out=B[nm][:, gl, :], data0=sm["rho"][:, gp:gp + 1].broadcast_to([128, 128]), data1=B[xs][:, gl, :],
                    initial=cR[:, ci, gp:gp + 1], op0=ALU.mult, op1=ALU.add), [kb(xs), ("cR", r)], [kb(nm)])
        yield
        if is_main:
            G_(lambda e, B=B, gsl=gsl: e.tensor_copy(out=B["Sr"][:, :, 0], in_=SL[:, 0, gsl]), [("SL", r)], [kb("Sr")])
            G_(lambda e, B=B, gsl=gsl: e.tensor_copy(out=B["Si"][:, :, 0], in_=SL[:, 1, gsl]), [("SL", r)], [kb("Si")])
        wr4, wi4 = sm["w128r"][:, gsl], sm["w128i"][:, gsl]
        er7, ei7 = Er[:, gsl, 127], Ei[:, gsl, 127]
        Rr7, Ri7 = B["Rr"][:, :, 127], B["Ri"][:, :, 127]
        for (xr_, xi_, dst, negim, key) in ((wr4, wi4, cR, False, "cR"), (er7, ei7, SL, True, "SL")):
            G_(lambda e, B=B, a=Rr7, w=xr_: e.tensor_tensor(out=B["c1"], in0=a, in1=w, op=ALU.mult), [kb("Rr")], [kb("c1")])
            G_(lambda e, B=B, a=Ri7, w=xi_: e.tensor_tensor(out=B["c2"], in0=a, in1=w, op=ALU.mult), [kb("Ri")], [kb("c2")])
            G_(lambda e, B=B, a=Ri7, w=xr_: e.tensor_tensor(out=B["c3"], in0=a, in1=w, op=ALU.mult), [kb("Ri")], [kb("c3")])
            G_(lambda e, B=B, a=Rr7, w=xi_: e.tensor_tensor(out=B["c4"], in0=a, in1=w, op=ALU.mult), [kb("Rr")], [kb("c4")])
            G_(lambda e, B=B, dst=dst, gsl=gsl: e.tensor_tensor(out=dst[:, 0, gsl], in0=B["c1"], in1=B["c2"], op=ALU.subtract),
               [kb("c1"), kb("c2")], [(key, r)])
            if negim:
                V(lambda e, B=B, dst=dst, gsl=gsl: e.scalar_tensor_tensor(out=dst[:, 1, gsl], in0=B["c3"], scalar=-1.0, in1=B["c4"], op0=ALU.mult, op1=ALU.subtract),
                   [kb("c3"), kb("c4")], [(key, r)])
            else:
                G_(lambda e, B=B, dst=dst, gsl=gsl: e.tensor_tensor(out=dst[:, 1, gsl], in0=B["c3"], in1=B["c4"], op=ALU.add),
                   [kb("c3"), kb("c4")], [(key, r)])
        yield
        if not is_main:
            return
        G_(lambda e, B=B, t=Erg: e.tensor_tensor(out=B["t1"], in0=B["Rr"], in1=t, op=ALU.mult), [kb("Rr")], [kb("t1")])
        G_(lambda e, B=B, t=Eig: e.tensor_tensor(out=B["t2"], in0=B["Ri"], in1=t, op=ALU.mult), [kb("Ri")], [kb("t2")])
        yield
        V(lambda e, B=B, t=Erg: e.tensor_tensor(out=B["t3"], in0=B["Ri"], in1=t, op=ALU.mult), [kb("Ri")], [kb("t3")])
        V(lambda e, B=B, t=Eig: e.tensor_tensor(out=B["t4"], in0=B["Rr"], in1=t, op=ALU.mult), [kb("Rr")], [kb("t4")])
        yield
        V(lambda e, B=B: e.tensor_tensor(out=B["Sr"][:, :, 1:129], in0=B["t1"], in1=B["t2"], op=ALU.subtract), [kb("t1"), kb("t2")], [kb("Sr")])
        V(lambda e, B=B: e.scalar_tensor_tensor(out=B["Si"][:, :, 1:129], in0=B["t3"], scalar=-1.0, in1=B["t4"], op0=ALU.mult, op1=ALU.subtract),
          [kb("t3"), kb("t4")], [kb("Si")])
        zb = q_
        hb = q_
        yield
        for h2 in range(2):
            py, pky = psum()
            for j4 in range(4):
                j = 4 * h2 + j4
                o = py[:, j4 * 128:(j4 + 1) * 128]
                nmm = (j + 1) + 8
                n_ = 0
                for k in range(j + 1):
                    mm(o, Kpad[:, r, k, :], uTj[ub][:, j - k, :], n_ == 0, n_ == nmm - 1, [f"uTj{ub}", "Kpad"], pky, False)
                    n_ += 1
                for gl in range(4):
                    for ri, Sn in enumerate(("Sr", "Si")):
                        mm(o, ECr[:, j, gl, ri, :], B[Sn][:, gl, 0:128], n_ == 0, n_ == nmm - 1, [kb(Sn), kEC], pky,
                           (j4 == 3 and n_ == nmm - 1))
                        n_ += 1
            yield
            A_(lambda e, py=py, hb=hb, h2=h2: e.activation(out=ysb[hb].rearrange("p (c j) -> p c j", j=8)[:, :, 4 * h2:4 * h2 + 4],
                                                           in_=py.rearrange("p (j c) -> p c j", c=128), func=AF.Identity), [pky], [f"ysb{hb}"])
        yield
        G_(lambda e, hb=hb: e.tensor_tensor(out=g1b[hb], in0=ysb[hb], in1=ysb[hb], op=ALU.mult), [f"ysb{hb}"], [f"g1b{hb}"])
        G_(lambda e, hb=hb: e.tensor_scalar(out=g1b[hb], in0=g1b[hb], scalar1=0.044715, scalar2=1.0, op0=ALU.mult, op1=ALU.add), [f"g1b{hb}"], [f"g1b{hb}"])
        G_(lambda e, hb=hb: e.tensor_tensor(out=g1b[hb], in0=g1b[hb], in1=ysb[hb], op=ALU.mult), [f"g1b{hb}", f"ysb{hb}"], [f"g1b{hb}"])
        yield
        A_(lambda e, hb=hb: e.activation(out=g2b[hb], in_=g1b[hb], func=AF.Sigmoid, scale=1.5957691216057308), [f"g1b{hb}"], [f"g2b{hb}"])
        yield
        V(lambda e, hb=hb, zb=zb: e.tensor_tensor(out=zTr[zb], in0=ysb[hb], in1=g2b[hb], op=ALU.mult), [f"ysb{hb}", f"g2b{hb}"], [f"zTr{zb}"])
        m8 = 8 * (st - NPS)
        fw.defer_dma("sync", zT_s[m8:m8 + 8, :, r, :].rearrange("n p t -> p n t"), zTr[zb].rearrange("p (n t) -> p n t", t=128),
               reads=[f"zTr{zb}"], writes=[("zT_s", st, r)])

    for rp in range(4):
        for q_ in range(2):
            fw.dma("sync", DBr2[q_], DB_s[2 * rp + q_], writes=[f"DBr{q_}"])
            fw.dma("sync", ECr2[q_], EC_s[2 * rp + q_], writes=[f"ECr{q_}"])
        for st in range(NPS + NMS):
            gens = [round_gen(rp, st, 0), round_gen(rp, st, 1)]
            alive = True
            while alive:
                alive = False
                for g_ in gens:
                    try:
                        next(g_)
                        alive = True
                    except StopIteration:
                        pass
    fw.barrier()
    ar.reset(base_persist)

    if upto <= 3:
        fw.emit()
        return nc
    Wg = ar.alloc("Wg", [128, 8, 2048], BF16); Wo = ar.alloc("Wo", [128, 8, 1024], BF16)
    load_weights(Wg, w_glu, 4, 8, "Wg")
    load_weights(Wo, w_out, 2, 8, "Wo")
    kme = ar.alloc("kme", [128, 2, 128], BF16); vme = ar.alloc("vme", [128, 4, 65], BF16)
    fw.dma("sync", kme, kT_s[0], writes=["kme"]); fw.dma("sync", vme, v_s[0], writes=["vme"])
    qTl = [ar.alloc("qTl", [128, 8, 128], BF16) for _ in range(2)]
    kTl = [ar.alloc("kTl", [128, 2, 128], BF16) for _ in range(3)]
    vl = [ar.alloc("vl", [128, 4, 65], BF16) for _ in range(3)]
    gl_ = [ar.alloc("gl", [128, 2048], BF16) for _ in range(2)]
    zTl = [ar.alloc("zTl", [128, 8, 128], BF16) for _ in range(2)]
    xr = [ar.alloc("xr", [128, D], F32) for _ in range(2)]
    Pc = [ar.alloc("Pc", [128, 512], BF16) for _ in range(2)]
    Pp = [ar.alloc("Pp", [128, 512], BF16) for _ in range(2)]
    Pm = [ar.alloc("Pm", [128, 512], BF16) for _ in range(2)]
    den_ = [ar.alloc("den", [128, 4], F32) for _ in range(2)]
    for b_ in range(2):
        V(lambda e, b_=b_: e.memset(Pm[b_], 0.0), [], [f"Pm{b_}"])
    attn_ = [ar.alloc("attn", [128, D], F32) for _ in range(2)]; An_ = [ar.alloc("An", [128, D], F32) for _ in range(2)]
    sig_ = [ar.alloc("sig", [128, 512], F32) for _ in range(2)]; ssm_ = [ar.alloc("ssm", [128, D], F32) for _ in range(2)]
    Bn_ = [ar.alloc("Bn", [128, D], F32) for _ in range(2)]
    mg_ = [ar.alloc("mg", [128, D], BF16) for _ in range(2)]; mgT_ = [ar.alloc("mgT", [128, 8, 128], BF16) for _ in range(2)]
    h1 = [ar.alloc("h1", [128, D], F32) for _ in range(2)]
    fw.dma("sync", kTl[1], kT_s[1], writes=["kTl1"]); fw.dma("sync", vl[1], v_s[1], writes=["vl1"])
    def s3_loads(i):
        b = i % 2
        jc = 2 + i
        sc = jc % 3
        fw.dma("sync", kTl[sc], kT_s[jc], reads=[("kT_s", jc)], writes=[f"kTl{sc}"])
        fw.dma("sync", vl[sc], v_s[jc], reads=[("v_s", jc)], writes=[f"vl{sc}"])
        fw.dma("sync", qTl[b], qT_s[i], writes=[f"qTl{b}"])
        fw.dma("sync", gl_[b], g_s[i], writes=[f"gl{b}"])
        fw.dma("sync", zTl[b], zT_s[i], writes=[f"zTl{b}"])
        fw.dma("sync", xr[b], xmain[i * 128:(i + 1) * 128, :], writes=[f"xr{b}"])
        fw.flush()

    def s3_A(n):
        i, grp = n // 4, n % 4
        b, pb = i % 2, n % 2
        sc, sp = (2 + i) % 3, (1 + i) % 3
        bs, kc = (grp % 2) * 64, grp // 2
        qsel = qTl[b][bs:bs + 64, kc * 4:(kc + 1) * 4, :]
        pS, pkS = psum()
        mm(pS, kTl[sc][bs:bs + 64, kc, :], qsel, True, True, [f"kTl{sc}", f"qTl{b}"], pkS, True)
        A_(lambda e: e.activation(out=Pc[pb], in_=pS, func=AF.Exp, scale=0.125), [pkS], [f"Pc{pb}"])
        G_(lambda e: e.tensor_tensor(out=Pc[pb], in0=Pc[pb], in1=maskb[:, 0, :], op=ALU.mult), [f"Pc{pb}", "maskb"], [f"Pc{pb}"])
        pS2, pkS2 = psum()
        mm(pS2, kTl[sp][bs:bs + 64, kc, :], qsel, True, True, [f"kTl{sp}", f"qTl{b}"], pkS2, True)
        A_(lambda e: e.activation(out=Pp[pb], in_=pS2, func=AF.Exp, scale=0.125), [pkS2], [f"Pp{pb}"])
        mi = 2 if i == 0 else 1
        G_(lambda e: e.tensor_tensor(out=Pp[pb], in0=Pp[pb], in1=maskb[:, mi, :], op=ALU.mult), [f"Pp{pb}", "maskb"], [f"Pp{pb}"])
        pS3, pkS3 = psum()
        mm(pS3[0:16, :], kme[bs:bs + 64, kc, 0:16], qsel, True, True, ["kme", f"qTl{b}"], pkS3, True)
        A_(lambda e: e.activation(out=Pm[pb][0:16, :], in_=pS3[0:16, :], func=AF.Exp, scale=0.125), [pkS3], [f"Pm{pb}"])

    def s3_B(n):
        i, grp = n // 4, n % 4
        b, pb = i % 2, n % 2
        sc, sp = (2 + i) % 3, (1 + i) % 3
        attn, kA = attn_[b], f"attn{b}"
        pO, pkO = psum()
        for r in range(4):
            o = pO[:, r * 65:(r + 1) * 65]
            mm(o, Pm[pb][:, r * 128:(r + 1) * 128], vme[:, grp, :], True, False, [f"Pm{pb}", "vme"], pkO, False)
            mm(o, Pp[pb][:, r * 128:(r + 1) * 128], vl[sp][:, grp, :], False, False, [f"Pp{pb}", f"vl{sp}"], pkO, False)
            mm(o, Pc[pb][:, r * 128:(r + 1) * 128], vl[sc][:, grp, :], False, True, [f"Pc{pb}", f"vl{sc}"], pkO, r == 3)
        pO3 = pO[:, 0:260].rearrange("p (r c) -> p r c", c=65)
        den = den_[pb]
        kd = f"den{pb}"
        V(lambda e: e.tensor_tensor(out=den, in0=pO3[:, :, 64], in1=esink[:, grp * 4:(grp + 1) * 4], op=ALU.add), [pkO, "esink"], [kd])
        V(lambda e: e.reciprocal(out=den, in_=den), [kd], [kd])
        V(lambda e: e.tensor_tensor(out=attn[:, grp * 256:(grp + 1) * 256].rearrange("p (r d) -> p r d", d=64), in0=pO3[:, :, 0:64],
                                    in1=den.unsqueeze(2).broadcast_to([128, 4, 64]), op=ALU.mult), [pkO, kd], [kA])

    def s3_tail(i):
        b = i % 2
        attn, An, ssm, Bn, mg, mgT = attn_[b], An_[b], ssm_[b], Bn_[b], mg_[b], mgT_[b]
        kA, kAn, kss, kBn, kmg, kmT = f"attn{b}", f"An{b}", f"ssm{b}", f"Bn{b}", f"mg{b}", f"mgT{b}"
        rms_scale(attn, 1, An, [kA], [kAn])
        G_(lambda e: e.tensor_tensor(out=An, in0=An, in1=gl_[b][:, 0:1024], op=ALU.mult), [kAn, f"gl{b}"], [kAn])
        for half in range(2):
            pa, pka = psum()
            for k in range(8):
                mm(pa, zTl[b][:, k, :], Wg[:, k, half * 512:(half + 1) * 512], k == 0, k == 7, [f"zTl{b}", "Wg"], pka, k == 7)
            pz, pkz = psum()
            for k in range(8):
                mm(pz, zTl[b][:, k, :], Wg[:, k, 1024 + half * 512:1024 + (half + 1) * 512], k == 0, k == 7, [f"zTl{b}", "Wg"], pkz, k == 7)
            sig = sig_[half]
            A_(lambda e, pz=pz, sig=sig: e.activation(out=sig, in_=pz, func=AF.Sigmoid), [pkz], [f"sig{half}"])
            V(lambda e, pa=pa, half=half, sig=sig: e.tensor_tensor(out=ssm[:, half * 512:(half + 1) * 512], in0=pa, in1=sig, op=ALU.mult), [pka, f"sig{half}"], [kss])
        rms_scale(ssm, 2, Bn, [kss], [kBn])
        G_(lambda e: e.tensor_tensor(out=Bn, in0=Bn, in1=gl_[b][:, 1024:2048], op=ALU.mult), [kBn, f"gl{b}"], [kBn])
        V(lambda e: e.tensor_tensor(out=mg, in0=An, in1=Bn, op=ALU.add), [kAn, kBn], [kmg])
        transpose8(mg, mgT, kmg, kmT)
        for half in range(2):
            p, pk = psum()
            for k in range(8):
                mm(p, mgT[:, k, :], Wo[:, k, half * 512:(half + 1) * 512], k == 0, k == 7, [kmT, "Wo"], pk, k == 7)
            V(lambda e, p=p, half=half: e.tensor_tensor(out=h1[b][:, half * 512:(half + 1) * 512], in0=p, in1=xr[b][:, half * 512:(half + 1) * 512], op=ALU.add),
              [pk, f"xr{b}"], [f"h1{b}"])
        fw.defer_dma("sync", h1_s[i * 128:(i + 1) * 128, :], h1[b], reads=[f"h1{b}"], writes=[("h1_s", i)])

    NG = 4 * TM_
    s3_loads(0)
    s3_A(0)
    for n in range(NG):
        if n + 1 < NG:
            if (n + 1) % 4 == 0:
                s3_loads((n + 1) // 4)
            s3_A(n + 1)
        s3_B(n)
        if n % 4 == 3:
            s3_tail(n // 4)
    fw.barrier()
    ar.reset(base_persist)

    if upto <= 4:
        fw.emit()
        return nc
    W1 = ar.alloc("W1", [128, 8, 5632], BF16); W2 = ar.alloc("W2", [128, 22, 1024], BF16)
    load_weights(W1, w_f1, 11, 8, "W1")
    load_weights(W2, w_f2, 2, 22, "W2")
    GT = 4
    hl = [ar.alloc("hl", [128, D], F32) for _ in range(2)]
    hn = [ar.alloc("hn", [128, D], BF16) for _ in range(2)]
    hnT = ar.alloc("hnT", [128, 8, GT * 128], BF16)
    sg = [ar.alloc("sg", [128, 512], F32) for _ in range(2)]
    actT = ar.alloc("actT", [128, 22, GT * 128], BF16)
    hres = hl
    ob = junk
    tcount = 0
    for g in range(TM_ // GT):
        for t4 in range(GT):
            i = g * GT + t4
            b = tcount % 2
            tcount += 1
            fw.dma("sync", hl[b], h1_s[i * 128:(i + 1) * 128, :], reads=[("h1_s", i)], writes=[f"hl{b}"])
            fw.flush()
            rms_scale(hl[b], 3, hn[b], [f"hl{b}"], [f"hn{b}"])
            for half in range(2):
                p, pk = psum()
                for jj in range(4):
                    c = half * 4 + jj
                    mm(p[:, jj * 128:(jj + 1) * 128], hn[b][:, c * 128:(c + 1) * 128], identb, True, True, [f"hn{b}", "identb"], pk, jj == 3)
                V(lambda e, p=p, half=half, t4=t4: e.tensor_copy(out=hnT[:, half * 4:half * 4 + 4, t4 * 128:(t4 + 1) * 128],
                                                               in_=p.rearrange("p (a c) -> p a c", c=128)), [pk], [("hnT", t4)])
        hk = [("hnT", t4) for t4 in range(GT)]
        for fc in range(22):
            fp, q2 = fc // 2, fc % 2
            pg, pkg = psum()
            for k in range(8):
                mm(pg, W1[:, k, fp * 512 + q2 * 256:fp * 512 + q2 * 256 + 128], hnT[:, k, :], k == 0, k == 7, hk + ["W1"], pkg, k == 7)
            pu, pku = psum()
            for k in range(8):
                mm(pu, W1[:, k, fp * 512 + q2 * 256 + 128:fp * 512 + q2 * 256 + 256], hnT[:, k, :], k == 0, k == 7, hk + ["W1"], pku, k == 7)
            s_ = sg[fc % 2]
            ks_ = f"sg{fc % 2}"
            A_(lambda e, pg=pg, s_=s_: e.activation(out=s_, in_=pg, func=AF.Sigmoid), [pkg], [ks_])
            V(lambda e, pg=pg, s_=s_: e.tensor_tensor(out=s_, in0=pg, in1=s_, op=ALU.mult), [pkg, ks_], [ks_])
            V(lambda e, pu=pu, s_=s_, fc=fc: e.tensor_tensor(out=actT[:, fc, :], in0=pu, in1=s_, op=ALU.mult), [pku, ks_], [("actT", fc)])
        ak = [("actT", fc) for fc in range(22)]
        for t4 in range(GT):
            i = g * GT + t4
            b = t4 % 2
            fw.dma("sync", hres[b], h1_s[i * 128:(i + 1) * 128, :], reads=[("h1_s", i)], writes=[f"hl{b}"])
            fw.flush()
            for half in range(2):
                p, pk = psum()
                for k in range(22):
                    mm(p, actT[:, k, t4 * 128:(t4 + 1) * 128], W2[:, k, half * 512:(half + 1) * 512], k == 0, k == 21, ak + ["W2"], pk, k == 21)
                V(lambda e, p=p, half=half, b=b: e.tensor_tensor(out=ob[b][:, half * 512:(half + 1) * 512], in0=p, in1=hres[b][:, half * 512:(half + 1) * 512], op=ALU.add),
                  [pk, f"hl{b}"], [f"junk{b}"])
            fw.defer_dma("sync", out[i * 128:(i + 1) * 128, :], ob[b], reads=[f"junk{b}"], writes=[("out", i)])
    fw.emit()
    return nc


def _panels(w, kk):
    n = w.shape[1] // 512
    return np.ascontiguousarray(w.reshape(kk, 128, n, 512).transpose(2, 1, 0, 3))


def prep_shared(inp):
    f = lambda a: np.asarray(a, dtype=np.float32)
    w_in = f(inp["w_in"])[0]
    qcols = []
    for j in range(8):
        for s in range(2):
            head = ((j // 4) * 2 + s) * 4 + (j % 4)
            qcols.extend(range(head * 64, head * 64 + 64))
    w_in_r = np.concatenate([w_in[:, qcols], w_in[:, 1024:1536], w_in[:, 1536:]], axis=1)
    wf1 = f(inp["w_ffn_in"])[0]
    cols = []
    for c in range(22):
        cols.extend(range(c * 128, (c + 1) * 128))
        cols.extend(range(DFF + c * 128, DFF + (c + 1) * 128))
    wf1_r = wf1[:, cols]
    rep = lambda v, n: np.ascontiguousarray(np.broadcast_to(f(v).reshape(1, -1), (128, n)))
    gains = np.stack([rep(inp["norm_mix"][0], D), rep(inp["attn_branch_norm"][0], D), rep(inp["ssm_branch_norm"][0], D), rep(inp["norm_ffn"][0], D)])

    def sp(a):
        return np.ascontiguousarray(f(a).reshape(32, 2, 64).transpose(1, 2, 0).reshape(128, 32))

    lam = np.stack([sp(inp["lam_re"][0]), sp(inp["lam_im"][0]), sp(np.broadcast_to(f(inp["log_dt"])[0][:, None], (64, 64)))])
    bre, bim = f(inp["ssm_b_re"])[0], f(inp["ssm_b_im"])[0]
    def spc(a):
        return a.reshape(32, 2, 64, a.shape[-1]).transpose(1, 2, 0, 3).reshape(128, 32, a.shape[-1])
    btc = np.ascontiguousarray(np.stack([spc(bre), spc(bim)], axis=2))
    cre, cim = f(inp["ssm_c_re"])[0], f(inp["ssm_c_im"])[0]
    cc = np.ascontiguousarray(np.stack([spc(cre.transpose(0, 2, 1)), spc(cim.transpose(0, 2, 1))]))
    kk, qq = np.arange(128)[:, None], np.arange(128)[None, :]
    mcur = np.where(kk <= qq, 1.0, 0.0).astype(np.float32)
    mprev = np.where(kk > qq, 1.0, 0.0).astype(np.float32)
    return dict(
        w_in=_panels(w_in_r, 8), w_glu=_panels(f(inp["w_glu"])[0], 8), w_out=_panels(f(inp["w_out"])[0], 8),
        w_f1=_panels(wf1_r, 8), w_f2=_panels(f(inp["w_ffn_out"])[0], 22), gains=gains,
        gq=rep(np.tile(f(inp["q_norm"])[0], 4), 256), gk=rep(np.tile(f(inp["k_norm"])[0], 4), 256),
        sinks=rep(inp["attn_sinks"][0], 16), ident=np.eye(128, dtype=np.float32), lam=lam, btc=btc, cc=cc,
        dcol=np.ascontiguousarray(f(inp["ssm_d"])[0].reshape(8, 128).T),
    ), mcur, mprev


def prep_core(x_b, meta, h, NM, NP, mcur, mprev):
    xmain = np.ascontiguousarray(x_b[h * NM:(h + 1) * NM])
    xpre = np.zeros((NP, D), np.float32)
    xctx = np.zeros((256, D), np.float32)
    xctx[0:16] = meta
    if h == 0:
        xpre[NP - 16:] = meta
        m0 = np.zeros((128, 128), np.float32)
    else:
        xpre[1008:1024] = meta
        xpre[1024:] = x_b[0:NM]
        xctx[128:256] = x_b[NM - 128:NM]
        m0 = mprev
    masks = np.stack([np.tile(mcur, (1, 4)), np.tile(mprev, (1, 4)), np.tile(m0, (1, 4))]).astype(np.float32)
    return dict(xmain=xmain, xpre=xpre, xctx=xctx, masks=masks)


_NC_CACHE = {}


def kernel(**inputs):
    x = np.asarray(inputs["x"], dtype=np.float32)
    Bsz, S, _ = x.shape
    NM = S // 2
    NP = NM + 1024
    meta = np.asarray(inputs["meta_tokens"], dtype=np.float32)
    shared, mcur, mprev = prep_shared(inputs)
    in_maps = []
    for b in range(Bsz):
        for h in range(2):
            d = dict(shared)
            d.update(prep_core(x[b], meta, h, NM, NP, mcur, mprev))
            in_maps.append(d)
    nc = build(NM, NP)
    res = run_bass_kernel_spmd(nc, in_maps, core_ids=list(range(len(in_maps))))
    outp = np.zeros((Bsz, S, D), np.float32)
    for b in range(Bsz):
        for h in range(2):
            outp[b, h * NM:(h + 1) * NM] = res.results[2 * b + h]["out"]
    return outp
```

```python
import math
import contextlib
import numpy as np
import concourse.bass as bass
import concourse.mybir as mybir
from concourse.bass_utils import run_bass_kernel_spmd

F32 = mybir.dt.float32
BF16 = mybir.dt.bfloat16
AF = mybir.ActivationFunctionType
ALU = mybir.AluOpType
AX = mybir.AxisListType
ENGS = ("tensor", "vector", "scalar", "gpsimd", "sync")
D = 1024
DFF = 2816
NEG = -30000.0


class FW:
    def __init__(self, nc, n_dma_sems=40):
        self.nc = nc
        self.ops = {e: [] for e in ENGS}
        self.cnt = {e: 0 for e in ENGS}
        self.known = {e: {} for e in ENGS}
        self.last_w = {}
        self.readers = {}
        self.n_dma_sems = n_dma_sems
        self.dma_gen = [0] * n_dma_sems
        self.dma_rr = 0
        self.sem_names = [f"s_{e}" for e in ENGS] + [f"d_{i}" for i in range(n_dma_sems)]

    def _deps(self, reads, writes):
        evs = []
        for k in reads:
            if k in self.last_w:
                evs.append(self.last_w[k])
        for k in writes:
            if k in self.last_w:
                evs.append(self.last_w[k])
            evs.extend(self.readers.get(k, ()))
        return evs

    def _commit(self, ev, reads, writes):
        for k in reads:
            self.readers.setdefault(k, []).append(ev)
        for k in writes:
            self.last_w[k] = ev
            self.readers[k] = []

    def _waits(self, eng, evs):
        best = {}
        for (s, v) in evs:
            if v > best.get(s, 0):
                best[s] = v
        out = []
        kn = self.known[eng]
        for s, v in best.items():
            if eng == "tensor" and s == "s_tensor":
                continue
            if kn.get(s, 0) >= v:
                continue
            kn[s] = v
            out.append((s, v))
        return out

    def op(self, eng, fn, reads=(), writes=(), inc=True):
        evs = self._deps(reads, writes)
        waits = self._waits(eng, evs)
        sname = f"s_{eng}"
        ev = (sname, self.cnt[eng] + 1)
        if inc:
            self.cnt[eng] += 1
        self.ops[eng].append((waits, fn, (sname, 1) if inc else None))
        self._commit(ev, reads, writes)
        return ev

    def dma(self, queue, out, in_, reads=(), writes=(), **kw):
        i = self.dma_rr
        self.dma_rr = (self.dma_rr + 1) % self.n_dma_sems
        sname = f"d_{i}"
        evs = self._deps(reads, writes)
        if self.dma_gen[i] > 0:
            evs.append((sname, 16 * self.dma_gen[i]))
        waits = self._waits(queue, evs)
        self.dma_gen[i] += 1
        ev = (sname, 16 * self.dma_gen[i])
        self.ops[queue].append((waits, lambda e: e.dma_start(out=out, in_=in_, **kw), (sname, 16)))
        self._commit(ev, reads, writes)
        return ev

    def defer_dma(self, *a, **kw):
        if not hasattr(self, "_deferred"):
            self._deferred = []
        self._deferred.append((a, kw))

    def flush(self):
        for a, kw in getattr(self, "_deferred", []):
            self.dma(*a, **kw)
        self._deferred = []

    def barrier(self):
        self.flush()
        fin = []
        for e in ENGS:
            if self.cnt[e] > 0:
                fin.append((f"s_{e}", self.cnt[e]))
        for i in range(self.n_dma_sems):
            if self.dma_gen[i] > 0:
                fin.append((f"d_{i}", 16 * self.dma_gen[i]))
        for e in ENGS:
            w = self._waits(e, fin)
            if w:
                self.ops[e].append((w, None, None))
        self.last_w = {}
        self.readers = {}

    def emit(self):
        nc = self.nc
        self.barrier()
        with contextlib.ExitStack() as st:
            sems = {n: st.enter_context(nc.semaphore(n)) for n in self.sem_names}
            block = st.enter_context(nc.Block())

            def mk(engname):
                lst = self.ops[engname]

                def body(eng):
                    for (waits, fn, inc) in lst:
                        for (s, v) in waits:
                            eng.wait_ge(sems[s], v)
                        if fn is None:
                            continue
                        ins = fn(eng)
                        if inc is not None:
                            ins.then_inc(sems[inc[0]], inc[1])
                return body

            block.tensor(mk("tensor"))
            block.vector(mk("vector"))
            block.scalar(mk("scalar"))
            block.gpsimd(mk("gpsimd"))
            block.sync(mk("sync"))


class Arena:
    def __init__(self, nc, base=16640, limit=224 * 1024):
        self.nc, self.off, self.limit, self.n = nc, base, limit, 0

    def alloc(self, name, shape, dt):
        per = int(np.prod(shape[1:])) * (4 if dt == F32 else 2)
        per = (per + 63) // 64 * 64
        assert self.off + per <= self.limit, (name, self.off, per)
        self.n += 1
        t = self.nc.alloc_sbuf_tensor_at(f"{name}_{self.n}_{self.off}", list(shape), dt, offset=self.off)
        self.off += per
        return t.ap()

    def mark(self):
        return self.off

    def reset(self, off):
        self.off = off


def build(NM, NP, upto=9):
    nc = bass.Bass("TRN2", target_bir_lowering=False)
    fw = FW(nc)
    TM_, TP_ = NM // 128, NP // 128
    NS = TP_ + TM_
    NK = 2 + TM_

    def din(name, shape, dt=F32):
        return nc.dram_tensor(name, list(shape), dt, kind="ExternalInput").ap()

    xmain = din("xmain", [NM, D]); xpre = din("xpre", [NP, D]); xctx = din("xctx", [256, D])
    w_in = din("w_in", [9, 128, 8, 512]); w_glu = din("w_glu", [4, 128, 8, 512]); w_out = din("w_out", [2, 128, 8, 512])
    w_f1 = din("w_f1", [11, 128, 8, 512]); w_f2 = din("w_f2", [2, 128, 22, 512])
    gains = din("gains", [4, 128, D])
    gq = din("gq", [128, 256]); gk = din("gk", [128, 256]); sinks = din("sinks", [128, 16])
    masks = din("masks", [3, 128, 512])
    ident = din("ident", [128, 128])
    lam = din("lam", [3, 128, 32])
    btc = din("btc", [128, 32, 2, 16]); cc = din("cc", [2, 128, 32, 16]); dcol = din("dcol", [128, 8])
    out = nc.dram_tensor("out", [NM, D], F32, kind="ExternalOutput").ap()

    def dscr(name, shape, dt):
        return nc.dram_tensor(name, list(shape), dt, kind="Internal").ap()

    uT_s = dscr("uT_s", [NS, 128, 8, 128], BF16); qT_s = dscr("qT_s", [TM_, 128, 8, 128], BF16)
    kT_s = dscr("kT_s", [NK, 128, 2, 128], BF16); v_s = dscr("v_s", [NK, 128, 4, 65], BF16)
    g_s = dscr("g_s", [TM_, 128, 2048], BF16); zT_s = dscr("zT_s", [TM_, 128, 8, 128], BF16)
    h1_s = dscr("h1_s", [NM, D], F32)
    DB_s = dscr("DB_s", [8, 128, 8, 4, 2, 128], BF16); EC_s = dscr("EC_s", [8, 128, 8, 4, 2, 128], BF16)

    ar = Arena(nc)
    identf = ar.alloc("identf", [128, 128], F32); identb = ar.alloc("identb", [128, 128], BF16)
    gsb = ar.alloc("gsb", [128, 4, D], F32)
    fw.dma("sync", identf, ident, writes=["identf"])
    fw.op("vector", lambda e: e.tensor_copy(out=identb, in_=identf), reads=["identf"], writes=["identb"])
    fw.dma("sync", gsb, gains.rearrange("a p d -> p a d"), writes=["gsb"])
    pbank = [nc.alloc_psum_tensor(f"pb{i}", [128, 512], F32).ap() for i in range(8)]
    pcnt = [0]

    def psum():
        i = pcnt[0] % 8
        pcnt[0] += 1
        return pbank[i], f"pb{i}"

    rr = [0]

    def alt():
        rr[0] += 1
        return "vector" if rr[0] % 2 else "gpsimd"

    base0 = ar.mark()

    def load_weights(dst, src, npan, kk, key):
        m = ar.mark()
        nst = 3 if ar.off + 3 * 16384 <= ar.limit else 2
        st = [ar.alloc("wst", [128, 8, 512], F32) for _ in range(nst)]
        cyc = ["vector", "scalar", "gpsimd", "vector", "scalar"]
        n = 0
        for pi in range(npan):
            for k0 in range(0, kk, 8):
                kc = min(8, kk - k0)
                s = st[n % nst]
                fw.dma("sync", s[:, :kc, :], src[pi][:, k0:k0 + kc, :], writes=[f"wst{n % nst}"])
                eng = cyc[n % len(cyc)]
                o_ = dst[:, k0:k0 + kc, pi * 512:(pi + 1) * 512]
                if eng == "scalar":
                    fw.op(eng, lambda e, s=s, kc=kc, o_=o_: e.activation(out=o_, in_=s[:, :kc, :], func=AF.Copy), reads=[f"wst{n % nst}"], writes=[key])
                else:
                    fw.op(eng, lambda e, s=s, kc=kc, o_=o_: e.tensor_copy(out=o_, in_=s[:, :kc, :]), reads=[f"wst{n % nst}"], writes=[key])
                n += 1
        fw.barrier()
        ar.reset(m)

    rmsc = [0]

    def rms_scale(xin, gidx, xn_out, rkeys, wkeys, ncol=D):
        pr = rmsc[0] % 2
        rmsc[0] += 1
        jk, sq_ = junk[pr], ssq[pr]
        kj, ks = f"junk{pr}", f"ssq{pr}"
        fw.op("scalar", lambda e: e.activation(out=jk[:, :ncol], in_=xin, func=AF.Square, accum_out=sq_),
              reads=rkeys, writes=[kj, ks])
        fw.op("vector", lambda e: e.tensor_scalar(out=sq_, in0=sq_, scalar1=1.0 / ncol, scalar2=1e-6, op0=ALU.mult, op1=ALU.add),
              reads=[ks], writes=[ks])
        fw.op("scalar", lambda e: e.activation(out=sq_, in_=sq_, func=AF.Sqrt), reads=[ks], writes=[ks])
        fw.op("vector", lambda e: e.reciprocal(out=sq_, in_=sq_), reads=[ks], writes=[ks])
        fw.op("vector", lambda e: e.scalar_tensor_tensor(out=xn_out, in0=xin, scalar=sq_, in1=gsb[:, gidx, :ncol],
                                                         op0=ALU.mult, op1=ALU.mult),
              reads=list(rkeys) + [ks, "gsb"], writes=wkeys)

    def transpose8(src_bf, dstT, rkey, wkey, n=8):
        for half in range((n + 3) // 4):
            p, pk = psum()
            m = min(4, n - half * 4)
            for j in range(m):
                c = half * 4 + j
                fw.op("tensor", lambda e, c=c, j=j, p=p: e.matmul(p[:, j * 128:(j + 1) * 128], lhsT=src_bf[:, c * 128:(c + 1) * 128],
                                                                 rhs=identb, start=True, stop=True),
                      reads=[rkey, "identb"], writes=[pk], inc=(j == m - 1))
            fw.op("vector", lambda e, p=p, half=half, m=m: e.tensor_copy(
                out=dstT[:, half * 4:half * 4 + m, :], in_=p[:, :m * 128].rearrange("p (a b) -> p a b", b=128)),
                reads=[pk], writes=[wkey])

    junk = [ar.alloc("junk", [128, D], F32) for _ in range(2)]; ssq = [ar.alloc("ssq", [128, 1], F32) for _ in range(2)]
    base1 = ar.mark()

    dbg = {}
    V = lambda fn, r, w: fw.op("vector", fn, reads=r, writes=w)
    A_ = lambda fn, r, w: fw.op("scalar", fn, reads=r, writes=w)
    G_ = lambda fn, r, w: fw.op("gpsimd", fn, reads=r, writes=w)

    def mm(out_ap, lhsT, rhs, start, stop, reads, pk, inc):
        fw.op("tensor", lambda e: e.matmul(out_ap, lhsT=lhsT, rhs=rhs, start=start, stop=stop), reads=reads, writes=[pk], inc=inc)

    dcs = ar.alloc("dcs", [128, 8], F32)
    esink = ar.alloc("esink", [128, 16], F32); gqk = ar.alloc("gqk", [128, 256], F32)
    maskb = ar.alloc("maskb", [128, 3, 512], BF16)
    base_persist = ar.mark()
    st32 = ar.alloc("st32", [128, 8192], F32)
    fw.dma("sync", dcs, dcol, writes=["dcs"])
    fw.dma("sync", st32[:, 0:16], sinks, writes=["a"])
    A_(lambda e: e.activation(out=esink, in_=st32[:, 0:16], func=AF.Exp), ["a"], ["esink"])
    fw.dma("sync", st32[:, 1024:1280], gq, writes=["b"])
    fw.dma("sync", st32[:, 2048:2304], gk, writes=["c"])
    V(lambda e: e.tensor_tensor(out=gqk, in0=st32[:, 1024:1280], in1=st32[:, 2048:2304], op=ALU.mult), ["b", "c"], ["gqk"])
    fw.dma("sync", st32[:, 4096:5632].rearrange("p (a c) -> p a c", a=3), masks.rearrange("a p c -> p a c"), writes=["d"])
    V(lambda e: e.tensor_copy(out=maskb, in_=st32[:, 4096:5632].rearrange("p (a c) -> p a c", a=3)), ["d"], ["maskb"])
    fw.barrier()
    ar.reset(base_persist)

    m1 = ar.mark()
    Win = ar.alloc("Win", [128, 8, 4608], BF16)
    load_weights(Win, w_in, 9, 8, "Win")
    QO, KVO, UO, GO = 0, 1024, 1536, 2560
    xt = [ar.alloc("xt", [128, D], F32) for _ in range(2)]
    xnb = [ar.alloc("xnb", [128, D], BF16) for _ in range(2)]
    xnT4 = [ar.alloc("xnT4", [128, 8, 512], BF16) for _ in range(2)]
    uTb4 = [ar.alloc("uTb4", [128, 8, 512], BF16) for _ in range(2)]
    qsq_ = [ar.alloc("qsq", [128, 512], F32) for _ in range(2)]
    qss_ = [ar.alloc("qss", [128, 8], F32) for _ in range(2)]
    hnc = [0]
    qn = [ar.alloc("qn", [128, D], BF16) for _ in range(2)]
    qTb = [ar.alloc("qTb", [128, 8, 128], BF16) for _ in range(2)]
    kf_ = [ar.alloc("kf", [128, 256], F32) for _ in range(2)]
    kn = [ar.alloc("kn", [128, 256], BF16) for _ in range(2)]
    kTb = [ar.alloc("kTb", [128, 2, 128], BF16) for _ in range(2)]
    vab = [ar.alloc("vab", [128, 4, 65], BF16) for _ in range(2)]
    gb = [ar.alloc("gb", [128, 2048], BF16) for _ in range(2)]
    for b in range(2):
        V(lambda e, b=b: e.memset(vab[b][:, :, 64:65], 1.0), [], [f"vab{b}"])

    groups = [[("ctx", xctx[0:128, :], 0, None), ("ctx", xctx[128:256, :], 1, None)]]
    for t in range(0, TP_, 4):
        groups.append([("pre", xpre[(t + q) * 128:(t + q + 1) * 128, :], t + q, None) for q in range(4)])
    for t in range(0, TM_, 4):
        groups.append([("main", xmain[(t + q) * 128:(t + q + 1) * 128, :], TP_ + t + q, t + q) for q in range(4)])

    def headnorm(p, pk, ncol, nh, dst, dkey, gain=None):
        pr = hnc[0] % 2
        hnc[0] += 1
        qsq, qss, kf = qsq_[pr], qss_[pr], kf_[pr]
        kq, ks, kk_ = f"qsq{pr}", f"qss{pr}", f"kf{pr}"
        A_(lambda e: e.activation(out=qsq[:, :ncol], in_=p[:, :ncol], func=AF.Square), [pk], [kq])
        V(lambda e: e.tensor_reduce(out=qss[:, :nh], in_=qsq[:, :ncol].rearrange("p (h d) -> p h d", d=64), axis=AX.X, op=ALU.add), [kq], [ks])
        V(lambda e: e.tensor_scalar(out=qss[:, :nh], in0=qss[:, :nh], scalar1=1.0 / 64, scalar2=1e-6, op0=ALU.mult, op1=ALU.add), [ks], [ks])
        A_(lambda e: e.activation(out=qss[:, :nh], in_=qss[:, :nh], func=AF.Sqrt), [ks], [ks])
        V(lambda e: e.reciprocal(out=qss[:, :nh], in_=qss[:, :nh]), [ks], [ks])
        rb = qss[:, :nh].unsqueeze(2).broadcast_to([128, nh, 64])
        if gain is None:
            V(lambda e: e.tensor_tensor(out=dst.rearrange("p (h d) -> p h d", d=64), in0=p[:, :ncol].rearrange("p (h d) -> p h d", d=64), in1=rb, op=ALU.mult),
              [pk, ks], [dkey])
        else:
            V(lambda e: e.tensor_tensor(out=kf.rearrange("p (h d) -> p h d", d=64), in0=p[:, :ncol].rearrange("p (h d) -> p h d", d=64), in1=rb, op=ALU.mult),
              [pk, ks], [kk_])
            V(lambda e: e.tensor_tensor(out=dst, in0=kf, in1=gain, op=ALU.mult), [kk_, "gqk"], [dkey])

    def tile_gen(gpar, t4, tile):
        kind, src, sidx, midx = tile
        b = t4 % 2
        X4 = xnT4[gpar]
        fw.dma("sync", xt[b], src, writes=[f"xt{b}"])
        fw.flush()
        rms_scale(xt[b], 0, xnb[b], [f"xt{b}"], [f"xnb{b}"])
        yield
        xk = f"xnT{gpar}_{t4}"
        XT = X4[:, :, t4 * 128:(t4 + 1) * 128]
        transpose8(xnb[b], XT, f"xnb{b}", xk)
        yield
        if kind == "main":
            qps = []
            for half in range(2):
                p, pk = psum()
                for k in range(8):
                    mm(p, XT[:, k, :], Win[:, k, QO + half * 512:QO + (half + 1) * 512], k == 0, k == 7, [xk, "Win"], pk, k == 7)
                yield
                headnorm(p, pk, 512, 8, qn[b][:, half * 512:(half + 1) * 512], f"qn{b}")
                yield
        if kind in ("ctx", "main"):
            kidx = sidx if kind == "ctx" else 2 + midx
            p, pk = psum()
            for k in range(8):
                mm(p, XT[:, k, :], Win[:, k, KVO:KVO + 512], k == 0, k == 7, [xk, "Win"], pk, k == 7)
            yield
            headnorm(p, pk, 256, 4, kn[b], f"kn{b}", gain=gqk)
            V(lambda e, p=p, b=b: e.tensor_copy(out=vab[b][:, :, 0:64], in_=p[:, 256:512].rearrange("p (h d) -> p h d", d=64)), [pk], [f"vab{b}"])
            fw.defer_dma("sync", v_s[kidx], vab[b], reads=[f"vab{b}"], writes=[("v_s", kidx)])
            yield
        if kind == "main":
            for j in range(4):
                p, pk = psum()
                for k in range(8):
                    mm(p, XT[:, k, :], Win[:, k, GO + j * 512:GO + (j + 1) * 512], k == 0, k == 7, [xk, "Win"], pk, k == 7)
                A_(lambda e, p=p, j=j, b=b: e.activation(out=gb[b][:, j * 512:(j + 1) * 512], in_=p, func=AF.Sigmoid), [pk], [f"gb{b}"])
                yield
            fw.defer_dma("sync", g_s[midx], gb[b], reads=[f"gb{b}"], writes=[("g_s", midx)])
            transpose8(qn[b], qTb[b], f"qn{b}", f"qTb{b}")
            fw.defer_dma("sync", qT_s[midx], qTb[b], reads=[f"qTb{b}"], writes=[("qT_s", midx)])
            yield
        if kind in ("ctx", "main"):
            transpose8(kn[b], kTb[b], f"kn{b}", f"kTb{b}", n=2)
            fw.defer_dma("sync", kT_s[kidx], kTb[b], reads=[f"kTb{b}"], writes=[("kT_s", kidx)])
            yield

    def lockstep(gens):
        alive = True
        while alive:
            alive = False
            for g_ in gens:
                try:
                    next(g_)
                    alive = True
                except StopIteration:
                    pass

    for gi, grp_tiles in enumerate(groups):
        gpar = gi % 2
        X4 = xnT4[gpar]
        xkeys = [f"xnT{gpar}_{t4}" for t4 in range(len(grp_tiles))]
        for t0 in range(0, len(grp_tiles), 2):
            lockstep([tile_gen(gpar, t0 + q, grp_tiles[t0 + q]) for q in range(2)])
        if grp_tiles[0][0] in ("pre", "main"):
            s0 = grp_tiles[0][2]
            for ct in range(8):
                p, pk = psum()
                for k in range(8):
                    mm(p, Win[:, k, UO + ct * 128:UO + (ct + 1) * 128], X4[:, k, :], k == 0, k == 7, xkeys + ["Win"], pk, k == 7)
                if ct % 2 == 0:
                    V(lambda e, p=p, ct=ct, gpar=gpar: e.tensor_copy(out=uTb4[gpar][:, ct, :], in_=p), [pk], [f"uTb4{gpar}"])
                else:
                    A_(lambda e, p=p, ct=ct, gpar=gpar: e.activation(out=uTb4[gpar][:, ct, :], in_=p, func=AF.Copy), [pk], [f"uTb4{gpar}"])
            fw.defer_dma("sync", uT_s[s0:s0 + 4].rearrange("n p c t -> p c n t"), uTb4[gpar].rearrange("p c (n t) -> p c n t", t=128),
                         reads=[f"uTb4{gpar}"], writes=[("uT_s", s0)])
    fw.barrier()
    ar.reset(m1)

    if upto <= 1:
        fw.emit()
        return nc
    lamsb = ar.alloc("lamsb", [128, 3, 32], F32)
    fw.dma("sync", lamsb, lam.rearrange("a p g -> p a g"), writes=["lam"])
    smn = ["dt", "th", "rho", "sn", "cs", "ar", "ai", "fr", "fi", "t1", "t2", "t3", "den", "wr", "wi", "w128r", "w128i", "mk", "x2", "lrdt"]
    sm = {n: ar.alloc(n, [128, 32], F32) for n in smn}
    pwr = [ar.alloc("pwr", [128, 32], F32) for _ in range(9)]; pwi = [ar.alloc("pwi", [128, 32], F32) for _ in range(9)]
    Er = ar.alloc("Er", [128, 32, 128], F32); Ei = ar.alloc("Ei", [128, 32, 128], F32)
    Kpad = ar.alloc("Kpad", [128, 8, 8, 128], BF16)
    cR = ar.alloc("cR", [128, 2, 32], F32); SL = ar.alloc("SL", [128, 2, 32], F32)
    base_ssm = ar.mark()
    lr, li, ld = lamsb[:, 0, :], lamsb[:, 1, :], lamsb[:, 2, :]
    K = ["ssm0"]
    A_(lambda e: e.activation(out=sm["dt"], in_=ld, func=AF.Exp), ["lam"], K)
    V(lambda e: e.tensor_tensor(out=sm["th"], in0=li, in1=sm["dt"], op=ALU.mult), K, K)
    V(lambda e: e.tensor_tensor(out=sm["lrdt"], in0=lr, in1=sm["dt"], op=ALU.mult), K, K)
    A_(lambda e: e.activation(out=sm["rho"], in_=sm["lrdt"], func=AF.Exp), K, K)
    for _ in range(5):
        V(lambda e: e.tensor_single_scalar(out=sm["mk"], in_=sm["th"], scalar=math.pi, op=ALU.is_gt), K, K)
        V(lambda e: e.scalar_tensor_tensor(out=sm["th"], in0=sm["mk"], scalar=-2.0 * math.pi, in1=sm["th"], op0=ALU.mult, op1=ALU.add), K, K)
    V(lambda e: e.tensor_scalar(out=sm["t3"], in0=sm["th"], scalar1=0.125, scalar2=None, op0=ALU.mult), K, K)
    V(lambda e: e.tensor_tensor(out=sm["x2"], in0=sm["t3"], in1=sm["t3"], op=ALU.mult), K, K)

    def horner(o, coefs):
        V(lambda e: e.memset(o, coefs[0]), K, K)
        for c in coefs[1:]:
            V(lambda e: e.tensor_tensor(out=o, in0=o, in1=sm["x2"], op=ALU.mult), K, K)
            V(lambda e, c=c: e.tensor_scalar(out=o, in0=o, scalar1=float(c), scalar2=None, op0=ALU.add), K, K)

    def cdouble(sn_, cs_):
        V(lambda e: e.tensor_tensor(out=sm["t1"], in0=sn_, in1=cs_, op=ALU.mult), K, K)
        V(lambda e: e.tensor_tensor(out=sm["t2"], in0=cs_, in1=cs_, op=ALU.mult), K, K)
        V(lambda e: e.tensor_tensor(out=sm["t3"], in0=sn_, in1=sn_, op=ALU.mult), K, K)
        V(lambda e: e.tensor_scalar(out=sn_, in0=sm["t1"], scalar1=2.0, scalar2=None, op0=ALU.mult), K, K)
        V(lambda e: e.tensor_tensor(out=cs_, in0=sm["t2"], in1=sm["t3"], op=ALU.subtract), K, K)

    horner(sm["sn"], [-1 / 39916800.0, 1 / 362880.0, -1 / 5040.0, 1 / 120.0, -1 / 6.0, 1.0])
    V(lambda e: e.tensor_tensor(out=sm["sn"], in0=sm["sn"], in1=sm["t3"], op=ALU.mult), K, K)
    horner(sm["cs"], [-1 / 3628800.0, 1 / 40320.0, -1 / 720.0, 1 / 24.0, -0.5, 1.0])
    for _ in range(3):
        cdouble(sm["sn"], sm["cs"])
    V(lambda e: e.tensor_tensor(out=sm["ar"], in0=sm["rho"], in1=sm["cs"], op=ALU.mult), K, K)
    V(lambda e: e.tensor_tensor(out=sm["ai"], in0=sm["rho"], in1=sm["sn"], op=ALU.mult), K, K)
    V(lambda e: e.tensor_scalar(out=sm["t1"], in0=sm["ar"], scalar1=-1.0, scalar2=None, op0=ALU.add), K, K)
    V(lambda e: e.tensor_tensor(out=sm["den"], in0=lr, in1=lr, op=ALU.mult), K, K)
    V(lambda e: e.tensor_tensor(out=sm["t2"], in0=li, in1=li, op=ALU.mult), K, K)
    V(lambda e: e.tensor_tensor(out=sm["den"], in0=sm["den"], in1=sm["t2"], op=ALU.add), K, K)
    V(lambda e: e.reciprocal(out=sm["den"], in_=sm["den"]), K, K)
    V(lambda e: e.tensor_tensor(out=sm["t2"], in0=sm["t1"], in1=lr, op=ALU.mult), K, K)
    V(lambda e: e.tensor_tensor(out=sm["t3"], in0=sm["ai"], in1=li, op=ALU.mult), K, K)
    V(lambda e: e.tensor_tensor(out=sm["t2"], in0=sm["t2"], in1=sm["t3"], op=ALU.add), K, K)
    V(lambda e: e.tensor_tensor(out=sm["fr"], in0=sm["t2"], in1=sm["den"], op=ALU.mult), K, K)
    V(lambda e: e.tensor_tensor(out=sm["t2"], in0=sm["ai"], in1=lr, op=ALU.mult), K, K)
    V(lambda e: e.tensor_tensor(out=sm["t3"], in0=sm["t1"], in1=li, op=ALU.mult), K, K)
    V(lambda e: e.tensor_tensor(out=sm["t2"], in0=sm["t2"], in1=sm["t3"], op=ALU.subtract), K, K)
    V(lambda e: e.tensor_tensor(out=sm["fi"], in0=sm["t2"], in1=sm["den"], op=ALU.mult), K, K)
    V(lambda e: e.memset(pwr[0], 1.0), K, K)
    V(lambda e: e.memset(pwi[0], 0.0), K, K)
    for k in range(1, 9):
        V(lambda e, k=k: e.tensor_tensor(out=sm["t1"], in0=pwr[k - 1], in1=sm["ar"], op=ALU.mult), K, K)
        V(lambda e, k=k: e.tensor_tensor(out=sm["t2"], in0=pwi[k - 1], in1=sm["ai"], op=ALU.mult), K, K)
        V(lambda e, k=k: e.tensor_tensor(out=pwr[k], in0=sm["t1"], in1=sm["t2"], op=ALU.subtract), K, K)
        V(lambda e, k=k: e.tensor_tensor(out=sm["t1"], in0=pwr[k - 1], in1=sm["ai"], op=ALU.mult), K, K)
        V(lambda e, k=k: e.tensor_tensor(out=sm["t2"], in0=pwi[k - 1], in1=sm["ar"], op=ALU.mult), K, K)
        V(lambda e, k=k: e.tensor_tensor(out=pwi[k], in0=sm["t1"], in1=sm["t2"], op=ALU.add), K, K)
    V(lambda e: e.tensor_scalar(out=sm["t1"], in0=sm["lrdt"], scalar1=8.0, scalar2=None, op0=ALU.mult), K, K)
    A_(lambda e: e.activation(out=sm["rho"], in_=sm["t1"], func=AF.Exp), K, K)
    for _ in range(3):
        cdouble(sm["sn"], sm["cs"])
    V(lambda e: e.memset(Er[:, :, 0:1], 1.0), K, K)
    V(lambda e: e.memset(Ei[:, :, 0:1], 0.0), K, K)
    V(lambda e: e.tensor_copy(out=sm["wr"], in_=sm["cs"]), K, K)
    V(lambda e: e.tensor_copy(out=sm["wi"], in_=sm["sn"]), K, K)
    m0 = ar.mark()
    tA = ar.alloc("tA", [128, 32, 64], F32); tB = ar.alloc("tB", [128, 32, 64], F32)
    for k in range(7):
        n = 1 << k
        wrb = sm["wr"].unsqueeze(2).broadcast_to([128, 32, n]); wib = sm["wi"].unsqueeze(2).broadcast_to([128, 32, n])
        V(lambda e, n=n, wrb=wrb: e.tensor_tensor(out=tA[:, :, :n], in0=Er[:, :, :n], in1=wrb, op=ALU.mult), K, K)
        V(lambda e, n=n, wib=wib: e.tensor_tensor(out=tB[:, :, :n], in0=Ei[:, :, :n], in1=wib, op=ALU.mult), K, K)
        V(lambda e, n=n: e.tensor_tensor(out=Er[:, :, n:2 * n], in0=tA[:, :, :n], in1=tB[:, :, :n], op=ALU.subtract), K, K)
        V(lambda e, n=n, wib=wib: e.tensor_tensor(out=tA[:, :, :n], in0=Er[:, :, :n], in1=wib, op=ALU.mult), K, K)
        V(lambda e, n=n, wrb=wrb: e.tensor_tensor(out=tB[:, :, :n], in0=Ei[:, :, :n], in1=wrb, op=ALU.mult), K, K)
        V(lambda e, n=n: e.tensor_tensor(out=Ei[:, :, n:2 * n], in0=tA[:, :, :n], in1=tB[:, :, :n], op=ALU.add), K, K)
        V(lambda e: e.tensor_tensor(out=sm["t1"], in0=sm["wr"], in1=sm["wr"], op=ALU.mult), K, K)
        V(lambda e: e.tensor_tensor(out=sm["t2"], in0=sm["wi"], in1=sm["wi"], op=ALU.mult), K, K)
        V(lambda e: e.tensor_tensor(out=sm["t3"], in0=sm["wr"], in1=sm["wi"], op=ALU.mult), K, K)
        V(lambda e: e.tensor_tensor(out=sm["wr"], in0=sm["t1"], in1=sm["t2"], op=ALU.subtract), K, K)
        V(lambda e: e.tensor_scalar(out=sm["wi"], in0=sm["t3"], scalar1=2.0, scalar2=None, op0=ALU.mult), K, K)
    V(lambda e: e.tensor_copy(out=sm["w128r"], in_=sm["wr"]), K, K)
    V(lambda e: e.tensor_copy(out=sm["w128i"], in_=sm["wi"]), K, K)
    V(lambda e: e.memset(cR, 0.0), K, K)
    V(lambda e: e.memset(SL, 0.0), K, K)
    fw.barrier()
    ar.reset(m0)
    BTc = ar.alloc("BTc", [128, 32, 2, 16], F32); Cc = ar.alloc("Cc", [128, 2, 32, 16], F32)
    Cfc = ar.alloc("Cfc", [128, 32, 2, 16], F32); Xc = ar.alloc("Xc", [128, 32, 2, 16], F32)
    c1 = ar.alloc("c1", [128, 32, 16], F32); c2 = ar.alloc("c2", [128, 32, 16], F32)
    Cfp = ar.alloc("Cfp", [128, 32, 2, 128], BF16)
    padb = ar.alloc("padb", [128, 32, 2, 128], BF16); DBsb = ar.alloc("DBsb", [128, 32, 2, 128], BF16)
    fw.dma("sync", BTc, btc, writes=["BTc"])
    fw.dma("sync", Cc, cc.rearrange("a p g c -> p a g c"), writes=["Cc"])
    G_(lambda e: e.memset(padb, 0.0), [], ["padb"])
    G_(lambda e: e.memset(Cfp, 0.0), [], ["Cfp"])

    def cmul_compact(dst, src_r, src_i, sr, si, rk, wk, neg_im=False):
        srb = sr.unsqueeze(2).broadcast_to([128, 32, 16]); sib = si.unsqueeze(2).broadcast_to([128, 32, 16])
        V(lambda e: e.tensor_tensor(out=c1, in0=src_r, in1=srb, op=ALU.mult), rk, ["c1"])
        V(lambda e: e.tensor_tensor(out=c2, in0=src_i, in1=sib, op=ALU.mult), rk, ["c2"])
        V(lambda e: e.tensor_tensor(out=dst[:, :, 0, :], in0=c1, in1=c2, op=ALU.subtract), ["c1", "c2"], wk)
        V(lambda e: e.tensor_tensor(out=c1, in0=src_r, in1=sib, op=ALU.mult), rk + wk, ["c1"])
        V(lambda e: e.tensor_tensor(out=c2, in0=src_i, in1=srb, op=ALU.mult), rk + wk, ["c2"])
        if neg_im:
            V(lambda e: e.scalar_tensor_tensor(out=dst[:, :, 1, :], in0=c1, scalar=-1.0, in1=c2, op0=ALU.mult, op1=ALU.subtract), ["c1", "c2"], wk)
        else:
            V(lambda e: e.tensor_tensor(out=dst[:, :, 1, :], in0=c1, in1=c2, op=ALU.add), ["c1", "c2"], wk)

    def scatter(dst_pad, src_c, rk, wk):
        for g2 in range(2):
            for q in range(4):
                blk = 2 * q + g2
                G_(lambda e, g2=g2, q=q, blk=blk: e.tensor_copy(out=dst_pad[g2 * 64:(g2 + 1) * 64, q::4, :, blk * 16:(blk + 1) * 16],
                                                                 in_=src_c[g2 * 64:(g2 + 1) * 64, q::4, :, :]), rk, wk)

    cmul_compact(Cfc, Cc[:, 0], Cc[:, 1], sm["fr"], sm["fi"], ["Cc"], ["Cfc"])
    cmul_compact(Xc, Cc[:, 0], Cc[:, 1], sm["fr"], sm["fi"], ["Cc"], ["Xc"], neg_im=True)
    scatter(Cfp, Xc, ["Xc"], ["Cfp"])
    for k in range(8):
        j = 7 - k
        cmul_compact(Xc, BTc[:, :, 0, :], BTc[:, :, 1, :], pwr[k], pwi[k], ["BTc"], ["Xc"])
        scatter(padb, Xc, ["Xc"], ["padb"])
        for r in range(8):
            p, pk = psum()
            n_ = 0
            for gl in range(4):
                for ri in range(2):
                    mm(p[:, 0:128], padb[:, 4 * r + gl, ri, :], Cfp[:, 4 * r + gl, ri, :], n_ == 0, n_ == 7, ["padb", "Cfp"], pk, n_ == 7)
                    n_ += 1
            if k == 0:
                V(lambda e, p=p, r=r: e.scalar_tensor_tensor(out=Kpad[:, r, 0, :], in0=identf, scalar=dcs[:, r:r + 1], in1=p[:, 0:128], op0=ALU.mult, op1=ALU.add),
                  [pk, "identf", "dcs"], ["Kpad"])
            else:
                V(lambda e, p=p, r=r, k=k: e.tensor_copy(out=Kpad[:, r, k, :], in_=p[:, 0:128]), [pk], ["Kpad"])
        for g4 in range(16):
            p, pk = psum()
            for q in range(4):
                gi = g4 * 4 + q
                mm(p[:, q * 128:(q + 1) * 128], padb[:, gi // 2, gi % 2, :], identb, True, True, ["padb", "identb"], pk, q == 3)
            V(lambda e, p=p, g4=g4: e.tensor_copy(out=DBsb.rearrange("p g r s -> p (g r) s")[:, g4 * 4:(g4 + 1) * 4, :], in_=p.rearrange("p (a c) -> p a c", c=128)),
              [pk], ["DBsb"])
        fw.dma("sync", DB_s[:, :, j].rearrange("r p g a s -> p r g a s"), DBsb.rearrange("p (r g) a s -> p r g a s", r=8), reads=["DBsb"], writes=[("DB_s", j)])
    for j in range(8):
        cmul_compact(Xc, Cfc[:, :, 0, :], Cfc[:, :, 1, :], pwr[j + 1], pwi[j + 1], ["Cfc"], ["Xc"])
        scatter(padb, Xc, ["Xc"], ["padb"])
        fw.dma("sync", EC_s[:, :, j].rearrange("r p g a s -> p r g a s"), padb.rearrange("p (r g) a s -> p r g a s", r=8), reads=["padb"], writes=[("EC_s", j)])
    fw.barrier()
    ar.reset(base_ssm)

    if upto <= 2:
        fw.emit()
        return nc
    NPS, NMS = NP // 1024, NM // 1024
    DBr2 = [ar.alloc("DBr", [128, 8, 4, 2, 128], BF16) for _ in range(2)]
    ECr2 = [ar.alloc("ECr", [128, 8, 4, 2, 128], BF16) for _ in range(2)]
    uTr = [ar.alloc("uTr", [128, 1024], BF16) for _ in range(2)]
    uTj = [ar.alloc("uTj", [128, 8, 128], BF16) for _ in range(2)]
    zTr = [ar.alloc("zTr", [128, 1024], BF16) for _ in range(2)]
    RB = []
    for b in range(2):
        d = {n: ar.alloc(n, [128, 4, 128], F32) for n in ["t1", "t2", "t3", "t4", "Rr", "Ri"]}
        d["Xr"], d["Xi"] = d["t1"], d["t3"]
        d["Sr"] = ar.alloc("Sr", [128, 4, 130], BF16); d["Si"] = ar.alloc("Si", [128, 4, 130], BF16)
        for n in ["c1", "c2", "c3", "c4"]:
            d[n] = ar.alloc(n, [128, 4], F32)
        RB.append(d)
    ysb = [ar.alloc("ysb", [128, 1024], F32) for _ in range(2)]
    g1b = [ar.alloc("g1b", [128, 1024], F32) for _ in range(2)]
    g2b = g1b
    rcount = 0
    hcount = 0
    def round_gen(rp, st, q_):
        r = 2 * rp + q_
        gsl = slice(4 * r, 4 * r + 4)
        DBr, ECr = DBr2[q_], ECr2[q_]
        kDB, kEC = f"DBr{q_}", f"ECr{q_}"
        is_main = st >= NPS
        ub = q_
        fw.dma("sync", uTr[ub].rearrange("p (n t) -> p n t", t=128), uT_s[8 * st:8 * st + 8, :, r, :].rearrange("n p t -> p n t"),
               writes=[f"uTr{ub}"])
        fw.flush()
        A_(lambda e, ub=ub: e.activation(out=uTj[ub], in_=uTr[ub].rearrange("p (c j) -> p j c", j=8), func=AF.Copy), [f"uTr{ub}"], [f"uTj{ub}"])
        yield
        b = q_
        B = RB[b]
        kb = lambda n, b=b: f"{n}{b}"
        pXr, pkr = psum()
        pXi, pki = psum()
        for ri, (pX, pk) in enumerate(((pXr, pkr), (pXi, pki))):
            for gl in range(4):
                for j in range(8):
                    mm(pX[:, gl * 128:(gl + 1) * 128], DBr[:, j, gl, ri, :], uTj[ub][:, j, :], j == 0, j == 7,
                       [f"uTj{ub}", kDB], pk, (gl == 3 and j == 7))
        yield
        pXr3 = pXr.rearrange("p (a c) -> p a c", c=128); pXi3 = pXi.rearrange("p (a c) -> p a c", c=128)
        Erg, Eig = Er[:, gsl, :], Ei[:, gsl, :]
        V(lambda e, B=B, a=pXr3, t=Erg: e.tensor_tensor(out=B["t1"], in0=a, in1=t, op=ALU.mult), [pkr], [kb("t1")])
        V(lambda e, B=B, a=pXi3, t=Eig: e.tensor_tensor(out=B["t2"], in0=a, in1=t, op=ALU.mult), [pki], [kb("t2")])
        V(lambda e, B=B, a=pXi3, t=Erg: e.tensor_tensor(out=B["t3"], in0=a, in1=t, op=ALU.mult), [pki], [kb("t3")])
        V(lambda e, B=B, a=pXr3, t=Eig: e.tensor_tensor(out=B["t4"], in0=a, in1=t, op=ALU.mult), [pkr], [kb("t4")])
        yield
        G_(lambda e, B=B: e.tensor_tensor(out=B["t1"], in0=B["t1"], in1=B["t2"], op=ALU.add), [kb("t1"), kb("t2")], [kb("t1")])
        G_(lambda e, B=B: e.tensor_tensor(out=B["t3"], in0=B["t3"], in1=B["t4"], op=ALU.subtract), [kb("t3"), kb("t4")], [kb("t3")])
        yield
        for gl in range(4):
            gp = 4 * r + gl
            for nm, xs, ci in (("Rr", "t1", 0), ("Ri", "t3", 1)):
                V(lambda e, B=B, gl=gl, gp=gp, nm=nm, xs=xs, ci=ci: e.tensor_tensor_scan(
                    out=B[nm][:, gl, :], data0=sm["rho"][:, gp:gp + 1].broadcast_to([128, 128]), data1=B[xs][:, gl, :],
                    initial=cR[:, ci, gp:gp + 1], op0=ALU.mult, op1=ALU.add), [kb(xs), ("cR", r)], [kb(nm)])
        yield
        if is_main:
            G_(lambda e, B=B, gsl=gsl: e.tensor_copy(out=B["Sr"][:, :, 0], in_=SL[:, 0, gsl]), [("SL", r)], [kb("Sr")])
            G_(lambda e, B=B, gsl=gsl: e.tensor_copy(out=B["Si"][:, :, 0], in_=SL[:, 1, gsl]), [("SL", r)], [kb("Si")])
        wr4, wi4 = sm["w128r"][:, gsl], sm["w128i"][:, gsl]
        er7, ei7 = Er[:, gsl, 127], Ei[:, gsl, 127]
        Rr7, Ri7 = B["Rr"][:, :, 127], B["Ri"][:, :, 127]
        for (xr_, xi_, dst, negim, key) in ((wr4, wi4, cR, False, "cR"), (er7, ei7, SL, True, "SL")):
            G_(lambda e, B=B, a=Rr7, w=xr_: e.tensor_tensor(out=B["c1"], in0=a, in1=w, op=ALU.mult), [kb("Rr")], [kb("c1")])
            G_(lambda e, B=B, a=Ri7, w=xi_: e.tensor_tensor(out=B["c2"], in0=a, in1=w, op=ALU.mult), [kb("Ri")], [kb("c2")])
            G_(lambda e, B=B, a=Ri7, w=xr_: e.tensor_tensor(out=B["c3"], in0=a, in1=w, op=ALU.mult), [kb("Ri")], [kb("c3")])
            G_(lambda e, B=B, a=Rr7, w=xi_: e.tensor_tensor(out=B["c4"], in0=a, in1=w, op=ALU.mult), [kb("Rr")], [kb("c4")])
            G_(lambda e, B=B, dst=dst, gsl=gsl: e.tensor_tensor(out=dst[:, 0, gsl], in0=B["c1"], in1=B["c2"], op=ALU.subtract),
               [kb("c1"), kb("c2")], [(key, r)])
            if negim:
                V(lambda e, B=B, dst=dst, gsl=gsl: e.scalar_tensor_tensor(out=dst[:, 1, gsl], in0=B["c3"], scalar=-1.0, in1=B["c4"], op0=ALU.mult, op1=ALU.subtract),
                   [kb("c3"), kb("c4")], [(key, r)])
            else:
                G_(lambda e, B=B, dst=dst, gsl=gsl: e.tensor_tensor(out=dst[:, 1, gsl], in0=B["c3"], in1=B["c4"], op=ALU.add),
                   [kb("c3"), kb("c4")], [(key, r)])
        yield
        if not is_main:
            return
        G_(lambda e, B=B, t=Erg: e.tensor_tensor(out=B["t1"], in0=B["Rr"], in1=t, op=ALU.mult), [kb("Rr")], [kb("t1")])
        G_(lambda e, B=B, t=Eig: e.tensor_tensor(out=B["t2"], in0=B["Ri"], in1=t, op=ALU.mult), [kb("Ri")], [kb("t2")])
        yield
        V(lambda e, B=B, t=Erg: e.tensor_tensor(out=B["t3"], in0=B["Ri"], in1=t, op=ALU.mult), [kb("Ri")], [kb("t3")])
        V(lambda e, B=B, t=Eig: e.tensor_tensor(out=B["t4"], in0=B["Rr"], in1=t, op=ALU.mult), [kb("Rr")], [kb("t4")])
        yield
        V(lambda e, B=B: e.tensor_tensor(out=B["Sr"][:, :, 1:129], in0=B["t1"], in1=B["t2"], op=ALU.subtract), [kb("t1"), kb("t2")], [kb("Sr")])
        V(lambda e, B=B: e.scalar_tensor_tensor(out=B["Si"][:, :, 1:129], in0=B["t3"], scalar=-1.0, in1=B["t4"], op0=ALU.mult, op1=ALU.subtract),
          [kb("t3"), kb("t4")], [kb("Si")])
        zb = q_
        hb = q_
        yield
        for h2 in range(2):
            py, pky = psum()
            for j4 in range(4):
                j = 4 * h2 + j4
                o = py[:, j4 * 128:(j4 + 1) * 128]
                nmm = (j + 1) + 8
                n_ = 0
                for k in range(j + 1):
                    mm(o, Kpad[:, r, k, :], uTj[ub][:, j - k, :], n_ == 0, n_ == nmm - 1, [f"uTj{ub}", "Kpad"], pky, False)
                    n_ += 1
                for gl in range(4):
                    for ri, Sn in enumerate(("Sr", "Si")):
                        mm(o, ECr[:, j, gl, ri, :], B[Sn][:, gl, 0:128], n_ == 0, n_ == nmm - 1, [kb(Sn), kEC], pky,
                           (j4 == 3 and n_ == nmm - 1))
                        n_ += 1
            yield
            A_(lambda e, py=py, hb=hb, h2=h2: e.activation(out=ysb[hb].rearrange("p (c j) -> p c j", j=8)[:, :, 4 * h2:4 * h2 + 4],
                                                           in_=py.rearrange("p (j c) -> p c j", c=128), func=AF.Identity), [pky], [f"ysb{hb}"])
        yield
        G_(lambda e, hb=hb: e.tensor_tensor(out=g1b[hb], in0=ysb[hb], in1=ysb[hb], op=ALU.mult), [f"ysb{hb}"], [f"g1b{hb}"])
        G_(lambda e, hb=hb: e.tensor_scalar(out=g1b[hb], in0=g1b[hb], scalar1=0.044715, scalar2=1.0, op0=ALU.mult, op1=ALU.add), [f"g1b{hb}"], [f"g1b{hb}"])
        G_(lambda e, hb=hb: e.tensor_tensor(out=g1b[hb], in0=g1b[hb], in1=ysb[hb], op=ALU.mult), [f"g1b{hb}", f"ysb{hb}"], [f"g1b{hb}"])
        yield
        A_(lambda e, hb=hb: e.activation(out=g2b[hb], in_=g1b[hb], func=AF.Sigmoid, scale=1.5957691216057308), [f"g1b{hb}"], [f"g2b{hb}"])
        yield
        V(lambda e, hb=hb, zb=zb: e.tensor_tensor(out=zTr[zb], in0=ysb[hb], in1=g2b[hb], op=ALU.mult), [f"ysb{hb}", f"g2b{hb}"], [f"zTr{zb}"])
        m8 = 8 * (st - NPS)
        fw.defer_dma("sync", zT_s[m8:m8 + 8, :, r, :].rearrange("n p t -> p n t"), zTr[zb].rearrange("p (n t) -> p n t", t=128),
               reads=[f"zTr{zb}"], writes=[("zT_s", st, r)])

    for rp in range(4):
        for q_ in range(2):
            fw.dma("sync", DBr2[q_], DB_s[2 * rp + q_], writes=[f"DBr{q_}"])
            fw.dma("sync", ECr2[q_], EC_s[2 * rp + q_], writes=[f"ECr{q_}"])
        for st in range(NPS + NMS):
            gens = [round_gen(rp, st, 0), round_gen(rp, st, 1)]
            alive = True
            while alive:
                alive = False
                for g_ in gens:
                    try:
                        next(g_)
                        alive = True
                    except StopIteration:
                        pass
    fw.barrier()
    ar.reset(base_persist)

    if upto <= 3:
        fw.emit()
        return nc
    Wg = ar.alloc("Wg", [128, 8, 2048], BF16); Wo = ar.alloc("Wo", [128, 8, 1024], BF16)
    load_weights(Wg, w_glu, 4, 8, "Wg")
    load_weights(Wo, w_out, 2, 8, "Wo")
    kme = ar.alloc("kme", [128, 2, 128], BF16); vme = ar.alloc("vme", [128, 4, 65], BF16)
    fw.dma("sync", kme, kT_s[0], writes=["kme"]); fw.dma("sync", vme, v_s[0], writes=["vme"])
    qTl = [ar.alloc("qTl", [128, 8, 128], BF16) for _ in range(2)]
    kTl = [ar.alloc("kTl", [128, 2, 128], BF16) for _ in range(3)]
    vl = [ar.alloc("vl", [128, 4, 65], BF16) for _ in range(3)]
    gl_ = [ar.alloc("gl", [128, 2048], BF16) for _ in range(2)]
    zTl = [ar.alloc("zTl", [128, 8, 128], BF16) for _ in range(2)]
    xr = [ar.alloc("xr", [128, D], F32) for _ in range(2)]
    Pc = [ar.alloc("Pc", [128, 512], BF16) for _ in range(2)]
    Pp = [ar.alloc("Pp", [128, 512], BF16) for _ in range(2)]
    Pm = [ar.alloc("Pm", [128, 512], BF16) for _ in range(2)]
    den_ = [ar.alloc("den", [128, 4], F32) for _ in range(2)]
    for b_ in range(2):
        V(lambda e, b_=b_: e.memset(Pm[b_], 0.0), [], [f"Pm{b_}"])
    attn_ = [ar.alloc("attn", [128, D], F32) for _ in range(2)]; An_ = [ar.alloc("An", [128, D], F32) for _ in range(2)]
    sig_ = [ar.alloc("sig", [128, 512], F32) for _ in range(2)]; ssm_ = [ar.alloc("ssm", [128, D], F32) for _ in range(2)]
    Bn_ = [ar.alloc("Bn", [128, D], F32) for _ in range(2)]
    mg_ = [ar.alloc("mg", [128, D], BF16) for _ in range(2)]; mgT_ = [ar.alloc("mgT", [128, 8, 128], BF16) for _ in range(2)]
    h1 = [ar.alloc("h1", [128, D], F32) for _ in range(2)]
    fw.dma("sync", kTl[1], kT_s[1], writes=["kTl1"]); fw.dma("sync", vl[1], v_s[1], writes=["vl1"])
    def s3_loads(i):
        b = i % 2
        jc = 2 + i
        sc = jc % 3
        fw.dma("sync", kTl[sc], kT_s[jc], reads=[("kT_s", jc)], writes=[f"kTl{sc}"])
        fw.dma("sync", vl[sc], v_s[jc], reads=[("v_s", jc)], writes=[f"vl{sc}"])
        fw.dma("sync", qTl[b], qT_s[i], writes=[f"qTl{b}"])
        fw.dma("sync", gl_[b], g_s[i], writes=[f"gl{b}"])
        fw.dma("sync", zTl[b], zT_s[i], writes=[f"zTl{b}"])
        fw.dma("sync", xr[b], xmain[i * 128:(i + 1) * 128, :], writes=[f"xr{b}"])
        fw.flush()

    def s3_A(n):
        i, grp = n // 4, n % 4
        b, pb = i % 2, n % 2
        sc, sp = (2 + i) % 3, (1 + i) % 3
        bs, kc = (grp % 2) * 64, grp // 2
        qsel = qTl[b][bs:bs + 64, kc * 4:(kc + 1) * 4, :]
        pS, pkS = psum()
        mm(pS, kTl[sc][bs:bs + 64, kc, :], qsel, True, True, [f"kTl{sc}", f"qTl{b}"], pkS, True)
        A_(lambda e: e.activation(out=Pc[pb], in_=pS, func=AF.Exp, scale=0.125), [pkS], [f"Pc{pb}"])
        G_(lambda e: e.tensor_tensor(out=Pc[pb], in0=Pc[pb], in1=maskb[:, 0, :], op=ALU.mult), [f"Pc{pb}", "maskb"], [f"Pc{pb}"])
        pS2, pkS2 = psum()
        mm(pS2, kTl[sp][bs:bs + 64, kc, :], qsel, True, True, [f"kTl{sp}", f"qTl{b}"], pkS2, True)
        A_(lambda e: e.activation(out=Pp[pb], in_=pS2, func=AF.Exp, scale=0.125), [pkS2], [f"Pp{pb}"])
        mi = 2 if i == 0 else 1
        G_(lambda e: e.tensor_tensor(out=Pp[pb], in0=Pp[pb], in1=maskb[:, mi, :], op=ALU.mult), [f"Pp{pb}", "maskb"], [f"Pp{pb}"])
        pS3, pkS3 = psum()
        mm(pS3[0:16, :], kme[bs:bs + 64, kc, 0:16], qsel, True, True, ["kme", f"qTl{b}"], pkS3, True)
        A_(lambda e: e.activation(out=Pm[pb][0:16, :], in_=pS3[0:16, :], func=AF.Exp, scale=0.125), [pkS3], [f"Pm{pb}"])

    def s3_B(n):
        i, grp = n // 4, n % 4
        b, pb = i % 2, n % 2
        sc, sp = (2 + i) % 3, (1 + i) % 3
        attn, kA = attn_[b], f"attn{b}"
        pO, pkO = psum()
        for r in range(4):
            o = pO[:, r * 65:(r + 1) * 65]
            mm(o, Pm[pb][:, r * 128:(r + 1) * 128], vme[:, grp, :], True, False, [f"Pm{pb}", "vme"], pkO, False)
            mm(o, Pp[pb][:, r * 128:(r + 1) * 128], vl[sp][:, grp, :], False, False, [f"Pp{pb}", f"vl{sp}"], pkO, False)
            mm(o, Pc[pb][:, r * 128:(r + 1) * 128], vl[sc][:, grp, :], False, True, [f"Pc{pb}", f"vl{sc}"], pkO, r == 3)
        pO3 = pO[:, 0:260].rearrange("p (r c) -> p r c", c=65)
        den = den_[pb]
        kd = f"den{pb}"
        V(lambda e: e.tensor_tensor(out=den, in0=pO3[:, :, 64], in1=esink[:, grp * 4:(grp + 1) * 4], op=ALU.add), [pkO, "esink"], [kd])
        V(lambda e: e.reciprocal(out=den, in_=den), [kd], [kd])
        V(lambda e: e.tensor_tensor(out=attn[:, grp * 256:(grp + 1) * 256].rearrange("p (r d) -> p r d", d=64), in0=pO3[:, :, 0:64],
                                    in1=den.unsqueeze(2).broadcast_to([128, 4, 64]), op=ALU.mult), [pkO, kd], [kA])

    def s3_tail(i):
        b = i % 2
        attn, An, ssm, Bn, mg, mgT = attn_[b], An_[b], ssm_[b], Bn_[b], mg_[b], mgT_[b]
        kA, kAn, kss, kBn, kmg, kmT = f"attn{b}", f"An{b}", f"ssm{b}", f"Bn{b}", f"mg{b}", f"mgT{b}"
        rms_scale(attn, 1, An, [kA], [kAn])
        G_(lambda e: e.tensor_tensor(out=An, in0=An, in1=gl_[b][:, 0:1024], op=ALU.mult), [kAn, f"gl{b}"], [kAn])
        for half in range(2):
            pa, pka = psum()
            for k in range(8):
                mm(pa, zTl[b][:, k, :], Wg[:, k, half * 512:(half + 1) * 512], k == 0, k == 7, [f"zTl{b}", "Wg"], pka, k == 7)
            pz, pkz = psum()
            for k in range(8):
                mm(pz, zTl[b][:, k, :], Wg[:, k, 1024 + half * 512:1024 + (half + 1) * 512], k == 0, k == 7, [f"zTl{b}", "Wg"], pkz, k == 7)
            sig = sig_[half]
            A_(lambda e, pz=pz, sig=sig: e.activation(out=sig, in_=pz, func=AF.Sigmoid), [pkz], [f"sig{half}"])
            V(lambda e, pa=pa, half=half, sig=sig: e.tensor_tensor(out=ssm[:, half * 512:(half + 1) * 512], in0=pa, in1=sig, op=ALU.mult), [pka, f"sig{half}"], [kss])
        rms_scale(ssm, 2, Bn, [kss], [kBn])
        G_(lambda e: e.tensor_tensor(out=Bn, in0=Bn, in1=gl_[b][:, 1024:2048], op=ALU.mult), [kBn, f"gl{b}"], [kBn])
        V(lambda e: e.tensor_tensor(out=mg, in0=An, in1=Bn, op=ALU.add), [kAn, kBn], [kmg])
        transpose8(mg, mgT, kmg, kmT)
        for half in range(2):
            p, pk = psum()
            for k in range(8):
                mm(p, mgT[:, k, :], Wo[:, k, half * 512:(half + 1) * 512], k == 0, k == 7, [kmT, "Wo"], pk, k == 7)
            V(lambda e, p=p, half=half: e.tensor_tensor(out=h1[b][:, half * 512:(half + 1) * 512], in0=p, in1=xr[b][:, half * 512:(half + 1) * 512], op=ALU.add),
              [pk, f"xr{b}"], [f"h1{b}"])
        fw.defer_dma("sync", h1_s[i * 128:(i + 1) * 128, :], h1[b], reads=[f"h1{b}"], writes=[("h1_s", i)])

    NG = 4 * TM_
    s3_loads(0)
    s3_A(0)
    for n in range(NG):
        if n + 1 < NG:
            if (n + 1) % 4 == 0:
                s3_loads((n + 1) // 4)
            s3_A(n + 1)
        s3_B(n)
        if n % 4 == 3:
            s3_tail(n // 4)
    fw.barrier()
    ar.reset(base_persist)

    if upto <= 4:
        fw.emit()
        return nc
    W1 = ar.alloc("W1", [128, 8, 5632], BF16); W2 = ar.alloc("W2", [128, 22, 1024], BF16)
    load_weights(W1, w_f1, 11, 8, "W1")
    load_weights(W2, w_f2, 2, 22, "W2")
    GT = 4
    hl = [ar.alloc("hl", [128, D], F32) for _ in range(2)]
    hn = [ar.alloc("hn", [128, D], BF16) for _ in range(2)]
    hnT = ar.alloc("hnT", [128, 8, GT * 128], BF16)
    sg = [ar.alloc("sg", [128, 512], F32) for _ in range(2)]
    actT = ar.alloc("actT", [128, 22, GT * 128], BF16)
    hres = hl
    ob = junk
    tcount = 0
    for g in range(TM_ // GT):
        for t4 in range(GT):
            i = g * GT + t4
            b = tcount % 2
            tcount += 1
            fw.dma("sync", hl[b], h1_s[i * 128:(i + 1) * 128, :], reads=[("h1_s", i)], writes=[f"hl{b}"])
            fw.flush()
            rms_scale(hl[b], 3, hn[b], [f"hl{b}"], [f"hn{b}"])
            for half in range(2):
                p, pk = psum()
                for jj in range(4):
                    c = half * 4 + jj
                    mm(p[:, jj * 128:(jj + 1) * 128], hn[b][:, c * 128:(c + 1) * 128], identb, True, True, [f"hn{b}", "identb"], pk, jj == 3)
                V(lambda e, p=p, half=half, t4=t4: e.tensor_copy(out=hnT[:, half * 4:half * 4 + 4, t4 * 128:(t4 + 1) * 128],
                                                               in_=p.rearrange("p (a c) -> p a c", c=128)), [pk], [("hnT", t4)])
        hk = [("hnT", t4) for t4 in range(GT)]
        for fc in range(22):
            fp, q2 = fc // 2, fc % 2
            pg, pkg = psum()
            for k in range(8):
                mm(pg, W1[:, k, fp * 512 + q2 * 256:fp * 512 + q2 * 256 + 128], hnT[:, k, :], k == 0, k == 7, hk + ["W1"], pkg, k == 7)
            pu, pku = psum()
            for k in range(8):
                mm(pu, W1[:, k, fp * 512 + q2 * 256 + 128:fp * 512 + q2 * 256 + 256], hnT[:, k, :], k == 0, k == 7, hk + ["W1"], pku, k == 7)
            s_ = sg[fc % 2]
            ks_ = f"sg{fc % 2}"
            A_(lambda e, pg=pg, s_=s_: e.activation(out=s_, in_=pg, func=AF.Sigmoid), [pkg], [ks_])
            V(lambda e, pg=pg, s_=s_: e.tensor_tensor(out=s_, in0=pg, in1=s_, op=ALU.mult), [pkg, ks_], [ks_])
            V(lambda e, pu=pu, s_=s_, fc=fc: e.tensor_tensor(out=actT[:, fc, :], in0=pu, in1=s_, op=ALU.mult), [pku, ks_], [("actT", fc)])
        ak = [("actT", fc) for fc in range(22)]
        for t4 in range(GT):
            i = g * GT + t4
            b = t4 % 2
            fw.dma("sync", hres[b], h1_s[i * 128:(i + 1) * 128, :], reads=[("h1_s", i)], writes=[f"hl{b}"])
            fw.flush()
            for half in range(2):
                p, pk = psum()
                for k in range(22):
                    mm(p, actT[:, k, t4 * 128:(t4 + 1) * 128], W2[:, k, half * 512:(half + 1) * 512], k == 0, k == 21, ak + ["W2"], pk, k == 21)
                V(lambda e, p=p, half=half, b=b: e.tensor_tensor(out=ob[b][:, half * 512:(half + 1) * 512], in0=p, in1=hres[b][:, half * 512:(half + 1) * 512], op=ALU.add),
                  [pk, f"hl{b}"], [f"junk{b}"])
            fw.defer_dma("sync", out[i * 128:(i + 1) * 128, :], ob[b], reads=[f"junk{b}"], writes=[("out", i)])
    fw.emit()
    return nc


def _panels(w, kk):
    n = w.shape[1] // 512
    return np.ascontiguousarray(w.reshape(kk, 128, n, 512).transpose(2, 1, 0, 3))


def prep_shared(inp):
    f = lambda a: np.asarray(a, dtype=np.float32)
    w_in = f(inp["w_in"])[0]
    qcols = []
    for j in range(8):
        for s in range(2):
            head = ((j // 4) * 2 + s) * 4 + (j % 4)
            qcols.extend(range(head * 64, head * 64 + 64))
    w_in_r = np.concatenate([w_in[:, qcols], w_in[:, 1024:1536], w_in[:, 1536:]], axis=1)
    wf1 = f(inp["w_ffn_in"])[0]
    cols = []
    for c in range(22):
        cols.extend(range(c * 128, (c + 1) * 128))
        cols.extend(range(DFF + c * 128, DFF + (c + 1) * 128))
    wf1_r = wf1[:, cols]
    rep = lambda v, n: np.ascontiguousarray(np.broadcast_to(f(v).reshape(1, -1), (128, n)))
    gains = np.stack([rep(inp["norm_mix"][0], D), rep(inp["attn_branch_norm"][0], D), rep(inp["ssm_branch_norm"][0], D), rep(inp["norm_ffn"][0], D)])

    def sp(a):
        return np.ascontiguousarray(f(a).reshape(32, 2, 64).transpose(1, 2, 0).reshape(128, 32))

    lam = np.stack([sp(inp["lam_re"][0]), sp(inp["lam_im"][0]), sp(np.broadcast_to(f(inp["log_dt"])[0][:, None], (64, 64)))])
    bre, bim = f(inp["ssm_b_re"])[0], f(inp["ssm_b_im"])[0]
    def spc(a):
        return a.reshape(32, 2, 64, a.shape[-1]).transpose(1, 2, 0, 3).reshape(128, 32, a.shape[-1])
    btc = np.ascontiguousarray(np.stack([spc(bre), spc(bim)], axis=2))
    cre, cim = f(inp["ssm_c_re"])[0], f(inp["ssm_c_im"])[0]
    cc = np.ascontiguousarray(np.stack([spc(cre.transpose(0, 2, 1)), spc(cim.transpose(0, 2, 1))]))
    kk, qq = np.arange(128)[:, None], np.arange(128)[None, :]
    mcur = np.where(kk <= qq, 1.0, 0.0).astype(np.float32)
    mprev = np.where(kk > qq, 1.0, 0.0).astype(np.float32)
    return dict(
        w_in=_panels(w_in_r, 8), w_glu=_panels(f(inp["w_glu"])[0], 8), w_out=_panels(f(inp["w_out"])[0], 8),
        w_f1=_panels(wf1_r, 8), w_f2=_panels(f(inp["w_ffn_out"])[0], 22), gains=gains,
        gq=rep(np.tile(f(inp["q_norm"])[0], 4), 256), gk=rep(np.tile(f(inp["k_norm"])[0], 4), 256),
        sinks=rep(inp["attn_sinks"][0], 16), ident=np.eye(128, dtype=np.float32), lam=lam, btc=btc, cc=cc,
        dcol=np.ascontiguousarray(f(inp["ssm_d"])[0].reshape(8, 128).T),
    ), mcur, mprev


def prep_core(x_b, meta, h, NM, NP, mcur, mprev):
    xmain = np.ascontiguousarray(x_b[h * NM:(h + 1) * NM])
    xpre = np.zeros((NP, D), np.float32)
    xctx = np.zeros((256, D), np.float32)
    xctx[0:16] = meta
    if h == 0:
        xpre[NP - 16:] = meta
        m0 = np.zeros((128, 128), np.float32)
    else:
        xpre[1008:1024] = meta
        xpre[1024:] = x_b[0:NM]
        xctx[128:256] = x_b[NM - 128:NM]
        m0 = mprev
    masks = np.stack([np.tile(mcur, (1, 4)), np.tile(mprev, (1, 4)), np.tile(m0, (1, 4))]).astype(np.float32)
    return dict(xmain=xmain, xpre=xpre, xctx=xctx, masks=masks)


_NC_CACHE = {}


def kernel(**inputs):
    x = np.asarray(inputs["x"], dtype=np.float32)
    Bsz, S, _ = x.shape
    NM = S // 2
    NP = NM + 1024
    meta = np.asarray(inputs["meta_tokens"], dtype=np.float32)
    shared, mcur, mprev = prep_shared(inputs)
    in_maps = []
    for b in range(Bsz):
        for h in range(2):
            d = dict(shared)
            d.update(prep_core(x[b], meta, h, NM, NP, mcur, mprev))
            in_maps.append(d)
    nc = build(NM, NP)
    res = run_bass_kernel_spmd(nc, in_maps, core_ids=list(range(len(in_maps))))
    outp = np.zeros((Bsz, S, D), np.float32)
    for b in range(Bsz):
        for h in range(2):
            outp[b, h * NM:(h + 1) * NM] = res.results[2 * b + h]["out"]
    return outp
```

```python
import math
import contextlib
import numpy as np
import concourse.bass as bass
import concourse.mybir as mybir
from concourse.bass_utils import run_bass_kernel_spmd

F32 = mybir.dt.float32
BF16 = mybir.dt.bfloat16
AF = mybir.ActivationFunctionType
ALU = mybir.AluOpType
AX = mybir.AxisListType
ENGS = ("tensor", "vector", "scalar", "gpsimd", "sync")
D = 1024
DFF = 2816
NEG = -30000.0


class FW:
    def __init__(self, nc, n_dma_sems=40):
        self.nc = nc
        self.ops = {e: [] for e in ENGS}
        self.cnt = {e: 0 for e in ENGS}
        self.known = {e: {} for e in ENGS}
        self.last_w = {}
        self.readers = {}
        self.n_dma_sems = n_dma_sems
        self.dma_gen = [0] * n_dma_sems
        self.dma_rr = 0
        self.sem_names = [f"s_{e}" for e in ENGS] + [f"d_{i}" for i in range(n_dma_sems)]

    def _deps(self, reads, writes):
        evs = []
        for k in reads:
            if k in self.last_w:
                evs.append(self.last_w[k])
        for k in writes:
            if k in self.last_w:
                evs.append(self.last_w[k])
            evs.extend(self.readers.get(k, ()))
        return evs

    def _commit(self, ev, reads, writes):
        for k in reads:
            self.readers.setdefault(k, []).append(ev)
        for k in writes:
            self.last_w[k] = ev
            self.readers[k] = []

    def _waits(self, eng, evs):
        best = {}
        for (s, v) in evs:
            if v > best.get(s, 0):
                best[s] = v
        out = []
        kn = self.known[eng]
        for s, v in best.items():
            if eng == "tensor" and s == "s_tensor":
                continue
            if kn.get(s, 0) >= v:
                continue
            kn[s] = v
            out.append((s, v))
        return out

    def op(self, eng, fn, reads=(), writes=(), inc=True):
        evs = self._deps(reads, writes)
        waits = self._waits(eng, evs)
        sname = f"s_{eng}"
        ev = (sname, self.cnt[eng] + 1)
        if inc:
            self.cnt[eng] += 1
        self.ops[eng].append((waits, fn, (sname, 1) if inc else None))
        self._commit(ev, reads, writes)
        return ev

    def dma(self, queue, out, in_, reads=(), writes=(), **kw):
        i = self.dma_rr
        self.dma_rr = (self.dma_rr + 1) % self.n_dma_sems
        sname = f"d_{i}"
        evs = self._deps(reads, writes)
        if self.dma_gen[i] > 0:
            evs.append((sname, 16 * self.dma_gen[i]))
        waits = self._waits(queue, evs)
        self.dma_gen[i] += 1
        ev = (sname, 16 * self.dma_gen[i])
        self.ops[queue].append((waits, lambda e: e.dma_start(out=out, in_=in_, **kw), (sname, 16)))
        self._commit(ev, reads, writes)
        return ev

    def defer_dma(self, *a, **kw):
        if not hasattr(self, "_deferred"):
            self._deferred = []
        self._deferred.append((a, kw))

    def flush(self):
        for a, kw in getattr(self, "_deferred", []):
            self.dma(*a, **kw)
        self._deferred = []

    def barrier(self):
        self.flush()
        fin = []
        for e in ENGS:
            if self.cnt[e] > 0:
                fin.append((f"s_{e}", self.cnt[e]))
        for i in range(self.n_dma_sems):
            if self.dma_gen[i] > 0:
                fin.append((f"d_{i}", 16 * self.dma_gen[i]))
        for e in ENGS:
            w = self._waits(e, fin)
            if w:
                self.ops[e].append((w, None, None))
        self.last_w = {}
        self.readers = {}

    def emit(self):
        nc = self.nc
        self.barrier()
        with contextlib.ExitStack() as st:
            sems = {n: st.enter_context(nc.semaphore(n)) for n in self.sem_names}
            block = st.enter_context(nc.Block())

            def mk(engname):
                lst = self.ops[engname]

                def body(eng):
                    for (waits, fn, inc) in lst:
                        for (s, v) in waits:
                            eng.wait_ge(sems[s], v)
                        if fn is None:
                            continue
                        ins = fn(eng)
                        if inc is not None:
                            ins.then_inc(sems[inc[0]], inc[1])
                return body

            block.tensor(mk("tensor"))
            block.vector(mk("vector"))
            block.scalar(mk("scalar"))
            block.gpsimd(mk("gpsimd"))
            block.sync(mk("sync"))


class Arena:
    def __init__(self, nc, base=16640, limit=224 * 1024):
        self.nc, self.off, self.limit, self.n = nc, base, limit, 0

    def alloc(self, name, shape, dt):
        per = int(np.prod(shape[1:])) * (4 if dt == F32 else 2)
        per = (per + 63) // 64 * 64
        assert self.off + per <= self.limit, (name, self.off, per)
        self.n += 1
        t = self.nc.alloc_sbuf_tensor_at(f"{name}_{self.n}_{self.off}", list(shape), dt, offset=self.off)
        self.off += per
        return t.ap()

    def mark(self):
        return self.off

    def reset(self, off):
        self.off = off


def build(NM, NP, upto=9):
    nc = bass.Bass("TRN2", target_bir_lowering=False)
    fw = FW(nc)
    TM_, TP_ = NM // 128, NP // 128
    NS = TP_ + TM_
    NK = 2 + TM_

    def din(name, shape, dt=F32):
        return nc.dram_tensor(name, list(shape), dt, kind="ExternalInput").ap()

    xmain = din("xmain", [NM, D]); xpre = din("xpre", [NP, D]); xctx = din("xctx", [256, D])
    w_in = din("w_in", [9, 128, 8, 512]); w_glu = din("w_glu", [4, 128, 8, 512]); w_out = din("w_out", [2, 128, 8, 512])
    w_f1 = din("w_f1", [11, 128, 8, 512]); w_f2 = din("w_f2", [2, 128, 22, 512])
    gains = din("gains", [4, 128, D])
    gq = din("gq", [128, 256]); gk = din("gk", [128, 256]); sinks = din("sinks", [128, 16])
    masks = din("masks", [3, 128, 512])
    ident = din("ident", [128, 128])
    lam = din("lam", [3, 128, 32])
    btc = din("btc", [128, 32, 2, 16]); cc = din("cc", [2, 128, 32, 16]); dcol = din("dcol", [128, 8])
    out = nc.dram_tensor("out", [NM, D], F32, kind="ExternalOutput").ap()

    def dscr(name, shape, dt):
        return nc.dram_tensor(name, list(shape), dt, kind="Internal").ap()

    uT_s = dscr("uT_s", [NS, 128, 8, 128], BF16); qT_s = dscr("qT_s", [TM_, 128, 8, 128], BF16)
    kT_s = dscr("kT_s", [NK, 128, 2, 128], BF16); v_s = dscr("v_s", [NK, 128, 4, 65], BF16)
    g_s = dscr("g_s", [TM_, 128, 2048], BF16); zT_s = dscr("zT_s", [TM_, 128, 8, 128], BF16)
    h1_s = dscr("h1_s", [NM, D], F32)
    DB_s = dscr("DB_s", [8, 128, 8, 4, 2, 128], BF16); EC_s = dscr("EC_s", [8, 128, 8, 4, 2, 128], BF16)

    ar = Arena(nc)
    identf = ar.alloc("identf", [128, 128], F32); identb = ar.alloc("identb", [128, 128], BF16)
    gsb = ar.alloc("gsb", [128, 4, D], F32)
    fw.dma("sync", identf, ident, writes=["identf"])
    fw.op("vector", lambda e: e.tensor_copy(out=identb, in_=identf), reads=["identf"], writes=["identb"])
    fw.dma("sync", gsb, gains.rearrange("a p d -> p a d"), writes=["gsb"])
    pbank = [nc.alloc_psum_tensor(f"pb{i}", [128, 512], F32).ap() for i in range(8)]
    pcnt = [0]

    def psum():
        i = pcnt[0] % 8
        pcnt[0] += 1
        return pbank[i], f"pb{i}"

    rr = [0]

    def alt():
        rr[0] += 1
        return "vector" if rr[0] % 2 else "gpsimd"

    base0 = ar.mark()

    def load_weights(dst, src, npan, kk, key):
        m = ar.mark()
        nst = 3 if ar.off + 3 * 16384 <= ar.limit else 2
        st = [ar.alloc("wst", [128, 8, 512], F32) for _ in range(nst)]
        cyc = ["vector", "scalar", "gpsimd", "vector", "scalar"]
        n = 0
        for pi in range(npan):
            for k0 in range(0, kk, 8):
                kc = min(8, kk - k0)
                s = st[n % nst]
                fw.dma("sync", s[:, :kc, :], src[pi][:, k0:k0 + kc, :], writes=[f"wst{n % nst}"])
                eng = cyc[n % len(cyc)]
                o_ = dst[:, k0:k0 + kc, pi * 512:(pi + 1) * 512]
                if eng == "scalar":
                    fw.op(eng, lambda e, s=s, kc=kc, o_=o_: e.activation(out=o_, in_=s[:, :kc, :], func=AF.Copy), reads=[f"wst{n % nst}"], writes=[key])
                else:
                    fw.op(eng, lambda e, s=s, kc=kc, o_=o_: e.tensor_copy(out=o_, in_=s[:, :kc, :]), reads=[f"wst{n % nst}"], writes=[key])
                n += 1
        fw.barrier()
        ar.reset(m)

    rmsc = [0]

    def rms_scale(xin, gidx, xn_out, rkeys, wkeys, ncol=D):
        pr = rmsc[0] % 2
        rmsc[0] += 1
        jk, sq_ = junk[pr], ssq[pr]
        kj, ks = f"junk{pr}", f"ssq{pr}"
        fw.op("scalar", lambda e: e.activation(out=jk[:, :ncol], in_=xin, func=AF.Square, accum_out=sq_),
              reads=rkeys, writes=[kj, ks])
        fw.op("vector", lambda e: e.tensor_scalar(out=sq_, in0=sq_, scalar1=1.0 / ncol, scalar2=1e-6, op0=ALU.mult, op1=ALU.add),
              reads=[ks], writes=[ks])
        fw.op("scalar", lambda e: e.activation(out=sq_, in_=sq_, func=AF.Sqrt), reads=[ks], writes=[ks])
        fw.op("vector", lambda e: e.reciprocal(out=sq_, in_=sq_), reads=[ks], writes=[ks])
        fw.op("vector", lambda e: e.scalar_tensor_tensor(out=xn_out, in0=xin, scalar=sq_, in1=gsb[:, gidx, :ncol],
                                                         op0=ALU.mult, op1=ALU.mult),
              reads=list(rkeys) + [ks, "gsb"], writes=wkeys)

    def transpose8(src_bf, dstT, rkey, wkey, n=8):
        for half in range((n + 3) // 4):
            p, pk = psum()
            m = min(4, n - half * 4)
            for j in range(m):
                c = half * 4 + j
                fw.op("tensor", lambda e, c=c, j=j, p=p: e.matmul(p[:, j * 128:(j + 1) * 128], lhsT=src_bf[:, c * 128:(c + 1) * 128],
                                                                 rhs=identb, start=True, stop=True),
                      reads=[rkey, "identb"], writes=[pk], inc=(j == m - 1))
            fw.op("vector", lambda e, p=p, half=half, m=m: e.tensor_copy(
                out=dstT[:, half * 4:half * 4 + m, :], in_=p[:, :m * 128].rearrange("p (a b) -> p a b", b=128)),
                reads=[pk], writes=[wkey])

    junk = [ar.alloc("junk", [128, D], F32) for _ in range(2)]; ssq = [ar.alloc("ssq", [128, 1], F32) for _ in range(2)]
    base1 = ar.mark()

    dbg = {}
    V = lambda fn, r, w: fw.op("vector", fn, reads=r, writes=w)
    A_ = lambda fn, r, w: fw.op("scalar", fn, reads=r, writes=w)
    G_ = lambda fn, r, w: fw.op("gpsimd", fn, reads=r, writes=w)

    def mm(out_ap, lhsT, rhs, start, stop, reads, pk, inc):
        fw.op("tensor", lambda e: e.matmul(out_ap, lhsT=lhsT, rhs=rhs, start=start, stop=stop), reads=reads, writes=[pk], inc=inc)

    dcs = ar.alloc("dcs", [128, 8], F32)
    esink = ar.alloc("esink", [128, 16], F32); gqk = ar.alloc("gqk", [128, 256], F32)
    maskb = ar.alloc("maskb", [128, 3, 512], BF16)
    base_persist = ar.mark()
    st32 = ar.alloc("st32", [128, 8192], F32)
    fw.dma("sync", dcs, dcol, writes=["dcs"])
    fw.dma("sync", st32[:, 0:16], sinks, writes=["a"])
    A_(lambda e: e.activation(out=esink, in_=st32[:, 0:16], func=AF.Exp), ["a"], ["esink"])
    fw.dma("sync", st32[:, 1024:1280], gq, writes=["b"])
    fw.dma("sync", st32[:, 2048:2304], gk, writes=["c"])
    V(lambda e: e.tensor_tensor(out=gqk, in0=st32[:, 1024:1280], in1=st32[:, 2048:2304], op=ALU.mult), ["b", "c"], ["gqk"])
    fw.dma("sync", st32[:, 4096:5632].rearrange("p (a c) -> p a c", a=3), masks.rearrange("a p c -> p a c"), writes=["d"])
    V(lambda e: e.tensor_copy(out=maskb, in_=st32[:, 4096:5632].rearrange("p (a c) -> p a c", a=3)), ["d"], ["maskb"])
    fw.barrier()
    ar.reset(base_persist)

    m1 = ar.mark()
    Win = ar.alloc("Win", [128, 8, 4608], BF16)
    load_weights(Win, w_in, 9, 8, "Win")
    QO, KVO, UO, GO = 0, 1024, 1536, 2560
    xt = [ar.alloc("xt", [128, D], F32) for _ in range(2)]
    xnb = [ar.alloc("xnb", [128, D], BF16) for _ in range(2)]
    xnT4 = [ar.alloc("xnT4", [128, 8, 512], BF16) for _ in range(2)]
    uTb4 = [ar.alloc("uTb4", [128, 8, 512], BF16) for _ in range(2)]
    qsq_ = [ar.alloc("qsq", [128, 512], F32) for _ in range(2)]
    qss_ = [ar.alloc("qss", [128, 8], F32) for _ in range(2)]
    hnc = [0]
    qn = [ar.alloc("qn", [128, D], BF16) for _ in range(2)]
    qTb = [ar.alloc("qTb", [128, 8, 128], BF16) for _ in range(2)]
    kf_ = [ar.alloc("kf", [128, 256], F32) for _ in range(2)]
    kn = [ar.alloc("kn", [128, 256], BF16) for _ in range(2)]
    kTb = [ar.alloc("kTb", [128, 2, 128], BF16) for _ in range(2)]
    vab = [ar.alloc("vab", [128, 4, 65], BF16) for _ in range(2)]
    gb = [ar.alloc("gb", [128, 2048], BF16) for _ in range(2)]
    for b in range(2):
        V(lambda e, b=b: e.memset(vab[b][:, :, 64:65], 1.0), [], [f"vab{b}"])

    groups = [[("ctx", xctx[0:128, :], 0, None), ("ctx", xctx[128:256, :], 1, None)]]
    for t in range(0, TP_, 4):
        groups.append([("pre", xpre[(t + q) * 128:(t + q + 1) * 128, :], t + q, None) for q in range(4)])
    for t in range(0, TM_, 4):
        groups.append([("main", xmain[(t + q) * 128:(t + q + 1) * 128, :], TP_ + t + q, t + q) for q in range(4)])

    def headnorm(p, pk, ncol, nh, dst, dkey, gain=None):
        pr = hnc[0] % 2
        hnc[0] += 1
        qsq, qss, kf = qsq_[pr], qss_[pr], kf_[pr]
        kq, ks, kk_ = f"qsq{pr}", f"qss{pr}", f"kf{pr}"
        A_(lambda e: e.activation(out=qsq[:, :ncol], in_=p[:, :ncol], func=AF.Square), [pk], [kq])
        V(lambda e: e.tensor_reduce(out=qss[:, :nh], in_=qsq[:, :ncol].rearrange("p (h d) -> p h d", d=64), axis=AX.X, op=ALU.add), [kq], [ks])
        V(lambda e: e.tensor_scalar(out=qss[:, :nh], in0=qss[:, :nh], scalar1=1.0 / 64, scalar2=1e-6, op0=ALU.mult, op1=ALU.add), [ks], [ks])
        A_(lambda e: e.activation(out=qss[:, :nh], in_=qss[:, :nh], func=AF.Sqrt), [ks], [ks])
        V(lambda e: e.reciprocal(out=qss[:, :nh], in_=qss[:, :nh]), [ks], [ks])
        rb = qss[:, :nh].unsqueeze(2).broadcast_to([128, nh, 64])
        if gain is None:
            V(lambda e: e.tensor_tensor(out=dst.rearrange("p (h d) -> p h d", d=64), in0=p[:, :ncol].rearrange("p (h d) -> p h d", d=64), in1=rb, op=ALU.mult),
              [pk, ks], [dkey])
        else:
            V(lambda e: e.tensor_tensor(out=kf.rearrange("p (h d) -> p h d", d=64), in0=p[:, :ncol].rearrange("p (h d) -> p h d", d=64), in1=rb, op=ALU.mult),
              [pk, ks], [kk_])
            V(lambda e: e.tensor_tensor(out=dst, in0=kf, in1=gain, op=ALU.mult), [kk_, "gqk"], [dkey])

    def tile_gen(gpar, t4, tile):
        kind, src, sidx, midx = tile
        b = t4 % 2
        X4 = xnT4[gpar]
        fw.dma("sync", xt[b], src, writes=[f"xt{b}"])
        fw.flush()
        rms_scale(xt[b], 0, xnb[b], [f"xt{b}"], [f"xnb{b}"])
        yield
        xk = f"xnT{gpar}_{t4}"
        XT = X4[:, :, t4 * 128:(t4 + 1) * 128]
        transpose8(xnb[b], XT, f"xnb{b}", xk)
        yield
        if kind == "main":
            qps = []
            for half in range(2):
                p, pk = psum()
                for k in range(8):
                    mm(p, XT[:, k, :], Win[:, k, QO + half * 512:QO + (half + 1) * 512], k == 0, k == 7, [xk, "Win"], pk, k == 7)
                yield
                headnorm(p, pk, 512, 8, qn[b][:, half * 512:(half + 1) * 512], f"qn{b}")
                yield
        if kind in ("ctx", "main"):
            kidx = sidx if kind == "ctx" else 2 + midx
            p, pk = psum()
            for k in range(8):
                mm(p, XT[:, k, :], Win[:, k, KVO:KVO + 512], k == 0, k == 7, [xk, "Win"], pk, k == 7)
            yield
            headnorm(p, pk, 256, 4, kn[b], f"kn{b}", gain=gqk)
            V(lambda e, p=p, b=b: e.tensor_copy(out=vab[b][:, :, 0:64], in_=p[:, 256:512].rearrange("p (h d) -> p h d", d=64)), [pk], [f"vab{b}"])
            fw.defer_dma("sync", v_s[kidx], vab[b], reads=[f"vab{b}"], writes=[("v_s", kidx)])
            yield
        if kind == "main":
            for j in range(4):
                p, pk = psum()
                for k in range(8):
                    mm(p, XT[:, k, :], Win[:, k, GO + j * 512:GO + (j + 1) * 512], k == 0, k == 7, [xk, "Win"], pk, k == 7)
                A_(lambda e, p=p, j=j, b=b: e.activation(out=gb[b][:, j * 512:(j + 1) * 512], in_=p, func=AF.Sigmoid), [pk], [f"gb{b}"])
                yield
            fw.defer_dma("sync", g_s[midx], gb[b], reads=[f"gb{b}"], writes=[("g_s", midx)])
            transpose8(qn[b], qTb[b], f"qn{b}", f"qTb{b}")
            fw.defer_dma("sync", qT_s[midx], qTb[b], reads=[f"qTb{b}"], writes=[("qT_s", midx)])
            yield
        if kind in ("ctx", "main"):
            transpose8(kn[b], kTb[b], f"kn{b}", f"kTb{b}", n=2)
            fw.defer_dma("sync", kT_s[kidx], kTb[b], reads=[f"kTb{b}"], writes=[("kT_s", kidx)])
            yield

    def lockstep(gens):
        alive = True
        while alive:
            alive = False
            for g_ in gens:
                try:
                    next(g_)
                    alive = True
                except StopIteration:
                    pass

    for gi, grp_tiles in enumerate(groups):
        gpar = gi % 2
        X4 = xnT4[gpar]
        xkeys = [f"xnT{gpar}_{t4}" for t4 in range(len(grp_tiles))]
        for t0 in range(0, len(grp_tiles), 2):
            lockstep([tile_gen(gpar, t0 + q, grp_tiles[t0 + q]) for q in range(2)])
        if grp_tiles[0][0] in ("pre", "main"):
            s0 = grp_tiles[0][2]
            for ct in range(8):
                p, pk = psum()
                for k in range(8):
                    mm(p, Win[:, k, UO + ct * 128:UO + (ct + 1) * 128], X4[:, k, :], k == 0, k == 7, xkeys + ["Win"], pk, k == 7)
                if ct % 2 == 0:
                    V(lambda e, p=p, ct=ct, gpar=gpar: e.tensor_copy(out=uTb4[gpar][:, ct, :], in_=p), [pk], [f"uTb4{gpar}"])
                else:
                    A_(lambda e, p=p, ct=ct, gpar=gpar: e.activation(out=uTb4[gpar][:, ct, :], in_=p, func=AF.Copy), [pk], [f"uTb4{gpar}"])
            fw.defer_dma("sync", uT_s[s0:s0 + 4].rearrange("n p c t -> p c n t"), uTb4[gpar].rearrange("p c (n t) -> p c n t", t=128),
                         reads=[f"uTb4{gpar}"], writes=[("uT_s", s0)])
    fw.barrier()
    ar.reset(m1)

    if upto <= 1:
        fw.emit()
        return nc
    lamsb = ar.alloc("lamsb", [128, 3, 32], F32)
    fw.dma("sync", lamsb, lam.rearrange("a p g -> p a g"), writes=["lam"])
    smn = ["dt", "th", "rho", "sn", "cs", "ar", "ai", "fr", "fi", "t1", "t2", "t3", "den", "wr", "wi", "w128r", "w128i", "mk", "x2", "lrdt"]
    sm = {n: ar.alloc(n, [128, 32], F32) for n in smn}
    pwr = [ar.alloc("pwr", [128, 32], F32) for _ in range(9)]; pwi = [ar.alloc("pwi", [128, 32], F32) for _ in range(9)]
    Er = ar.alloc("Er", [128, 32, 128], F32); Ei = ar.alloc("Ei", [128, 32, 128], F32)
    Kpad = ar.alloc("Kpad", [128, 8, 8, 128], BF16)
    cR = ar.alloc("cR", [128, 2, 32], F32); SL = ar.alloc("SL", [128, 2, 32], F32)
    base_ssm = ar.mark()
    lr, li, ld = lamsb[:, 0, :], lamsb[:, 1, :], lamsb[:, 2, :]
    K = ["ssm0"]
    A_(lambda e: e.activation(out=sm["dt"], in_=ld, func=AF.Exp), ["lam"], K)
    V(lambda e: e.tensor_tensor(out=sm["th"], in0=li, in1=sm["dt"], op=ALU.mult), K, K)
    V(lambda e: e.tensor_tensor(out=sm["lrdt"], in0=lr, in1=sm["dt"], op=ALU.mult), K, K)
    A_(lambda e: e.activation(out=sm["rho"], in_=sm["lrdt"], func=AF.Exp), K, K)
    for _ in range(5):
        V(lambda e: e.tensor_single_scalar(out=sm["mk"], in_=sm["th"], scalar=math.pi, op=ALU.is_gt), K, K)
        V(lambda e: e.scalar_tensor_tensor(out=sm["th"], in0=sm["mk"], scalar=-2.0 * math.pi, in1=sm["th"], op0=ALU.mult, op1=ALU.add), K, K)
    V(lambda e: e.tensor_scalar(out=sm["t3"], in0=sm["th"], scalar1=0.125, scalar2=None, op0=ALU.mult), K, K)
    V(lambda e: e.tensor_tensor(out=sm["x2"], in0=sm["t3"], in1=sm["t3"], op=ALU.mult), K, K)

    def horner(o, coefs):
        V(lambda e: e.memset(o, coefs[0]), K, K)
        for c in coefs[1:]:
            V(lambda e: e.tensor_tensor(out=o, in0=o, in1=sm["x2"], op=ALU.mult), K, K)
            V(lambda e, c=c: e.tensor_scalar(out=o, in0=o, scalar1=float(c), scalar2=None, op0=ALU.add), K, K)

    def cdouble(sn_, cs_):
        V(lambda e: e.tensor_tensor(out=sm["t1"], in0=sn_, in1=cs_, op=ALU.mult), K, K)
        V(lambda e: e.tensor_tensor(out=sm["t2"], in0=cs_, in1=cs_, op=ALU.mult), K, K)
        V(lambda e: e.tensor_tensor(out=sm["t3"], in0=sn_, in1=sn_, op=ALU.mult), K, K)
        V(lambda e: e.tensor_scalar(out=sn_, in0=sm["t1"], scalar1=2.0, scalar2=None, op0=ALU.mult), K, K)
        V(lambda e: e.tensor_tensor(out=cs_, in0=sm["t2"], in1=sm["t3"], op=ALU.subtract), K, K)

    horner(sm["sn"], [-1 / 39916800.0, 1 / 362880.0, -1 / 5040.0, 1 / 120.0, -1 / 6.0, 1.0])
    V(lambda e: e.tensor_tensor(out=sm["sn"], in0=sm["sn"], in1=sm["t3"], op=ALU.mult), K, K)
    horner(sm["cs"], [-1 / 3628800.0, 1 / 40320.0, -1 / 720.0, 1 / 24.0, -0.5, 1.0])
    for _ in range(3):
        cdouble(sm["sn"], sm["cs"])
    V(lambda e: e.tensor_tensor(out=sm["ar"], in0=sm["rho"], in1=sm["cs"], op=ALU.mult), K, K)
    V(lambda e: e.tensor_tensor(out=sm["ai"], in0=sm["rho"], in1=sm["sn"], op=ALU.mult), K, K)
    V(lambda e: e.tensor_scalar(out=sm["t1"], in0=sm["ar"], scalar1=-1.0, scalar2=None, op0=ALU.add), K, K)
    V(lambda e: e.tensor_tensor(out=sm["den"], in0=lr, in1=lr, op=ALU.mult), K, K)
    V(lambda e: e.tensor_tensor(out=sm["t2"], in0=li, in1=li, op=ALU.mult), K, K)
    V(lambda e: e.tensor_tensor(out=sm["den"], in0=sm["den"], in1=sm["t2"], op=ALU.add), K, K)
    V(lambda e: e.reciprocal(out=sm["den"], in_=sm["den"]), K, K)
    V(lambda e: e.tensor_tensor(out=sm["t2"], in0=sm["t1"], in1=lr, op=ALU.mult), K, K)
    V(lambda e: e.tensor_tensor(out=sm["t3"], in0=sm["ai"], in1=li, op=ALU.mult), K, K)
    V(lambda e: e.tensor_tensor(out=sm["t2"], in0=sm["t2"], in1=sm["t3"], op=ALU.add), K, K)
    V(lambda e: e.tensor_tensor(out=sm["fr"], in0=sm["t2"], in1=sm["den"], op=ALU.mult), K, K)
    V(lambda e: e.tensor_tensor(out=sm["t2"], in0=sm["ai"], in1=lr, op=ALU.mult), K, K)
    V(lambda e: e.tensor_tensor(out=sm["t3"], in0=sm["t1"], in1=li, op=ALU.mult), K, K)
    V(lambda e: e.tensor_tensor(out=sm["t2"], in0=sm["t2"], in1=sm["t3"], op=ALU.subtract), K, K)
    V(lambda e: e.tensor_tensor(out=sm["fi"], in0=sm["t2"], in1=sm["den"], op=ALU.mult), K, K)
    V(lambda e: e.memset(pwr[0], 1.0), K, K)
    V(lambda e: e.memset(pwi[0], 0.0), K, K)
    for k in range(1, 9):
        V(lambda e, k=k: e.tensor_tensor(out=sm["t1"], in0=pwr[k - 1], in1=sm["ar"], op=ALU.mult), K, K)
        V(lambda e, k=k: e.tensor_tensor(out=sm["t2"], in0=pwi[k - 1], in1=sm["ai"], op=ALU.mult), K, K)
        V(lambda e, k=k: e.tensor_tensor(out=pwr[k], in0=sm["t1"], in1=sm["t2"], op=ALU.subtract), K, K)
        V(lambda e, k=k: e.tensor_tensor(out=sm["t1"], in0=pwr[k - 1], in1=sm["ai"], op=ALU.mult), K, K)
        V(lambda e, k=k: e.tensor_tensor(out=sm["t2"], in0=pwi[k - 1], in1=sm["ar"], op=ALU.mult), K, K)
        V(lambda e, k=k: e.tensor_tensor(out=pwi[k], in0=sm["t1"], in1=sm["t2"], op=ALU.add), K, K)
    V(lambda e: e.tensor_scalar(out=sm["t1"], in0=sm["lrdt"], scalar1=8.0, scalar2=None, op0=ALU.mult), K, K)
    A_(lambda e: e.activation(out=sm["rho"], in_=sm["t1"], func=AF.Exp), K, K)
    for _ in range(3):
        cdouble(sm["sn"], sm["cs"])
    V(lambda e: e.memset(Er[:, :, 0:1], 1.0), K, K)
    V(lambda e: e.memset(Ei[:, :, 0:1], 0.0), K, K)
    V(lambda e: e.tensor_copy(out=sm["wr"], in_=sm["cs"]), K, K)
    V(lambda e: e.tensor_copy(out=sm["wi"], in_=sm["sn"]), K, K)
    m0 = ar.mark()
    tA = ar.alloc("tA", [128, 32, 64], F32); tB = ar.alloc("tB", [128, 32, 64], F32)
    for k in range(7):
        n = 1 << k
        wrb = sm["wr"].unsqueeze(2).broadcast_to([128, 32, n]); wib = sm["wi"].unsqueeze(2).broadcast_to([128, 32, n])
        V(lambda e, n=n, wrb=wrb: e.tensor_tensor(out=tA[:, :, :n], in0=Er[:, :, :n], in1=wrb, op=ALU.mult), K, K)
        V(lambda e, n=n, wib=wib: e.tensor_tensor(out=tB[:, :, :n], in0=Ei[:, :, :n], in1=wib, op=ALU.mult), K, K)
        V(lambda e, n=n: e.tensor_tensor(out=Er[:, :, n:2 * n], in0=tA[:, :, :n], in1=tB[:, :, :n], op=ALU.subtract), K, K)
        V(lambda e, n=n, wib=wib: e.tensor_tensor(out=tA[:, :, :n], in0=Er[:, :, :n], in1=wib, op=ALU.mult), K, K)
        V(lambda e, n=n, wrb=wrb: e.tensor_tensor(out=tB[:, :, :n], in0=Ei[:, :, :n], in1=wrb, op=ALU.mult), K, K)
        V(lambda e, n=n: e.tensor_tensor(out=Ei[:, :, n:2 * n], in0=tA[:, :, :n], in1=tB[:, :, :n], op=ALU.add), K, K)
        V(lambda e: e.tensor_tensor(out=sm["t1"], in0=sm["wr"], in1=sm["wr"], op=ALU.mult), K, K)
        V(lambda e: e.tensor_tensor(out=sm["t2"], in0=sm["wi"], in1=sm["wi"], op=ALU.mult), K, K)
        V(lambda e: e.tensor_tensor(out=sm["t3"], in0=sm["wr"], in1=sm["wi"], op=ALU.mult), K, K)
        V(lambda e: e.tensor_tensor(out=sm["wr"], in0=sm["t1"], in1=sm["t2"], op=ALU.subtract), K, K)
        V(lambda e: e.tensor_scalar(out=sm["wi"], in0=sm["t3"], scalar1=2.0, scalar2=None, op0=ALU.mult), K, K)
    V(lambda e: e.tensor_copy(out=sm["w128r"], in_=sm["wr"]), K, K)
    V(lambda e: e.tensor_copy(out=sm["w128i"], in_=sm["wi"]), K, K)
    V(lambda e: e.memset(cR, 0.0), K, K)
    V(lambda e: e.memset(SL, 0.0), K, K)
    fw.barrier()
    ar.reset(m0)
    BTc = ar.alloc("BTc", [128, 32, 2, 16], F32); Cc = ar.alloc("Cc", [128, 2, 32, 16], F32)
    Cfc = ar.alloc("Cfc", [128, 32, 2, 16], F32); Xc2 = [ar.alloc("Xc", [128, 32, 2, 16], F32) for _ in range(2)]
    c1 = ar.alloc("c1", [128, 32, 16], F32); c2 = ar.alloc("c2", [128, 32, 16], F32)
    Cfp = ar.alloc("Cfp", [128, 32, 2, 128], BF16)
    padb2 = [ar.alloc("padb", [128, 32, 2, 128], BF16) for _ in range(2)]; DBsb2 = [ar.alloc("DBsb", [128, 32, 2, 128], BF16) for _ in range(2)]
    fw.dma("sync", BTc, btc, writes=["BTc"])
    fw.dma("sync", Cc, cc.rearrange("a p g c -> p a g c"), writes=["Cc"])
    for q_ in range(2):
        G_(lambda e, q_=q_: e.memset(padb2[q_], 0.0), [], [f"padb{q_}"])
    G_(lambda e: e.memset(Cfp, 0.0), [], ["Cfp"])

    def cmul_compact(dst, src_r, src_i, sr, si, rk, wk, neg_im=False):
        srb = sr.unsqueeze(2).broadcast_to([128, 32, 16]); sib = si.unsqueeze(2).broadcast_to([128, 32, 16])
        V(lambda e: e.tensor_tensor(out=c1, in0=src_r, in1=srb, op=ALU.mult), rk, ["c1"])
        V(lambda e: e.tensor_tensor(out=c2, in0=src_i, in1=sib, op=ALU.mult), rk, ["c2"])
        V(lambda e: e.tensor_tensor(out=dst[:, :, 0, :], in0=c1, in1=c2, op=ALU.subtract), ["c1", "c2"], wk)
        V(lambda e: e.tensor_tensor(out=c1, in0=src_r, in1=sib, op=ALU.mult), rk + wk, ["c1"])
        V(lambda e: e.tensor_tensor(out=c2, in0=src_i, in1=srb, op=ALU.mult), rk + wk, ["c2"])
        if neg_im:
            V(lambda e: e.scalar_tensor_tensor(out=dst[:, :, 1, :], in0=c1, scalar=-1.0, in1=c2, op0=ALU.mult, op1=ALU.subtract), ["c1", "c2"], wk)
        else:
            V(lambda e: e.tensor_tensor(out=dst[:, :, 1, :], in0=c1, in1=c2, op=ALU.add), ["c1", "c2"], wk)

    def scatter(dst_pad, src_c, rk, wk):
        for g2 in range(2):
            for q in range(4):
                blk = 2 * q + g2
                G_(lambda e, g2=g2, q=q, blk=blk: e.tensor_copy(out=dst_pad[g2 * 64:(g2 + 1) * 64, q::4, :, blk * 16:(blk + 1) * 16],
                                                                 in_=src_c[g2 * 64:(g2 + 1) * 64, q::4, :, :]), rk, wk)

    cmul_compact(Cfc, Cc[:, 0], Cc[:, 1], sm["fr"], sm["fi"], ["Cc"], ["Cfc"])
    cmul_compact(Xc2[0], Cc[:, 0], Cc[:, 1], sm["fr"], sm["fi"], ["Cc"], ["Xc0"], neg_im=True)
    scatter(Cfp, Xc2[0], ["Xc0"], ["Cfp"])
    for k in range(8):
        j = 7 - k
        q_ = k % 2
        Xc, padb, DBsb = Xc2[q_], padb2[q_], DBsb2[q_]
        kX, kP, kD = f"Xc{q_}", f"padb{q_}", f"DBsb{q_}"
        cmul_compact(Xc, BTc[:, :, 0, :], BTc[:, :, 1, :], pwr[k], pwi[k], ["BTc"], [kX])
        scatter(padb, Xc, [kX], [kP])
        for r in range(8):
            p, pk = psum()
            n_ = 0
            for gl in range(4):
                for ri in range(2):
                    mm(p[:, 0:128], padb[:, 4 * r + gl, ri, :], Cfp[:, 4 * r + gl, ri, :], n_ == 0, n_ == 7, [kP, "Cfp"], pk, n_ == 7)
                    n_ += 1
            if k == 0:
                V(lambda e, p=p, r=r: e.scalar_tensor_tensor(out=Kpad[:, r, 0, :], in0=identf, scalar=dcs[:, r:r + 1], in1=p[:, 0:128], op0=ALU.mult, op1=ALU.add),
                  [pk, "identf", "dcs"], ["Kpad"])
            else:
                A_(lambda e, p=p, r=r, k=k: e.activation(out=Kpad[:, r, k, :], in_=p[:, 0:128], func=AF.Copy), [pk], ["Kpad"])
        for g4 in range(16):
            p, pk = psum()
            for q in range(4):
                gi = g4 * 4 + q
                mm(p[:, q * 128:(q + 1) * 128], padb[:, gi // 2, gi % 2, :], identb, True, True, [kP, "identb"], pk, q == 3)
            dst_ = DBsb.rearrange("p g r s -> p (g r) s")[:, g4 * 4:(g4 + 1) * 4, :]
            if g4 % 2 == 0:
                V(lambda e, p=p, dst_=dst_: e.tensor_copy(out=dst_, in_=p.rearrange("p (a c) -> p a c", c=128)), [pk], [kD])
            else:
                A_(lambda e, p=p, dst_=dst_: e.activation(out=dst_, in_=p.rearrange("p (a c) -> p a c", c=128), func=AF.Copy), [pk], [kD])
        fw.dma("sync", DB_s[:, :, j].rearrange("r p g a s -> p r g a s"), DBsb.rearrange("p (r g) a s -> p r g a s", r=8), reads=[kD], writes=[("DB_s", j)])
    for j in range(8):
        q_ = j % 2
        Xc, padb = Xc2[q_], padb2[q_]
        kX, kP = f"Xc{q_}", f"padb{q_}"
        cmul_compact(Xc, Cfc[:, :, 0, :], Cfc[:, :, 1, :], pwr[j + 1], pwi[j + 1], ["Cfc"], [kX])
        scatter(padb, Xc, [kX], [kP])
        fw.dma("sync", EC_s[:, :, j].rearrange("r p g a s -> p r g a s"), padb.rearrange("p (r g) a s -> p r g a s", r=8), reads=[kP], writes=[("EC_s", j)])
    fw.barrier()
    ar.reset(base_ssm)

    if upto <= 2:
        fw.emit()
        return nc
    NPS, NMS = NP // 1024, NM // 1024
    DBr2 = [ar.alloc("DBr", [128, 8, 4, 2, 128], BF16) for _ in range(2)]
    ECr2 = [ar.alloc("ECr", [128, 8, 4, 2, 128], BF16) for _ in range(2)]
    uTr = [ar.alloc("uTr", [128, 1024], BF16) for _ in range(2)]
    uTj = [ar.alloc("uTj", [128, 8, 128], BF16) for _ in range(2)]
    zTr = [ar.alloc("zTr", [128, 1024], BF16) for _ in range(2)]
    RB = []
    for b in range(2):
        d = {n: ar.alloc(n, [128, 4, 128], F32) for n in ["t1", "t2", "t3", "t4", "Rr", "Ri"]}
        d["Xr"], d["Xi"] = d["t1"], d["t3"]
        d["Sr"] = ar.alloc("Sr", [128, 4, 130], BF16); d["Si"] = ar.alloc("Si", [128, 4, 130], BF16)
        for n in ["c1", "c2", "c3", "c4"]:
            d[n] = ar.alloc(n, [128, 4], F32)
        RB.append(d)
    ysb = [ar.alloc("ysb", [128, 1024], F32) for _ in range(2)]
    g1b = [ar.alloc("g1b", [128, 1024], F32) for _ in range(2)]
    g2b = g1b
    rcount = 0
    hcount = 0
    def round_gen(rp, st, q_):
        r = 2 * rp + q_
        gsl = slice(4 * r, 4 * r + 4)
        DBr, ECr = DBr2[q_], ECr2[q_]
        kDB, kEC = f"DBr{q_}", f"ECr{q_}"
        is_main = st >= NPS
        ub = q_
        fw.dma("sync", uTr[ub].rearrange("p (n t) -> p n t", t=128), uT_s[8 * st:8 * st + 8, :, r, :].rearrange("n p t -> p n t"),
               writes=[f"uTr{ub}"])
        fw.flush()
        A_(lambda e, ub=ub: e.activation(out=uTj[ub], in_=uTr[ub].rearrange("p (c j) -> p j c", j=8), func=AF.Copy), [f"uTr{ub}"], [f"uTj{ub}"])
        yield
        b = q_
        B = RB[b]
        kb = lambda n, b=b: f"{n}{b}"
        pXr, pkr = psum()
        pXi, pki = psum()
        for ri, (pX, pk) in enumerate(((pXr, pkr), (pXi, pki))):
            for gl in range(4):
                for j in range(8):
                    mm(pX[:, gl * 128:(gl + 1) * 128], DBr[:, j, gl, ri, :], uTj[ub][:, j, :], j == 0, j == 7,
                       [f"uTj{ub}", kDB], pk, (gl == 3 and j == 7))
        yield
        pXr3 = pXr.rearrange("p (a c) -> p a c", c=128); pXi3 = pXi.rearrange("p (a c) -> p a c", c=128)
        Erg, Eig = Er[:, gsl, :], Ei[:, gsl, :]
        V(lambda e, B=B, a=pXr3, t=Erg: e.tensor_tensor(out=B["t1"], in0=a, in1=t, op=ALU.mult), [pkr], [kb("t1")])
        V(lambda e, B=B, a=pXi3, t=Eig: e.tensor_tensor(out=B["t2"], in0=a, in1=t, op=ALU.mult), [pki], [kb("t2")])
        V(lambda e, B=B, a=pXi3, t=Erg: e.tensor_tensor(out=B["t3"], in0=a, in1=t, op=ALU.mult), [pki], [kb("t3")])
        V(lambda e, B=B, a=pXr3, t=Eig: e.tensor_tensor(out=B["t4"], in0=a, in1=t, op=ALU.mult), [pkr], [kb("t4")])
        yield
        G_(lambda e, B=B: e.tensor_tensor(out=B["t1"], in0=B["t1"], in1=B["t2"], op=ALU.add), [kb("t1"), kb("t2")], [kb("t1")])
        G_(lambda e, B=B: e.tensor_tensor(out=B["t3"], in0=B["t3"], in1=B["t4"], op=ALU.subtract), [kb("t3"), kb("t4")], [kb("t3")])
        yield
        for gl in range(4):
            gp = 4 * r + gl
            for nm, xs, ci in (("Rr", "t1", 0), ("Ri", "t3", 1)):
                V(lambda e, B=B, gl=gl, gp=gp, nm=nm, xs=xs, ci=ci: e.tensor_tensor_scan(
                    out=B[nm][:, gl, :], data0=sm["rho"][:, gp:gp + 1].broadcast_to([128, 128]), data1=B[xs][:, gl, :],
                    initial=cR[:, ci, gp:gp + 1], op0=ALU.mult, op1=ALU.add), [kb(xs), ("cR", r)], [kb(nm)])
        yield
        if is_main:
            G_(lambda e, B=B, gsl=gsl: e.tensor_copy(out=B["Sr"][:, :, 0], in_=SL[:, 0, gsl]), [("SL", r)], [kb("Sr")])
            G_(lambda e, B=B, gsl=gsl: e.tensor_copy(out=B["Si"][:, :, 0], in_=SL[:, 1, gsl]), [("SL", r)], [kb("Si")])
        wr4, wi4 = sm["w128r"][:, gsl], sm["w128i"][:, gsl]
        er7, ei7 = Er[:, gsl, 127], Ei[:, gsl, 127]
        Rr7, Ri7 = B["Rr"][:, :, 127], B["Ri"][:, :, 127]
        for (xr_, xi_, dst, negim, key) in ((wr4, wi4, cR, False, "cR"), (er7, ei7, SL, True, "SL")):
            G_(lambda e, B=B, a=Rr7, w=xr_: e.tensor_tensor(out=B["c1"], in0=a, in1=w, op=ALU.mult), [kb("Rr")], [kb("c1")])
            G_(lambda e, B=B, a=Ri7, w=xi_: e.tensor_tensor(out=B["c2"], in0=a, in1=w, op=ALU.mult), [kb("Ri")], [kb("c2")])
            G_(lambda e, B=B, a=Ri7, w=xr_: e.tensor_tensor(out=B["c3"], in0=a, in1=w, op=ALU.mult), [kb("Ri")], [kb("c3")])
            G_(lambda e, B=B, a=Rr7, w=xi_: e.tensor_tensor(out=B["c4"], in0=a, in1=w, op=ALU.mult), [kb("Rr")], [kb("c4")])
            G_(lambda e, B=B, dst=dst, gsl=gsl: e.tensor_tensor(out=dst[:, 0, gsl], in0=B["c1"], in1=B["c2"], op=ALU.subtract),
               [kb("c1"), kb("c2")], [(key, r)])
            if negim:
                V(lambda e, B=B, dst=dst, gsl=gsl: e.scalar_tensor_tensor(out=dst[:, 1, gsl], in0=B["c3"], scalar=-1.0, in1=B["c4"], op0=ALU.mult, op1=ALU.subtract),
                   [kb("c3"), kb("c4")], [(key, r)])
            else:
                G_(lambda e, B=B, dst=dst, gsl=gsl: e.tensor_tensor(out=dst[:, 1, gsl], in0=B["c3"], in1=B["c4"], op=ALU.add),
                   [kb("c3"), kb("c4")], [(key, r)])
        yield
        if not is_main:
            return
        G_(lambda e, B=B, t=Erg: e.tensor_tensor(out=B["t1"], in0=B["Rr"], in1=t, op=ALU.mult), [kb("Rr")], [kb("t1")])
        G_(lambda e, B=B, t=Eig: e.tensor_tensor(out=B["t2"], in0=B["Ri"], in1=t, op=ALU.mult), [kb("Ri")], [kb("t2")])
        yield
        V(lambda e, B=B, t=Erg: e.tensor_tensor(out=B["t3"], in0=B["Ri"], in1=t, op=ALU.mult), [kb("Ri")], [kb("t3")])
        V(lambda e, B=B, t=Eig: e.tensor_tensor(out=B["t4"], in0=B["Rr"], in1=t, op=ALU.mult), [kb("Rr")], [kb("t4")])
        yield
        V(lambda e, B=B: e.tensor_tensor(out=B["Sr"][:, :, 1:129], in0=B["t1"], in1=B["t2"], op=ALU.subtract), [kb("t1"), kb("t2")], [kb("Sr")])
        V(lambda e, B=B: e.scalar_tensor_tensor(out=B["Si"][:, :, 1:129], in0=B["t3"], scalar=-1.0, in1=B["t4"], op0=ALU.mult, op1=ALU.subtract),
          [kb("t3"), kb("t4")], [kb("Si")])
        zb = q_
        hb = q_
        yield
        for h2 in range(2):
            py, pky = psum()
            for j4 in range(4):
                j = 4 * h2 + j4
                o = py[:, j4 * 128:(j4 + 1) * 128]
                nmm = (j + 1) + 8
                n_ = 0
                for k in range(j + 1):
                    mm(o, Kpad[:, r, k, :], uTj[ub][:, j - k, :], n_ == 0, n_ == nmm - 1, [f"uTj{ub}", "Kpad"], pky, False)
                    n_ += 1
                for gl in range(4):
                    for ri, Sn in enumerate(("Sr", "Si")):
                        mm(o, ECr[:, j, gl, ri, :], B[Sn][:, gl, 0:128], n_ == 0, n_ == nmm - 1, [kb(Sn), kEC], pky,
                           (j4 == 3 and n_ == nmm - 1))
                        n_ += 1
            yield
            A_(lambda e, py=py, hb=hb, h2=h2: e.activation(out=ysb[hb].rearrange("p (c j) -> p c j", j=8)[:, :, 4 * h2:4 * h2 + 4],
                                                           in_=py.rearrange("p (j c) -> p c j", c=128), func=AF.Identity), [pky], [f"ysb{hb}"])
        yield
        G_(lambda e, hb=hb: e.tensor_tensor(out=g1b[hb], in0=ysb[hb], in1=ysb[hb], op=ALU.mult), [f"ysb{hb}"], [f"g1b{hb}"])
        G_(lambda e, hb=hb: e.tensor_scalar(out=g1b[hb], in0=g1b[hb], scalar1=0.044715, scalar2=1.0, op0=ALU.mult, op1=ALU.add), [f"g1b{hb}"], [f"g1b{hb}"])
        G_(lambda e, hb=hb: e.tensor_tensor(out=g1b[hb], in0=g1b[hb], in1=ysb[hb], op=ALU.mult), [f"g1b{hb}", f"ysb{hb}"], [f"g1b{hb}"])
        yield
        A_(lambda e, hb=hb: e.activation(out=g2b[hb], in_=g1b[hb], func=AF.Sigmoid, scale=1.5957691216057308), [f"g1b{hb}"], [f"g2b{hb}"])
        yield
        V(lambda e, hb=hb, zb=zb: e.tensor_tensor(out=zTr[zb], in0=ysb[hb], in1=g2b[hb], op=ALU.mult), [f"ysb{hb}", f"g2b{hb}"], [f"zTr{zb}"])
        m8 = 8 * (st - NPS)
        fw.defer_dma("sync", zT_s[m8:m8 + 8, :, r, :].rearrange("n p t -> p n t"), zTr[zb].rearrange("p (n t) -> p n t", t=128),
               reads=[f"zTr{zb}"], writes=[("zT_s", st, r)])

    for rp in range(4):
        for q_ in range(2):
            fw.dma("sync", DBr2[q_], DB_s[2 * rp + q_], writes=[f"DBr{q_}"])
            fw.dma("sync", ECr2[q_], EC_s[2 * rp + q_], writes=[f"ECr{q_}"])
        for st in range(NPS + NMS):
            gens = [round_gen(rp, st, 0), round_gen(rp, st, 1)]
            alive = True
            while alive:
                alive = False
                for g_ in gens:
                    try:
                        next(g_)
                        alive = True
                    except StopIteration:
                        pass
    fw.barrier()
    ar.reset(base_persist)

    if upto <= 3:
        fw.emit()
        return nc
    Wg = ar.alloc("Wg", [128, 8, 2048], BF16); Wo = ar.alloc("Wo", [128, 8, 1024], BF16)
    load_weights(Wg, w_glu, 4, 8, "Wg")
    load_weights(Wo, w_out, 2, 8, "Wo")
    kme = ar.alloc("kme", [128, 2, 128], BF16); vme = ar.alloc("vme", [128, 4, 65], BF16)
    fw.dma("sync", kme, kT_s[0], writes=["kme"]); fw.dma("sync", vme, v_s[0], writes=["vme"])
    qTl = [ar.alloc("qTl", [128, 8, 128], BF16) for _ in range(2)]
    kTl = [ar.alloc("kTl", [128, 2, 128], BF16) for _ in range(3)]
    vl = [ar.alloc("vl", [128, 4, 65], BF16) for _ in range(3)]
    gl_ = [ar.alloc("gl", [128, 2048], BF16) for _ in range(2)]
    zTl = [ar.alloc("zTl", [128, 8, 128], BF16) for _ in range(2)]
    xr = [ar.alloc("xr", [128, D], F32) for _ in range(2)]
    Pc = [ar.alloc("Pc", [128, 512], BF16) for _ in range(4)]
    Pp = [ar.alloc("Pp", [128, 512], BF16) for _ in range(4)]
    Pm = [ar.alloc("Pm", [128, 512], BF16) for _ in range(4)]
    den_ = [ar.alloc("den", [128, 4], F32) for _ in range(4)]
    for b_ in range(4):
        V(lambda e, b_=b_: e.memset(Pm[b_], 0.0), [], [f"Pm{b_}"])
    attn_ = [ar.alloc("attn", [128, D], F32) for _ in range(2)]; An_ = [ar.alloc("An", [128, D], F32) for _ in range(2)]
    sig_ = [ar.alloc("sig", [128, 512], F32) for _ in range(2)]; ssm_ = [ar.alloc("ssm", [128, D], F32) for _ in range(2)]
    Bn_ = [ar.alloc("Bn", [128, D], F32) for _ in range(2)]
    mg_ = [ar.alloc("mg", [128, D], BF16) for _ in range(2)]; mgT_ = [ar.alloc("mgT", [128, 8, 128], BF16) for _ in range(2)]
    h1 = [ar.alloc("h1", [128, D], F32) for _ in range(2)]
    fw.dma("sync", kTl[1], kT_s[1], writes=["kTl1"]); fw.dma("sync", vl[1], v_s[1], writes=["vl1"])
    def s3_loads(i):
        b = i % 2
        jc = 2 + i
        sc = jc % 3
        fw.dma("sync", kTl[sc], kT_s[jc], reads=[("kT_s", jc)], writes=[f"kTl{sc}"])
        fw.dma("sync", vl[sc], v_s[jc], reads=[("v_s", jc)], writes=[f"vl{sc}"])
        fw.dma("sync", qTl[b], qT_s[i], writes=[f"qTl{b}"])
        fw.dma("sync", gl_[b], g_s[i], writes=[f"gl{b}"])
        fw.dma("sync", zTl[b], zT_s[i], writes=[f"zTl{b}"])
        fw.dma("sync", xr[b], xmain[i * 128:(i + 1) * 128, :], writes=[f"xr{b}"])
        fw.flush()

    def s3_A(n):
        i, grp = n // 4, n % 4
        b, pb = i % 2, 2 * (i % 2) + n % 2
        sc, sp = (2 + i) % 3, (1 + i) % 3
        bs, kc = (grp % 2) * 64, grp // 2
        qsel = qTl[b][bs:bs + 64, kc * 4:(kc + 1) * 4, :]
        pS, pkS = psum()
        mm(pS, kTl[sc][bs:bs + 64, kc, :], qsel, True, True, [f"kTl{sc}", f"qTl{b}"], pkS, True)
        A_(lambda e: e.activation(out=Pc[pb], in_=pS, func=AF.Exp, scale=0.125), [pkS], [f"Pc{pb}"])
        G_(lambda e: e.tensor_tensor(out=Pc[pb], in0=Pc[pb], in1=maskb[:, 0, :], op=ALU.mult), [f"Pc{pb}", "maskb"], [f"Pc{pb}"])
        pS2, pkS2 = psum()
        mm(pS2, kTl[sp][bs:bs + 64, kc, :], qsel, True, True, [f"kTl{sp}", f"qTl{b}"], pkS2, True)
        A_(lambda e: e.activation(out=Pp[pb], in_=pS2, func=AF.Exp, scale=0.125), [pkS2], [f"Pp{pb}"])
        mi = 2 if i == 0 else 1
        G_(lambda e: e.tensor_tensor(out=Pp[pb], in0=Pp[pb], in1=maskb[:, mi, :], op=ALU.mult), [f"Pp{pb}", "maskb"], [f"Pp{pb}"])
        pS3, pkS3 = psum()
        mm(pS3[0:16, :], kme[bs:bs + 64, kc, 0:16], qsel, True, True, ["kme", f"qTl{b}"], pkS3, True)
        A_(lambda e: e.activation(out=Pm[pb][0:16, :], in_=pS3[0:16, :], func=AF.Exp, scale=0.125), [pkS3], [f"Pm{pb}"])

    def s3_B(n):
        i, grp = n // 4, n % 4
        b, pb = i % 2, 2 * (i % 2) + n % 2
        sc, sp = (2 + i) % 3, (1 + i) % 3
        attn, kA = attn_[b], f"attn{b}"
        pO, pkO = psum()
        for r in range(4):
            o = pO[:, r * 65:(r + 1) * 65]
            mm(o, Pm[pb][:, r * 128:(r + 1) * 128], vme[:, grp, :], True, False, [f"Pm{pb}", "vme"], pkO, False)
            mm(o, Pp[pb][:, r * 128:(r + 1) * 128], vl[sp][:, grp, :], False, False, [f"Pp{pb}", f"vl{sp}"], pkO, False)
            mm(o, Pc[pb][:, r * 128:(r + 1) * 128], vl[sc][:, grp, :], False, True, [f"Pc{pb}", f"vl{sc}"], pkO, r == 3)
        pO3 = pO[:, 0:260].rearrange("p (r c) -> p r c", c=65)
        den = den_[pb]
        kd = f"den{pb}"
        V(lambda e: e.tensor_tensor(out=den, in0=pO3[:, :, 64], in1=esink[:, grp * 4:(grp + 1) * 4], op=ALU.add), [pkO, "esink"], [kd])
        V(lambda e: e.reciprocal(out=den, in_=den), [kd], [kd])
        V(lambda e: e.tensor_tensor(out=attn[:, grp * 256:(grp + 1) * 256].rearrange("p (r d) -> p r d", d=64), in0=pO3[:, :, 0:64],
                                    in1=den.unsqueeze(2).broadcast_to([128, 4, 64]), op=ALU.mult), [pkO, kd], [kA])

    def s3_tail(i):
        b = i % 2
        attn, An, ssm, Bn, mg, mgT = attn_[b], An_[b], ssm_[b], Bn_[b], mg_[b], mgT_[b]
        kA, kAn, kss, kBn, kmg, kmT = f"attn{b}", f"An{b}", f"ssm{b}", f"Bn{b}", f"mg{b}", f"mgT{b}"
        rms_scale(attn, 1, An, [kA], [kAn])
        yield
        G_(lambda e: e.tensor_tensor(out=An, in0=An, in1=gl_[b][:, 0:1024], op=ALU.mult), [kAn, f"gl{b}"], [kAn])
        for half in range(2):
            pa, pka = psum()
            for k in range(8):
                mm(pa, zTl[b][:, k, :], Wg[:, k, half * 512:(half + 1) * 512], k == 0, k == 7, [f"zTl{b}", "Wg"], pka, k == 7)
            pz, pkz = psum()
            for k in range(8):
                mm(pz, zTl[b][:, k, :], Wg[:, k, 1024 + half * 512:1024 + (half + 1) * 512], k == 0, k == 7, [f"zTl{b}", "Wg"], pkz, k == 7)
            yield
            sig = sig_[half]
            A_(lambda e, pz=pz, sig=sig: e.activation(out=sig, in_=pz, func=AF.Sigmoid), [pkz], [f"sig{half}"])
            V(lambda e, pa=pa, half=half, sig=sig: e.tensor_tensor(out=ssm[:, half * 512:(half + 1) * 512], in0=pa, in1=sig, op=ALU.mult), [pka, f"sig{half}"], [kss])
        yield
        rms_scale(ssm, 2, Bn, [kss], [kBn])
        yield
        G_(lambda e: e.tensor_tensor(out=Bn, in0=Bn, in1=gl_[b][:, 1024:2048], op=ALU.mult), [kBn, f"gl{b}"], [kBn])
        V(lambda e: e.tensor_tensor(out=mg, in0=An, in1=Bn, op=ALU.add), [kAn, kBn], [kmg])
        yield
        transpose8(mg, mgT, kmg, kmT)
        yield
        for half in range(2):
            p, pk = psum()
            for k in range(8):
                mm(p, mgT[:, k, :], Wo[:, k, half * 512:(half + 1) * 512], k == 0, k == 7, [kmT, "Wo"], pk, k == 7)
            V(lambda e, p=p, half=half: e.tensor_tensor(out=h1[b][:, half * 512:(half + 1) * 512], in0=p, in1=xr[b][:, half * 512:(half + 1) * 512], op=ALU.add),
              [pk, f"xr{b}"], [f"h1{b}"])
        fw.defer_dma("sync", h1_s[i * 128:(i + 1) * 128, :], h1[b], reads=[f"h1{b}"], writes=[("h1_s", i)])

    def s3_tile_gen(i):
        s3_loads(i)
        yield
        n0 = 4 * i
        for step in (("A", 0), ("A", 1), ("B", 0), ("A", 2), ("B", 1), ("A", 3), ("B", 2), ("B", 3)):
            (s3_A if step[0] == "A" else s3_B)(n0 + step[1])
            yield
        yield from s3_tail(i)

    for i in range(0, TM_, 2):
        lockstep([s3_tile_gen(i), s3_tile_gen(i + 1)])
    fw.barrier()
    ar.reset(base_persist)

    if upto <= 4:
        fw.emit()
        return nc
    W1 = ar.alloc("W1", [128, 8, 5632], BF16); W2 = ar.alloc("W2", [128, 22, 1024], BF16)
    load_weights(W1, w_f1, 11, 8, "W1")
    load_weights(W2, w_f2, 2, 22, "W2")
    GT = 4
    hl = [ar.alloc("hl", [128, D], F32) for _ in range(2)]
    hn = [ar.alloc("hn", [128, D], BF16) for _ in range(2)]
    hnT = ar.alloc("hnT", [128, 8, GT * 128], BF16)
    sg = [ar.alloc("sg", [128, 512], F32) for _ in range(2)]
    actT = ar.alloc("actT", [128, 22, GT * 128], BF16)
    hres = hl
    ob = junk
    tcount = 0
    for g in range(TM_ // GT):
        for t4 in range(GT):
            i = g * GT + t4
            b = tcount % 2
            tcount += 1
            fw.dma("sync", hl[b], h1_s[i * 128:(i + 1) * 128, :], reads=[("h1_s", i)], writes=[f"hl{b}"])
            fw.flush()
            rms_scale(hl[b], 3, hn[b], [f"hl{b}"], [f"hn{b}"])
            for half in range(2):
                p, pk = psum()
                for jj in range(4):
                    c = half * 4 + jj
                    mm(p[:, jj * 128:(jj + 1) * 128], hn[b][:, c * 128:(c + 1) * 128], identb, True, True, [f"hn{b}", "identb"], pk, jj == 3)
                V(lambda e, p=p, half=half, t4=t4: e.tensor_copy(out=hnT[:, half * 4:half * 4 + 4, t4 * 128:(t4 + 1) * 128],
                                                               in_=p.rearrange("p (a c) -> p a c", c=128)), [pk], [("hnT", t4)])
        hk = [("hnT", t4) for t4 in range(GT)]
        for fc in range(22):
            fp, q2 = fc // 2, fc % 2
            pg, pkg = psum()
            for k in range(8):
                mm(pg, W1[:, k, fp * 512 + q2 * 256:fp * 512 + q2 * 256 + 128], hnT[:, k, :], k == 0, k == 7, hk + ["W1"], pkg, k == 7)
            pu, pku = psum()
            for k in range(8):
                mm(pu, W1[:, k, fp * 512 + q2 * 256 + 128:fp * 512 + q2 * 256 + 256], hnT[:, k, :], k == 0, k == 7, hk + ["W1"], pku, k == 7)
            s_ = sg[fc % 2]
            ks_ = f"sg{fc % 2}"
            A_(lambda e, pg=pg, s_=s_: e.activation(out=s_, in_=pg, func=AF.Sigmoid), [pkg], [ks_])
            V(lambda e, pg=pg, s_=s_: e.tensor_tensor(out=s_, in0=pg, in1=s_, op=ALU.mult), [pkg, ks_], [ks_])
            V(lambda e, pu=pu, s_=s_, fc=fc: e.tensor_tensor(out=actT[:, fc, :], in0=pu, in1=s_, op=ALU.mult), [pku, ks_], [("actT", fc)])
        ak = [("actT", fc) for fc in range(22)]
        for t4 in range(GT):
            i = g * GT + t4
            b = t4 % 2
            fw.dma("sync", hres[b], h1_s[i * 128:(i + 1) * 128, :], reads=[("h1_s", i)], writes=[f"hl{b}"])
            fw.flush()
            for half in range(2):
                p, pk = psum()
                for k in range(22):
                    mm(p, actT[:, k, t4 * 128:(t4 + 1) * 128], W2[:, k, half * 512:(half + 1) * 512], k == 0, k == 21, ak + ["W2"], pk, k == 21)
                V(lambda e, p=p, half=half, b=b: e.tensor_tensor(out=ob[b][:, half * 512:(half + 1) * 512], in0=p, in1=hres[b][:, half * 512:(half + 1) * 512], op=ALU.add),
                  [pk, f"hl{b}"], [f"junk{b}"])
            fw.defer_dma("sync", out[i * 128:(i + 1) * 128, :], ob[b], reads=[f"junk{b}"], writes=[("out", i)])
    fw.emit()
    return nc


def _panels(w, kk):
    n = w.shape[1] // 512
    return np.ascontiguousarray(w.reshape(kk, 128, n, 512).transpose(2, 1, 0, 3))


def prep_shared(inp):
    f = lambda a: np.asarray(a, dtype=np.float32)
    w_in = f(inp["w_in"])[0]
    qcols = []
    for j in range(8):
        for s in range(2):
            head = ((j // 4) * 2 + s) * 4 + (j % 4)
            qcols.extend(range(head * 64, head * 64 + 64))
    w_in_r = np.concatenate([w_in[:, qcols], w_in[:, 1024:1536], w_in[:, 1536:]], axis=1)
    wf1 = f(inp["w_ffn_in"])[0]
    cols = []
    for c in range(22):
        cols.extend(range(c * 128, (c + 1) * 128))
        cols.extend(range(DFF + c * 128, DFF + (c + 1) * 128))
    wf1_r = wf1[:, cols]
    rep = lambda v, n: np.ascontiguousarray(np.broadcast_to(f(v).reshape(1, -1), (128, n)))
    gains = np.stack([rep(inp["norm_mix"][0], D), rep(inp["attn_branch_norm"][0], D), rep(inp["ssm_branch_norm"][0], D), rep(inp["norm_ffn"][0], D)])

    def sp(a):
        return np.ascontiguousarray(f(a).reshape(32, 2, 64).transpose(1, 2, 0).reshape(128, 32))

    lam = np.stack([sp(inp["lam_re"][0]), sp(inp["lam_im"][0]), sp(np.broadcast_to(f(inp["log_dt"])[0][:, None], (64, 64)))])
    bre, bim = f(inp["ssm_b_re"])[0], f(inp["ssm_b_im"])[0]
    def spc(a):
        return a.reshape(32, 2, 64, a.shape[-1]).transpose(1, 2, 0, 3).reshape(128, 32, a.shape[-1])
    btc = np.ascontiguousarray(np.stack([spc(bre), spc(bim)], axis=2))
    cre, cim = f(inp["ssm_c_re"])[0], f(inp["ssm_c_im"])[0]
    cc = np.ascontiguousarray(np.stack([spc(cre.transpose(0, 2, 1)), spc(cim.transpose(0, 2, 1))]))
    kk, qq = np.arange(128)[:, None], np.arange(128)[None, :]
    mcur = np.where(kk <= qq, 1.0, 0.0).astype(np.float32)
    mprev = np.where(kk > qq, 1.0, 0.0).astype(np.float32)
    return dict(
        w_in=_panels(w_in_r, 8), w_glu=_panels(f(inp["w_glu"])[0], 8), w_out=_panels(f(inp["w_out"])[0], 8),
        w_f1=_panels(wf1_r, 8), w_f2=_panels(f(inp["w_ffn_out"])[0], 22), gains=gains,
        gq=rep(np.tile(f(inp["q_norm"])[0], 4), 256), gk=rep(np.tile(f(inp["k_norm"])[0], 4), 256),
        sinks=rep(inp["attn_sinks"][0], 16), ident=np.eye(128, dtype=np.float32), lam=lam, btc=btc, cc=cc,
        dcol=np.ascontiguousarray(f(inp["ssm_d"])[0].reshape(8, 128).T),
    ), mcur, mprev


def prep_core(x_b, meta, h, NM, NP, mcur, mprev):
    xmain = np.ascontiguousarray(x_b[h * NM:(h + 1) * NM])
    xpre = np.zeros((NP, D), np.float32)
    xctx = np.zeros((256, D), np.float32)
    xctx[0:16] = meta
    if h == 0:
        xpre[NP - 16:] = meta
        m0 = np.zeros((128, 128), np.float32)
    else:
        xpre[1008:1024] = meta
        xpre[1024:] = x_b[0:NM]
        xctx[128:256] = x_b[NM - 128:NM]
        m0 = mprev
    masks = np.stack([np.tile(mcur, (1, 4)), np.tile(mprev, (1, 4)), np.tile(m0, (1, 4))]).astype(np.float32)
    return dict(xmain=xmain, xpre=xpre, xctx=xctx, masks=masks)


_NC_CACHE = {}


def kernel(**inputs):
    x = np.asarray(inputs["x"], dtype=np.float32)
    Bsz, S, _ = x.shape
    NM = S // 2
    NP = NM + 1024
    meta = np.asarray(inputs["meta_tokens"], dtype=np.float32)
    shared, mcur, mprev = prep_shared(inputs)
    in_maps = []
    for b in range(Bsz):
        for h in range(2):
            d = dict(shared)
            d.update(prep_core(x[b], meta, h, NM, NP, mcur, mprev))
            in_maps.append(d)
    nc = build(NM, NP)
    res = run_bass_kernel_spmd(nc, in_maps, core_ids=list(range(len(in_maps))))
    outp = np.zeros((Bsz, S, D), np.float32)
    for b in range(Bsz):
        for h in range(2):
            outp[b, h * NM:(h + 1) * NM] = res.results[2 * b + h]["out"]
    return outp
```

```python
import math
import contextlib
import numpy as np
import concourse.bass as bass
import concourse.mybir as mybir
from concourse.bass_utils import run_bass_kernel_spmd

F32 = mybir.dt.float32
BF16 = mybir.dt.bfloat16
AF = mybir.ActivationFunctionType
ALU = mybir.AluOpType
AX = mybir.AxisListType
ENGS = ("tensor", "vector", "scalar", "gpsimd", "sync")
D = 1024
DFF = 2816
NEG = -30000.0


class FW:
    def __init__(self, nc, n_dma_sems=40):
        self.nc = nc
        self.ops = {e: [] for e in ENGS}
        self.cnt = {e: 0 for e in ENGS}
        self.known = {e: {} for e in ENGS}
        self.last_w = {}
        self.readers = {}
        self.n_dma_sems = n_dma_sems
        self.dma_gen = [0] * n_dma_sems
        self.dma_rr = 0
        self.sem_names = [f"s_{e}" for e in ENGS] + [f"d_{i}" for i in range(n_dma_sems)]

    def _deps(self, reads, writes):
        evs = []
        for k in reads:
            if k in self.last_w:
                evs.append(self.last_w[k])
        for k in writes:
            if k in self.last_w:
                evs.append(self.last_w[k])
            evs.extend(self.readers.get(k, ()))
        return evs

    def _commit(self, ev, reads, writes):
        for k in reads:
            self.readers.setdefault(k, []).append(ev)
        for k in writes:
            self.last_w[k] = ev
            self.readers[k] = []

    def _waits(self, eng, evs):
        best = {}
        for (s, v) in evs:
            if v > best.get(s, 0):
                best[s] = v
        out = []
        kn = self.known[eng]
        for s, v in best.items():
            if eng == "tensor" and s == "s_tensor":
                continue
            if kn.get(s, 0) >= v:
                continue
            kn[s] = v
            out.append((s, v))
        return out

    def op(self, eng, fn, reads=(), writes=(), inc=True):
        evs = self._deps(reads, writes)
        waits = self._waits(eng, evs)
        sname = f"s_{eng}"
        ev = (sname, self.cnt[eng] + 1)
        if inc:
            self.cnt[eng] += 1
        self.ops[eng].append((waits, fn, (sname, 1) if inc else None))
        self._commit(ev, reads, writes)
        return ev

    def dma(self, queue, out, in_, reads=(), writes=(), **kw):
        i = self.dma_rr
        self.dma_rr = (self.dma_rr + 1) % self.n_dma_sems
        sname = f"d_{i}"
        evs = self._deps(reads, writes)
        if self.dma_gen[i] > 0:
            evs.append((sname, 16 * self.dma_gen[i]))
        waits = self._waits(queue, evs)
        self.dma_gen[i] += 1
        ev = (sname, 16 * self.dma_gen[i])
        self.ops[queue].append((waits, lambda e: e.dma_start(out=out, in_=in_, **kw), (sname, 16)))
        self._commit(ev, reads, writes)
        return ev

    def defer_dma(self, *a, **kw):
        if not hasattr(self, "_deferred"):
            self._deferred = []
        self._deferred.append((a, kw))

    def flush(self):
        for a, kw in getattr(self, "_deferred", []):
            self.dma(*a, **kw)
        self._deferred = []

    def barrier(self):
        self.flush()
        fin = []
        for e in ENGS:
            if self.cnt[e] > 0:
                fin.append((f"s_{e}", self.cnt[e]))
        for i in range(self.n_dma_sems):
            if self.dma_gen[i] > 0:
                fin.append((f"d_{i}", 16 * self.dma_gen[i]))
        for e in ENGS:
            w = self._waits(e, fin)
            if w:
                self.ops[e].append((w, None, None))
        self.last_w = {}
        self.readers = {}

    def emit(self):
        nc = self.nc
        self.barrier()
        with contextlib.ExitStack() as st:
            sems = {n: st.enter_context(nc.semaphore(n)) for n in self.sem_names}
            block = st.enter_context(nc.Block())

            def mk(engname):
                lst = self.ops[engname]

                def body(eng):
                    for (waits, fn, inc) in lst:
                        for (s, v) in waits:
                            eng.wait_ge(sems[s], v)
                        if fn is None:
                            continue
                        ins = fn(eng)
                        if inc is not None:
                            ins.then_inc(sems[inc[0]], inc[1])
                return body

            block.tensor(mk("tensor"))
            block.vector(mk("vector"))
            block.scalar(mk("scalar"))
            block.gpsimd(mk("gpsimd"))
            block.sync(mk("sync"))


class Arena:
    def __init__(self, nc, base=16640, limit=224 * 1024):
        self.nc, self.off, self.limit, self.n = nc, base, limit, 0

    def alloc(self, name, shape, dt):
        per = int(np.prod(shape[1:])) * (4 if dt == F32 else 2)
        per = (per + 63) // 64 * 64
        assert self.off + per <= self.limit, (name, self.off, per)
        self.n += 1
        t = self.nc.alloc_sbuf_tensor_at(f"{name}_{self.n}_{self.off}", list(shape), dt, offset=self.off)
        self.off += per
        return t.ap()

    def mark(self):
        return self.off

    def reset(self, off):
        self.off = off


def build(NM, NP, upto=9):
    nc = bass.Bass("TRN2", target_bir_lowering=False)
    fw = FW(nc)
    TM_, TP_ = NM // 128, NP // 128
    NS = TP_ + TM_
    NK = 2 + TM_

    def din(name, shape, dt=F32):
        return nc.dram_tensor(name, list(shape), dt, kind="ExternalInput").ap()

    xmain = din("xmain", [NM, D]); xpre = din("xpre", [NP, D]); xctx = din("xctx", [256, D])
    w_in = din("w_in", [9, 128, 8, 512]); w_glu = din("w_glu", [4, 128, 8, 512]); w_out = din("w_out", [2, 128, 8, 512])
    w_f1 = din("w_f1", [11, 128, 8, 512]); w_f2 = din("w_f2", [2, 128, 22, 512])
    gains = din("gains", [4, 128, D])
    gq = din("gq", [128, 256]); gk = din("gk", [128, 256]); sinks = din("sinks", [128, 16])
    masks = din("masks", [3, 128, 512])
    ident = din("ident", [128, 128])
    lam = din("lam", [3, 128, 32])
    btc = din("btc", [128, 32, 2, 16]); cc = din("cc", [2, 128, 32, 16]); dcol = din("dcol", [128, 8])
    out = nc.dram_tensor("out", [NM, D], F32, kind="ExternalOutput").ap()

    def dscr(name, shape, dt):
        return nc.dram_tensor(name, list(shape), dt, kind="Internal").ap()

    uT_s = dscr("uT_s", [NS, 128, 8, 128], BF16); qT_s = dscr("qT_s", [TM_, 128, 8, 128], BF16)
    kT_s = dscr("kT_s", [NK, 128, 2, 128], BF16); v_s = dscr("v_s", [NK, 128, 4, 65], BF16)
    g_s = dscr("g_s", [TM_, 128, 2048], BF16); zT_s = dscr("zT_s", [TM_, 128, 8, 128], BF16)
    h1_s = dscr("h1_s", [NM, D], F32)
    DB_s = dscr("DB_s", [8, 128, 8, 4, 2, 128], BF16); EC_s = dscr("EC_s", [8, 128, 8, 4, 2, 128], BF16)

    ar = Arena(nc)
    identf = ar.alloc("identf", [128, 128], F32); identb = ar.alloc("identb", [128, 128], BF16)
    gsb = ar.alloc("gsb", [128, 4, D], F32)
    fw.dma("sync", identf, ident, writes=["identf"])
    fw.op("vector", lambda e: e.tensor_copy(out=identb, in_=identf), reads=["identf"], writes=["identb"])
    fw.dma("sync", gsb, gains.rearrange("a p d -> p a d"), writes=["gsb"])
    pbank = [nc.alloc_psum_tensor(f"pb{i}", [128, 512], F32).ap() for i in range(8)]
    pcnt = [0]

    def psum():
        i = pcnt[0] % 8
        pcnt[0] += 1
        return pbank[i], f"pb{i}"

    rr = [0]

    def alt():
        rr[0] += 1
        return "vector" if rr[0] % 2 else "gpsimd"

    base0 = ar.mark()

    def load_weights(dst, src, npan, kk, key):
        m = ar.mark()
        nst = 3 if ar.off + 3 * 16384 <= ar.limit else 2
        st = [ar.alloc("wst", [128, 8, 512], F32) for _ in range(nst)]
        cyc = ["vector", "scalar", "gpsimd", "vector", "scalar"]
        n = 0
        for pi in range(npan):
            for k0 in range(0, kk, 8):
                kc = min(8, kk - k0)
                s = st[n % nst]
                fw.dma("sync", s[:, :kc, :], src[pi][:, k0:k0 + kc, :], writes=[f"wst{n % nst}"])
                eng = cyc[n % len(cyc)]
                o_ = dst[:, k0:k0 + kc, pi * 512:(pi + 1) * 512]
                if eng == "scalar":
                    fw.op(eng, lambda e, s=s, kc=kc, o_=o_: e.activation(out=o_, in_=s[:, :kc, :], func=AF.Copy), reads=[f"wst{n % nst}"], writes=[key])
                else:
                    fw.op(eng, lambda e, s=s, kc=kc, o_=o_: e.tensor_copy(out=o_, in_=s[:, :kc, :]), reads=[f"wst{n % nst}"], writes=[key])
                n += 1
        fw.barrier()
        ar.reset(m)

    rmsc = [0]

    def rms_scale(xin, gidx, xn_out, rkeys, wkeys, ncol=D):
        pr = rmsc[0] % 2
        rmsc[0] += 1
        jk, sq_ = junk[pr], ssq[pr]
        kj, ks = f"junk{pr}", f"ssq{pr}"
        fw.op("scalar", lambda e: e.activation(out=jk[:, :ncol], in_=xin, func=AF.Square, accum_out=sq_),
              reads=rkeys, writes=[kj, ks])
        fw.op("vector", lambda e: e.tensor_scalar(out=sq_, in0=sq_, scalar1=1.0 / ncol, scalar2=1e-6, op0=ALU.mult, op1=ALU.add),
              reads=[ks], writes=[ks])
        fw.op("scalar", lambda e: e.activation(out=sq_, in_=sq_, func=AF.Sqrt), reads=[ks], writes=[ks])
        fw.op("vector", lambda e: e.reciprocal(out=sq_, in_=sq_), reads=[ks], writes=[ks])
        fw.op("vector", lambda e: e.scalar_tensor_tensor(out=xn_out, in0=xin, scalar=sq_, in1=gsb[:, gidx, :ncol],
                                                         op0=ALU.mult, op1=ALU.mult),
              reads=list(rkeys) + [ks, "gsb"], writes=wkeys)

    def transpose8(src_bf, dstT, rkey, wkey, n=8):
        for half in range((n + 3) // 4):
            p, pk = psum()
            m = min(4, n - half * 4)
            for j in range(m):
                c = half * 4 + j
                fw.op("tensor", lambda e, c=c, j=j, p=p: e.matmul(p[:, j * 128:(j + 1) * 128], lhsT=src_bf[:, c * 128:(c + 1) * 128],
                                                                 rhs=identb, start=True, stop=True),
                      reads=[rkey, "identb"], writes=[pk], inc=(j == m - 1))
            fw.op("vector", lambda e, p=p, half=half, m=m: e.tensor_copy(
                out=dstT[:, half * 4:half * 4 + m, :], in_=p[:, :m * 128].rearrange("p (a b) -> p a b", b=128)),
                reads=[pk], writes=[wkey])

    junk = [ar.alloc("junk", [128, D], F32) for _ in range(2)]; ssq = [ar.alloc("ssq", [128, 1], F32) for _ in range(2)]
    base1 = ar.mark()

    dbg = {}
    V = lambda fn, r, w: fw.op("vector", fn, reads=r, writes=w)
    A_ = lambda fn, r, w: fw.op("scalar", fn, reads=r, writes=w)
    G_ = lambda fn, r, w: fw.op("gpsimd", fn, reads=r, writes=w)

    def mm(out_ap, lhsT, rhs, start, stop, reads, pk, inc):
        fw.op("tensor", lambda e: e.matmul(out_ap, lhsT=lhsT, rhs=rhs, start=start, stop=stop), reads=reads, writes=[pk], inc=inc)

    dcs = ar.alloc("dcs", [128, 8], F32)
    esink = ar.alloc("esink", [128, 16], F32); gqk = ar.alloc("gqk", [128, 256], F32)
    maskb = ar.alloc("maskb", [128, 3, 512], BF16)
    base_persist = ar.mark()
    st32 = ar.alloc("st32", [128, 8192], F32)
    fw.dma("sync", dcs, dcol, writes=["dcs"])
    fw.dma("sync", st32[:, 0:16], sinks, writes=["a"])
    A_(lambda e: e.activation(out=esink, in_=st32[:, 0:16], func=AF.Exp), ["a"], ["esink"])
    fw.dma("sync", st32[:, 1024:1280], gq, writes=["b"])
    fw.dma("sync", st32[:, 2048:2304], gk, writes=["c"])
    V(lambda e: e.tensor_tensor(out=gqk, in0=st32[:, 1024:1280], in1=st32[:, 2048:2304], op=ALU.mult), ["b", "c"], ["gqk"])
    fw.dma("sync", st32[:, 4096:5632].rearrange("p (a c) -> p a c", a=3), masks.rearrange("a p c -> p a c"), writes=["d"])
    V(lambda e: e.tensor_copy(out=maskb, in_=st32[:, 4096:5632].rearrange("p (a c) -> p a c", a=3)), ["d"], ["maskb"])
    fw.barrier()
    ar.reset(base_persist)

    m1 = ar.mark()
    Win = ar.alloc("Win", [128, 8, 4608], BF16)
    load_weights(Win, w_in, 9, 8, "Win")
    QO, KVO, UO, GO = 0, 1024, 1536, 2560
    xt = [ar.alloc("xt", [128, D], F32) for _ in range(4)]
    xnb = [ar.alloc("xnb", [128, D], BF16) for _ in range(4)]
    xnT4 = [ar.alloc("xnT4", [128, 8, 512], BF16) for _ in range(2)]
    uTb4 = [ar.alloc("uTb4", [128, 8, 512], BF16) for _ in range(2)]
    qsq_ = [ar.alloc("qsq", [128, 512], F32) for _ in range(2)]
    qss_ = [ar.alloc("qss", [128, 8], F32) for _ in range(2)]
    hnc = [0]
    qn = [ar.alloc("qn", [128, D], BF16) for _ in range(2)]
    qTb = [ar.alloc("qTb", [128, 8, 128], BF16) for _ in range(2)]
    kf_ = [ar.alloc("kf", [128, 256], F32) for _ in range(2)]
    kn = [ar.alloc("kn", [128, 256], BF16) for _ in range(2)]
    kTb = [ar.alloc("kTb", [128, 2, 128], BF16) for _ in range(2)]
    vab = [ar.alloc("vab", [128, 4, 65], BF16) for _ in range(2)]
    gb = [ar.alloc("gb", [128, 2048], BF16) for _ in range(2)]
    for b in range(2):
        V(lambda e, b=b: e.memset(vab[b][:, :, 64:65], 1.0), [], [f"vab{b}"])

    groups = [[("ctx", xctx[0:128, :], 0, None), ("ctx", xctx[128:256, :], 1, None)]]
    for t in range(0, TP_, 4):
        groups.append([("pre", xpre[(t + q) * 128:(t + q + 1) * 128, :], t + q, None) for q in range(4)])
    for t in range(0, TM_, 4):
        groups.append([("main", xmain[(t + q) * 128:(t + q + 1) * 128, :], TP_ + t + q, t + q) for q in range(4)])

    def headnorm(p, pk, ncol, nh, dst, dkey, gain=None):
        pr = hnc[0] % 2
        hnc[0] += 1
        qsq, qss, kf = qsq_[pr], qss_[pr], kf_[pr]
        kq, ks, kk_ = f"qsq{pr}", f"qss{pr}", f"kf{pr}"
        A_(lambda e: e.activation(out=qsq[:, :ncol], in_=p[:, :ncol], func=AF.Square), [pk], [kq])
        V(lambda e: e.tensor_reduce(out=qss[:, :nh], in_=qsq[:, :ncol].rearrange("p (h d) -> p h d", d=64), axis=AX.X, op=ALU.add), [kq], [ks])
        V(lambda e: e.tensor_scalar(out=qss[:, :nh], in0=qss[:, :nh], scalar1=1.0 / 64, scalar2=1e-6, op0=ALU.mult, op1=ALU.add), [ks], [ks])
        A_(lambda e: e.activation(out=qss[:, :nh], in_=qss[:, :nh], func=AF.Sqrt), [ks], [ks])
        V(lambda e: e.reciprocal(out=qss[:, :nh], in_=qss[:, :nh]), [ks], [ks])
        rb = qss[:, :nh].unsqueeze(2).broadcast_to([128, nh, 64])
        if gain is None:
            V(lambda e: e.tensor_tensor(out=dst.rearrange("p (h d) -> p h d", d=64), in0=p[:, :ncol].rearrange("p (h d) -> p h d", d=64), in1=rb, op=ALU.mult),
              [pk, ks], [dkey])
        else:
            V(lambda e: e.tensor_tensor(out=kf.rearrange("p (h d) -> p h d", d=64), in0=p[:, :ncol].rearrange("p (h d) -> p h d", d=64), in1=rb, op=ALU.mult),
              [pk, ks], [kk_])
            V(lambda e: e.tensor_tensor(out=dst, in0=kf, in1=gain, op=ALU.mult), [kk_, "gqk"], [dkey])

    def tile_gen(gpar, t4, tile):
        kind, src, sidx, midx = tile
        b = t4 % 2
        b4 = t4 % 4
        X4 = xnT4[gpar]
        fw.dma("sync", xt[b4], src, writes=[f"xt{b4}"])
        fw.flush()
        rms_scale(xt[b4], 0, xnb[b4], [f"xt{b4}"], [f"xnb{b4}"])
        yield
        xk = f"xnT{gpar}_{t4}"
        XT = X4[:, :, t4 * 128:(t4 + 1) * 128]
        transpose8(xnb[b4], XT, f"xnb{b4}", xk)
        yield
        if kind == "main":
            qps = []
            for half in range(2):
                p, pk = psum()
                for k in range(8):
                    mm(p, XT[:, k, :], Win[:, k, QO + half * 512:QO + (half + 1) * 512], k == 0, k == 7, [xk, "Win"], pk, k == 7)
                yield
                headnorm(p, pk, 512, 8, qn[b][:, half * 512:(half + 1) * 512], f"qn{b}")
                yield
        if kind in ("ctx", "main"):
            kidx = sidx if kind == "ctx" else 2 + midx
            p, pk = psum()
            for k in range(8):
                mm(p, XT[:, k, :], Win[:, k, KVO:KVO + 512], k == 0, k == 7, [xk, "Win"], pk, k == 7)
            yield
            headnorm(p, pk, 256, 4, kn[b], f"kn{b}", gain=gqk)
            V(lambda e, p=p, b=b: e.tensor_copy(out=vab[b][:, :, 0:64], in_=p[:, 256:512].rearrange("p (h d) -> p h d", d=64)), [pk], [f"vab{b}"])
            fw.defer_dma("sync", v_s[kidx], vab[b], reads=[f"vab{b}"], writes=[("v_s", kidx)])
            yield
        if kind == "main":
            for j in range(4):
                p, pk = psum()
                for k in range(8):
                    mm(p, XT[:, k, :], Win[:, k, GO + j * 512:GO + (j + 1) * 512], k == 0, k == 7, [xk, "Win"], pk, k == 7)
                A_(lambda e, p=p, j=j, b=b: e.activation(out=gb[b][:, j * 512:(j + 1) * 512], in_=p, func=AF.Sigmoid), [pk], [f"gb{b}"])
                yield
            fw.defer_dma("sync", g_s[midx], gb[b], reads=[f"gb{b}"], writes=[("g_s", midx)])
            transpose8(qn[b], qTb[b], f"qn{b}", f"qTb{b}")
            fw.defer_dma("sync", qT_s[midx], qTb[b], reads=[f"qTb{b}"], writes=[("qT_s", midx)])
            yield
        if kind in ("ctx", "main"):
            transpose8(kn[b], kTb[b], f"kn{b}", f"kTb{b}", n=2)
            fw.defer_dma("sync", kT_s[kidx], kTb[b], reads=[f"kTb{b}"], writes=[("kT_s", kidx)])
            yield

    def lockstep(gens):
        alive = True
        while alive:
            alive = False
            for g_ in gens:
                try:
                    next(g_)
                    alive = True
                except StopIteration:
                    pass

    for gi, grp_tiles in enumerate(groups):
        gpar = gi % 2
        X4 = xnT4[gpar]
        xkeys = [f"xnT{gpar}_{t4}" for t4 in range(len(grp_tiles))]
        width = 4 if grp_tiles[0][0] == "pre" else 2
        for t0 in range(0, len(grp_tiles), width):
            lockstep([tile_gen(gpar, t0 + q, grp_tiles[t0 + q]) for q in range(width)])
        if grp_tiles[0][0] in ("pre", "main"):
            s0 = grp_tiles[0][2]
            for ct in range(8):
                p, pk = psum()
                for k in range(8):
                    mm(p, Win[:, k, UO + ct * 128:UO + (ct + 1) * 128], X4[:, k, :], k == 0, k == 7, xkeys + ["Win"], pk, k == 7)
                if ct % 2 == 0:
                    V(lambda e, p=p, ct=ct, gpar=gpar: e.tensor_copy(out=uTb4[gpar][:, ct, :], in_=p), [pk], [f"uTb4{gpar}"])
                else:
                    A_(lambda e, p=p, ct=ct, gpar=gpar: e.activation(out=uTb4[gpar][:, ct, :], in_=p, func=AF.Copy), [pk], [f"uTb4{gpar}"])
            fw.defer_dma("sync", uT_s[s0:s0 + 4].rearrange("n p c t -> p c n t"), uTb4[gpar].rearrange("p c (n t) -> p c n t", t=128),
                         reads=[f"uTb4{gpar}"], writes=[("uT_s", s0)])
    fw.barrier()
    ar.reset(m1)

    if upto <= 1:
        fw.emit()
        return nc
    lamsb = ar.alloc("lamsb", [128, 3, 32], F32)
    fw.dma("sync", lamsb, lam.rearrange("a p g -> p a g"), writes=["lam"])
    smn = ["dt", "th", "rho", "sn", "cs", "ar", "ai", "fr", "fi", "t1", "t2", "t3", "den", "wr", "wi", "w128r", "w128i", "mk", "x2", "lrdt"]
    sm = {n: ar.alloc(n, [128, 32], F32) for n in smn}
    pwr = [ar.alloc("pwr", [128, 32], F32) for _ in range(9)]; pwi = [ar.alloc("pwi", [128, 32], F32) for _ in range(9)]
    Er = ar.alloc("Er", [128, 32, 128], F32); Ei = ar.alloc("Ei", [128, 32, 128], F32)
    Kpad = ar.alloc("Kpad", [128, 8, 8, 128], BF16)
    cR = ar.alloc("cR", [128, 2, 32], F32); SL = ar.alloc("SL", [128, 2, 32], F32)
    base_ssm = ar.mark()
    lr, li, ld = lamsb[:, 0, :], lamsb[:, 1, :], lamsb[:, 2, :]
    K = ["ssm0"]
    A_(lambda e: e.activation(out=sm["dt"], in_=ld, func=AF.Exp), ["lam"], K)
    V(lambda e: e.tensor_tensor(out=sm["th"], in0=li, in1=sm["dt"], op=ALU.mult), K, K)
    V(lambda e: e.tensor_tensor(out=sm["lrdt"], in0=lr, in1=sm["dt"], op=ALU.mult), K, K)
    A_(lambda e: e.activation(out=sm["rho"], in_=sm["lrdt"], func=AF.Exp), K, K)
    for _ in range(5):
        V(lambda e: e.tensor_single_scalar(out=sm["mk"], in_=sm["th"], scalar=math.pi, op=ALU.is_gt), K, K)
        V(lambda e: e.scalar_tensor_tensor(out=sm["th"], in0=sm["mk"], scalar=-2.0 * math.pi, in1=sm["th"], op0=ALU.mult, op1=ALU.add), K, K)
    V(lambda e: e.tensor_scalar(out=sm["t3"], in0=sm["th"], scalar1=0.125, scalar2=None, op0=ALU.mult), K, K)
    V(lambda e: e.tensor_tensor(out=sm["x2"], in0=sm["t3"], in1=sm["t3"], op=ALU.mult), K, K)

    def horner(o, coefs):
        V(lambda e: e.memset(o, coefs[0]), K, K)
        for c in coefs[1:]:
            V(lambda e: e.tensor_tensor(out=o, in0=o, in1=sm["x2"], op=ALU.mult), K, K)
            V(lambda e, c=c: e.tensor_scalar(out=o, in0=o, scalar1=float(c), scalar2=None, op0=ALU.add), K, K)

    def cdouble(sn_, cs_):
        V(lambda e: e.tensor_tensor(out=sm["t1"], in0=sn_, in1=cs_, op=ALU.mult), K, K)
        V(lambda e: e.tensor_tensor(out=sm["t2"], in0=cs_, in1=cs_, op=ALU.mult), K, K)
        V(lambda e: e.tensor_tensor(out=sm["t3"], in0=sn_, in1=sn_, op=ALU.mult), K, K)
        V(lambda e: e.tensor_scalar(out=sn_, in0=sm["t1"], scalar1=2.0, scalar2=None, op0=ALU.mult), K, K)
        V(lambda e: e.tensor_tensor(out=cs_, in0=sm["t2"], in1=sm["t3"], op=ALU.subtract), K, K)

    horner(sm["sn"], [-1 / 39916800.0, 1 / 362880.0, -1 / 5040.0, 1 / 120.0, -1 / 6.0, 1.0])
    V(lambda e: e.tensor_tensor(out=sm["sn"], in0=sm["sn"], in1=sm["t3"], op=ALU.mult), K, K)
    horner(sm["cs"], [-1 / 3628800.0, 1 / 40320.0, -1 / 720.0, 1 / 24.0, -0.5, 1.0])
    for _ in range(3):
        cdouble(sm["sn"], sm["cs"])
    V(lambda e: e.tensor_tensor(out=sm["ar"], in0=sm["rho"], in1=sm["cs"], op=ALU.mult), K, K)
    V(lambda e: e.tensor_tensor(out=sm["ai"], in0=sm["rho"], in1=sm["sn"], op=ALU.mult), K, K)
    V(lambda e: e.tensor_scalar(out=sm["t1"], in0=sm["ar"], scalar1=-1.0, scalar2=None, op0=ALU.add), K, K)
    V(lambda e: e.tensor_tensor(out=sm["den"], in0=lr, in1=lr, op=ALU.mult), K, K)
    V(lambda e: e.tensor_tensor(out=sm["t2"], in0=li, in1=li, op=ALU.mult), K, K)
    V(lambda e: e.tensor_tensor(out=sm["den"], in0=sm["den"], in1=sm["t2"], op=ALU.add), K, K)
    V(lambda e: e.reciprocal(out=sm["den"], in_=sm["den"]), K, K)
    V(lambda e: e.tensor_tensor(out=sm["t2"], in0=sm["t1"], in1=lr, op=ALU.mult), K, K)
    V(lambda e: e.tensor_tensor(out=sm["t3"], in0=sm["ai"], in1=li, op=ALU.mult), K, K)
    V(lambda e: e.tensor_tensor(out=sm["t2"], in0=sm["t2"], in1=sm["t3"], op=ALU.add), K, K)
    V(lambda e: e.tensor_tensor(out=sm["fr"], in0=sm["t2"], in1=sm["den"], op=ALU.mult), K, K)
    V(lambda e: e.tensor_tensor(out=sm["t2"], in0=sm["ai"], in1=lr, op=ALU.mult), K, K)
    V(lambda e: e.tensor_tensor(out=sm["t3"], in0=sm["t1"], in1=li, op=ALU.mult), K, K)
    V(lambda e: e.tensor_tensor(out=sm["t2"], in0=sm["t2"], in1=sm["t3"], op=ALU.subtract), K, K)
    V(lambda e: e.tensor_tensor(out=sm["fi"], in0=sm["t2"], in1=sm["den"], op=ALU.mult), K, K)
    V(lambda e: e.memset(pwr[0], 1.0), K, K)
    V(lambda e: e.memset(pwi[0], 0.0), K, K)
    for k in range(1, 9):
        V(lambda e, k=k: e.tensor_tensor(out=sm["t1"], in0=pwr[k - 1], in1=sm["ar"], op=ALU.mult), K, K)
        V(lambda e, k=k: e.tensor_tensor(out=sm["t2"], in0=pwi[k - 1], in1=sm["ai"], op=ALU.mult), K, K)
        V(lambda e, k=k: e.tensor_tensor(out=pwr[k], in0=sm["t1"], in1=sm["t2"], op=ALU.subtract), K, K)
        V(lambda e, k=k: e.tensor_tensor(out=sm["t1"], in0=pwr[k - 1], in1=sm["ai"], op=ALU.mult), K, K)
        V(lambda e, k=k: e.tensor_tensor(out=sm["t2"], in0=pwi[k - 1], in1=sm["ar"], op=ALU.mult), K, K)
        V(lambda e, k=k: e.tensor_tensor(out=pwi[k], in0=sm["t1"], in1=sm["t2"], op=ALU.add), K, K)
    V(lambda e: e.tensor_scalar(out=sm["t1"], in0=sm["lrdt"], scalar1=8.0, scalar2=None, op0=ALU.mult), K, K)
    A_(lambda e: e.activation(out=sm["rho"], in_=sm["t1"], func=AF.Exp), K, K)
    for _ in range(3):
        cdouble(sm["sn"], sm["cs"])
    V(lambda e: e.memset(Er[:, :, 0:1], 1.0), K, K)
    V(lambda e: e.memset(Ei[:, :, 0:1], 0.0), K, K)
    V(lambda e: e.tensor_copy(out=sm["wr"], in_=sm["cs"]), K, K)
    V(lambda e: e.tensor_copy(out=sm["wi"], in_=sm["sn"]), K, K)
    m0 = ar.mark()
    tA = ar.alloc("tA", [128, 32, 64], F32); tB = ar.alloc("tB", [128, 32, 64], F32)
    for k in range(7):
        n = 1 << k
        wrb = sm["wr"].unsqueeze(2).broadcast_to([128, 32, n]); wib = sm["wi"].unsqueeze(2).broadcast_to([128, 32, n])
        V(lambda e, n=n, wrb=wrb: e.tensor_tensor(out=tA[:, :, :n], in0=Er[:, :, :n], in1=wrb, op=ALU.mult), K, K)
        V(lambda e, n=n, wib=wib: e.tensor_tensor(out=tB[:, :, :n], in0=Ei[:, :, :n], in1=wib, op=ALU.mult), K, K)
        V(lambda e, n=n: e.tensor_tensor(out=Er[:, :, n:2 * n], in0=tA[:, :, :n], in1=tB[:, :, :n], op=ALU.subtract), K, K)
        V(lambda e, n=n, wib=wib: e.tensor_tensor(out=tA[:, :, :n], in0=Er[:, :, :n], in1=wib, op=ALU.mult), K, K)
        V(lambda e, n=n, wrb=wrb: e.tensor_tensor(out=tB[:, :, :n], in0=Ei[:, :, :n], in1=wrb, op=ALU.mult), K, K)
        V(lambda e, n=n: e.tensor_tensor(out=Ei[:, :, n:2 * n], in0=tA[:, :, :n], in1=tB[:, :, :n], op=ALU.add), K, K)
        V(lambda e: e.tensor_tensor(out=sm["t1"], in0=sm["wr"], in1=sm["wr"], op=ALU.mult), K, K)
        V(lambda e: e.tensor_tensor(out=sm["t2"], in0=sm["wi"], in1=sm["wi"], op=ALU.mult), K, K)
        V(lambda e: e.tensor_tensor(out=sm["t3"], in0=sm["wr"], in1=sm["wi"], op=ALU.mult), K, K)
        V(lambda e: e.tensor_tensor(out=sm["wr"], in0=sm["t1"], in1=sm["t2"], op=ALU.subtract), K, K)
        V(lambda e: e.tensor_scalar(out=sm["wi"], in0=sm["t3"], scalar1=2.0, scalar2=None, op0=ALU.mult), K, K)
    V(lambda e: e.tensor_copy(out=sm["w128r"], in_=sm["wr"]), K, K)
    V(lambda e: e.tensor_copy(out=sm["w128i"], in_=sm["wi"]), K, K)
    V(lambda e: e.memset(cR, 0.0), K, K)
    V(lambda e: e.memset(SL, 0.0), K, K)
    fw.barrier()
    ar.reset(m0)
    BTc = ar.alloc("BTc", [128, 32, 2, 16], F32); Cc = ar.alloc("Cc", [128, 2, 32, 16], F32)
    Cfc = ar.alloc("Cfc", [128, 32, 2, 16], F32); Xc2 = [ar.alloc("Xc", [128, 32, 2, 16], F32) for _ in range(2)]
    c1 = ar.alloc("c1", [128, 32, 16], F32); c2 = ar.alloc("c2", [128, 32, 16], F32)
    Cfp = ar.alloc("Cfp", [128, 32, 2, 128], BF16)
    padb2 = [ar.alloc("padb", [128, 32, 2, 128], BF16) for _ in range(2)]; DBsb2 = [ar.alloc("DBsb", [128, 32, 2, 128], BF16) for _ in range(2)]
    fw.dma("sync", BTc, btc, writes=["BTc"])
    fw.dma("sync", Cc, cc.rearrange("a p g c -> p a g c"), writes=["Cc"])
    for q_ in range(2):
        G_(lambda e, q_=q_: e.memset(padb2[q_], 0.0), [], [f"padb{q_}"])
    G_(lambda e: e.memset(Cfp, 0.0), [], ["Cfp"])

    def cmul_compact(dst, src_r, src_i, sr, si, rk, wk, neg_im=False):
        srb = sr.unsqueeze(2).broadcast_to([128, 32, 16]); sib = si.unsqueeze(2).broadcast_to([128, 32, 16])
        V(lambda e: e.tensor_tensor(out=c1, in0=src_r, in1=srb, op=ALU.mult), rk, ["c1"])
        V(lambda e: e.tensor_tensor(out=c2, in0=src_i, in1=sib, op=ALU.mult), rk, ["c2"])
        V(lambda e: e.tensor_tensor(out=dst[:, :, 0, :], in0=c1, in1=c2, op=ALU.subtract), ["c1", "c2"], wk)
        V(lambda e: e.tensor_tensor(out=c1, in0=src_r, in1=sib, op=ALU.mult), rk + wk, ["c1"])
        V(lambda e: e.tensor_tensor(out=c2, in0=src_i, in1=srb, op=ALU.mult), rk + wk, ["c2"])
        if neg_im:
            V(lambda e: e.scalar_tensor_tensor(out=dst[:, :, 1, :], in0=c1, scalar=-1.0, in1=c2, op0=ALU.mult, op1=ALU.subtract), ["c1", "c2"], wk)
        else:
            V(lambda e: e.tensor_tensor(out=dst[:, :, 1, :], in0=c1, in1=c2, op=ALU.add), ["c1", "c2"], wk)

    def scatter(dst_pad, src_c, rk, wk):
        for g2 in range(2):
            for q in range(4):
                blk = 2 * q + g2
                G_(lambda e, g2=g2, q=q, blk=blk: e.tensor_copy(out=dst_pad[g2 * 64:(g2 + 1) * 64, q::4, :, blk * 16:(blk + 1) * 16],
                                                                 in_=src_c[g2 * 64:(g2 + 1) * 64, q::4, :, :]), rk, wk)

    cmul_compact(Cfc, Cc[:, 0], Cc[:, 1], sm["fr"], sm["fi"], ["Cc"], ["Cfc"])
    cmul_compact(Xc2[0], Cc[:, 0], Cc[:, 1], sm["fr"], sm["fi"], ["Cc"], ["Xc0"], neg_im=True)
    scatter(Cfp, Xc2[0], ["Xc0"], ["Cfp"])
    for k in range(8):
        j = 7 - k
        q_ = k % 2
        Xc, padb, DBsb = Xc2[q_], padb2[q_], DBsb2[q_]
        kX, kP, kD = f"Xc{q_}", f"padb{q_}", f"DBsb{q_}"
        cmul_compact(Xc, BTc[:, :, 0, :], BTc[:, :, 1, :], pwr[k], pwi[k], ["BTc"], [kX])
        scatter(padb, Xc, [kX], [kP])
        for r in range(8):
            p, pk = psum()
            n_ = 0
            for gl in range(4):
                for ri in range(2):
                    mm(p[:, 0:128], padb[:, 4 * r + gl, ri, :], Cfp[:, 4 * r + gl, ri, :], n_ == 0, n_ == 7, [kP, "Cfp"], pk, n_ == 7)
                    n_ += 1
            if k == 0:
                V(lambda e, p=p, r=r: e.scalar_tensor_tensor(out=Kpad[:, r, 0, :], in0=identf, scalar=dcs[:, r:r + 1], in1=p[:, 0:128], op0=ALU.mult, op1=ALU.add),
                  [pk, "identf", "dcs"], ["Kpad"])
            else:
                A_(lambda e, p=p, r=r, k=k: e.activation(out=Kpad[:, r, k, :], in_=p[:, 0:128], func=AF.Copy), [pk], ["Kpad"])
        for g4 in range(16):
            p, pk = psum()
            for q in range(4):
                gi = g4 * 4 + q
                mm(p[:, q * 128:(q + 1) * 128], padb[:, gi // 2, gi % 2, :], identb, True, True, [kP, "identb"], pk, q == 3)
            dst_ = DBsb.rearrange("p g r s -> p (g r) s")[:, g4 * 4:(g4 + 1) * 4, :]
            if g4 % 2 == 0:
                V(lambda e, p=p, dst_=dst_: e.tensor_copy(out=dst_, in_=p.rearrange("p (a c) -> p a c", c=128)), [pk], [kD])
            else:
                A_(lambda e, p=p, dst_=dst_: e.activation(out=dst_, in_=p.rearrange("p (a c) -> p a c", c=128), func=AF.Copy), [pk], [kD])
        fw.dma("sync", DB_s[:, :, j].rearrange("r p g a s -> p r g a s"), DBsb.rearrange("p (r g) a s -> p r g a s", r=8), reads=[kD], writes=[("DB_s", j)])
    for j in range(8):
        q_ = j % 2
        Xc, padb = Xc2[q_], padb2[q_]
        kX, kP = f"Xc{q_}", f"padb{q_}"
        cmul_compact(Xc, Cfc[:, :, 0, :], Cfc[:, :, 1, :], pwr[j + 1], pwi[j + 1], ["Cfc"], [kX])
        scatter(padb, Xc, [kX], [kP])
        fw.dma("sync", EC_s[:, :, j].rearrange("r p g a s -> p r g a s"), padb.rearrange("p (r g) a s -> p r g a s", r=8), reads=[kP], writes=[("EC_s", j)])
    fw.barrier()
    ar.reset(base_ssm)

    if upto <= 2:
        fw.emit()
        return nc
    NPS, NMS = NP // 1024, NM // 1024
    DBr2 = [ar.alloc("DBr", [128, 8, 4, 2, 128], BF16) for _ in range(2)]
    ECr2 = [ar.alloc("ECr", [128, 8, 4, 2, 128], BF16) for _ in range(2)]
    uTr = [ar.alloc("uTr", [128, 1024], BF16) for _ in range(2)]
    uTj = [ar.alloc("uTj", [128, 8, 128], BF16) for _ in range(2)]
    zTr = [ar.alloc("zTr", [128, 1024], BF16) for _ in range(2)]
    RB = []
    for b in range(2):
        d = {n: ar.alloc(n, [128, 4, 128], F32) for n in ["t1", "t2", "t3", "t4", "Rr", "Ri"]}
        d["Xr"], d["Xi"] = d["t1"], d["t3"]
        d["Sr"] = ar.alloc("Sr", [128, 4, 130], BF16); d["Si"] = ar.alloc("Si", [128, 4, 130], BF16)
        for n in ["c1", "c2", "c3", "c4"]:
            d[n] = ar.alloc(n, [128, 4], F32)
        RB.append(d)
    ysb = [ar.alloc("ysb", [128, 1024], F32) for _ in range(2)]
    g1b = [ar.alloc("g1b", [128, 1024], F32) for _ in range(2)]
    g2b = g1b
    rcount = 0
    hcount = 0
    def round_gen(rp, st, q_):
        r = 2 * rp + q_
        gsl = slice(4 * r, 4 * r + 4)
        DBr, ECr = DBr2[q_], ECr2[q_]
        kDB, kEC = f"DBr{q_}", f"ECr{q_}"
        is_main = st >= NPS
        ub = q_
        fw.dma("sync", uTr[ub].rearrange("p (n t) -> p n t", t=128), uT_s[8 * st:8 * st + 8, :, r, :].rearrange("n p t -> p n t"),
               writes=[f"uTr{ub}"])
        fw.flush()
        A_(lambda e, ub=ub: e.activation(out=uTj[ub], in_=uTr[ub].rearrange("p (c j) -> p j c", j=8), func=AF.Copy), [f"uTr{ub}"], [f"uTj{ub}"])
        yield
        b = q_
        B = RB[b]
        kb = lambda n, b=b: f"{n}{b}"
        pXr, pkr = psum()
        pXi, pki = psum()
        for ri, (pX, pk) in enumerate(((pXr, pkr), (pXi, pki))):
            for gl in range(4):
                for j in range(8):
                    mm(pX[:, gl * 128:(gl + 1) * 128], DBr[:, j, gl, ri, :], uTj[ub][:, j, :], j == 0, j == 7,
                       [f"uTj{ub}", kDB], pk, (gl == 3 and j == 7))
        yield
        pXr3 = pXr.rearrange("p (a c) -> p a c", c=128); pXi3 = pXi.rearrange("p (a c) -> p a c", c=128)
        Erg, Eig = Er[:, gsl, :], Ei[:, gsl, :]
        V(lambda e, B=B, a=pXr3, t=Erg: e.tensor_tensor(out=B["t1"], in0=a, in1=t, op=ALU.mult), [pkr], [kb("t1")])
        V(lambda e, B=B, a=pXi3, t=Eig: e.tensor_tensor(out=B["t2"], in0=a, in1=t, op=ALU.mult), [pki], [kb("t2")])
        V(lambda e, B=B, a=pXi3, t=Erg: e.tensor_tensor(out=B["t3"], in0=a, in1=t, op=ALU.mult), [pki], [kb("t3")])
        V(lambda e, B=B, a=pXr3, t=Eig: e.tensor_tensor(out=B["t4"], in0=a, in1=t, op=ALU.mult), [pkr], [kb("t4")])
        yield
        G_(lambda e, B=B: e.tensor_tensor(out=B["t1"], in0=B["t1"], in1=B["t2"], op=ALU.add), [kb("t1"), kb("t2")], [kb("t1")])
        G_(lambda e, B=B: e.tensor_tensor(out=B["t3"], in0=B["t3"], in1=B["t4"], op=ALU.subtract), [kb("t3"), kb("t4")], [kb("t3")])
        yield
        for gl in range(4):
            gp = 4 * r + gl
            for nm, xs, ci in (("Rr", "t1", 0), ("Ri", "t3", 1)):
                V(lambda e, B=B, gl=gl, gp=gp, nm=nm, xs=xs, ci=ci: e.tensor_tensor_scan(
                    out=B[nm][:, gl, :], data0=sm["rho"][:, gp:gp + 1].broadcast_to([128, 128]), data1=B[xs][:, gl, :],
                    initial=cR[:, ci, gp:gp + 1], op0=ALU.mult, op1=ALU.add), [kb(xs), ("cR", r)], [kb(nm)])
        yield
        if is_main:
            G_(lambda e, B=B, gsl=gsl: e.tensor_copy(out=B["Sr"][:, :, 0], in_=SL[:, 0, gsl]), [("SL", r)], [kb("Sr")])
            G_(lambda e, B=B, gsl=gsl: e.tensor_copy(out=B["Si"][:, :, 0], in_=SL[:, 1, gsl]), [("SL", r)], [kb("Si")])
        wr4, wi4 = sm["w128r"][:, gsl], sm["w128i"][:, gsl]
        er7, ei7 = Er[:, gsl, 127], Ei[:, gsl, 127]
        Rr7, Ri7 = B["Rr"][:, :, 127], B["Ri"][:, :, 127]
        for (xr_, xi_, dst, negim, key) in ((wr4, wi4, cR, False, "cR"), (er7, ei7, SL, True, "SL")):
            G_(lambda e, B=B, a=Rr7, w=xr_: e.tensor_tensor(out=B["c1"], in0=a, in1=w, op=ALU.mult), [kb("Rr")], [kb("c1")])
            G_(lambda e, B=B, a=Ri7, w=xi_: e.tensor_tensor(out=B["c2"], in0=a, in1=w, op=ALU.mult), [kb("Ri")], [kb("c2")])
            G_(lambda e, B=B, a=Ri7, w=xr_: e.tensor_tensor(out=B["c3"], in0=a, in1=w, op=ALU.mult), [kb("Ri")], [kb("c3")])
            G_(lambda e, B=B, a=Rr7, w=xi_: e.tensor_tensor(out=B["c4"], in0=a, in1=w, op=ALU.mult), [kb("Rr")], [kb("c4")])
            G_(lambda e, B=B, dst=dst, gsl=gsl: e.tensor_tensor(out=dst[:, 0, gsl], in0=B["c1"], in1=B["c2"], op=ALU.subtract),
               [kb("c1"), kb("c2")], [(key, r)])
            if negim:
                V(lambda e, B=B, dst=dst, gsl=gsl: e.scalar_tensor_tensor(out=dst[:, 1, gsl], in0=B["c3"], scalar=-1.0, in1=B["c4"], op0=ALU.mult, op1=ALU.subtract),
                   [kb("c3"), kb("c4")], [(key, r)])
            else:
                G_(lambda e, B=B, dst=dst, gsl=gsl: e.tensor_tensor(out=dst[:, 1, gsl], in0=B["c3"], in1=B["c4"], op=ALU.add),
                   [kb("c3"), kb("c4")], [(key, r)])
        yield
        if not is_main:
            return
        G_(lambda e, B=B, t=Erg: e.tensor_tensor(out=B["t1"], in0=B["Rr"], in1=t, op=ALU.mult), [kb("Rr")], [kb("t1")])
        G_(lambda e, B=B, t=Eig: e.tensor_tensor(out=B["t2"], in0=B["Ri"], in1=t, op=ALU.mult), [kb("Ri")], [kb("t2")])
        yield
        V(lambda e, B=B, t=Erg: e.tensor_tensor(out=B["t3"], in0=B["Ri"], in1=t, op=ALU.mult), [kb("Ri")], [kb("t3")])
        V(lambda e, B=B, t=Eig: e.tensor_tensor(out=B["t4"], in0=B["Rr"], in1=t, op=ALU.mult), [kb("Rr")], [kb("t4")])
        yield
        V(lambda e, B=B: e.tensor_tensor(out=B["Sr"][:, :, 1:129], in0=B["t1"], in1=B["t2"], op=ALU.subtract), [kb("t1"), kb("t2")], [kb("Sr")])
        V(lambda e, B=B: e.scalar_tensor_tensor(out=B["Si"][:, :, 1:129], in0=B["t3"], scalar=-1.0, in1=B["t4"], op0=ALU.mult, op1=ALU.subtract),
          [kb("t3"), kb("t4")], [kb("Si")])
        zb = q_
        hb = q_
        yield
        for h2 in range(2):
            py, pky = psum()
            for j4 in range(4):
                j = 4 * h2 + j4
                o = py[:, j4 * 128:(j4 + 1) * 128]
                nmm = (j + 1) + 8
                n_ = 0
                for k in range(j + 1):
                    mm(o, Kpad[:, r, k, :], uTj[ub][:, j - k, :], n_ == 0, n_ == nmm - 1, [f"uTj{ub}", "Kpad"], pky, False)
                    n_ += 1
                for gl in range(4):
                    for ri, Sn in enumerate(("Sr", "Si")):
                        mm(o, ECr[:, j, gl, ri, :], B[Sn][:, gl, 0:128], n_ == 0, n_ == nmm - 1, [kb(Sn), kEC], pky,
                           (j4 == 3 and n_ == nmm - 1))
                        n_ += 1
            yield
            A_(lambda e, py=py, hb=hb, h2=h2: e.activation(out=ysb[hb].rearrange("p (c j) -> p c j", j=8)[:, :, 4 * h2:4 * h2 + 4],
                                                           in_=py.rearrange("p (j c) -> p c j", c=128), func=AF.Identity), [pky], [f"ysb{hb}"])
        yield
        G_(lambda e, hb=hb: e.tensor_tensor(out=g1b[hb], in0=ysb[hb], in1=ysb[hb], op=ALU.mult), [f"ysb{hb}"], [f"g1b{hb}"])
        G_(lambda e, hb=hb: e.tensor_scalar(out=g1b[hb], in0=g1b[hb], scalar1=0.044715, scalar2=1.0, op0=ALU.mult, op1=ALU.add), [f"g1b{hb}"], [f"g1b{hb}"])
        G_(lambda e, hb=hb: e.tensor_tensor(out=g1b[hb], in0=g1b[hb], in1=ysb[hb], op=ALU.mult), [f"g1b{hb}", f"ysb{hb}"], [f"g1b{hb}"])
        yield
        A_(lambda e, hb=hb: e.activation(out=g2b[hb], in_=g1b[hb], func=AF.Sigmoid, scale=1.5957691216057308), [f"g1b{hb}"], [f"g2b{hb}"])
        yield
        V(lambda e, hb=hb, zb=zb: e.tensor_tensor(out=zTr[zb], in0=ysb[hb], in1=g2b[hb], op=ALU.mult), [f"ysb{hb}", f"g2b{hb}"], [f"zTr{zb}"])
        m8 = 8 * (st - NPS)
        fw.defer_dma("sync", zT_s[m8:m8 + 8, :, r, :].rearrange("n p t -> p n t"), zTr[zb].rearrange("p (n t) -> p n t", t=128),
               reads=[f"zTr{zb}"], writes=[("zT_s", st, r)])

    for rp in range(4):
        for q_ in range(2):
            fw.dma("sync", DBr2[q_], DB_s[2 * rp + q_], writes=[f"DBr{q_}"])
            fw.dma("sync", ECr2[q_], EC_s[2 * rp + q_], writes=[f"ECr{q_}"])
        for st in range(NPS + NMS):
            gens = [round_gen(rp, st, 0), round_gen(rp, st, 1)]
            alive = True
            while alive:
                alive = False
                for g_ in gens:
                    try:
                        next(g_)
                        alive = True
                    except StopIteration:
                        pass
    fw.barrier()
    ar.reset(base_persist)

    if upto <= 3:
        fw.emit()
        return nc
    Wg = ar.alloc("Wg", [128, 8, 2048], BF16); Wo = ar.alloc("Wo", [128, 8, 1024], BF16)
    load_weights(Wg, w_glu, 4, 8, "Wg")
    load_weights(Wo, w_out, 2, 8, "Wo")
    kme = ar.alloc("kme", [128, 2, 128], BF16); vme = ar.alloc("vme", [128, 4, 65], BF16)
    fw.dma("sync", kme, kT_s[0], writes=["kme"]); fw.dma("sync", vme, v_s[0], writes=["vme"])
    qTl = [ar.alloc("qTl", [128, 8, 128], BF16) for _ in range(2)]
    kTl = [ar.alloc("kTl", [128, 2, 128], BF16) for _ in range(3)]
    vl = [ar.alloc("vl", [128, 4, 65], BF16) for _ in range(3)]
    gl_ = [ar.alloc("gl", [128, 2048], BF16) for _ in range(2)]
    zTl = [ar.alloc("zTl", [128, 8, 128], BF16) for _ in range(2)]
    xr = [ar.alloc("xr", [128, D], F32) for _ in range(2)]
    Pc = [ar.alloc("Pc", [128, 512], BF16) for _ in range(4)]
    Pp = [ar.alloc("Pp", [128, 512], BF16) for _ in range(4)]
    Pm = [ar.alloc("Pm", [128, 512], BF16) for _ in range(4)]
    den_ = [ar.alloc("den", [128, 4], F32) for _ in range(4)]
    for b_ in range(4):
        V(lambda e, b_=b_: e.memset(Pm[b_], 0.0), [], [f"Pm{b_}"])
    attn_ = [ar.alloc("attn", [128, D], F32) for _ in range(2)]; An_ = [ar.alloc("An", [128, D], F32) for _ in range(2)]
    sig_ = [ar.alloc("sig", [128, 512], F32) for _ in range(2)]; ssm_ = [ar.alloc("ssm", [128, D], F32) for _ in range(2)]
    Bn_ = [ar.alloc("Bn", [128, D], F32) for _ in range(2)]
    mg_ = [ar.alloc("mg", [128, D], BF16) for _ in range(2)]; mgT_ = [ar.alloc("mgT", [128, 8, 128], BF16) for _ in range(2)]
    h1 = [ar.alloc("h1", [128, D], F32) for _ in range(2)]
    fw.dma("sync", kTl[1], kT_s[1], writes=["kTl1"]); fw.dma("sync", vl[1], v_s[1], writes=["vl1"])
    def s3_loads(i):
        b = i % 2
        jc = 2 + i
        sc = jc % 3
        fw.dma("sync", kTl[sc], kT_s[jc], reads=[("kT_s", jc)], writes=[f"kTl{sc}"])
        fw.dma("sync", vl[sc], v_s[jc], reads=[("v_s", jc)], writes=[f"vl{sc}"])
        fw.dma("sync", qTl[b], qT_s[i], writes=[f"qTl{b}"])
        fw.dma("sync", gl_[b], g_s[i], writes=[f"gl{b}"])
        fw.dma("sync", zTl[b], zT_s[i], writes=[f"zTl{b}"])
        fw.dma("sync", xr[b], xmain[i * 128:(i + 1) * 128, :], writes=[f"xr{b}"])
        fw.flush()

    def s3_A(n):
        i, grp = n // 4, n % 4
        b, pb = i % 2, 2 * (i % 2) + n % 2
        sc, sp = (2 + i) % 3, (1 + i) % 3
        bs, kc = (grp % 2) * 64, grp // 2
        qsel = qTl[b][bs:bs + 64, kc * 4:(kc + 1) * 4, :]
        pS, pkS = psum()
        mm(pS, kTl[sc][bs:bs + 64, kc, :], qsel, True, True, [f"kTl{sc}", f"qTl{b}"], pkS, True)
        A_(lambda e: e.activation(out=Pc[pb], in_=pS, func=AF.Exp, scale=0.125), [pkS], [f"Pc{pb}"])
        G_(lambda e: e.tensor_tensor(out=Pc[pb], in0=Pc[pb], in1=maskb[:, 0, :], op=ALU.mult), [f"Pc{pb}", "maskb"], [f"Pc{pb}"])
        pS2, pkS2 = psum()
        mm(pS2, kTl[sp][bs:bs + 64, kc, :], qsel, True, True, [f"kTl{sp}", f"qTl{b}"], pkS2, True)
        A_(lambda e: e.activation(out=Pp[pb], in_=pS2, func=AF.Exp, scale=0.125), [pkS2], [f"Pp{pb}"])
        mi = 2 if i == 0 else 1
        G_(lambda e: e.tensor_tensor(out=Pp[pb], in0=Pp[pb], in1=maskb[:, mi, :], op=ALU.mult), [f"Pp{pb}", "maskb"], [f"Pp{pb}"])
        pS3, pkS3 = psum()
        mm(pS3[0:16, :], kme[bs:bs + 64, kc, 0:16], qsel, True, True, ["kme", f"qTl{b}"], pkS3, True)
        A_(lambda e: e.activation(out=Pm[pb][0:16, :], in_=pS3[0:16, :], func=AF.Exp, scale=0.125), [pkS3], [f"Pm{pb}"])

    def s3_B(n):
        i, grp = n // 4, n % 4
        b, pb = i % 2, 2 * (i % 2) + n % 2
        sc, sp = (2 + i) % 3, (1 + i) % 3
        attn, kA = attn_[b], f"attn{b}"
        pO, pkO = psum()
        for r in range(4):
            o = pO[:, r * 65:(r + 1) * 65]
            mm(o, Pm[pb][:, r * 128:(r + 1) * 128], vme[:, grp, :], True, False, [f"Pm{pb}", "vme"], pkO, False)
            mm(o, Pp[pb][:, r * 128:(r + 1) * 128], vl[sp][:, grp, :], False, False, [f"Pp{pb}", f"vl{sp}"], pkO, False)
            mm(o, Pc[pb][:, r * 128:(r + 1) * 128], vl[sc][:, grp, :], False, True, [f"Pc{pb}", f"vl{sc}"], pkO, r == 3)
        pO3 = pO[:, 0:260].rearrange("p (r c) -> p r c", c=65)
        den = den_[pb]
        kd = f"den{pb}"
        V(lambda e: e.tensor_tensor(out=den, in0=pO3[:, :, 64], in1=esink[:, grp * 4:(grp + 1) * 4], op=ALU.add), [pkO, "esink"], [kd])
        V(lambda e: e.reciprocal(out=den, in_=den), [kd], [kd])
        V(lambda e: e.tensor_tensor(out=attn[:, grp * 256:(grp + 1) * 256].rearrange("p (r d) -> p r d", d=64), in0=pO3[:, :, 0:64],
                                    in1=den.unsqueeze(2).broadcast_to([128, 4, 64]), op=ALU.mult), [pkO, kd], [kA])

    def s3_tail(i):
        b = i % 2
        attn, An, ssm, Bn, mg, mgT = attn_[b], An_[b], ssm_[b], Bn_[b], mg_[b], mgT_[b]
        kA, kAn, kss, kBn, kmg, kmT = f"attn{b}", f"An{b}", f"ssm{b}", f"Bn{b}", f"mg{b}", f"mgT{b}"
        rms_scale(attn, 1, An, [kA], [kAn])
        yield
        G_(lambda e: e.tensor_tensor(out=An, in0=An, in1=gl_[b][:, 0:1024], op=ALU.mult), [kAn, f"gl{b}"], [kAn])
        for half in range(2):
            pa, pka = psum()
            for k in range(8):
                mm(pa, zTl[b][:, k, :], Wg[:, k, half * 512:(half + 1) * 512], k == 0, k == 7, [f"zTl{b}", "Wg"], pka, k == 7)
            pz, pkz = psum()
            for k in range(8):
                mm(pz, zTl[b][:, k, :], Wg[:, k, 1024 + half * 512:1024 + (half + 1) * 512], k == 0, k == 7, [f"zTl{b}", "Wg"], pkz, k == 7)
            yield
            sig = sig_[half]
            A_(lambda e, pz=pz, sig=sig: e.activation(out=sig, in_=pz, func=AF.Sigmoid), [pkz], [f"sig{half}"])
            V(lambda e, pa=pa, half=half, sig=sig: e.tensor_tensor(out=ssm[:, half * 512:(half + 1) * 512], in0=pa, in1=sig, op=ALU.mult), [pka, f"sig{half}"], [kss])
        yield
        rms_scale(ssm, 2, Bn, [kss], [kBn])
        yield
        G_(lambda e: e.tensor_tensor(out=Bn, in0=Bn, in1=gl_[b][:, 1024:2048], op=ALU.mult), [kBn, f"gl{b}"], [kBn])
        V(lambda e: e.tensor_tensor(out=mg, in0=An, in1=Bn, op=ALU.add), [kAn, kBn], [kmg])
        yield
        transpose8(mg, mgT, kmg, kmT)
        yield
        for half in range(2):
            p, pk = psum()
            for k in range(8):
                mm(p, mgT[:, k, :], Wo[:, k, half * 512:(half + 1) * 512], k == 0, k == 7, [kmT, "Wo"], pk, k == 7)
            V(lambda e, p=p, half=half: e.tensor_tensor(out=h1[b][:, half * 512:(half + 1) * 512], in0=p, in1=xr[b][:, half * 512:(half + 1) * 512], op=ALU.add),
              [pk, f"xr{b}"], [f"h1{b}"])
        fw.defer_dma("sync", h1_s[i * 128:(i + 1) * 128, :], h1[b], reads=[f"h1{b}"], writes=[("h1_s", i)])

    def s3_tile_gen(i):
        s3_loads(i)
        yield
        n0 = 4 * i
        for step in (("A", 0), ("A", 1), ("B", 0), ("A", 2), ("B", 1), ("A", 3), ("B", 2), ("B", 3)):
            (s3_A if step[0] == "A" else s3_B)(n0 + step[1])
            yield
        yield from s3_tail(i)

    for i in range(0, TM_, 2):
        lockstep([s3_tile_gen(i), s3_tile_gen(i + 1)])
    fw.barrier()
    ar.reset(base_persist)

    if upto <= 4:
        fw.emit()
        return nc
    W1 = ar.alloc("W1", [128, 8, 5632], BF16); W2 = ar.alloc("W2", [128, 22, 1024], BF16)
    load_weights(W1, w_f1, 11, 8, "W1")
    load_weights(W2, w_f2, 2, 22, "W2")
    GT = 4
    hl = [ar.alloc("hl", [128, D], F32) for _ in range(2)]
    hn = [ar.alloc("hn", [128, D], BF16) for _ in range(2)]
    hnT = ar.alloc("hnT", [128, 8, GT * 128], BF16)
    sg = [ar.alloc("sg", [128, 512], F32) for _ in range(2)]
    actT = ar.alloc("actT", [128, 22, GT * 128], BF16)
    hres = hl
    ob = junk
    tcount = 0
    for g in range(TM_ // GT):
        for t4 in range(GT):
            i = g * GT + t4
            b = tcount % 2
            tcount += 1
            fw.dma("sync", hl[b], h1_s[i * 128:(i + 1) * 128, :], reads=[("h1_s", i)], writes=[f"hl{b}"])
            fw.flush()
            rms_scale(hl[b], 3, hn[b], [f"hl{b}"], [f"hn{b}"])
            for half in range(2):
                p, pk = psum()
                for jj in range(4):
                    c = half * 4 + jj
                    mm(p[:, jj * 128:(jj + 1) * 128], hn[b][:, c * 128:(c + 1) * 128], identb, True, True, [f"hn{b}", "identb"], pk, jj == 3)
                V(lambda e, p=p, half=half, t4=t4: e.tensor_copy(out=hnT[:, half * 4:half * 4 + 4, t4 * 128:(t4 + 1) * 128],
                                                               in_=p.rearrange("p (a c) -> p a c", c=128)), [pk], [("hnT", t4)])
        hk = [("hnT", t4) for t4 in range(GT)]
        for fc in range(22):
            fp, q2 = fc // 2, fc % 2
            pg, pkg = psum()
            for k in range(8):
                mm(pg, W1[:, k, fp * 512 + q2 * 256:fp * 512 + q2 * 256 + 128], hnT[:, k, :], k == 0, k == 7, hk + ["W1"], pkg, k == 7)
            pu, pku = psum()
            for k in range(8):
                mm(pu, W1[:, k, fp * 512 + q2 * 256 + 128:fp * 512 + q2 * 256 + 256], hnT[:, k, :], k == 0, k == 7, hk + ["W1"], pku, k == 7)
            s_ = sg[fc % 2]
            ks_ = f"sg{fc % 2}"
            A_(lambda e, pg=pg, s_=s_: e.activation(out=s_, in_=pg, func=AF.Sigmoid), [pkg], [ks_])
            V(lambda e, pg=pg, s_=s_: e.tensor_tensor(out=s_, in0=pg, in1=s_, op=ALU.mult), [pkg, ks_], [ks_])
            V(lambda e, pu=pu, s_=s_, fc=fc: e.tensor_tensor(out=actT[:, fc, :], in0=pu, in1=s_, op=ALU.mult), [pku, ks_], [("actT", fc)])
        ak = [("actT", fc) for fc in range(22)]
        for t4 in range(GT):
            i = g * GT + t4
            b = t4 % 2
            fw.dma("sync", hres[b], h1_s[i * 128:(i + 1) * 128, :], reads=[("h1_s", i)], writes=[f"hl{b}"])
            fw.flush()
            for half in range(2):
                p, pk = psum()
                for k in range(22):
                    mm(p, actT[:, k, t4 * 128:(t4 + 1) * 128], W2[:, k, half * 512:(half + 1) * 512], k == 0, k == 21, ak + ["W2"], pk, k == 21)
                V(lambda e, p=p, half=half, b=b: e.tensor_tensor(out=ob[b][:, half * 512:(half + 1) * 512], in0=p, in1=hres[b][:, half * 512:(half + 1) * 512], op=ALU.add),
                  [pk, f"hl{b}"], [f"junk{b}"])
            fw.defer_dma("sync", out[i * 128:(i + 1) * 128, :], ob[b], reads=[f"junk{b}"], writes=[("out", i)])
    fw.emit()
    return nc


def _panels(w, kk):
    n = w.shape[1] // 512
    return np.ascontiguousarray(w.reshape(kk, 128, n, 512).transpose(2, 1, 0, 3))


def prep_shared(inp):
    f = lambda a: np.asarray(a, dtype=np.float32)
    w_in = f(inp["w_in"])[0]
    qcols = []
    for j in range(8):
        for s in range(2):
            head = ((j // 4) * 2 + s) * 4 + (j % 4)
            qcols.extend(range(head * 64, head * 64 + 64))
    w_in_r = np.concatenate([w_in[:, qcols], w_in[:, 1024:1536], w_in[:, 1536:]], axis=1)
    wf1 = f(inp["w_ffn_in"])[0]
    cols = []
    for c in range(22):
        cols.extend(range(c * 128, (c + 1) * 128))
        cols.extend(range(DFF + c * 128, DFF + (c + 1) * 128))
    wf1_r = wf1[:, cols]
    rep = lambda v, n: np.ascontiguousarray(np.broadcast_to(f(v).reshape(1, -1), (128, n)))
    gains = np.stack([rep(inp["norm_mix"][0], D), rep(inp["attn_branch_norm"][0], D), rep(inp["ssm_branch_norm"][0], D), rep(inp["norm_ffn"][0], D)])

    def sp(a):
        return np.ascontiguousarray(f(a).reshape(32, 2, 64).transpose(1, 2, 0).reshape(128, 32))

    lam = np.stack([sp(inp["lam_re"][0]), sp(inp["lam_im"][0]), sp(np.broadcast_to(f(inp["log_dt"])[0][:, None], (64, 64)))])
    bre, bim = f(inp["ssm_b_re"])[0], f(inp["ssm_b_im"])[0]
    def spc(a):
        return a.reshape(32, 2, 64, a.shape[-1]).transpose(1, 2, 0, 3).reshape(128, 32, a.shape[-1])
    btc = np.ascontiguousarray(np.stack([spc(bre), spc(bim)], axis=2))
    cre, cim = f(inp["ssm_c_re"])[0], f(inp["ssm_c_im"])[0]
    cc = np.ascontiguousarray(np.stack([spc(cre.transpose(0, 2, 1)), spc(cim.transpose(0, 2, 1))]))
    kk, qq = np.arange(128)[:, None], np.arange(128)[None, :]
    mcur = np.where(kk <= qq, 1.0, 0.0).astype(np.float32)
    mprev = np.where(kk > qq, 1.0, 0.0).astype(np.float32)
    return dict(
        w_in=_panels(w_in_r, 8), w_glu=_panels(f(inp["w_glu"])[0], 8), w_out=_panels(f(inp["w_out"])[0], 8),
        w_f1=_panels(wf1_r, 8), w_f2=_panels(f(inp["w_ffn_out"])[0], 22), gains=gains,
        gq=rep(np.tile(f(inp["q_norm"])[0], 4), 256), gk=rep(np.tile(f(inp["k_norm"])[0], 4), 256),
        sinks=rep(inp["attn_sinks"][0], 16), ident=np.eye(128, dtype=np.float32), lam=lam, btc=btc, cc=cc,
        dcol=np.ascontiguousarray(f(inp["ssm_d"])[0].reshape(8, 128).T),
    ), mcur, mprev


def prep_core(x_b, meta, h, NM, NP, mcur, mprev):
    xmain = np.ascontiguousarray(x_b[h * NM:(h + 1) * NM])
    xpre = np.zeros((NP, D), np.float32)
    xctx = np.zeros((256, D), np.float32)
    xctx[0:16] = meta
    if h == 0:
        xpre[NP - 16:] = meta
        m0 = np.zeros((128, 128), np.float32)
    else:
        xpre[1008:1024] = meta
        xpre[1024:] = x_b[0:NM]
        xctx[128:256] = x_b[NM - 128:NM]
        m0 = mprev
    masks = np.stack([np.tile(mcur, (1, 4)), np.tile(mprev, (1, 4)), np.tile(m0, (1, 4))]).astype(np.float32)
    return dict(xmain=xmain, xpre=xpre, xctx=xctx, masks=masks)


_NC_CACHE = {}


def kernel(**inputs):
    x = np.asarray(inputs["x"], dtype=np.float32)
    Bsz, S, _ = x.shape
    NM = S // 2
    NP = NM + 1024
    meta = np.asarray(inputs["meta_tokens"], dtype=np.float32)
    shared, mcur, mprev = prep_shared(inputs)
    in_maps = []
    for b in range(Bsz):
        for h in range(2):
            d = dict(shared)
            d.update(prep_core(x[b], meta, h, NM, NP, mcur, mprev))
            in_maps.append(d)
    nc = build(NM, NP)
    res = run_bass_kernel_spmd(nc, in_maps, core_ids=list(range(len(in_maps))))
    outp = np.zeros((Bsz, S, D), np.float32)
    for b in range(Bsz):
        for h in range(2):
            outp[b, h * NM:(h + 1) * NM] = res.results[2 * b + h]["out"]
    return outp
```

```python
import math
import contextlib
import numpy as np
import concourse.bass as bass
import concourse.mybir as mybir
from concourse.bass_utils import run_bass_kernel_spmd

F32 = mybir.dt.float32
BF16 = mybir.dt.bfloat16
AF = mybir.ActivationFunctionType
ALU = mybir.AluOpType
AX = mybir.AxisListType
ENGS = ("tensor", "vector", "scalar", "gpsimd", "sync")
D = 1024
DFF = 2816
NEG = -30000.0


class FW:
    def __init__(self, nc, n_dma_sems=40):
        self.nc = nc
        self.ops = {e: [] for e in ENGS}
        self.cnt = {e: 0 for e in ENGS}
        self.known = {e: {} for e in ENGS}
        self.last_w = {}
        self.readers = {}
        self.n_dma_sems = n_dma_sems
        self.dma_gen = [0] * n_dma_sems
        self.dma_rr = 0
        self.sem_names = [f"s_{e}" for e in ENGS] + [f"d_{i}" for i in range(n_dma_sems)]

    def _deps(self, reads, writes):
        evs = []
        for k in reads:
            if k in self.last_w:
                evs.append(self.last_w[k])
        for k in writes:
            if k in self.last_w:
                evs.append(self.last_w[k])
            evs.extend(self.readers.get(k, ()))
        return evs

    def _commit(self, ev, reads, writes):
        for k in reads:
            self.readers.setdefault(k, []).append(ev)
        for k in writes:
            self.last_w[k] = ev
            self.readers[k] = []

    def _waits(self, eng, evs):
        best = {}
        for (s, v) in evs:
            if v > best.get(s, 0):
                best[s] = v
        out = []
        kn = self.known[eng]
        for s, v in best.items():
            if eng == "tensor" and s == "s_tensor":
                continue
            if kn.get(s, 0) >= v:
                continue
            kn[s] = v
            out.append((s, v))
        return out

    def op(self, eng, fn, reads=(), writes=(), inc=True):
        evs = self._deps(reads, writes)
        waits = self._waits(eng, evs)
        sname = f"s_{eng}"
        ev = (sname, self.cnt[eng] + 1)
        if inc:
            self.cnt[eng] += 1
        self.ops[eng].append((waits, fn, (sname, 1) if inc else None))
        self._commit(ev, reads, writes)
        return ev

    def dma(self, queue, out, in_, reads=(), writes=(), **kw):
        i = self.dma_rr
        self.dma_rr = (self.dma_rr + 1) % self.n_dma_sems
        sname = f"d_{i}"
        evs = self._deps(reads, writes)
        if self.dma_gen[i] > 0:
            evs.append((sname, 16 * self.dma_gen[i]))
        waits = self._waits(queue, evs)
        self.dma_gen[i] += 1
        ev = (sname, 16 * self.dma_gen[i])
        self.ops[queue].append((waits, lambda e: e.dma_start(out=out, in_=in_, **kw), (sname, 16)))
        self._commit(ev, reads, writes)
        return ev

    def defer_dma(self, *a, **kw):
        if not hasattr(self, "_deferred"):
            self._deferred = []
        self._deferred.append((a, kw))

    def flush(self):
        for a, kw in getattr(self, "_deferred", []):
            self.dma(*a, **kw)
        self._deferred = []

    def barrier(self):
        self.flush()
        fin = []
        for e in ENGS:
            if self.cnt[e] > 0:
                fin.append((f"s_{e}", self.cnt[e]))
        for i in range(self.n_dma_sems):
            if self.dma_gen[i] > 0:
                fin.append((f"d_{i}", 16 * self.dma_gen[i]))
        for e in ENGS:
            w = self._waits(e, fin)
            if w:
                self.ops[e].append((w, None, None))
        self.last_w = {}
        self.readers = {}

    def emit(self):
        nc = self.nc
        self.barrier()
        with contextlib.ExitStack() as st:
            sems = {n: st.enter_context(nc.semaphore(n)) for n in self.sem_names}
            block = st.enter_context(nc.Block())

            def mk(engname):
                lst = self.ops[engname]

                def body(eng):
                    for (waits, fn, inc) in lst:
                        for (s, v) in waits:
                            eng.wait_ge(sems[s], v)
                        if fn is None:
                            continue
                        ins = fn(eng)
                        if inc is not None:
                            ins.then_inc(sems[inc[0]], inc[1])
                return body

            block.tensor(mk("tensor"))
            block.vector(mk("vector"))
            block.scalar(mk("scalar"))
            block.gpsimd(mk("gpsimd"))
            block.sync(mk("sync"))


class Arena:
    def __init__(self, nc, base=16640, limit=224 * 1024):
        self.nc, self.off, self.limit, self.n = nc, base, limit, 0

    def alloc(self, name, shape, dt):
        per = int(np.prod(shape[1:])) * (4 if dt == F32 else 2)
        per = (per + 63) // 64 * 64
        assert self.off + per <= self.limit, (name, self.off, per)
        self.n += 1
        t = self.nc.alloc_sbuf_tensor_at(f"{name}_{self.n}_{self.off}", list(shape), dt, offset=self.off)
        self.off += per
        return t.ap()

    def mark(self):
        return self.off

    def reset(self, off):
        self.off = off


def build(NM, NP, upto=9):
    nc = bass.Bass("TRN2", target_bir_lowering=False)
    fw = FW(nc)
    TM_, TP_ = NM // 128, NP // 128
    NS = TP_ + TM_
    NK = 2 + TM_

    def din(name, shape, dt=F32):
        return nc.dram_tensor(name, list(shape), dt, kind="ExternalInput").ap()

    xmain = din("xmain", [NM, D]); xpre = din("xpre", [NP, D]); xctx = din("xctx", [256, D])
    w_in = din("w_in", [9, 128, 8, 512]); w_glu = din("w_glu", [4, 128, 8, 512]); w_out = din("w_out", [2, 128, 8, 512])
    w_f1 = din("w_f1", [11, 128, 8, 512]); w_f2 = din("w_f2", [2, 128, 22, 512])
    gains = din("gains", [4, 128, D])
    gq = din("gq", [128, 256]); gk = din("gk", [128, 256]); sinks = din("sinks", [128, 16])
    masks = din("masks", [3, 128, 512])
    ident = din("ident", [128, 128])
    lam = din("lam", [3, 128, 32])
    btc = din("btc", [128, 32, 2, 16]); cc = din("cc", [2, 128, 32, 16]); dcol = din("dcol", [128, 8])
    out = nc.dram_tensor("out", [NM, D], F32, kind="ExternalOutput").ap()

    def dscr(name, shape, dt):
        return nc.dram_tensor(name, list(shape), dt, kind="Internal").ap()

    uT_s = dscr("uT_s", [NS, 128, 8, 128], BF16); qT_s = dscr("qT_s", [TM_, 128, 8, 128], BF16)
    kT_s = dscr("kT_s", [NK, 128, 2, 128], BF16); v_s = dscr("v_s", [NK, 128, 4, 65], BF16)
    g_s = dscr("g_s", [TM_, 128, 2048], BF16); zT_s = dscr("zT_s", [TM_, 128, 8, 128], BF16)
    h1_s = dscr("h1_s", [NM, D], F32)
    DB_s = dscr("DB_s", [8, 128, 8, 4, 2, 128], BF16); EC_s = dscr("EC_s", [8, 128, 8, 4, 2, 128], BF16)

    ar = Arena(nc)
    identf = ar.alloc("identf", [128, 128], F32); identb = ar.alloc("identb", [128, 128], BF16)
    gsb = ar.alloc("gsb", [128, 4, D], F32)
    fw.dma("sync", identf, ident, writes=["identf"])
    fw.op("vector", lambda e: e.tensor_copy(out=identb, in_=identf), reads=["identf"], writes=["identb"])
    fw.dma("sync", gsb, gains.rearrange("a p d -> p a d"), writes=["gsb"])
    pbank = [nc.alloc_psum_tensor(f"pb{i}", [128, 512], F32).ap() for i in range(8)]
    pcnt = [0]

    def psum():
        i = pcnt[0] % 8
        pcnt[0] += 1
        return pbank[i], f"pb{i}"

    rr = [0]

    def alt():
        rr[0] += 1
        return "vector" if rr[0] % 2 else "gpsimd"

    base0 = ar.mark()

    def load_weights(dst, src, npan, kk, key):
        m = ar.mark()
        nst = 3 if ar.off + 3 * 16384 <= ar.limit else 2
        st = [ar.alloc("wst", [128, 8, 512], F32) for _ in range(nst)]
        cyc = ["vector", "scalar", "gpsimd", "vector", "scalar"] if nst == 3 else ["vector", "scalar"]
        n = 0
        for pi in range(npan):
            for k0 in range(0, kk, 8):
                kc = min(8, kk - k0)
                s = st[n % nst]
                fw.dma("sync", s[:, :kc, :], src[pi][:, k0:k0 + kc, :], writes=[f"wst{n % nst}"])
                eng = cyc[n % len(cyc)]
                o_ = dst[:, k0:k0 + kc, pi * 512:(pi + 1) * 512]
                if eng == "scalar":
                    fw.op(eng, lambda e, s=s, kc=kc, o_=o_: e.activation(out=o_, in_=s[:, :kc, :], func=AF.Copy), reads=[f"wst{n % nst}"], writes=[key])
                else:
                    fw.op(eng, lambda e, s=s, kc=kc, o_=o_: e.tensor_copy(out=o_, in_=s[:, :kc, :]), reads=[f"wst{n % nst}"], writes=[key])
                n += 1
        fw.barrier()
        ar.reset(m)

    rmsc = [0]

    def rms_scale(xin, gidx, xn_out, rkeys, wkeys, ncol=D):
        pr = rmsc[0] % 2
        rmsc[0] += 1
        jk, sq_ = junk[pr], ssq[pr]
        kj, ks = f"junk{pr}", f"ssq{pr}"
        fw.op("scalar", lambda e: e.activation(out=jk[:, :ncol], in_=xin, func=AF.Square, accum_out=sq_),
              reads=rkeys, writes=[kj, ks])
        fw.op("vector", lambda e: e.tensor_scalar(out=sq_, in0=sq_, scalar1=1.0 / ncol, scalar2=1e-6, op0=ALU.mult, op1=ALU.add),
              reads=[ks], writes=[ks])
        fw.op("scalar", lambda e: e.activation(out=sq_, in_=sq_, func=AF.Sqrt), reads=[ks], writes=[ks])
        fw.op("vector", lambda e: e.reciprocal(out=sq_, in_=sq_), reads=[ks], writes=[ks])
        fw.op("vector", lambda e: e.scalar_tensor_tensor(out=xn_out, in0=xin, scalar=sq_, in1=gsb[:, gidx, :ncol],
                                                         op0=ALU.mult, op1=ALU.mult),
              reads=list(rkeys) + [ks, "gsb"], writes=wkeys)

    def transpose8(src_bf, dstT, rkey, wkey, n=8):
        for half in range((n + 3) // 4):
            p, pk = psum()
            m = min(4, n - half * 4)
            for j in range(m):
                c = half * 4 + j
                fw.op("tensor", lambda e, c=c, j=j, p=p: e.matmul(p[:, j * 128:(j + 1) * 128], lhsT=src_bf[:, c * 128:(c + 1) * 128],
                                                                 rhs=identb, start=True, stop=True),
                      reads=[rkey, "identb"], writes=[pk], inc=(j == m - 1))
            fw.op("vector", lambda e, p=p, half=half, m=m: e.tensor_copy(
                out=dstT[:, half * 4:half * 4 + m, :], in_=p[:, :m * 128].rearrange("p (a b) -> p a b", b=128)),
                reads=[pk], writes=[wkey])

    junk = [ar.alloc("junk", [128, D], F32) for _ in range(2)]; ssq = [ar.alloc("ssq", [128, 1], F32) for _ in range(2)]
    base1 = ar.mark()

    dbg = {}
    V = lambda fn, r, w: fw.op("vector", fn, reads=r, writes=w)
    A_ = lambda fn, r, w: fw.op("scalar", fn, reads=r, writes=w)
    G_ = lambda fn, r, w: fw.op("gpsimd", fn, reads=r, writes=w)

    def mm(out_ap, lhsT, rhs, start, stop, reads, pk, inc):
        fw.op("tensor", lambda e: e.matmul(out_ap, lhsT=lhsT, rhs=rhs, start=start, stop=stop), reads=reads, writes=[pk], inc=inc)

    dcs = ar.alloc("dcs", [128, 8], F32)
    esink = ar.alloc("esink", [128, 16], F32); gqk = ar.alloc("gqk", [128, 256], F32)
    maskb = ar.alloc("maskb", [128, 3, 512], BF16)
    base_persist = ar.mark()
    st32 = ar.alloc("st32", [128, 8192], F32)
    fw.dma("sync", dcs, dcol, writes=["dcs"])
    fw.dma("sync", st32[:, 0:16], sinks, writes=["a"])
    A_(lambda e: e.activation(out=esink, in_=st32[:, 0:16], func=AF.Exp), ["a"], ["esink"])
    fw.dma("sync", st32[:, 1024:1280], gq, writes=["b"])
    fw.dma("sync", st32[:, 2048:2304], gk, writes=["c"])
    V(lambda e: e.tensor_tensor(out=gqk, in0=st32[:, 1024:1280], in1=st32[:, 2048:2304], op=ALU.mult), ["b", "c"], ["gqk"])
    fw.dma("sync", st32[:, 4096:5632].rearrange("p (a c) -> p a c", a=3), masks.rearrange("a p c -> p a c"), writes=["d"])
    V(lambda e: e.tensor_copy(out=maskb, in_=st32[:, 4096:5632].rearrange("p (a c) -> p a c", a=3)), ["d"], ["maskb"])
    fw.barrier()
    ar.reset(base_persist)

    m1 = ar.mark()
    Win = ar.alloc("Win", [128, 8, 4608], BF16)
    load_weights(Win, w_in, 9, 8, "Win")
    QO, KVO, UO, GO = 0, 1024, 1536, 2560
    xt = [ar.alloc("xt", [128, D], F32) for _ in range(4)]
    xnb = [ar.alloc("xnb", [128, D], BF16) for _ in range(4)]
    xnT4 = [ar.alloc("xnT4", [128, 8, 512], BF16) for _ in range(2)]
    uTb4 = [ar.alloc("uTb4", [128, 8, 512], BF16) for _ in range(2)]
    qsq_ = [ar.alloc("qsq", [128, 512], F32) for _ in range(2)]
    qss_ = [ar.alloc("qss", [128, 8], F32) for _ in range(2)]
    hnc = [0]
    qn = [ar.alloc("qn", [128, D], BF16) for _ in range(2)]
    qTb = [ar.alloc("qTb", [128, 8, 128], BF16) for _ in range(2)]
    kf_ = [ar.alloc("kf", [128, 256], F32) for _ in range(2)]
    kn = [ar.alloc("kn", [128, 256], BF16) for _ in range(2)]
    kTb = [ar.alloc("kTb", [128, 2, 128], BF16) for _ in range(2)]
    vab = [ar.alloc("vab", [128, 4, 65], BF16) for _ in range(2)]
    gb = [ar.alloc("gb", [128, 2048], BF16) for _ in range(2)]
    for b in range(2):
        V(lambda e, b=b: e.memset(vab[b][:, :, 64:65], 1.0), [], [f"vab{b}"])

    groups = [[("ctx", xctx[0:128, :], 0, None), ("ctx", xctx[128:256, :], 1, None)]]
    for t in range(0, TP_, 4):
        groups.append([("pre", xpre[(t + q) * 128:(t + q + 1) * 128, :], t + q, None) for q in range(4)])
    for t in range(0, TM_, 4):
        groups.append([("main", xmain[(t + q) * 128:(t + q + 1) * 128, :], TP_ + t + q, t + q) for q in range(4)])

    def headnorm(p, pk, ncol, nh, dst, dkey, gain=None):
        pr = hnc[0] % 2
        hnc[0] += 1
        qsq, qss, kf = qsq_[pr], qss_[pr], kf_[pr]
        kq, ks, kk_ = f"qsq{pr}", f"qss{pr}", f"kf{pr}"
        A_(lambda e: e.activation(out=qsq[:, :ncol], in_=p[:, :ncol], func=AF.Square), [pk], [kq])
        V(lambda e: e.tensor_reduce(out=qss[:, :nh], in_=qsq[:, :ncol].rearrange("p (h d) -> p h d", d=64), axis=AX.X, op=ALU.add), [kq], [ks])
        V(lambda e: e.tensor_scalar(out=qss[:, :nh], in0=qss[:, :nh], scalar1=1.0 / 64, scalar2=1e-6, op0=ALU.mult, op1=ALU.add), [ks], [ks])
        A_(lambda e: e.activation(out=qss[:, :nh], in_=qss[:, :nh], func=AF.Sqrt), [ks], [ks])
        V(lambda e: e.reciprocal(out=qss[:, :nh], in_=qss[:, :nh]), [ks], [ks])
        rb = qss[:, :nh].unsqueeze(2).broadcast_to([128, nh, 64])
        if gain is None:
            V(lambda e: e.tensor_tensor(out=dst.rearrange("p (h d) -> p h d", d=64), in0=p[:, :ncol].rearrange("p (h d) -> p h d", d=64), in1=rb, op=ALU.mult),
              [pk, ks], [dkey])
        else:
            V(lambda e: e.tensor_tensor(out=kf.rearrange("p (h d) -> p h d", d=64), in0=p[:, :ncol].rearrange("p (h d) -> p h d", d=64), in1=rb, op=ALU.mult),
              [pk, ks], [kk_])
            V(lambda e: e.tensor_tensor(out=dst, in0=kf, in1=gain, op=ALU.mult), [kk_, "gqk"], [dkey])

    def tile_gen(gpar, t4, tile):
        kind, src, sidx, midx = tile
        b = t4 % 2
        b4 = t4 % 4
        X4 = xnT4[gpar]
        fw.dma("sync", xt[b4], src, writes=[f"xt{b4}"])
        fw.flush()
        rms_scale(xt[b4], 0, xnb[b4], [f"xt{b4}"], [f"xnb{b4}"])
        yield
        xk = f"xnT{gpar}_{t4}"
        XT = X4[:, :, t4 * 128:(t4 + 1) * 128]
        transpose8(xnb[b4], XT, f"xnb{b4}", xk)
        yield
        if kind == "main":
            qps = []
            for half in range(2):
                p, pk = psum()
                for k in range(8):
                    mm(p, XT[:, k, :], Win[:, k, QO + half * 512:QO + (half + 1) * 512], k == 0, k == 7, [xk, "Win"], pk, k == 7)
                yield
                headnorm(p, pk, 512, 8, qn[b][:, half * 512:(half + 1) * 512], f"qn{b}")
                yield
        if kind in ("ctx", "main"):
            kidx = sidx if kind == "ctx" else 2 + midx
            p, pk = psum()
            for k in range(8):
                mm(p, XT[:, k, :], Win[:, k, KVO:KVO + 512], k == 0, k == 7, [xk, "Win"], pk, k == 7)
            yield
            headnorm(p, pk, 256, 4, kn[b], f"kn{b}", gain=gqk)
            V(lambda e, p=p, b=b: e.tensor_copy(out=vab[b][:, :, 0:64], in_=p[:, 256:512].rearrange("p (h d) -> p h d", d=64)), [pk], [f"vab{b}"])
            fw.defer_dma("sync", v_s[kidx], vab[b], reads=[f"vab{b}"], writes=[("v_s", kidx)])
            yield
        if kind == "main":
            for j in range(4):
                p, pk = psum()
                for k in range(8):
                    mm(p, XT[:, k, :], Win[:, k, GO + j * 512:GO + (j + 1) * 512], k == 0, k == 7, [xk, "Win"], pk, k == 7)
                A_(lambda e, p=p, j=j, b=b: e.activation(out=gb[b][:, j * 512:(j + 1) * 512], in_=p, func=AF.Sigmoid), [pk], [f"gb{b}"])
                yield
            fw.defer_dma("sync", g_s[midx], gb[b], reads=[f"gb{b}"], writes=[("g_s", midx)])
            transpose8(qn[b], qTb[b], f"qn{b}", f"qTb{b}")
            fw.defer_dma("sync", qT_s[midx], qTb[b], reads=[f"qTb{b}"], writes=[("qT_s", midx)])
            yield
        if kind in ("ctx", "main"):
            transpose8(kn[b], kTb[b], f"kn{b}", f"kTb{b}", n=2)
            fw.defer_dma("sync", kT_s[kidx], kTb[b], reads=[f"kTb{b}"], writes=[("kT_s", kidx)])
            yield

    def lockstep(gens):
        alive = True
        while alive:
            alive = False
            for g_ in gens:
                try:
                    next(g_)
                    alive = True
                except StopIteration:
                    pass

    for gi, grp_tiles in enumerate(groups):
        gpar = gi % 2
        X4 = xnT4[gpar]
        xkeys = [f"xnT{gpar}_{t4}" for t4 in range(len(grp_tiles))]
        width = 4 if grp_tiles[0][0] == "pre" else 2
        for t0 in range(0, len(grp_tiles), width):
            lockstep([tile_gen(gpar, t0 + q, grp_tiles[t0 + q]) for q in range(width)])
        if grp_tiles[0][0] in ("pre", "main"):
            s0 = grp_tiles[0][2]
            for ct in range(8):
                p, pk = psum()
                for k in range(8):
                    mm(p, Win[:, k, UO + ct * 128:UO + (ct + 1) * 128], X4[:, k, :], k == 0, k == 7, xkeys + ["Win"], pk, k == 7)
                if ct % 2 == 0:
                    V(lambda e, p=p, ct=ct, gpar=gpar: e.tensor_copy(out=uTb4[gpar][:, ct, :], in_=p), [pk], [f"uTb4{gpar}"])
                else:
                    A_(lambda e, p=p, ct=ct, gpar=gpar: e.activation(out=uTb4[gpar][:, ct, :], in_=p, func=AF.Copy), [pk], [f"uTb4{gpar}"])
            fw.defer_dma("sync", uT_s[s0:s0 + 4].rearrange("n p c t -> p c n t"), uTb4[gpar].rearrange("p c (n t) -> p c n t", t=128),
                         reads=[f"uTb4{gpar}"], writes=[("uT_s", s0)])
    fw.barrier()
    ar.reset(m1)

    if upto <= 1:
        fw.emit()
        return nc
    lamsb = ar.alloc("lamsb", [128, 3, 32], F32)
    fw.dma("sync", lamsb, lam.rearrange("a p g -> p a g"), writes=["lam"])
    smn = ["dt", "th", "rho", "sn", "cs", "ar", "ai", "fr", "fi", "t1", "t2", "t3", "den", "wr", "wi", "w128r", "w128i", "mk", "x2", "lrdt"]
    sm = {n: ar.alloc(n, [128, 32], F32) for n in smn}
    pwr = [ar.alloc("pwr", [128, 32], F32) for _ in range(9)]; pwi = [ar.alloc("pwi", [128, 32], F32) for _ in range(9)]
    Er = ar.alloc("Er", [128, 32, 128], F32); Ei = ar.alloc("Ei", [128, 32, 128], F32)
    Kpad = ar.alloc("Kpad", [128, 8, 8, 128], BF16)
    cR = ar.alloc("cR", [128, 2, 32], F32); SL = ar.alloc("SL", [128, 2, 32], F32)
    base_ssm = ar.mark()
    lr, li, ld = lamsb[:, 0, :], lamsb[:, 1, :], lamsb[:, 2, :]
    K = ["ssm0"]
    A_(lambda e: e.activation(out=sm["dt"], in_=ld, func=AF.Exp), ["lam"], K)
    V(lambda e: e.tensor_tensor(out=sm["th"], in0=li, in1=sm["dt"], op=ALU.mult), K, K)
    V(lambda e: e.tensor_tensor(out=sm["lrdt"], in0=lr, in1=sm["dt"], op=ALU.mult), K, K)
    A_(lambda e: e.activation(out=sm["rho"], in_=sm["lrdt"], func=AF.Exp), K, K)
    for _ in range(5):
        V(lambda e: e.tensor_single_scalar(out=sm["mk"], in_=sm["th"], scalar=math.pi, op=ALU.is_gt), K, K)
        V(lambda e: e.scalar_tensor_tensor(out=sm["th"], in0=sm["mk"], scalar=-2.0 * math.pi, in1=sm["th"], op0=ALU.mult, op1=ALU.add), K, K)
    V(lambda e: e.tensor_scalar(out=sm["t3"], in0=sm["th"], scalar1=0.125, scalar2=None, op0=ALU.mult), K, K)
    V(lambda e: e.tensor_tensor(out=sm["x2"], in0=sm["t3"], in1=sm["t3"], op=ALU.mult), K, K)

    def horner(o, coefs):
        V(lambda e: e.memset(o, coefs[0]), K, K)
        for c in coefs[1:]:
            V(lambda e: e.tensor_tensor(out=o, in0=o, in1=sm["x2"], op=ALU.mult), K, K)
            V(lambda e, c=c: e.tensor_scalar(out=o, in0=o, scalar1=float(c), scalar2=None, op0=ALU.add), K, K)

    def cdouble(sn_, cs_):
        V(lambda e: e.tensor_tensor(out=sm["t1"], in0=sn_, in1=cs_, op=ALU.mult), K, K)
        V(lambda e: e.tensor_tensor(out=sm["t2"], in0=cs_, in1=cs_, op=ALU.mult), K, K)
        V(lambda e: e.tensor_tensor(out=sm["t3"], in0=sn_, in1=sn_, op=ALU.mult), K, K)
        V(lambda e: e.tensor_scalar(out=sn_, in0=sm["t1"], scalar1=2.0, scalar2=None, op0=ALU.mult), K, K)
        V(lambda e: e.tensor_tensor(out=cs_, in0=sm["t2"], in1=sm["t3"], op=ALU.subtract), K, K)

    horner(sm["sn"], [-1 / 39916800.0, 1 / 362880.0, -1 / 5040.0, 1 / 120.0, -1 / 6.0, 1.0])
    V(lambda e: e.tensor_tensor(out=sm["sn"], in0=sm["sn"], in1=sm["t3"], op=ALU.mult), K, K)
    horner(sm["cs"], [-1 / 3628800.0, 1 / 40320.0, -1 / 720.0, 1 / 24.0, -0.5, 1.0])
    for _ in range(3):
        cdouble(sm["sn"], sm["cs"])
    V(lambda e: e.tensor_tensor(out=sm["ar"], in0=sm["rho"], in1=sm["cs"], op=ALU.mult), K, K)
    V(lambda e: e.tensor_tensor(out=sm["ai"], in0=sm["rho"], in1=sm["sn"], op=ALU.mult), K, K)
    V(lambda e: e.tensor_scalar(out=sm["t1"], in0=sm["ar"], scalar1=-1.0, scalar2=None, op0=ALU.add), K, K)
    V(lambda e: e.tensor_tensor(out=sm["den"], in0=lr, in1=lr, op=ALU.mult), K, K)
    V(lambda e: e.tensor_tensor(out=sm["t2"], in0=li, in1=li, op=ALU.mult), K, K)
    V(lambda e: e.tensor_tensor(out=sm["den"], in0=sm["den"], in1=sm["t2"], op=ALU.add), K, K)
    V(lambda e: e.reciprocal(out=sm["den"], in_=sm["den"]), K, K)
    V(lambda e: e.tensor_tensor(out=sm["t2"], in0=sm["t1"], in1=lr, op=ALU.mult), K, K)
    V(lambda e: e.tensor_tensor(out=sm["t3"], in0=sm["ai"], in1=li, op=ALU.mult), K, K)
    V(lambda e: e.tensor_tensor(out=sm["t2"], in0=sm["t2"], in1=sm["t3"], op=ALU.add), K, K)
    V(lambda e: e.tensor_tensor(out=sm["fr"], in0=sm["t2"], in1=sm["den"], op=ALU.mult), K, K)
    V(lambda e: e.tensor_tensor(out=sm["t2"], in0=sm["ai"], in1=lr, op=ALU.mult), K, K)
    V(lambda e: e.tensor_tensor(out=sm["t3"], in0=sm["t1"], in1=li, op=ALU.mult), K, K)
    V(lambda e: e.tensor_tensor(out=sm["t2"], in0=sm["t2"], in1=sm["t3"], op=ALU.subtract), K, K)
    V(lambda e: e.tensor_tensor(out=sm["fi"], in0=sm["t2"], in1=sm["den"], op=ALU.mult), K, K)
    V(lambda e: e.memset(pwr[0], 1.0), K, K)
    V(lambda e: e.memset(pwi[0], 0.0), K, K)
    for k in range(1, 9):
        V(lambda e, k=k: e.tensor_tensor(out=sm["t1"], in0=pwr[k - 1], in1=sm["ar"], op=ALU.mult), K, K)
        V(lambda e, k=k: e.tensor_tensor(out=sm["t2"], in0=pwi[k - 1], in1=sm["ai"], op=ALU.mult), K, K)
        V(lambda e, k=k: e.tensor_tensor(out=pwr[k], in0=sm["t1"], in1=sm["t2"], op=ALU.subtract), K, K)
        V(lambda e, k=k: e.tensor_tensor(out=sm["t1"], in0=pwr[k - 1], in1=sm["ai"], op=ALU.mult), K, K)
        V(lambda e, k=k: e.tensor_tensor(out=sm["t2"], in0=pwi[k - 1], in1=sm["ar"], op=ALU.mult), K, K)
        V(lambda e, k=k: e.tensor_tensor(out=pwi[k], in0=sm["t1"], in1=sm["t2"], op=ALU.add), K, K)
    V(lambda e: e.tensor_scalar(out=sm["t1"], in0=sm["lrdt"], scalar1=8.0, scalar2=None, op0=ALU.mult), K, K)
    A_(lambda e: e.activation(out=sm["rho"], in_=sm["t1"], func=AF.Exp), K, K)
    for _ in range(3):
        cdouble(sm["sn"], sm["cs"])
    V(lambda e: e.memset(Er[:, :, 0:1], 1.0), K, K)
    V(lambda e: e.memset(Ei[:, :, 0:1], 0.0), K, K)
    V(lambda e: e.tensor_copy(out=sm["wr"], in_=sm["cs"]), K, K)
    V(lambda e: e.tensor_copy(out=sm["wi"], in_=sm["sn"]), K, K)
    m0 = ar.mark()
    tA = ar.alloc("tA", [128, 32, 64], F32); tB = ar.alloc("tB", [128, 32, 64], F32)
    for k in range(7):
        n = 1 << k
        wrb = sm["wr"].unsqueeze(2).broadcast_to([128, 32, n]); wib = sm["wi"].unsqueeze(2).broadcast_to([128, 32, n])
        V(lambda e, n=n, wrb=wrb: e.tensor_tensor(out=tA[:, :, :n], in0=Er[:, :, :n], in1=wrb, op=ALU.mult), K, K)
        V(lambda e, n=n, wib=wib: e.tensor_tensor(out=tB[:, :, :n], in0=Ei[:, :, :n], in1=wib, op=ALU.mult), K, K)
        V(lambda e, n=n: e.tensor_tensor(out=Er[:, :, n:2 * n], in0=tA[:, :, :n], in1=tB[:, :, :n], op=ALU.subtract), K, K)
        V(lambda e, n=n, wib=wib: e.tensor_tensor(out=tA[:, :, :n], in0=Er[:, :, :n], in1=wib, op=ALU.mult), K, K)
        V(lambda e, n=n, wrb=wrb: e.tensor_tensor(out=tB[:, :, :n], in0=Ei[:, :, :n], in1=wrb, op=ALU.mult), K, K)
        V(lambda e, n=n: e.tensor_tensor(out=Ei[:, :, n:2 * n], in0=tA[:, :, :n], in1=tB[:, :, :n], op=ALU.add), K, K)
        V(lambda e: e.tensor_tensor(out=sm["t1"], in0=sm["wr"], in1=sm["wr"], op=ALU.mult), K, K)
        V(lambda e: e.tensor_tensor(out=sm["t2"], in0=sm["wi"], in1=sm["wi"], op=ALU.mult), K, K)
        V(lambda e: e.tensor_tensor(out=sm["t3"], in0=sm["wr"], in1=sm["wi"], op=ALU.mult), K, K)
        V(lambda e: e.tensor_tensor(out=sm["wr"], in0=sm["t1"], in1=sm["t2"], op=ALU.subtract), K, K)
        V(lambda e: e.tensor_scalar(out=sm["wi"], in0=sm["t3"], scalar1=2.0, scalar2=None, op0=ALU.mult), K, K)
    V(lambda e: e.tensor_copy(out=sm["w128r"], in_=sm["wr"]), K, K)
    V(lambda e: e.tensor_copy(out=sm["w128i"], in_=sm["wi"]), K, K)
    V(lambda e: e.memset(cR, 0.0), K, K)
    V(lambda e: e.memset(SL, 0.0), K, K)
    fw.barrier()
    ar.reset(m0)
    BTc = ar.alloc("BTc", [128, 32, 2, 16], F32); Cc = ar.alloc("Cc", [128, 2, 32, 16], F32)
    Cfc = ar.alloc("Cfc", [128, 32, 2, 16], F32); Xc2 = [ar.alloc("Xc", [128, 32, 2, 16], F32) for _ in range(2)]
    c1 = ar.alloc("c1", [128, 32, 16], F32); c2 = ar.alloc("c2", [128, 32, 16], F32)
    Cfp = ar.alloc("Cfp", [128, 32, 2, 128], BF16)
    padb2 = [ar.alloc("padb", [128, 32, 2, 128], BF16) for _ in range(2)]; DBsb2 = [ar.alloc("DBsb", [128, 32, 2, 128], BF16) for _ in range(2)]
    fw.dma("sync", BTc, btc, writes=["BTc"])
    fw.dma("sync", Cc, cc.rearrange("a p g c -> p a g c"), writes=["Cc"])
    for q_ in range(2):
        G_(lambda e, q_=q_: e.memset(padb2[q_], 0.0), [], [f"padb{q_}"])
    G_(lambda e: e.memset(Cfp, 0.0), [], ["Cfp"])

    def cmul_compact(dst, src_r, src_i, sr, si, rk, wk, neg_im=False):
        srb = sr.unsqueeze(2).broadcast_to([128, 32, 16]); sib = si.unsqueeze(2).broadcast_to([128, 32, 16])
        V(lambda e: e.tensor_tensor(out=c1, in0=src_r, in1=srb, op=ALU.mult), rk, ["c1"])
        V(lambda e: e.tensor_tensor(out=c2, in0=src_i, in1=sib, op=ALU.mult), rk, ["c2"])
        V(lambda e: e.tensor_tensor(out=dst[:, :, 0, :], in0=c1, in1=c2, op=ALU.subtract), ["c1", "c2"], wk)
        V(lambda e: e.tensor_tensor(out=c1, in0=src_r, in1=sib, op=ALU.mult), rk + wk, ["c1"])
        V(lambda e: e.tensor_tensor(out=c2, in0=src_i, in1=srb, op=ALU.mult), rk + wk, ["c2"])
        if neg_im:
            V(lambda e: e.scalar_tensor_tensor(out=dst[:, :, 1, :], in0=c1, scalar=-1.0, in1=c2, op0=ALU.mult, op1=ALU.subtract), ["c1", "c2"], wk)
        else:
            V(lambda e: e.tensor_tensor(out=dst[:, :, 1, :], in0=c1, in1=c2, op=ALU.add), ["c1", "c2"], wk)

    def scatter(dst_pad, src_c, rk, wk):
        for g2 in range(2):
            for q in range(4):
                blk = 2 * q + g2
                G_(lambda e, g2=g2, q=q, blk=blk: e.tensor_copy(out=dst_pad[g2 * 64:(g2 + 1) * 64, q::4, :, blk * 16:(blk + 1) * 16],
                                                                 in_=src_c[g2 * 64:(g2 + 1) * 64, q::4, :, :]), rk, wk)

    cmul_compact(Cfc, Cc[:, 0], Cc[:, 1], sm["fr"], sm["fi"], ["Cc"], ["Cfc"])
    cmul_compact(Xc2[0], Cc[:, 0], Cc[:, 1], sm["fr"], sm["fi"], ["Cc"], ["Xc0"], neg_im=True)
    scatter(Cfp, Xc2[0], ["Xc0"], ["Cfp"])
    for k in range(8):
        j = 7 - k
        q_ = k % 2
        Xc, padb, DBsb = Xc2[q_], padb2[q_], DBsb2[q_]
        kX, kP, kD = f"Xc{q_}", f"padb{q_}", f"DBsb{q_}"
        cmul_compact(Xc, BTc[:, :, 0, :], BTc[:, :, 1, :], pwr[k], pwi[k], ["BTc"], [kX])
        scatter(padb, Xc, [kX], [kP])
        for r in range(8):
            p, pk = psum()
            n_ = 0
            for gl in range(4):
                for ri in range(2):
                    mm(p[:, 0:128], padb[:, 4 * r + gl, ri, :], Cfp[:, 4 * r + gl, ri, :], n_ == 0, n_ == 7, [kP, "Cfp"], pk, n_ == 7)
                    n_ += 1
            if k == 0:
                V(lambda e, p=p, r=r: e.scalar_tensor_tensor(out=Kpad[:, r, 0, :], in0=identf, scalar=dcs[:, r:r + 1], in1=p[:, 0:128], op0=ALU.mult, op1=ALU.add),
                  [pk, "identf", "dcs"], ["Kpad"])
            else:
                A_(lambda e, p=p, r=r, k=k: e.activation(out=Kpad[:, r, k, :], in_=p[:, 0:128], func=AF.Copy), [pk], ["Kpad"])
        for g4 in range(16):
            p, pk = psum()
            for q in range(4):
                gi = g4 * 4 + q
                mm(p[:, q * 128:(q + 1) * 128], padb[:, gi // 2, gi % 2, :], identb, True, True, [kP, "identb"], pk, q == 3)
            dst_ = DBsb.rearrange("p g r s -> p (g r) s")[:, g4 * 4:(g4 + 1) * 4, :]
            if g4 % 2 == 0:
                V(lambda e, p=p, dst_=dst_: e.tensor_copy(out=dst_, in_=p.rearrange("p (a c) -> p a c", c=128)), [pk], [kD])
            else:
                A_(lambda e, p=p, dst_=dst_: e.activation(out=dst_, in_=p.rearrange("p (a c) -> p a c", c=128), func=AF.Copy), [pk], [kD])
        fw.dma("sync", DB_s[:, :, j].rearrange("r p g a s -> p r g a s"), DBsb.rearrange("p (r g) a s -> p r g a s", r=8), reads=[kD], writes=[("DB_s", j)])
    for j in range(8):
        q_ = j % 2
        Xc, padb = Xc2[q_], padb2[q_]
        kX, kP = f"Xc{q_}", f"padb{q_}"
        cmul_compact(Xc, Cfc[:, :, 0, :], Cfc[:, :, 1, :], pwr[j + 1], pwi[j + 1], ["Cfc"], [kX])
        scatter(padb, Xc, [kX], [kP])
        fw.dma("sync", EC_s[:, :, j].rearrange("r p g a s -> p r g a s"), padb.rearrange("p (r g) a s -> p r g a s", r=8), reads=[kP], writes=[("EC_s", j)])
    fw.barrier()
    ar.reset(base_ssm)

    if upto <= 2:
        fw.emit()
        return nc
    NPS, NMS = NP // 1024, NM // 1024
    DBr2 = [ar.alloc("DBr", [128, 8, 4, 2, 128], BF16) for _ in range(2)]
    ECr2 = [ar.alloc("ECr", [128, 8, 4, 2, 128], BF16) for _ in range(2)]
    uTr = [ar.alloc("uTr", [128, 1024], BF16) for _ in range(2)]
    uTj = [ar.alloc("uTj", [128, 8, 128], BF16) for _ in range(2)]
    zTr = [ar.alloc("zTr", [128, 1024], BF16) for _ in range(2)]
    RB = []
    for b in range(2):
        d = {n: ar.alloc(n, [128, 4, 128], F32) for n in ["t1", "t2", "t3", "t4", "Rr", "Ri"]}
        d["Xr"], d["Xi"] = d["t1"], d["t3"]
        d["Sr"] = ar.alloc("Sr", [128, 4, 130], BF16); d["Si"] = ar.alloc("Si", [128, 4, 130], BF16)
        for n in ["c1", "c2", "c3", "c4"]:
            d[n] = ar.alloc(n, [128, 4], F32)
        RB.append(d)
    ysb = [ar.alloc("ysb", [128, 1024], F32) for _ in range(2)]
    g1b = [ar.alloc("g1b", [128, 1024], F32) for _ in range(2)]
    g2b = g1b
    rcount = 0
    hcount = 0
    def round_gen(rp, st, q_):
        r = 2 * rp + q_
        gsl = slice(4 * r, 4 * r + 4)
        DBr, ECr = DBr2[q_], ECr2[q_]
        kDB, kEC = f"DBr{q_}", f"ECr{q_}"
        is_main = st >= NPS
        ub = q_
        fw.dma("sync", uTr[ub].rearrange("p (n t) -> p n t", t=128), uT_s[8 * st:8 * st + 8, :, r, :].rearrange("n p t -> p n t"),
               writes=[f"uTr{ub}"])
        fw.flush()
        A_(lambda e, ub=ub: e.activation(out=uTj[ub], in_=uTr[ub].rearrange("p (c j) -> p j c", j=8), func=AF.Copy), [f"uTr{ub}"], [f"uTj{ub}"])
        yield
        b = q_
        B = RB[b]
        kb = lambda n, b=b: f"{n}{b}"
        pXr, pkr = psum()
        pXi, pki = psum()
        for ri, (pX, pk) in enumerate(((pXr, pkr), (pXi, pki))):
            for gl in range(4):
                for j in range(8):
                    mm(pX[:, gl * 128:(gl + 1) * 128], DBr[:, j, gl, ri, :], uTj[ub][:, j, :], j == 0, j == 7,
                       [f"uTj{ub}", kDB], pk, (gl == 3 and j == 7))
        yield
        pXr3 = pXr.rearrange("p (a c) -> p a c", c=128); pXi3 = pXi.rearrange("p (a c) -> p a c", c=128)
        Erg, Eig = Er[:, gsl, :], Ei[:, gsl, :]
        V(lambda e, B=B, a=pXr3, t=Erg: e.tensor_tensor(out=B["t1"], in0=a, in1=t, op=ALU.mult), [pkr], [kb("t1")])
        V(lambda e, B=B, a=pXi3, t=Eig: e.tensor_tensor(out=B["t2"], in0=a, in1=t, op=ALU.mult), [pki], [kb("t2")])
        V(lambda e, B=B, a=pXi3, t=Erg: e.tensor_tensor(out=B["t3"], in0=a, in1=t, op=ALU.mult), [pki], [kb("t3")])
        V(lambda e, B=B, a=pXr3, t=Eig: e.tensor_tensor(out=B["t4"], in0=a, in1=t, op=ALU.mult), [pkr], [kb("t4")])
        yield
        G_(lambda e, B=B: e.tensor_tensor(out=B["t1"], in0=B["t1"], in1=B["t2"], op=ALU.add), [kb("t1"), kb("t2")], [kb("t1")])
        G_(lambda e, B=B: e.tensor_tensor(out=B["t3"], in0=B["t3"], in1=B["t4"], op=ALU.subtract), [kb("t3"), kb("t4")], [kb("t3")])
        yield
        for gl in range(4):
            gp = 4 * r + gl
            for nm, xs, ci in (("Rr", "t1", 0), ("Ri", "t3", 1)):
                V(lambda e, B=B, gl=gl, gp=gp, nm=nm, xs=xs, ci=ci: e.tensor_tensor_scan(
                    out=B[nm][:, gl, :], data0=sm["rho"][:, gp:gp + 1].broadcast_to([128, 128]), data1=B[xs][:, gl, :],
                    initial=cR[:, ci, gp:gp + 1], op0=ALU.mult, op1=ALU.add), [kb(xs), ("cR", r)], [kb(nm)])
        yield
        if is_main:
            G_(lambda e, B=B, gsl=gsl: e.tensor_copy(out=B["Sr"][:, :, 0], in_=SL[:, 0, gsl]), [("SL", r)], [kb("Sr")])
            G_(lambda e, B=B, gsl=gsl: e.tensor_copy(out=B["Si"][:, :, 0], in_=SL[:, 1, gsl]), [("SL", r)], [kb("Si")])
        wr4, wi4 = sm["w128r"][:, gsl], sm["w128i"][:, gsl]
        er7, ei7 = Er[:, gsl, 127], Ei[:, gsl, 127]
        Rr7, Ri7 = B["Rr"][:, :, 127], B["Ri"][:, :, 127]
        for (xr_, xi_, dst, negim, key) in ((wr4, wi4, cR, False, "cR"), (er7, ei7, SL, True, "SL")):
            G_(lambda e, B=B, a=Rr7, w=xr_: e.tensor_tensor(out=B["c1"], in0=a, in1=w, op=ALU.mult), [kb("Rr")], [kb("c1")])
            G_(lambda e, B=B, a=Ri7, w=xi_: e.tensor_tensor(out=B["c2"], in0=a, in1=w, op=ALU.mult), [kb("Ri")], [kb("c2")])
            G_(lambda e, B=B, a=Ri7, w=xr_: e.tensor_tensor(out=B["c3"], in0=a, in1=w, op=ALU.mult), [kb("Ri")], [kb("c3")])
            G_(lambda e, B=B, a=Rr7, w=xi_: e.tensor_tensor(out=B["c4"], in0=a, in1=w, op=ALU.mult), [kb("Rr")], [kb("c4")])
            G_(lambda e, B=B, dst=dst, gsl=gsl: e.tensor_tensor(out=dst[:, 0, gsl], in0=B["c1"], in1=B["c2"], op=ALU.subtract),
               [kb("c1"), kb("c2")], [(key, r)])
            if negim:
                V(lambda e, B=B, dst=dst, gsl=gsl: e.scalar_tensor_tensor(out=dst[:, 1, gsl], in0=B["c3"], scalar=-1.0, in1=B["c4"], op0=ALU.mult, op1=ALU.subtract),
                   [kb("c3"), kb("c4")], [(key, r)])
            else:
                G_(lambda e, B=B, dst=dst, gsl=gsl: e.tensor_tensor(out=dst[:, 1, gsl], in0=B["c3"], in1=B["c4"], op=ALU.add),
                   [kb("c3"), kb("c4")], [(key, r)])
        yield
        if not is_main:
            return
        G_(lambda e, B=B, t=Erg: e.tensor_tensor(out=B["t1"], in0=B["Rr"], in1=t, op=ALU.mult), [kb("Rr")], [kb("t1")])
        G_(lambda e, B=B, t=Eig: e.tensor_tensor(out=B["t2"], in0=B["Ri"], in1=t, op=ALU.mult), [kb("Ri")], [kb("t2")])
        yield
        V(lambda e, B=B, t=Erg: e.tensor_tensor(out=B["t3"], in0=B["Ri"], in1=t, op=ALU.mult), [kb("Ri")], [kb("t3")])
        V(lambda e, B=B, t=Eig: e.tensor_tensor(out=B["t4"], in0=B["Rr"], in1=t, op=ALU.mult), [kb("Rr")], [kb("t4")])
        yield
        V(lambda e, B=B: e.tensor_tensor(out=B["Sr"][:, :, 1:129], in0=B["t1"], in1=B["t2"], op=ALU.subtract), [kb("t1"), kb("t2")], [kb("Sr")])
        V(lambda e, B=B: e.scalar_tensor_tensor(out=B["Si"][:, :, 1:129], in0=B["t3"], scalar=-1.0, in1=B["t4"], op0=ALU.mult, op1=ALU.subtract),
          [kb("t3"), kb("t4")], [kb("Si")])
        zb = q_
        hb = q_
        yield
        for h2 in range(2):
            py, pky = psum()
            for j4 in range(4):
                j = 4 * h2 + j4
                o = py[:, j4 * 128:(j4 + 1) * 128]
                nmm = (j + 1) + 8
                n_ = 0
                for k in range(j + 1):
                    mm(o, Kpad[:, r, k, :], uTj[ub][:, j - k, :], n_ == 0, n_ == nmm - 1, [f"uTj{ub}", "Kpad"], pky, False)
                    n_ += 1
                for gl in range(4):
                    for ri, Sn in enumerate(("Sr", "Si")):
                        mm(o, ECr[:, j, gl, ri, :], B[Sn][:, gl, 0:128], n_ == 0, n_ == nmm - 1, [kb(Sn), kEC], pky,
                           (j4 == 3 and n_ == nmm - 1))
                        n_ += 1
            yield
            A_(lambda e, py=py, hb=hb, h2=h2: e.activation(out=ysb[hb].rearrange("p (c j) -> p c j", j=8)[:, :, 4 * h2:4 * h2 + 4],
                                                           in_=py.rearrange("p (j c) -> p c j", c=128), func=AF.Identity), [pky], [f"ysb{hb}"])
        yield
        G_(lambda e, hb=hb: e.tensor_tensor(out=g1b[hb], in0=ysb[hb], in1=ysb[hb], op=ALU.mult), [f"ysb{hb}"], [f"g1b{hb}"])
        G_(lambda e, hb=hb: e.tensor_scalar(out=g1b[hb], in0=g1b[hb], scalar1=0.044715, scalar2=1.0, op0=ALU.mult, op1=ALU.add), [f"g1b{hb}"], [f"g1b{hb}"])
        G_(lambda e, hb=hb: e.tensor_tensor(out=g1b[hb], in0=g1b[hb], in1=ysb[hb], op=ALU.mult), [f"g1b{hb}", f"ysb{hb}"], [f"g1b{hb}"])
        yield
        A_(lambda e, hb=hb: e.activation(out=g2b[hb], in_=g1b[hb], func=AF.Sigmoid, scale=1.5957691216057308), [f"g1b{hb}"], [f"g2b{hb}"])
        yield
        V(lambda e, hb=hb, zb=zb: e.tensor_tensor(out=zTr[zb], in0=ysb[hb], in1=g2b[hb], op=ALU.mult), [f"ysb{hb}", f"g2b{hb}"], [f"zTr{zb}"])
        m8 = 8 * (st - NPS)
        fw.defer_dma("sync", zT_s[m8:m8 + 8, :, r, :].rearrange("n p t -> p n t"), zTr[zb].rearrange("p (n t) -> p n t", t=128),
               reads=[f"zTr{zb}"], writes=[("zT_s", st, r)])

    for rp in range(4):
        for q_ in range(2):
            fw.dma("sync", DBr2[q_], DB_s[2 * rp + q_], writes=[f"DBr{q_}"])
            fw.dma("sync", ECr2[q_], EC_s[2 * rp + q_], writes=[f"ECr{q_}"])
        for st in range(NPS + NMS):
            gens = [round_gen(rp, st, 0), round_gen(rp, st, 1)]
            alive = True
            while alive:
                alive = False
                for g_ in gens:
                    try:
                        next(g_)
                        alive = True
                    except StopIteration:
                        pass
    fw.barrier()
    ar.reset(base_persist)

    if upto <= 3:
        fw.emit()
        return nc
    Wg = ar.alloc("Wg", [128, 8, 2048], BF16); Wo = ar.alloc("Wo", [128, 8, 1024], BF16)
    load_weights(Wg, w_glu, 4, 8, "Wg")
    load_weights(Wo, w_out, 2, 8, "Wo")
    kme = ar.alloc("kme", [128, 2, 128], BF16); vme = ar.alloc("vme", [128, 4, 65], BF16)
    fw.dma("sync", kme, kT_s[0], writes=["kme"]); fw.dma("sync", vme, v_s[0], writes=["vme"])
    qTl = [ar.alloc("qTl", [128, 8, 128], BF16) for _ in range(2)]
    kTl = [ar.alloc("kTl", [128, 2, 128], BF16) for _ in range(3)]
    vl = [ar.alloc("vl", [128, 4, 65], BF16) for _ in range(3)]
    gl_ = [ar.alloc("gl", [128, 2048], BF16) for _ in range(2)]
    zTl = [ar.alloc("zTl", [128, 8, 128], BF16) for _ in range(2)]
    xr = [ar.alloc("xr", [128, D], F32) for _ in range(2)]
    Pc = [ar.alloc("Pc", [128, 512], BF16) for _ in range(4)]
    Pp = [ar.alloc("Pp", [128, 512], BF16) for _ in range(4)]
    Pm = [ar.alloc("Pm", [128, 512], BF16) for _ in range(4)]
    den_ = [ar.alloc("den", [128, 4], F32) for _ in range(4)]
    for b_ in range(4):
        V(lambda e, b_=b_: e.memset(Pm[b_], 0.0), [], [f"Pm{b_}"])
    attn_ = [ar.alloc("attn", [128, D], F32) for _ in range(2)]; An_ = [ar.alloc("An", [128, D], F32) for _ in range(2)]
    sig_ = [ar.alloc("sig", [128, 512], F32) for _ in range(2)]; ssm_ = [ar.alloc("ssm", [128, D], F32) for _ in range(2)]
    Bn_ = [ar.alloc("Bn", [128, D], F32) for _ in range(2)]
    mg_ = [ar.alloc("mg", [128, D], BF16) for _ in range(2)]; mgT_ = [ar.alloc("mgT", [128, 8, 128], BF16) for _ in range(2)]
    h1 = [ar.alloc("h1", [128, D], F32) for _ in range(2)]
    fw.dma("sync", kTl[1], kT_s[1], writes=["kTl1"]); fw.dma("sync", vl[1], v_s[1], writes=["vl1"])
    def s3_loads(i):
        b = i % 2
        jc = 2 + i
        sc = jc % 3
        fw.dma("sync", kTl[sc], kT_s[jc], reads=[("kT_s", jc)], writes=[f"kTl{sc}"])
        fw.dma("sync", vl[sc], v_s[jc], reads=[("v_s", jc)], writes=[f"vl{sc}"])
        fw.dma("sync", qTl[b], qT_s[i], writes=[f"qTl{b}"])
        fw.dma("sync", gl_[b], g_s[i], writes=[f"gl{b}"])
        fw.dma("sync", zTl[b], zT_s[i], writes=[f"zTl{b}"])
        fw.dma("sync", xr[b], xmain[i * 128:(i + 1) * 128, :], writes=[f"xr{b}"])
        fw.flush()

    def s3_A(n):
        i, grp = n // 4, n % 4
        b, pb = i % 2, 2 * (i % 2) + n % 2
        sc, sp = (2 + i) % 3, (1 + i) % 3
        bs, kc = (grp % 2) * 64, grp // 2
        qsel = qTl[b][bs:bs + 64, kc * 4:(kc + 1) * 4, :]
        pS, pkS = psum()
        mm(pS, kTl[sc][bs:bs + 64, kc, :], qsel, True, True, [f"kTl{sc}", f"qTl{b}"], pkS, True)
        A_(lambda e: e.activation(out=Pc[pb], in_=pS, func=AF.Exp, scale=0.125), [pkS], [f"Pc{pb}"])
        G_(lambda e: e.tensor_tensor(out=Pc[pb], in0=Pc[pb], in1=maskb[:, 0, :], op=ALU.mult), [f"Pc{pb}", "maskb"], [f"Pc{pb}"])
        pS2, pkS2 = psum()
        mm(pS2, kTl[sp][bs:bs + 64, kc, :], qsel, True, True, [f"kTl{sp}", f"qTl{b}"], pkS2, True)
        A_(lambda e: e.activation(out=Pp[pb], in_=pS2, func=AF.Exp, scale=0.125), [pkS2], [f"Pp{pb}"])
        mi = 2 if i == 0 else 1
        G_(lambda e: e.tensor_tensor(out=Pp[pb], in0=Pp[pb], in1=maskb[:, mi, :], op=ALU.mult), [f"Pp{pb}", "maskb"], [f"Pp{pb}"])
        pS3, pkS3 = psum()
        mm(pS3[0:16, :], kme[bs:bs + 64, kc, 0:16], qsel, True, True, ["kme", f"qTl{b}"], pkS3, True)
        A_(lambda e: e.activation(out=Pm[pb][0:16, :], in_=pS3[0:16, :], func=AF.Exp, scale=0.125), [pkS3], [f"Pm{pb}"])

    def s3_B(n):
        i, grp = n // 4, n % 4
        b, pb = i % 2, 2 * (i % 2) + n % 2
        sc, sp = (2 + i) % 3, (1 + i) % 3
        attn, kA = attn_[b], f"attn{b}"
        pO, pkO = psum()
        for r in range(4):
            o = pO[:, r * 65:(r + 1) * 65]
            mm(o, Pm[pb][:, r * 128:(r + 1) * 128], vme[:, grp, :], True, False, [f"Pm{pb}", "vme"], pkO, False)
            mm(o, Pp[pb][:, r * 128:(r + 1) * 128], vl[sp][:, grp, :], False, False, [f"Pp{pb}", f"vl{sp}"], pkO, False)
            mm(o, Pc[pb][:, r * 128:(r + 1) * 128], vl[sc][:, grp, :], False, True, [f"Pc{pb}", f"vl{sc}"], pkO, r == 3)
        pO3 = pO[:, 0:260].rearrange("p (r c) -> p r c", c=65)
        den = den_[pb]
        kd = f"den{pb}"
        V(lambda e: e.tensor_tensor(out=den, in0=pO3[:, :, 64], in1=esink[:, grp * 4:(grp + 1) * 4], op=ALU.add), [pkO, "esink"], [kd])
        V(lambda e: e.reciprocal(out=den, in_=den), [kd], [kd])
        V(lambda e: e.tensor_tensor(out=attn[:, grp * 256:(grp + 1) * 256].rearrange("p (r d) -> p r d", d=64), in0=pO3[:, :, 0:64],
                                    in1=den.unsqueeze(2).broadcast_to([128, 4, 64]), op=ALU.mult), [pkO, kd], [kA])

    def s3_tail(i):
        b = i % 2
        attn, An, ssm, Bn, mg, mgT = attn_[b], An_[b], ssm_[b], Bn_[b], mg_[b], mgT_[b]
        kA, kAn, kss, kBn, kmg, kmT = f"attn{b}", f"An{b}", f"ssm{b}", f"Bn{b}", f"mg{b}", f"mgT{b}"
        rms_scale(attn, 1, An, [kA], [kAn])
        yield
        G_(lambda e: e.tensor_tensor(out=An, in0=An, in1=gl_[b][:, 0:1024], op=ALU.mult), [kAn, f"gl{b}"], [kAn])
        for half in range(2):
            pa, pka = psum()
            for k in range(8):
                mm(pa, zTl[b][:, k, :], Wg[:, k, half * 512:(half + 1) * 512], k == 0, k == 7, [f"zTl{b}", "Wg"], pka, k == 7)
            pz, pkz = psum()
            for k in range(8):
                mm(pz, zTl[b][:, k, :], Wg[:, k, 1024 + half * 512:1024 + (half + 1) * 512], k == 0, k == 7, [f"zTl{b}", "Wg"], pkz, k == 7)
            yield
            sig = sig_[half]
            A_(lambda e, pz=pz, sig=sig: e.activation(out=sig, in_=pz, func=AF.Sigmoid), [pkz], [f"sig{half}"])
            V(lambda e, pa=pa, half=half, sig=sig: e.tensor_tensor(out=ssm[:, half * 512:(half + 1) * 512], in0=pa, in1=sig, op=ALU.mult), [pka, f"sig{half}"], [kss])
        yield
        rms_scale(ssm, 2, Bn, [kss], [kBn])
        yield
        G_(lambda e: e.tensor_tensor(out=Bn, in0=Bn, in1=gl_[b][:, 1024:2048], op=ALU.mult), [kBn, f"gl{b}"], [kBn])
        V(lambda e: e.tensor_tensor(out=mg, in0=An, in1=Bn, op=ALU.add), [kAn, kBn], [kmg])
        yield
        transpose8(mg, mgT, kmg, kmT)
        yield
        for half in range(2):
            p, pk = psum()
            for k in range(8):
                mm(p, mgT[:, k, :], Wo[:, k, half * 512:(half + 1) * 512], k == 0, k == 7, [kmT, "Wo"], pk, k == 7)
            V(lambda e, p=p, half=half: e.tensor_tensor(out=h1[b][:, half * 512:(half + 1) * 512], in0=p, in1=xr[b][:, half * 512:(half + 1) * 512], op=ALU.add),
              [pk, f"xr{b}"], [f"h1{b}"])
        fw.defer_dma("sync", h1_s[i * 128:(i + 1) * 128, :], h1[b], reads=[f"h1{b}"], writes=[("h1_s", i)])

    def s3_tile_gen(i):
        s3_loads(i)
        yield
        n0 = 4 * i
        for step in (("A", 0), ("A", 1), ("B", 0), ("A", 2), ("B", 1), ("A", 3), ("B", 2), ("B", 3)):
            (s3_A if step[0] == "A" else s3_B)(n0 + step[1])
            yield
        yield from s3_tail(i)

    for i in range(0, TM_, 2):
        lockstep([s3_tile_gen(i), s3_tile_gen(i + 1)])
    fw.barrier()
    ar.reset(base_persist)

    if upto <= 4:
        fw.emit()
        return nc
    W1 = ar.alloc("W1", [128, 8, 5632], BF16); W2 = ar.alloc("W2", [128, 22, 1024], BF16)
    load_weights(W1, w_f1, 11, 8, "W1")
    load_weights(W2, w_f2, 2, 22, "W2")
    GT = 4
    hl = [ar.alloc("hl", [128, D], F32) for _ in range(2)]
    hn = [ar.alloc("hn", [128, D], BF16) for _ in range(2)]
    hnT = ar.alloc("hnT", [128, 8, GT * 128], BF16)
    sg = [ar.alloc("sg", [128, 512], F32) for _ in range(2)]
    actT = ar.alloc("actT", [128, 22, GT * 128], BF16)
    hres = hl
    ob = junk
    tcount = 0
    for g in range(TM_ // GT):
        for t4 in range(GT):
            i = g * GT + t4
            b = tcount % 2
            tcount += 1
            fw.dma("sync", hl[b], h1_s[i * 128:(i + 1) * 128, :], reads=[("h1_s", i)], writes=[f"hl{b}"])
            fw.flush()
            rms_scale(hl[b], 3, hn[b], [f"hl{b}"], [f"hn{b}"])
            for half in range(2):
                p, pk = psum()
                for jj in range(4):
                    c = half * 4 + jj
                    mm(p[:, jj * 128:(jj + 1) * 128], hn[b][:, c * 128:(c + 1) * 128], identb, True, True, [f"hn{b}", "identb"], pk, jj == 3)
                V(lambda e, p=p, half=half, t4=t4: e.tensor_copy(out=hnT[:, half * 4:half * 4 + 4, t4 * 128:(t4 + 1) * 128],
                                                               in_=p.rearrange("p (a c) -> p a c", c=128)), [pk], [("hnT", t4)])
        hk = [("hnT", t4) for t4 in range(GT)]
        for fc in range(22):
            fp, q2 = fc // 2, fc % 2
            pg, pkg = psum()
            for k in range(8):
                mm(pg, W1[:, k, fp * 512 + q2 * 256:fp * 512 + q2 * 256 + 128], hnT[:, k, :], k == 0, k == 7, hk + ["W1"], pkg, k == 7)
            pu, pku = psum()
            for k in range(8):
                mm(pu, W1[:, k, fp * 512 + q2 * 256 + 128:fp * 512 + q2 * 256 + 256], hnT[:, k, :], k == 0, k == 7, hk + ["W1"], pku, k == 7)
            s_ = sg[fc % 2]
            ks_ = f"sg{fc % 2}"
            A_(lambda e, pg=pg, s_=s_: e.activation(out=s_, in_=pg, func=AF.Sigmoid), [pkg], [ks_])
            V(lambda e, pg=pg, s_=s_: e.tensor_tensor(out=s_, in0=pg, in1=s_, op=ALU.mult), [pkg, ks_], [ks_])
            V(lambda e, pu=pu, s_=s_, fc=fc: e.tensor_tensor(out=actT[:, fc, :], in0=pu, in1=s_, op=ALU.mult), [pku, ks_], [("actT", fc)])
        ak = [("actT", fc) for fc in range(22)]
        for t4 in range(GT):
            i = g * GT + t4
            b = t4 % 2
            fw.dma("sync", hres[b], h1_s[i * 128:(i + 1) * 128, :], reads=[("h1_s", i)], writes=[f"hl{b}"])
            fw.flush()
            for half in range(2):
                p, pk = psum()
                for k in range(22):
                    mm(p, actT[:, k, t4 * 128:(t4 + 1) * 128], W2[:, k, half * 512:(half + 1) * 512], k == 0, k == 21, ak + ["W2"], pk, k == 21)
                V(lambda e, p=p, half=half, b=b: e.tensor_tensor(out=ob[b][:, half * 512:(half + 1) * 512], in0=p, in1=hres[b][:, half * 512:(half + 1) * 512], op=ALU.add),
                  [pk, f"hl{b}"], [f"junk{b}"])
            fw.defer_dma("sync", out[i * 128:(i + 1) * 128, :], ob[b], reads=[f"junk{b}"], writes=[("out", i)])
    fw.emit()
    return nc


def _panels(w, kk):
    n = w.shape[1] // 512
    return np.ascontiguousarray(w.reshape(kk, 128, n, 512).transpose(2, 1, 0, 3))


def prep_shared(inp):
    f = lambda a: np.asarray(a, dtype=np.float32)
    w_in = f(inp["w_in"])[0]
    qcols = []
    for j in range(8):
        for s in range(2):
            head = ((j // 4) * 2 + s) * 4 + (j % 4)
            qcols.extend(range(head * 64, head * 64 + 64))
    w_in_r = np.concatenate([w_in[:, qcols], w_in[:, 1024:1536], w_in[:, 1536:]], axis=1)
    wf1 = f(inp["w_ffn_in"])[0]
    cols = []
    for c in range(22):
        cols.extend(range(c * 128, (c + 1) * 128))
        cols.extend(range(DFF + c * 128, DFF + (c + 1) * 128))
    wf1_r = wf1[:, cols]
    rep = lambda v, n: np.ascontiguousarray(np.broadcast_to(f(v).reshape(1, -1), (128, n)))
    gains = np.stack([rep(inp["norm_mix"][0], D), rep(inp["attn_branch_norm"][0], D), rep(inp["ssm_branch_norm"][0], D), rep(inp["norm_ffn"][0], D)])

    def sp(a):
        return np.ascontiguousarray(f(a).reshape(32, 2, 64).transpose(1, 2, 0).reshape(128, 32))

    lam = np.stack([sp(inp["lam_re"][0]), sp(inp["lam_im"][0]), sp(np.broadcast_to(f(inp["log_dt"])[0][:, None], (64, 64)))])
    bre, bim = f(inp["ssm_b_re"])[0], f(inp["ssm_b_im"])[0]
    def spc(a):
        return a.reshape(32, 2, 64, a.shape[-1]).transpose(1, 2, 0, 3).reshape(128, 32, a.shape[-1])
    btc = np.ascontiguousarray(np.stack([spc(bre), spc(bim)], axis=2))
    cre, cim = f(inp["ssm_c_re"])[0], f(inp["ssm_c_im"])[0]
    cc = np.ascontiguousarray(np.stack([spc(cre.transpose(0, 2, 1)), spc(cim.transpose(0, 2, 1))]))
    kk, qq = np.arange(128)[:, None], np.arange(128)[None, :]
    mcur = np.where(kk <= qq, 1.0, 0.0).astype(np.float32)
    mprev = np.where(kk > qq, 1.0, 0.0).astype(np.float32)
    return dict(
        w_in=_panels(w_in_r, 8), w_glu=_panels(f(inp["w_glu"])[0], 8), w_out=_panels(f(inp["w_out"])[0], 8),
        w_f1=_panels(wf1_r, 8), w_f2=_panels(f(inp["w_ffn_out"])[0], 22), gains=gains,
        gq=rep(np.tile(f(inp["q_norm"])[0], 4), 256), gk=rep(np.tile(f(inp["k_norm"])[0], 4), 256),
        sinks=rep(inp["attn_sinks"][0], 16), ident=np.eye(128, dtype=np.float32), lam=lam, btc=btc, cc=cc,
        dcol=np.ascontiguousarray(f(inp["ssm_d"])[0].reshape(8, 128).T),
    ), mcur, mprev


def prep_core(x_b, meta, h, NM, NP, mcur, mprev):
    xmain = np.ascontiguousarray(x_b[h * NM:(h + 1) * NM])
    xpre = np.zeros((NP, D), np.float32)
    xctx = np.zeros((256, D), np.float32)
    xctx[0:16] = meta
    if h == 0:
        xpre[NP - 16:] = meta
        m0 = np.zeros((128, 128), np.float32)
    else:
        xpre[1008:1024] = meta
        xpre[1024:] = x_b[0:NM]
        xctx[128:256] = x_b[NM - 128:NM]
        m0 = mprev
    masks = np.stack([np.tile(mcur, (1, 4)), np.tile(mprev, (1, 4)), np.tile(m0, (1, 4))]).astype(np.float32)
    return dict(xmain=xmain, xpre=xpre, xctx=xctx, masks=masks)


_NC_CACHE = {}


def kernel(**inputs):
    x = np.asarray(inputs["x"], dtype=np.float32)
    Bsz, S, _ = x.shape
    NM = S // 2
    NP = NM + 1024
    meta = np.asarray(inputs["meta_tokens"], dtype=np.float32)
    shared, mcur, mprev = prep_shared(inputs)
    in_maps = []
    for b in range(Bsz):
        for h in range(2):
            d = dict(shared)
            d.update(prep_core(x[b], meta, h, NM, NP, mcur, mprev))
            in_maps.append(d)
    nc = build(NM, NP)
    res = run_bass_kernel_spmd(nc, in_maps, core_ids=list(range(len(in_maps))))
    outp = np.zeros((Bsz, S, D), np.float32)
    for b in range(Bsz):
        for h in range(2):
            outp[b, h * NM:(h + 1) * NM] = res.results[2 * b + h]["out"]
    return outp
```
